# Optimizing a Trainium2 kernel written in Bass

```python
import math
import jax, jax.numpy as jnp
from jax import lax
import numpy as np


D_MODEL = 1024
BATCH = 2
SEQ = 8192
DEPTH = 2

GRID_W = 64
EPS = 1e-6
Q_BLOCK = 128
ROPE_THETA = 10000.0
NEG_INF = -1e30

N_GROUPS = 4
GROUP_WIDTH = D_MODEL // N_GROUPS
MIX_WIDTH = N_GROUPS * GROUP_WIDTH
D_FF = 256 * ((8 * D_MODEL // 3 + 255) // 256)

NA_HEAD_DIM = 64
NA_HEADS = GROUP_WIDTH // NA_HEAD_DIM
NA_KH = 8
NA_KW = 16
MLA_HEADS = 4
MLA_V = GROUP_WIDTH // MLA_HEADS
MLA_NOPE = 64
MLA_ROPE = 32
MLA_Q_RANK = D_MODEL // 4
MLA_KV_RANK = D_MODEL // 8
DIFF_HEADS = 4
DIFF_V = GROUP_WIDTH // DIFF_HEADS
DIFF_QK = DIFF_V // 2
GQA_HEADS = 4
GQA_KV_HEADS = 2
GQA_DIM = GROUP_WIDTH // GQA_HEADS
T5_BUCKETS = 32
T5_MAX_DIST = 128

A_IN = 3 * NA_HEADS * NA_HEAD_DIM
B_IN = MLA_Q_RANK + MLA_KV_RANK + MLA_ROPE
C_IN = 3 * DIFF_HEADS * DIFF_V
D_IN = (GQA_HEADS + 2 * GQA_KV_HEADS) * GQA_DIM
IN_WIDTH = A_IN + B_IN + C_IN + D_IN
IN_SPLITS = [A_IN, A_IN + B_IN, A_IN + B_IN + C_IN]

kernel_name = 'hybrid_parallel_heads_encoder'


def rms_norm(x, g):
    xf = x.astype(jnp.float32)
    y = xf * lax.rsqrt(jnp.mean(xf * xf, axis=-1, keepdims=True) + EPS)
    return (y * g.astype(jnp.float32)).astype(x.dtype)


def swiglu(h, w_gate, w_up, w_down):
    return (jax.nn.silu(h @ w_gate) * (h @ w_up)) @ w_down


def rope_angles(pos, dim):
    inv = jnp.exp(-math.log(ROPE_THETA) * jnp.arange(0, dim, 2, dtype=jnp.float32) / dim)
    return pos.astype(jnp.float32)[:, None] * inv[None, :]


def apply_rope(x, ang):
    half = x.shape[-1] // 2
    xf = x.astype(jnp.float32)
    x1, x2 = xf[..., :half], xf[..., half:]
    cos, sin = jnp.cos(ang), jnp.sin(ang)
    return jnp.concatenate([x1 * cos - x2 * sin, x1 * sin + x2 * cos], axis=-1).astype(x.dtype)


def axial_rope(x, ang_row, ang_col):
    half = x.shape[-1] // 2
    return jnp.concatenate([apply_rope(x[..., :half], ang_row), apply_rope(x[..., half:], ang_col)], axis=-1)


def t5_bucket(rel):
    half = T5_BUCKETS // 2
    max_exact = half // 2
    n = jnp.abs(rel)
    large = max_exact + (jnp.log(jnp.maximum(n, 1).astype(jnp.float32) / max_exact)
                         / math.log(T5_MAX_DIST / max_exact) * (half - max_exact)).astype(jnp.int32)
    large = jnp.minimum(large, half - 1)
    return jnp.where(rel > 0, half, 0) + jnp.where(n < max_exact, n, large)


def sweep_query_blocks(block_fn, *q_arrays):
    b, s = q_arrays[0].shape[:2]
    nb = s // Q_BLOCK
    blocks = tuple(jnp.swapaxes(a.reshape((b, nb, Q_BLOCK) + a.shape[2:]), 0, 1) for a in q_arrays)
    out = lax.map(lambda a: block_fn(*a), (jnp.arange(nb, dtype=jnp.int32),) + blocks)
    out = jnp.swapaxes(out, 0, 1)
    return out.reshape((b, s) + out.shape[3:])


def neighbourhood_attention(q, k, v, rpb):
    b, s, h, d = q.shape
    rows = s // GRID_W
    kh = min(NA_KH, rows)
    n_cb = GRID_W // NA_KW
    kb = 2 * NA_KW
    r = jnp.arange(rows)
    row_idx = jnp.clip(r - kh // 2, 0, rows - kh)[:, None] + jnp.arange(kh)[None, :]
    cb_start = jnp.clip(jnp.arange(n_cb) * NA_KW - NA_KW // 2, 0, GRID_W - kb)
    col_idx = cb_start[:, None] + jnp.arange(kb)[None, :]
    q_col = jnp.arange(GRID_W).reshape(n_cb, NA_KW)
    q_start = jnp.clip(q_col - NA_KW // 2, 0, GRID_W - NA_KW)
    kc = col_idx[:, None, :]
    valid = (kc >= q_start[..., None]) & (kc < q_start[..., None] + NA_KW)
    dr = row_idx - r[:, None] + NA_KH - 1
    dc = jnp.clip(kc - q_col[..., None] + NA_KW - 1, 0, 2 * NA_KW - 2)
    bias = rpb[:, dr[:, :, None, None, None], dc[None, None]]
    bias = jnp.transpose(bias, (1, 3, 4, 0, 2, 5)).astype(jnp.float32)
    qg = q.reshape(b, rows, n_cb, NA_KW, h, d)
    kg = k.reshape(b, rows, GRID_W, h, d)[:, row_idx][:, :, :, col_idx]
    vg = v.reshape(b, rows, GRID_W, h, d)[:, row_idx][:, :, :, col_idx]
    logits = jnp.einsum('brcqhd,brkcjhd->brcqhkj', qg, kg,
                        preferred_element_type=jnp.float32) * (d ** -0.5) + bias
    logits = jnp.where(valid[:, :, None, None, :], logits, NEG_INF)
    p = jax.nn.softmax(logits.reshape(logits.shape[:-2] + (kh * kb,)), axis=-1)
    p = p.reshape(logits.shape).astype(v.dtype)
    out = jnp.einsum('brcqhkj,brkcjhd->brcqhd', p, vg)
    return out.reshape(b, s, h, d)


def dense_attention(q, k, v):
    scale = q.shape[-1] ** -0.5

    def block(i, qb):
        logits = jnp.einsum('bqhd,bkhd->bhqk', qb, k, preferred_element_type=jnp.float32) * scale
        p = jax.nn.softmax(logits, axis=-1).astype(v.dtype)
        return jnp.einsum('bhqk,bkhd->bqhd', p, v)

    return sweep_query_blocks(block, q)


def gqa_attention(q, k, v):
    b, s, hq, d = q.shape
    hkv = k.shape[2]
    qg = q.reshape(b, s, hkv, hq // hkv, d)
    scale = d ** -0.5

    def block(i, qb):
        logits = jnp.einsum('bqngd,bknd->bngqk', qb, k, preferred_element_type=jnp.float32) * scale
        p = jax.nn.softmax(logits, axis=-1).astype(v.dtype)
        return jnp.einsum('bngqk,bknd->bqngd', p, v)

    return sweep_query_blocks(block, qg).reshape(b, s, hq, d)


def diff_attention(q, k, v, lam, t5_table):
    s = q.shape[1]
    scale = q.shape[-1] ** -0.5
    kpos = jnp.arange(s, dtype=jnp.int32)

    def block(i, qb):
        qpos = i * Q_BLOCK + jnp.arange(Q_BLOCK, dtype=jnp.int32)
        bias = t5_table[t5_bucket(kpos[None, :] - qpos[:, None])]
        bias = jnp.transpose(bias, (2, 0, 1)).astype(jnp.float32)
        logits = jnp.einsum('bqhmd,bkhmd->bhmqk', qb, k,
                            preferred_element_type=jnp.float32) * scale + bias[None, :, None]
        p = jax.nn.softmax(logits, axis=-1)
        w = (p[:, :, 0] - lam * p[:, :, 1]).astype(v.dtype)
        return jnp.einsum('bhqk,bkhd->bqhd', w, v)

    return sweep_query_blocks(block, q)


def neighbourhood_mixer(p, q_norm, k_norm, rpb, beta):
    b, s, _ = p.shape
    p = p.reshape(b, s, 3, NA_HEADS, NA_HEAD_DIM)
    q = rms_norm(p[:, :, 0], q_norm)
    k = rms_norm(p[:, :, 1], k_norm)
    y = neighbourhood_attention(q, k, p[:, :, 2], rpb)
    return rms_norm(y.reshape(b, s, GROUP_WIDTH), beta)


def mla_mixer(p, q_lat_norm, w_uq, kv_lat_norm, w_ukv, q_norm, k_norm, beta, ang):
    b, s, _ = p.shape
    c_q, c_kv, k_rope = jnp.split(p, [MLA_Q_RANK, MLA_Q_RANK + MLA_KV_RANK], axis=-1)
    q = (rms_norm(c_q, q_lat_norm) @ w_uq).reshape(b, s, MLA_HEADS, MLA_NOPE + MLA_ROPE)
    kv = (rms_norm(c_kv, kv_lat_norm) @ w_ukv).reshape(b, s, MLA_HEADS, MLA_NOPE + MLA_V)
    k_nope, v = kv[..., :MLA_NOPE], kv[..., MLA_NOPE:]
    k_rope = jnp.broadcast_to(k_rope[:, :, None, :], (b, s, MLA_HEADS, MLA_ROPE))
    k = jnp.concatenate([k_nope, k_rope], axis=-1)
    q = rms_norm(q, q_norm)
    k = rms_norm(k, k_norm)
    q = jnp.concatenate([q[..., :MLA_NOPE], apply_rope(q[..., MLA_NOPE:], ang)], axis=-1)
    k = jnp.concatenate([k[..., :MLA_NOPE], apply_rope(k[..., MLA_NOPE:], ang)], axis=-1)
    y = dense_attention(q, k, v)
    return rms_norm(y.reshape(b, s, GROUP_WIDTH), beta)


def diff_mixer(p, q_norm, k_norm, lam_params, subln, t5_table, lambda_init):
    b, s, _ = p.shape
    pq, pk, pv = jnp.split(p, 3, axis=-1)
    q = rms_norm(pq.reshape(b, s, DIFF_HEADS, 2, DIFF_QK), q_norm)
    k = rms_norm(pk.reshape(b, s, DIFF_HEADS, 2, DIFF_QK), k_norm)
    v = pv.reshape(b, s, DIFF_HEADS, DIFF_V)
    lp = lam_params.astype(jnp.float32)
    lam = jnp.exp(jnp.sum(lp[0] * lp[1])) - jnp.exp(jnp.sum(lp[2] * lp[3])) + lambda_init
    y = diff_attention(q, k, v, lam, t5_table)
    y = rms_norm(y, subln) * (1.0 - lambda_init)
    return y.reshape(b, s, GROUP_WIDTH)


def gqa_mixer(p, q_norm, k_norm, beta, ang_row, ang_col):
    b, s, _ = p.shape
    nq = GQA_HEADS * GQA_DIM
    nk = GQA_KV_HEADS * GQA_DIM
    pq, pk, pv = jnp.split(p, [nq, nq + nk], axis=-1)
    q = rms_norm(pq.reshape(b, s, GQA_HEADS, GQA_DIM), q_norm)
    k = rms_norm(pk.reshape(b, s, GQA_KV_HEADS, GQA_DIM), k_norm)
    v = pv.reshape(b, s, GQA_KV_HEADS, GQA_DIM)
    q = axial_rope(q, ang_row, ang_col)
    k = axial_rope(k, ang_row, ang_col)
    y = gqa_attention(q, k, v)
    return rms_norm(y.reshape(b, s, GROUP_WIDTH), beta)


def diff_lambda_init(layer):
    return 0.8 - 0.6 * math.exp(-0.3 * layer)


def setup_inputs(seed: int = 0) -> dict:
    key = jax.random.key(seed)
    ks = iter(jax.random.split(key, 64))
    L = DEPTH

    def nrm(shape, scale):
        return scale * jax.random.normal(next(ks), shape, jnp.float32)

    def gain(shape):
        return 1.0 + nrm(shape, 0.05)

    return {
        'x': nrm((BATCH, SEQ, D_MODEL), 1.0),
        'ffn1_norm': gain((L, D_MODEL)),
        'ffn1_w_gate': nrm((L, D_MODEL, D_FF), D_MODEL ** -0.5),
        'ffn1_w_up': nrm((L, D_MODEL, D_FF), D_MODEL ** -0.5),
        'ffn1_w_down': nrm((L, D_FF, D_MODEL), D_FF ** -0.5),
        'mix_norm': gain((L, D_MODEL)),
        'w_in': nrm((L, D_MODEL, IN_WIDTH), D_MODEL ** -0.5),
        'na_q_norm': gain((L, NA_HEAD_DIM)),
        'na_k_norm': gain((L, NA_HEAD_DIM)),
        'na_rpb': nrm((L, NA_HEADS, 2 * NA_KH - 1, 2 * NA_KW - 1), 0.1),
        'na_beta': gain((L, GROUP_WIDTH)),
        'mla_q_lat_norm': gain((L, MLA_Q_RANK)),
        'mla_w_uq': nrm((L, MLA_Q_RANK, MLA_HEADS * (MLA_NOPE + MLA_ROPE)), MLA_Q_RANK ** -0.5),
        'mla_kv_lat_norm': gain((L, MLA_KV_RANK)),
        'mla_w_ukv': nrm((L, MLA_KV_RANK, MLA_HEADS * (MLA_NOPE + MLA_V)), MLA_KV_RANK ** -0.5),
        'mla_q_norm': gain((L, MLA_NOPE + MLA_ROPE)),
        'mla_k_norm': gain((L, MLA_NOPE + MLA_ROPE)),
        'mla_beta': gain((L, GROUP_WIDTH)),
        'diff_q_norm': gain((L, DIFF_QK)),
        'diff_k_norm': gain((L, DIFF_QK)),
        'diff_lambda': nrm((L, 4, DIFF_QK), 0.1),
        'diff_subln': gain((L, DIFF_V)),
        'gqa_q_norm': gain((L, GQA_DIM)),
        'gqa_k_norm': gain((L, GQA_DIM)),
        'gqa_beta': gain((L, GROUP_WIDTH)),
        'w_out': nrm((L, MIX_WIDTH, D_MODEL), MIX_WIDTH ** -0.5),
        'ffn2_norm': gain((L, D_MODEL)),
        'ffn2_w_gate': nrm((L, D_MODEL, D_FF), D_MODEL ** -0.5),
        'ffn2_w_up': nrm((L, D_MODEL, D_FF), D_MODEL ** -0.5),
        'ffn2_w_down': nrm((L, D_FF, D_MODEL), D_FF ** -0.5),
        'final_norm': gain((L, D_MODEL)),
        't5_bias': nrm((T5_BUCKETS, DIFF_HEADS), 0.1),
    }


def reference(x, ffn1_norm, ffn1_w_gate, ffn1_w_up, ffn1_w_down, mix_norm, w_in,
              na_q_norm, na_k_norm, na_rpb, na_beta,
              mla_q_lat_norm, mla_w_uq, mla_kv_lat_norm, mla_w_ukv, mla_q_norm, mla_k_norm, mla_beta,
              diff_q_norm, diff_k_norm, diff_lambda, diff_subln,
              gqa_q_norm, gqa_k_norm, gqa_beta, w_out,
              ffn2_norm, ffn2_w_gate, ffn2_w_up, ffn2_w_down, final_norm, t5_bias):
    s = x.shape[1]
    t = jnp.arange(s, dtype=jnp.int32)
    ang_seq = rope_angles(t, MLA_ROPE)[None, :, None, :]
    ang_row = rope_angles(t // GRID_W, GQA_DIM // 2)[None, :, None, :]
    ang_col = rope_angles(t % GRID_W, GQA_DIM // 2)[None, :, None, :]
    for l in range(DEPTH):
        x = x + 0.5 * swiglu(rms_norm(x, ffn1_norm[l]), ffn1_w_gate[l], ffn1_w_up[l], ffn1_w_down[l])
        h = rms_norm(x, mix_norm[l])
        pa, pb, pc, pd = jnp.split(h @ w_in[l], IN_SPLITS, axis=-1)
        ya = neighbourhood_mixer(pa, na_q_norm[l], na_k_norm[l], na_rpb[l], na_beta[l])
        yb = mla_mixer(pb, mla_q_lat_norm[l], mla_w_uq[l], mla_kv_lat_norm[l], mla_w_ukv[l],
                       mla_q_norm[l], mla_k_norm[l], mla_beta[l], ang_seq)
        yc = diff_mixer(pc, diff_q_norm[l], diff_k_norm[l], diff_lambda[l], diff_subln[l],
                        t5_bias, diff_lambda_init(l))
        yd = gqa_mixer(pd, gqa_q_norm[l], gqa_k_norm[l], gqa_beta[l], ang_row, ang_col)
        x = x + jnp.concatenate([ya, yb, yc, yd], axis=-1) @ w_out[l]
        x = x + 0.5 * swiglu(rms_norm(x, ffn2_norm[l]), ffn2_w_gate[l], ffn2_w_up[l], ffn2_w_down[l])
        x = rms_norm(x, final_norm[l])
    return x
```

```python
import contextlib
import math
import numpy as np
import ml_dtypes
import concourse.bass as bass
import concourse.mybir as mybir
from concourse.bass_utils import run_bass_kernel_spmd

F32 = mybir.dt.float32
BF16 = mybir.dt.bfloat16
AF = mybir.ActivationFunctionType
ALU = mybir.AluOpType

_UID = [0]


def _uid():
    _UID[0] += 1
    return _UID[0]

ENGS = ("pe", "act", "dve", "pool", "sp")
EPOCH = 16000
SAME_ENGINE_SYNC = True


class Res:
    __slots__ = ("name", "writer", "readers")

    def __init__(self, name=""):
        self.name = name
        self.writer = None
        self.readers = {}


class Op:
    __slots__ = ("eng", "fn", "deps", "dma_key", "signaled", "count", "semid", "idx", "inc")


class SemState:
    def __init__(self, stack):
        self.stack = stack
        self.cnt = {e: 0 for e in ENGS}
        self.dcnt = {}
        self.sems = {}


class Sched:
    def __init__(self, nc, semstate=None):
        self.nc = nc
        self.ops = []
        self.by_eng = {e: [] for e in ENGS}
        self.semstate = semstate
        self.last_dma = {}

    def drain(self):
        op = self.add("sp", None)
        op.deps = list(self.last_dma.values())
        return op

    def add(self, eng, fn, reads=(), writes=(), dma_key=None, inc=16):
        op = Op()
        op.inc = inc
        op.eng = eng
        op.fn = fn
        op.dma_key = dma_key
        op.signaled = dma_key is not None
        op.count = 0
        op.semid = None
        op.idx = len(self.ops)
        deps = {}
        for r in reads:
            if r.writer is not None:
                deps[r.writer.idx] = r.writer
        for w in writes:
            if w.writer is not None:
                deps[w.writer.idx] = w.writer
            for rd in w.readers.values():
                deps[rd.idx] = rd
        dl = []
        for d in deps.values():
            if d.dma_key is not None or dma_key is not None:
                need = True
            elif d.eng == eng:
                need = (eng != "pe") and SAME_ENGINE_SYNC
            else:
                need = True
            if need:
                d.signaled = True
                dl.append(d)
        op.deps = dl
        for r in reads:
            key = eng if dma_key is None else ("dma", op.idx)
            r.readers[key] = op
        for w in writes:
            w.writer = op
            w.readers = {}
        self.ops.append(op)
        self.by_eng[eng].append(op)
        if dma_key is not None:
            self.last_dma[dma_key] = op
        return op

    def emit(self, stack):
        nc = self.nc
        ss = self.semstate if self.semstate is not None else SemState(stack)
        cnt = ss.cnt
        dcnt = ss.dcnt
        for op in self.ops:
            if op.dma_key is not None:
                dcnt[op.dma_key] = dcnt.get(op.dma_key, 0) + op.inc
                op.count = dcnt[op.dma_key]
                op.semid = ("d", op.dma_key)
            elif op.signaled:
                c = cnt[op.eng]
                cnt[op.eng] = c + 1
                op.semid = ("e", op.eng, c // EPOCH)
                op.count = (c % EPOCH) + 1
        self.check(ss)
        sems = ss.sems
        for op in self.ops:
            if op.semid is not None and op.semid not in sems:
                sems[op.semid] = ss.stack.enter_context(nc.semaphore("s%d" % len(sems)))
        self.nsems = len(sems)
        block = stack.enter_context(nc.Block())
        engobj = {"pe": block.tensor, "act": block.scalar, "dve": block.vector,
                  "pool": block.gpsimd, "sp": block.sync}

        def make(ename):
            ops = self.by_eng[ename]

            def body(e):
                waited = {}
                wepoch = {}
                for op in ops:
                    for d in op.deps:
                        sid = d.semid
                        c = d.count
                        if sid[0] == "e" and wepoch.get(sid[1], -1) > sid[2]:
                            continue
                        if waited.get(sid, 0) >= c:
                            continue
                        e.wait_ge(sems[sid], c)
                        waited[sid] = c
                        if sid[0] == "e":
                            wepoch[sid[1]] = max(wepoch.get(sid[1], -1), sid[2])
                    if op.fn is None:
                        continue
                    ins = op.fn(e)
                    if op.dma_key is not None and op.inc == 1:
                        ins.then_inc(sems[op.semid])
                    elif op.dma_key is not None:
                        ins.then_inc(sems[op.semid], 16)
                    elif op.signaled:
                        ins.then_inc(sems[op.semid], 1)
            return body

        for ename in ENGS:
            if self.by_eng[ename]:
                engobj[ename](make(ename))


def _sched_check(self, ss):
    val = dict(getattr(ss, "val", {}))
    pos = {e: 0 for e in ENGS}
    progress = True
    while progress:
        progress = False
        for e in ENGS:
            ops = self.by_eng[e]
            while pos[e] < len(ops):
                op = ops[pos[e]]
                if all(val.get(d.semid, 0) >= d.count and (d.semid[0] != "e" or True) for d in op.deps):
                    if op.semid is not None:
                        if op.semid[0] == "d":
                            val[op.semid] = val.get(op.semid, 0) + op.inc
                        else:
                            val[op.semid] = val.get(op.semid, 0) + 1
                        assert val[op.semid] == op.count or op.semid[0] == "d", (op.semid, val[op.semid], op.count)
                    pos[e] += 1
                    progress = True
                else:
                    break
    stuck = {e: pos[e] for e in ENGS if pos[e] < len(self.by_eng[e])}
    assert not stuck, "deadlock in schedule: %r" % stuck
    ss.val = val
    print("sched ok: ops=%d per-eng=%r" % (len(self.ops), {e: len(self.by_eng[e]) for e in ENGS}))


Sched.check = _sched_check


class Ring:
    def __init__(self, stack, nc, name, n, shape, dtype, psum=False):
        self.bufs = []
        for i in range(n):
            nm = "r_%s%d" % (name, i)
            tn = "%s_%d" % (nm, _uid())
            if psum:
                t = stack.enter_context(nc.psum_tensor(tn, shape, dtype))
            else:
                t = stack.enter_context(nc.sbuf_tensor(tn, shape, dtype))
            self.bufs.append((t, Res(nm), nm))
        self.i = 0

    def next(self):
        b = self.bufs[self.i % len(self.bufs)]
        self.i += 1
        return b


D = 1024
SEQ = 8192
BATCH = 2
DEPTH = 2
DFF = 2816
NCORE = 8
T = 2048
NT = 4
TS = 512
EPS = 1e-6
NG = 59
G_FFN1, G_MIX, G_NAQ, G_NAK, G_QLAT, G_KVLAT, G_MQ, G_MK, G_DQ, G_DK, G_GQ, G_GK = \
    0, 8, 16, 17, 18, 20, 21, 22, 23, 24, 25, 26
G_BETA, G_FFN2, G_FIN = 27, 43, 51
WS_UQ, WS_KN, WS_V, WS_KR = 0, 768, 1152, 1408
WSM = 2176
C_ONES, C_BD64, C_BD32, C_RB, C_RD, C_ID = 0, 1, 2, 3, 4, 5
QROWS = 1152
KROWS = 1024
NVU = 14


def lambda_init(l):
    return 0.8 - 0.6 * math.exp(-0.3 * l)


class Ctx:
    pass


def _sb(st, nc, name, shape, dt):
    return st.enter_context(nc.sbuf_tensor("sb_%s_%d" % (name, _uid()), shape, dt))


class WStream:
    def __init__(self, S, ring, pf):
        self.S = S
        self.ring = ring
        self.pf = pf
        self.items = []
        self.issued = 0
        self.handles = []

    def plan(self, src_ap, ncols, part=128):
        self.items.append((src_ap, ncols, part))
        return len(self.items) - 1

    def get(self, i):
        while self.issued < min(len(self.items), i + 1 + self.pf):
            src, ncols, part = self.items[self.issued]
            buf, res, nm = self.ring.next()
            self.S.add("pool", (lambda e, buf=buf, src=src, ncols=ncols, part=part:
                                e.dma_start(out=buf[0:part, 0:ncols], in_=src)),
                       writes=[res], dma_key=nm)
            self.handles.append((buf, res))
            self.issued += 1
        return self.handles[i]


def rms_feat(c, srcs, P, ones_idx, n, gain_cols, outs):
    S = c.S
    ones = c.cmat[0:P, ones_idx * 128: ones_idx * 128 + P]
    ps, psr, _ = c.PS.next()
    last = len(srcs) - 1
    sc = float(n) ** -0.5
    for i, (src, sres) in enumerate(srcs):
        sq, sqr, _ = c.SQ.next()
        S.add("act", (lambda e, sq=sq, src=src: e.activation(out=sq[0:P, :], in_=src, func=AF.Square, scale=sc)),
              reads=[sres], writes=[sqr])
        S.add("pe", (lambda e, sq=sq, i=i: e.matmul(ps[0:P, :], ones, sq[0:P, :], start=(i == 0), stop=(i == last))),
              reads=[sqr, c.cmat_r], writes=[psr])
    rs, rsr, _ = c.RS.next()
    S.add("act", lambda e: e.activation(out=rs[0:P, :], in_=ps[0:P, :], func=AF.Ln, bias=c.eps[0:P, 0:1]),
          reads=[psr, c.eps_r], writes=[rsr])
    S.add("act", lambda e: e.activation(out=rs[0:P, :], in_=rs[0:P, :], func=AF.Exp, scale=-0.5),
          reads=[rsr], writes=[rsr])
    for i, ((src, sres), (out, ores)) in enumerate(zip(srcs, outs)):
        g = c.gains[0:P, gain_cols[i]:gain_cols[i] + 1]
        S.add("dve", (lambda e, src=src, out=out, g=g: e.scalar_tensor_tensor(
            out=out, in0=src, scalar=g, in1=rs[0:P, :], op0=ALU.mult, op1=ALU.mult)),
            reads=[sres, rsr, c.gains_r], writes=[ores])


def rope_feat(c, qn, qnr, P, r_idx, cos_ap, sin_ap, tab_r, out, outr):
    S = c.S
    R = c.cmat[0:P, r_idx * 128: r_idx * 128 + P]
    ps, psr, _ = c.PS.next()
    S.add("pe", lambda e: e.matmul(ps[0:P, :], R, qn, start=True, stop=True), reads=[qnr, c.cmat_r], writes=[psr])
    t1, t1r, _ = c.TF.next()
    t2, t2r, _ = c.TF.next()
    S.add("pool", lambda e: e.tensor_tensor(out=t1[0:P, :], in0=qn, in1=cos_ap, op=ALU.mult), reads=[qnr, tab_r], writes=[t1r])
    S.add("dve", lambda e: e.tensor_tensor(out=t2[0:P, :], in0=ps[0:P, :], in1=sin_ap, op=ALU.mult), reads=[psr, tab_r], writes=[t2r])
    S.add("pool", lambda e: e.tensor_tensor(out=out, in0=t1[0:P, :], in1=t2[0:P, :], op=ALU.add), reads=[t1r, t2r], writes=[outr])


def norm_x_tile(c, t, gcol0):
    hT, hTr, _ = c.HT.next()
    srcs = [(c.xT[:, k * T + t * TS: k * T + (t + 1) * TS], c.x_r[t]) for k in range(8)]
    outs = [(hT[:, k * TS:(k + 1) * TS], hTr) for k in range(8)]
    rms_feat(c, srcs, 128, C_ONES, D, [gcol0 + k for k in range(8)], outs)
    return hT, hTr


def ffn_tile(c, t, hT, hTr, ws, base):
    S = c.S
    for b in range(11):
        wg, wgr = ws.get(base + 2 * b)
        wu, wur = ws.get(base + 2 * b + 1)
        for jj in range(2):
            j = 2 * b + jj
            pg, pgr, _ = c.PS.next()
            pu, pur, _ = c.PS.next()
            for k in range(8):
                S.add("pe", (lambda e, k=k, pg=pg, wg=wg, jj=jj: e.matmul(
                    pg[:, :], wg[:, k * 256 + jj * 128: k * 256 + jj * 128 + 128], hT[:, k * TS:(k + 1) * TS],
                    start=(k == 0), stop=(k == 7))), reads=[wgr, hTr], writes=[pgr])
            for k in range(8):
                S.add("pe", (lambda e, k=k, pu=pu, wu=wu, jj=jj: e.matmul(
                    pu[:, :], wu[:, k * 256 + jj * 128: k * 256 + jj * 128 + 128], hT[:, k * TS:(k + 1) * TS],
                    start=(k == 0), stop=(k == 7))), reads=[wur, hTr], writes=[pur])
            sg, sgr, _ = c.TF.next()
            S.add("act", (lambda e, sg=sg, pg=pg: e.activation(out=sg[:, :], in_=pg[:, :], func=AF.Silu)),
                  reads=[pgr], writes=[sgr])
            S.add("dve", (lambda e, sg=sg, pu=pu, j=j: e.tensor_tensor(
                out=c.actT[:, j * TS:(j + 1) * TS], in0=sg[:, :], in1=pu[:, :], op=ALU.mult)),
                reads=[sgr, pur], writes=[c.act_r[j]])
    for m in range(8):
        wd, wdr = ws.get(base + 22 + m)
        pd, pdr, _ = c.PS.next()
        for j in range(22):
            S.add("pe", (lambda e, j=j, pd=pd, wd=wd: e.matmul(
                pd[:, :], wd[:, j * 128:(j + 1) * 128], c.actT[:, j * TS:(j + 1) * TS],
                start=(j == 0), stop=(j == 21))), reads=[wdr, c.act_r[j]], writes=[pdr])
        xs = c.xT[:, m * T + t * TS: m * T + (t + 1) * TS]
        S.add("dve", (lambda e, pd=pd, xs=xs: e.scalar_tensor_tensor(
            out=xs, in0=pd[:, :], scalar=0.5, in1=xs, op0=ALU.mult, op1=ALU.add)),
            reads=[pdr, c.x_r[t]], writes=[c.x_r[t]])


def plan_ffn(ws, wg, wu, wd):
    base = len(ws.items)
    for b in range(11):
        ws.plan(wg[b], 2048)
        ws.plan(wu[b], 2048)
    for m in range(8):
        ws.plan(wd[m], 2816)
    return base


def common_setup(c, st, nc, S, gains_d, lconst_d, cmat_d):
    c.S = S
    c.nc = nc
    c.gains = _sb(st, nc, "gains", [128, NG], F32)
    c.gains_r = Res("gains")
    c.lconst = _sb(st, nc, "lconst", [128, 2], F32)
    c.lconst_r = Res("lconst")
    c.cmat = _sb(st, nc, "cmat", [128, 6 * 128], BF16)
    c.cmat_r = Res("cmat")
    c.eps = _sb(st, nc, "eps", [128, 1], F32)
    c.eps_r = Res("eps")
    S.add("sp", lambda e: e.dma_start(out=c.gains[:, :], in_=gains_d), writes=[c.gains_r], dma_key="gains")
    S.add("sp", lambda e: e.dma_start(out=c.lconst[:, :], in_=lconst_d), writes=[c.lconst_r], dma_key="lconst")
    S.add("pool", lambda e: e.dma_start(out=c.cmat[:, :], in_=cmat_d), writes=[c.cmat_r], dma_key="cmat")
    S.add("dve", lambda e: e.memset(c.eps[:, :], EPS), writes=[c.eps_r])


def load_x(c, st, nc, S, x_d):
    c.xT = _sb(st, nc, "xT", [128, 8 * T], F32)
    c.x_r = [Res("x%d" % t) for t in range(NT)]
    for k in range(8):
        S.add("sp", (lambda e, k=k: e.dma_start(out=c.xT[:, k * T:(k + 1) * T], in_=x_d[:, k * T:(k + 1) * T])),
              writes=c.x_r, dma_key="xT")


def store_x(c, S, xo_d, xo_r):
    for k in range(8):
        S.add("sp", (lambda e, k=k: e.dma_start(out=xo_d[:, k * T:(k + 1) * T], in_=c.xT[:, k * T:(k + 1) * T])),
              reads=c.x_r, writes=[xo_r], dma_key="xT")


def emit_phase_a(c, st, nc, S, d):
    c.PS = Ring(st, nc, "ps", 8, [128, 512], F32, psum=True)
    c.SQ = Ring(st, nc, "sq", 3, [128, 512], BF16)
    c.RS = Ring(st, nc, "rs", 2, [128, 512], F32)
    c.TF = Ring(st, nc, "tf", 4, [128, 512], F32)
    c.HT = Ring(st, nc, "hT", 2, [128, 8 * TS], BF16)
    c.actT = _sb(st, nc, "actT", [128, 22 * TS], BF16)
    c.act_r = [Res("act%d" % j) for j in range(22)]
    wring = Ring(st, nc, "wb", 6, [128, 2816], BF16)
    ws = WStream(S, wring, 4)
    wsm = _sb(st, nc, "wsm", [128, WSM], BF16)
    wsm_r = Res("wsm")
    S.add("pool", lambda e: e.dma_start(out=wsm[:, :], in_=d["wsm"]), writes=[wsm_r], dma_key="wsm")
    tabs = _sb(st, nc, "tabs", [128, 4 * T], F32)
    tab_r = Res("tabs")
    for i in range(4):
        S.add("sp", (lambda e, i=i: e.dma_start(out=tabs[:, i * T:(i + 1) * T], in_=d["tabs"][i])),
              writes=[tab_r], dma_key="tabs")
    for col, sc in ((G_NAQ, 64 ** -0.5), (G_MQ, 96 ** -0.5), (G_DQ, 32 ** -0.5), (G_GQ, 64 ** -0.5)):
        S.add("dve", (lambda e, col=col, sc=sc: e.tensor_scalar(
            out=c.gains[:, col:col + 1], in0=c.gains[:, col:col + 1], scalar1=float(sc), scalar2=None, op0=ALU.mult)),
            reads=[c.gains_r], writes=[c.gains_r])

    bases_ffn = [plan_ffn(ws, d["wg"], d["wu"], d["wd"]) for t in range(NT)]
    bases_in = []
    for t in range(NT):
        bases_in.append(len(ws.items))
        for b in range(7):
            ws.plan(d["winf"][b], 2048)
        for vb in range(2):
            ws.plan(d["winv"][vb], 2560)

    import os
    n1 = int(os.environ.get("KA_N1", NT))
    n2 = int(os.environ.get("KA_N2", NT))
    for t in range(n1):
        hT, hTr = norm_x_tile(c, t, G_FFN1)
        ffn_tile(c, t, hT, hTr, ws, bases_ffn[t])

    STG = Ring(st, nc, "stg", 6, [128, 512], BF16)
    QN = Ring(st, nc, "qn", 3, [128, 512], BF16)
    cqn = _sb(st, nc, "cqn", [128, 2 * TS], BF16)
    cqn_r = Res("cqn")
    ckvn = _sb(st, nc, "ckvn", [128, TS], BF16)
    ckvn_r = Res("ckvn")
    vst = _sb(st, nc, "vst", [128, 4, NVU, 65], BF16)
    vst_r = Res("vst")
    S.add("pool", lambda e: e.memset(vst[:, :, :, :], 1.0), writes=[vst_r])
    q_r, k_r, v_r = d["q_r"], d["k_r"], d["v_r"]

    def store_rows(dst, dres, row0, P, t, src, sres, key):
        S.add("sp", lambda e: e.dma_start(out=dst[row0:row0 + P, t * TS:(t + 1) * TS], in_=src),
              reads=[sres], writes=[dres], dma_key=key)

    def stage2_tile(t, hT, hTr):

        def proj_chunk(wb, wbr, jj):
            p, pr, _ = c.PS.next()
            for k in range(8):
                S.add("pe", (lambda e, k=k: e.matmul(
                    p[:, :], wb[:, k * 256 + jj * 128: k * 256 + jj * 128 + 128], hT[:, k * TS:(k + 1) * TS],
                    start=(k == 0), stop=(k == 7))), reads=[wbr, hTr], writes=[pr])
            return p, pr

        def simple_head_chunk(wb, wbr, jj, ones_idx, n, gcol, dst, dres, row0, rope):
            p, pr = proj_chunk(wb, wbr, jj)
            if rope:
                qn, qnr, _ = QN.next()
                rms_feat(c, [(p[:, :], pr)], 128, ones_idx, n, [gcol], [(qn[:, :], qnr)])
                sg, sgr, key = STG.next()
                rope_feat(c, qn[:, :], qnr, 128, C_RD, tabs[:, 2 * T + t * TS: 2 * T + (t + 1) * TS],
                          tabs[:, 3 * T + t * TS: 3 * T + (t + 1) * TS], tab_r, sg[:, :], sgr)
            else:
                sg, sgr, key = STG.next()
                rms_feat(c, [(p[:, :], pr)], 128, ones_idx, n, [gcol], [(sg[:, :], sgr)])
            store_rows(dst, dres, row0, 128, t, sg[:, :], sgr, key)

        b0 = bases_in[t]
        wb, wbr = ws.get(b0 + 0)
        for jj in range(2):
            simple_head_chunk(wb, wbr, jj, C_BD64, 64, G_NAQ, d["qT"], q_r, 0 + jj * 128, False)
        wb, wbr = ws.get(b0 + 1)
        for jj in range(2):
            simple_head_chunk(wb, wbr, jj, C_BD64, 64, G_NAK, d["kT"], k_r, 0 + jj * 128, False)
        wb, wbr = ws.get(b0 + 2)
        p0, p0r = proj_chunk(wb, wbr, 0)
        p1, p1r = proj_chunk(wb, wbr, 1)
        rms_feat(c, [(p0[:, :], p0r), (p1[:, :], p1r)], 128, C_ONES, 256, [G_QLAT, G_QLAT + 1],
                 [(cqn[:, 0:TS], cqn_r), (cqn[:, TS:2 * TS], cqn_r)])
        for h in range(4):
            p, pr, _ = c.PS.next()
            for k in range(2):
                S.add("pe", (lambda e, k=k, h=h, p=p: e.matmul(
                    p[0:96, :], wsm[:, WS_UQ + k * 384 + h * 96: WS_UQ + k * 384 + h * 96 + 96],
                    cqn[:, k * TS:(k + 1) * TS], start=(k == 0), stop=(k == 1))),
                    reads=[wsm_r, cqn_r], writes=[pr])
            qn, qnr, _ = QN.next()
            rms_feat(c, [(p[0:96, :], pr)], 96, C_ONES, 96, [G_MQ], [(qn[0:96, :], qnr)])
            sg, sgr, key = STG.next()
            rope_feat(c, qn[0:96, :], qnr, 96, C_RB, tabs[0:96, 0 * T + t * TS: 0 * T + (t + 1) * TS],
                      tabs[0:96, 1 * T + t * TS: 1 * T + (t + 1) * TS], tab_r, sg[0:96, :], sgr)
            store_rows(d["qT"], q_r, 256 + h * 96, 96, t, sg[0:96, :], sgr, key)
        wb, wbr = ws.get(b0 + 3)
        p, pr = proj_chunk(wb, wbr, 0)
        rms_feat(c, [(p[:, :], pr)], 128, C_ONES, 128, [G_KVLAT], [(ckvn[:, :], ckvn_r)])
        for h in range(4):
            p, pr, _ = c.PS.next()
            S.add("pe", (lambda e, h=h, p=p: e.matmul(
                p[0:96, :], wsm[:, WS_KN + h * 96: WS_KN + h * 96 + 96], ckvn[:, :], start=True, stop=False)),
                reads=[wsm_r, ckvn_r], writes=[pr])
            for k in range(8):
                S.add("pe", (lambda e, k=k, p=p: e.matmul(
                    p[0:96, :], wsm[:, WS_KR + k * 96: WS_KR + k * 96 + 96], hT[:, k * TS:(k + 1) * TS],
                    start=False, stop=(k == 7))), reads=[wsm_r, hTr], writes=[pr])
            qn, qnr, _ = QN.next()
            rms_feat(c, [(p[0:96, :], pr)], 96, C_ONES, 96, [G_MK], [(qn[0:96, :], qnr)])
            sg, sgr, key = STG.next()
            rope_feat(c, qn[0:96, :], qnr, 96, C_RB, tabs[0:96, 0 * T + t * TS: 0 * T + (t + 1) * TS],
                      tabs[0:96, 1 * T + t * TS: 1 * T + (t + 1) * TS], tab_r, sg[0:96, :], sgr)
            store_rows(d["kT"], k_r, 256 + h * 96, 96, t, sg[0:96, :], sgr, key)
        for s in range(4):
            p, pr, _ = c.PS.next()
            S.add("pe", (lambda e, s=s, p=p: e.matmul(
                p[:, 0:256], ckvn[:, s * 128:(s + 1) * 128], wsm[:, WS_V:WS_V + 256], start=True, stop=True)),
                reads=[wsm_r, ckvn_r], writes=[pr])
            S.add("act", (lambda e, s=s, p=p: e.activation(out=vst[:, s, 10:14, 0:64], in_=p[:, 0:256].rearrange("p (u d) -> p u d", d=64), func=AF.Copy)),
                  reads=[pr], writes=[vst_r])
        simple_head_chunk(wb, wbr, 1, C_BD64, 64, G_GK, d["kT"], k_r, 896, True)
        wb, wbr = ws.get(b0 + 4)
        for jj in range(2):
            simple_head_chunk(wb, wbr, jj, C_BD32, 32, G_DQ, d["qT"], q_r, 640 + jj * 128, False)
        wb, wbr = ws.get(b0 + 5)
        for jj in range(2):
            simple_head_chunk(wb, wbr, jj, C_BD32, 32, G_DK, d["kT"], k_r, 640 + jj * 128, False)
        wb, wbr = ws.get(b0 + 6)
        for jj in range(2):
            simple_head_chunk(wb, wbr, jj, C_BD64, 64, G_GQ, d["qT"], q_r, 896 + jj * 128, True)
        for vb in range(2):
            wv, wvr = ws.get(b0 + 7 + vb)
            for s in range(4):
                p, pr, _ = c.PS.next()
                for k in range(8):
                    S.add("pe", (lambda e, k=k, s=s, p=p, wv=wv: e.matmul(
                        p[:, 0:320], hT[:, k * TS + s * 128: k * TS + (s + 1) * 128], wv[:, k * 320:(k + 1) * 320],
                        start=(k == 0), stop=(k == 7))), reads=[wvr, hTr], writes=[pr])
                S.add("dve", (lambda e, s=s, p=p, vb=vb: e.tensor_copy(out=vst[:, s, vb * 5:(vb + 1) * 5, 0:64], in_=p[:, 0:320].rearrange("p (u d) -> p u d", d=64))),
                      reads=[pr], writes=[vst_r])
        for u in range(NVU):
            S.add("sp", (lambda e, u=u: e.dma_start(out=d["v"][u, :, t * 4:(t + 1) * 4, :], in_=vst[:, :, u, :])),
                  reads=[vst_r], writes=[v_r], dma_key="vst")

    for t in range(n2):
        hT, hTr = norm_x_tile(c, t, G_MIX)
        stage2_tile(t, hT, hTr)


def build_a():
    nc = bass.Bass("TRN2", target_bir_lowering=False)
    dt = lambda name, shape, dtype, kind: nc.dram_tensor(name, shape, dtype, kind=kind).ap()
    d = {}
    xT = dt("xT", [128, 8 * T], F32, "ExternalInput")
    gains = dt("gains", [128, NG], F32, "ExternalInput")
    lconst = dt("lconst", [128, 2], F32, "ExternalInput")
    cmat = dt("cmat", [128, 768], F32, "ExternalInput")
    d["wg"] = dt("wg", [11, 128, 2048], F32, "ExternalInput")
    d["wu"] = dt("wu", [11, 128, 2048], F32, "ExternalInput")
    d["wd"] = dt("wd", [8, 128, 2816], F32, "ExternalInput")
    d["winf"] = dt("winf", [7, 128, 2048], F32, "ExternalInput")
    d["winv"] = dt("winv", [2, 128, 2560], F32, "ExternalInput")
    d["wsm"] = dt("wsm", [128, WSM], F32, "ExternalInput")
    d["tabs"] = dt("tabs", [4, 128, T], F32, "ExternalInput")
    xo = dt("x1T", [128, 8 * T], F32, "ExternalOutput")
    d["qT"] = dt("qT", [QROWS, T], BF16, "ExternalOutput")
    d["kT"] = dt("kT", [KROWS, T], BF16, "ExternalOutput")
    d["v"] = dt("v", [NVU, 128, 16, 65], BF16, "ExternalOutput")
    d["q_r"], d["k_r"], d["v_r"] = Res("qT"), Res("kT"), Res("v")
    xo_r = Res("xo")
    with contextlib.ExitStack() as st:
        S = Sched(nc)
        c = Ctx()
        common_setup(c, st, nc, S, gains, lconst, cmat)
        load_x(c, st, nc, S, xT)
        emit_phase_a(c, st, nc, S, d)
        store_x(c, S, xo, xo_r)
        S.add("sp", None, reads=[xo_r, d["q_r"], d["k_r"], d["v_r"]])
        S.emit(st)
    return nc


def blk_cols(W, cpb):
    K, N = W.shape
    kc = K // 128
    nb = N // cpb
    return np.ascontiguousarray(W.reshape(kc, 128, nb, cpb).transpose(2, 1, 0, 3).reshape(nb, 128, kc * cpb))


def const_mats():
    m = np.zeros((6, 128, 128), np.float32)
    m[C_ONES] = 1.0
    for b in range(2):
        m[C_BD64, b * 64:(b + 1) * 64, b * 64:(b + 1) * 64] = 1.0
    for b in range(4):
        m[C_BD32, b * 32:(b + 1) * 32, b * 32:(b + 1) * 32] = 1.0
    for i in range(16):
        m[C_RB, 80 + i, 64 + i] = -1.0
        m[C_RB, 64 + i, 80 + i] = 1.0
    for hb in (0, 64):
        for sb in (0, 32):
            for i in range(16):
                a = hb + sb + i
                b = a + 16
                m[C_RD, b, a] = -1.0
                m[C_RD, a, b] = 1.0
    m[C_ID] = np.eye(128, dtype=np.float32)
    return np.ascontiguousarray(m.transpose(1, 0, 2).reshape(128, 768))


def rope_tables(core):
    r = core % 4
    t = np.arange(r * T, (r + 1) * T)
    inv = np.exp(-math.log(10000.0) * np.arange(0, 32, 2, dtype=np.float32) / 32).astype(np.float32)
    a_seq = t.astype(np.float32)[:, None] * inv[None, :]
    a_row = (t // 64).astype(np.float32)[:, None] * inv[None, :]
    a_col = (t % 64).astype(np.float32)[:, None] * inv[None, :]
    tabs = np.zeros((4, 128, T), np.float32)
    tabs[0, 0:64] = 1.0
    for p in range(64, 96):
        tabs[0, p] = np.cos(a_seq[:, (p - 64) % 16])
        tabs[1, p] = np.sin(a_seq[:, (p - 64) % 16])
    for p in range(128):
        f = p % 64
        a = a_row if f < 32 else a_col
        tabs[2, p] = np.cos(a[:, f % 16])
        tabs[3, p] = np.sin(a[:, f % 16])
    return tabs


def tile64(v):
    return np.tile(v, 128 // v.shape[0])


def layer_host(inp, l):
    f = lambda k: np.asarray(inp[k][l], np.float32)
    h = {}
    for nm in ("ffn1", "ffn2"):
        h[nm + "_wg"] = blk_cols(f(nm + "_w_gate"), 256)
        h[nm + "_wu"] = blk_cols(f(nm + "_w_up"), 256)
        h[nm + "_wd"] = blk_cols(f(nm + "_w_down"), 128)
    win = f("w_in")
    cols_f = np.concatenate([np.arange(0, 256), np.arange(256, 512), np.arange(768, 1024), np.arange(1024, 1152),
                             np.arange(2208, 2336), np.arange(1184, 1440), np.arange(1440, 1696), np.arange(1952, 2208)])
    cols_v = np.concatenate([np.arange(512, 768), np.arange(1696, 1952), np.arange(2336, 2464)])
    h["winf"] = blk_cols(win[:, cols_f], 256)
    h["winv"] = blk_cols(win[:, cols_v], 320)
    wsm = np.zeros((128, WSM), np.float32)
    wuq = f("mla_w_uq")
    wsm[:, WS_UQ:WS_UQ + 768] = wuq.reshape(2, 128, 384).transpose(1, 0, 2).reshape(128, 768)
    wukv = f("mla_w_ukv")
    for hh in range(4):
        wsm[:, WS_KN + hh * 96: WS_KN + hh * 96 + 64] = wukv[:, hh * 128: hh * 128 + 64]
        wsm[:, WS_V + hh * 64: WS_V + (hh + 1) * 64] = wukv[:, hh * 128 + 64: hh * 128 + 128]
    kr = win[:, 1152:1184].reshape(8, 128, 32)
    for k in range(8):
        wsm[:, WS_KR + k * 96 + 64: WS_KR + k * 96 + 96] = kr[k]
    h["wsm"] = wsm
    g = np.zeros((128, NG), np.float32)
    g[:, G_FFN1:G_FFN1 + 8] = f("ffn1_norm").reshape(8, 128).T
    g[:, G_MIX:G_MIX + 8] = f("mix_norm").reshape(8, 128).T
    g[:, G_NAQ] = tile64(f("na_q_norm"))
    g[:, G_NAK] = tile64(f("na_k_norm"))
    g[:, G_QLAT:G_QLAT + 2] = f("mla_q_lat_norm").reshape(2, 128).T
    g[:, G_KVLAT] = f("mla_kv_lat_norm")
    g[0:96, G_MQ] = f("mla_q_norm")
    g[0:96, G_MK] = f("mla_k_norm")
    g[:, G_DQ] = tile64(f("diff_q_norm"))
    g[:, G_DK] = tile64(f("diff_k_norm"))
    g[:, G_GQ] = tile64(f("gqa_q_norm"))
    g[:, G_GK] = tile64(f("gqa_k_norm"))
    for hh in range(4):
        g[0:64, G_BETA + 0 + hh] = f("na_beta")[hh * 64:(hh + 1) * 64]
        g[0:64, G_BETA + 4 + hh] = f("mla_beta")[hh * 64:(hh + 1) * 64]
        g[0:64, G_BETA + 8 + hh] = f("diff_subln")
        g[0:64, G_BETA + 12 + hh] = f("gqa_beta")[hh * 64:(hh + 1) * 64]
    g[:, G_FFN2:G_FFN2 + 8] = f("ffn2_norm").reshape(8, 128).T
    g[:, G_FIN:G_FIN + 8] = f("final_norm").reshape(8, 128).T
    h["gains"] = g
    lc = np.zeros((128, 2), np.float32)
    lc[:, 0] = lambda_init(l)
    lc[:, 1] = 1.0 - lambda_init(l)
    h["lconst"] = lc
    return h


def emit_phase_c(c, st, nc, S, d):
    c.PS = Ring(st, nc, "ps", 8, [128, 512], F32, psum=True)
    c.SQ = Ring(st, nc, "sq", 3, [128, 512], BF16)
    c.RS = Ring(st, nc, "rs", 2, [128, 512], F32)
    c.TF = Ring(st, nc, "tf", 4, [128, 512], F32)
    c.HT = Ring(st, nc, "hT", 2, [128, 8 * TS], BF16)
    c.actT = _sb(st, nc, "actT", [128, 22 * TS], BF16)
    c.act_r = [Res("cact%d" % j) for j in range(22)]
    wring = Ring(st, nc, "wb", 6, [128, 2816], BF16)
    ws = WStream(S, wring, 4)
    YT = Ring(st, nc, "yt", 1, [64, 16 * TS], F32)
    yn = _sb(st, nc, "yn", [64, 16 * TS], BF16)
    yn_r = Res("yn")
    S.add("dve", lambda e: e.tensor_scalar(out=c.gains[0:64, G_BETA + 8:G_BETA + 12], in0=c.gains[0:64, G_BETA + 8:G_BETA + 12],
                                            scalar1=c.lconst[0:64, 1:2], scalar2=None, op0=ALU.mult),
          reads=[c.gains_r, c.lconst_r], writes=[c.gains_r])
    bases = []
    for t in range(NT):
        b0 = len(ws.items)
        for m in range(8):
            ws.plan(d["wout"][m], 2048, part=64)
        plan_ffn(ws, d["wg"], d["wu"], d["wd"])
        bases.append(b0)

    def tile_c(t, yt, ytr, key):
        for u in range(16):
            S.add("sp", (lambda e, u=u: e.dma_start(out=yt[:, u * TS:(u + 1) * TS], in_=d["y"][u, :, t * TS:(t + 1) * TS])),
                  reads=[d["y_r"]], writes=[ytr], dma_key=key)
        for g in range(4):
            if g == 2:
                for h in range(4):
                    u = 8 + h
                    rms_feat(c, [(yt[:, u * TS:(u + 1) * TS], ytr)], 64, C_ONES, 64, [G_BETA + u],
                             [(yn[:, u * TS:(u + 1) * TS], yn_r)])
            else:
                us = [4 * g + h for h in range(4)]
                rms_feat(c, [(yt[:, u * TS:(u + 1) * TS], ytr) for u in us], 64, C_ONES, 256, [G_BETA + u for u in us],
                         [(yn[:, u * TS:(u + 1) * TS], yn_r) for u in us])
        for m in range(8):
            wb, wbr = ws.get(bases[t] + m)
            p, pr, _ = c.PS.next()
            for hk in range(16):
                S.add("pe", (lambda e, hk=hk, p=p, wb=wb: e.matmul(
                    p[:, :], wb[0:64, hk * 128:(hk + 1) * 128], yn[:, hk * TS:(hk + 1) * TS],
                    start=(hk == 0), stop=(hk == 15))), reads=[wbr, yn_r], writes=[pr])
            xs = c.xT[:, m * T + t * TS: m * T + (t + 1) * TS]
            S.add("dve", (lambda e, p=p, xs=xs: e.tensor_tensor(out=xs, in0=p[:, :], in1=xs, op=ALU.add)),
                  reads=[pr, c.x_r[t]], writes=[c.x_r[t]])
        hT, hTr = norm_x_tile(c, t, G_FFN2)
        ffn_tile(c, t, hT, hTr, ws, bases[t] + 8)
        srcs = [(c.xT[:, k * T + t * TS: k * T + (t + 1) * TS], c.x_r[t]) for k in range(8)]
        rms_feat(c, srcs, 128, C_ONES, D, [G_FIN + k for k in range(8)], srcs)

    for t in range(NT):
        yt, ytr, key = YT.next()
        tile_c(t, yt, ytr, key)


def build_c():
    nc = bass.Bass("TRN2", target_bir_lowering=False)
    dt = lambda name, shape, dtype, kind: nc.dram_tensor(name, shape, dtype, kind=kind).ap()
    d = {}
    xT = dt("xT", [128, 8 * T], F32, "ExternalInput")
    gains = dt("gains", [128, NG], F32, "ExternalInput")
    lconst = dt("lconst", [128, 2], F32, "ExternalInput")
    cmat = dt("cmat", [128, 768], F32, "ExternalInput")
    d["wg"] = dt("wg", [11, 128, 2048], F32, "ExternalInput")
    d["wu"] = dt("wu", [11, 128, 2048], F32, "ExternalInput")
    d["wd"] = dt("wd", [8, 128, 2816], F32, "ExternalInput")
    d["wout"] = dt("wout", [8, 64, 2048], F32, "ExternalInput")
    d["y"] = dt("y", [16, 64, T], F32, "ExternalInput")
    d["y_r"] = Res("y")
    xo = dt("xoT", [128, 8 * T], F32, "ExternalOutput")
    xo_r = Res("xo")
    with contextlib.ExitStack() as st:
        S = Sched(nc)
        c = Ctx()
        common_setup(c, st, nc, S, gains, lconst, cmat)
        load_x(c, st, nc, S, xT)
        emit_phase_c(c, st, nc, S, d)
        store_x(c, S, xo, xo_r)
        S.add("sp", None, reads=[xo_r])
        S.emit(st)
    return nc


LA = 1


def near_entries(kind, qb):
    ent = []
    if kind == "A":
        lo, hi, halo = 4 * qb - 2, 4 * qb + 5, 2
    else:
        lo, hi, halo = 4 * qb - 1, 4 * qb + 4, 1
    for lc in range(max(lo, 0), min(hi, 15) + 1):
        ent.append(("own", lc))
    if qb == 0:
        for j in range(4):
            for x in range(16 - halo, 16):
                ent.append(("gath", 16 * j + x))
    if qb == 3:
        for j in range(4):
            for x in range(halo):
                ent.append(("gath", 16 * j + x))
    return ent


def tile_ids():
    ids = {}
    n = 0
    for kind in ("A", "C"):
        for h in range(4):
            for qb in range(4):
                for i, _e in enumerate(near_entries(kind, qb)):
                    ids[(kind, h, qb, i)] = n
                    n += 1
    return ids, n


TILE_IDS, NTILES = tile_ids()


def emit_phase_b(c, st, nc, S, d):
    PSS = Ring(st, nc, "pss", 2, [128, 1024], F32, psum=True)
    PO = Ring(st, nc, "po", 2, [128, 512], F32, psum=True)
    PBC = Ring(st, nc, "pbc", 2, [128, 512], F32, psum=True)
    PT = Ring(st, nc, "pt", 3, [128, 1024], BF16)
    KT = Ring(st, nc, "kt", 2, [128, SEQ], BF16)
    VT = Ring(st, nc, "vt", 2, [128, 64, 65], BF16)
    KTO = Ring(st, nc, "kto", 2, [128, T], BF16)
    VTO = Ring(st, nc, "vto", 2, [128, 16, 65], BF16)
    QT = Ring(st, nc, "qt", 2, [128, T], BF16)
    BT = Ring(st, nc, "bt", 12, [128, 512], BF16)
    VS = Ring(st, nc, "vs", 2, [128, 64, 65], BF16)
    SC = Ring(st, nc, "sc", 2, [128, 64], F32)
    OSB = Ring(st, nc, "osb", 2, [64, 512], F32)
    ON = Ring(st, nc, "on", 4, [64, 512], F32)
    RR = Ring(st, nc, "rr", 2, [128, 512], F32)
    bts = WStream(S, BT, 8)
    t5c = _sb(st, nc, "t5c", [128, 1024], F32)
    t5c_r = Res("t5c")
    S.add("sp", lambda e: e.dma_start(out=t5c[:, :], in_=d["t5c"]), writes=[t5c_r], dma_key="t5c")
    onesf = _sb(st, nc, "onesf", [128, 64], F32)
    onesf_r = Res("onesf")
    S.add("dve", lambda e: e.memset(onesf[:, :], 1.0), writes=[onesf_r])
    dl = _sb(st, nc, "dl", [64, 128], F32)
    dl_r = Res("dl")
    S.add("sp", lambda e: e.dma_start(out=dl[:, :], in_=d["dlam"]), writes=[dl_r], dma_key="dl")
    lt = _sb(st, nc, "lt", [64, 72], F32)
    lt_r = Res("lt")
    S.add("dve", lambda e: e.tensor_tensor(out=lt[:, 0:32], in0=dl[:, 0:32], in1=dl[:, 32:64], op=ALU.mult), reads=[dl_r], writes=[lt_r])
    S.add("dve", lambda e: e.tensor_tensor(out=lt[:, 32:64], in0=dl[:, 64:96], in1=dl[:, 96:128], op=ALU.mult), reads=[dl_r, lt_r], writes=[lt_r])
    S.add("dve", lambda e: e.reduce_sum(lt[:, 64:65], lt[:, 0:32], axis=mybir.AxisListType.X), reads=[lt_r], writes=[lt_r])
    S.add("dve", lambda e: e.reduce_sum(lt[:, 65:66], lt[:, 32:64], axis=mybir.AxisListType.X), reads=[lt_r], writes=[lt_r])
    S.add("act", lambda e: e.activation(out=lt[:, 66:68], in_=lt[:, 64:66], func=AF.Exp), reads=[lt_r], writes=[lt_r])
    S.add("dve", lambda e: e.tensor_tensor(out=lt[:, 68:69], in0=lt[:, 67:68], in1=lt[:, 66:67], op=ALU.subtract), reads=[lt_r], writes=[lt_r])
    S.add("dve", lambda e: e.tensor_scalar(out=lt[:, 69:70], in0=lt[:, 68:69], scalar1=c.lconst[0:64, 0:1], scalar2=None, op0=ALU.subtract),
          reads=[lt_r, c.lconst_r], writes=[lt_r])
    nlam = lt[:, 69:70]
    ident = c.cmat[:, C_ID * 128:(C_ID + 1) * 128]
    y_r = d["y_r"]

    def attend(qt, qtr, qp0, dq, qb, chunks):
        O, Or, _ = PO.next()
        steps = []
        i = 0
        isc = lambda ch: ch[5] is not None and ch[5][0] == "const"
        while i < len(chunks):
            if i + 1 < len(chunks) and not isc(chunks[i]) and not isc(chunks[i + 1]):
                steps.append([chunks[i], chunks[i + 1]])
                i += 2
            else:
                steps.append([chunks[i]])
                i += 1
        n = len(steps)
        total = len(chunks)
        pend = []
        pvi = [0]
        for i in range(n + LA):
            if i < n:
                st_ = steps[i]
                s_, sr, _ = PSS.next()
                w = 512 * len(st_)
                for hf, (kt, ktr, vt, vtr, kc, mode) in enumerate(st_):
                    sl = s_[:, hf * 512:(hf + 1) * 512]
                    if mode is not None and mode[0] == "tile":
                        bt, btr = bts.get(mode[1])
                        S.add("pe", (lambda e, sl=sl, bt=bt: e.matmul(sl, ident, bt[:, :], start=True, stop=False)),
                              reads=[btr, c.cmat_r], writes=[sr])
                        first = False
                    else:
                        first = True
                    S.add("pe", (lambda e, sl=sl, kc=kc, first=first, kt=kt: e.matmul(
                        sl, kt[qp0:qp0 + dq, kc * 128:(kc + 1) * 128], qt[qp0:qp0 + dq, qb * TS:(qb + 1) * TS],
                        start=first, stop=True)), reads=[ktr, qtr], writes=[sr])
                p_, pr, _ = PT.next()
                mode = st_[0][5]
                if len(st_) == 1 and mode is not None and mode[0] == "const":
                    bcol = mode[1]
                    S.add("act", (lambda e, s_=s_, p_=p_, bcol=bcol: e.activation(out=p_[:, 0:512], in_=s_[:, 0:512], func=AF.Exp, bias=bcol)),
                          reads=[sr, t5c_r], writes=[pr])
                else:
                    S.add("act", (lambda e, s_=s_, p_=p_, w=w: e.activation(out=p_[:, 0:w], in_=s_[:, 0:w], func=AF.Exp)),
                          reads=[sr], writes=[pr])
                pend.append((st_, p_, pr))
            j = i - LA
            if j >= 0:
                st_, p_, pr = pend[j]
                for hf, (kt, ktr, vt, vtr, kc, mode) in enumerate(st_):
                    k_ = pvi[0]
                    pvi[0] += 1
                    S.add("pe", (lambda e, kc=kc, p_=p_, k_=k_, vt=vt, hf=hf: e.matmul(
                        O[0:65, :], vt[:, kc, :], p_[:, hf * 512:(hf + 1) * 512], start=(k_ == 0), stop=(k_ == total - 1))),
                        reads=[vtr, pr], writes=[Or])
        rr, rrr, _ = RR.next()
        osb, osbr, _ = OSB.next()
        S.add("dve", lambda e: e.reciprocal(out=rr[64:65, :], in_=O[64:65, :]), reads=[Or], writes=[rrr])
        S.add("dve", lambda e: e.tensor_copy(out=osb[:, :], in_=O[0:64, :]), reads=[Or], writes=[osbr])
        bc, bcr, _ = PBC.next()
        S.add("pe", lambda e: e.matmul(bc[0:64, :], onesf[64:65, 0:64], rr[64:65, :], start=True, stop=True),
              reads=[rrr, onesf_r], writes=[bcr])
        on, onr, _ = ON.next()
        S.add("dve", lambda e: e.tensor_tensor(out=on[:, :], in0=osb[:, :], in1=bc[0:64, :], op=ALU.mult),
              reads=[osbr, bcr], writes=[onr])
        return on, onr

    def store_y(u, qb, on, onr):
        S.add("sp", lambda e: e.dma_start(out=d["y"][u, :, qb * TS:(qb + 1) * TS], in_=on[:, :]),
              reads=[onr], writes=[y_r], dma_key="ystore")

    plan = []
    for h in range(4):
        plan.append(("A", h * 64, 64, h, h * 64, 64, [h]))
    for h in range(4):
        plan.append(("B", 256 + h * 96, 96, 10 + h, 256 + h * 96, 96, [4 + h]))
    for h in range(4):
        plan.append(("C", 640 + h * 64, 64, 4 + h, 640 + h * 64, 64, [8 + h]))
    for g in range(2):
        plan.append(("D", 896 + g * 64, 64, 8 + g, 896 + 2 * g * 64, 128, [12 + 2 * g, 13 + 2 * g]))
    import os
    units = [int(x) for x in os.environ.get("KB_UNITS", ",".join(str(i) for i in range(len(plan)))).split(",")]
    bt_plan = {}
    for ui in units:
        kind = plan[ui][0]
        if kind in ("A", "C"):
            h = ui % 4
            for qb in range(4):
                for m in range(2 if kind == "C" else 1):
                    for i, _e in enumerate(near_entries(kind, qb)):
                        bt_plan[(kind, h, qb, m, i)] = bts.plan(d["btl"][TILE_IDS[(kind, h, qb, i)]], 512)

    loaded = {}

    def load_unit(ui):
        kind, krow, kd, vu, qrow, qd, yus = plan[ui]
        kt, ktr, kkey = KT.next()
        vt, vtr, vkey = VT.next()
        qt, qtr, qkey = QT.next()
        for j in range(4):
            kr0 = kg_row(j, krow)
            S.add("sp", (lambda e, j=j, kr0=kr0: e.dma_start(out=kt[0:kd, j * T:(j + 1) * T], in_=d["kTg"][kr0:kr0 + kd, :])),
                  writes=[ktr], dma_key=kkey)
            if kind == "D":
                S.add("sp", (lambda e, j=j, kr0=kr0: e.dma_start(out=kt[64:128, j * T:(j + 1) * T], in_=d["kTg"][kr0:kr0 + kd, :])),
                      writes=[ktr], dma_key=kkey)
            vr0 = vg_row(j, vu)
            S.add("sp", (lambda e, j=j, vr0=vr0: e.dma_start(out=vt[:, 16 * j:16 * (j + 1), :],
                                                              in_=d["vg"][vr0:vr0 + 128, :].rearrange("p (c d) -> p c d", d=65))),
                  writes=[vtr], dma_key=vkey)
        S.add("sp", lambda e: e.dma_start(out=qt[0:qd, :], in_=d["qT"][qrow:qrow + qd, :]), writes=[qtr], dma_key=qkey)
        own = None
        if kind in ("A", "C"):
            kto, ktor, kokey = KTO.next()
            vto, vtor, vokey = VTO.next()
            S.add("sp", lambda e: e.dma_start(out=kto[0:kd, :], in_=d["kT"][krow:krow + kd, :]), writes=[ktor], dma_key=kokey)
            S.add("sp", lambda e: e.dma_start(out=vto[:, :, :], in_=d["v"][vu]), writes=[vtor], dma_key=vokey)
            own = (kto, ktor, vto, vtor)
        loaded[ui] = (kt, ktr, vt, vtr, qt, qtr, own)

    load_unit(units[0])
    for n_, ui in enumerate(units):
        if n_ + 1 < len(units):
            load_unit(units[n_ + 1])
        kind, krow, kd, vu, qrow, qd, yus = plan[ui]
        kt, ktr, vt, vtr, qt, qtr, own = loaded.pop(ui)
        h = ui % 4
        dense = [(kt, ktr, vt, vtr, kc, None) for kc in range(64)]

        def near_list(kind, qb, m):
            out = []
            for i, (src, ch) in enumerate(near_entries(kind, qb)):
                mode = ("tile", bt_plan[(kind, h, qb, m, i)])
                if src == "own":
                    out.append((own[0], own[1], own[2], own[3], ch, mode))
                else:
                    out.append((kt, ktr, vt, vtr, ch, mode))
            return out

        for qb in range(4):
            if kind == "A":
                on, onr = attend(qt, qtr, 0, 64, qb, near_list("A", qb, 0))
                store_y(yus[0], qb, on, onr)
            elif kind == "B":
                on, onr = attend(qt, qtr, 0, 96, qb, dense)
                store_y(yus[0], qb, on, onr)
            elif kind == "D":
                for hh in range(2):
                    on, onr = attend(qt, qtr, hh * 64, 64, qb, dense)
                    store_y(yus[hh], qb, on, onr)
            else:
                sc, scr, _ = SC.next()
                S.add("act", (lambda e, sc=sc, qb=qb, h=h: e.activation(out=sc[:, :], in_=t5c[:, (h * 4 + qb) * 64:(h * 4 + qb + 1) * 64], func=AF.Exp)),
                      reads=[t5c_r], writes=[scr])
                vs, vsr, _ = VS.next()
                S.add("dve", (lambda e, sc=sc, vs=vs, vt=vt: e.tensor_tensor(
                    out=vs[:, :, :], in0=vt[:, :, :], in1=sc[:, :].unsqueeze(2).broadcast_to([128, 64, 65]), op=ALU.mult)),
                    reads=[vtr, scr], writes=[vsr])
                ons = []
                for m in range(2):
                    ch = [(kt, ktr, vs, vsr, kc, None) for kc in range(64)] + near_list("C", qb, m)
                    ons.append(attend(qt, qtr, 32 * m, 32, qb, ch))
                (on0, on0r), (on1, on1r) = ons
                yo, yor, _ = ON.next()
                S.add("dve", (lambda e, yo=yo, on0=on0, on1=on1: e.scalar_tensor_tensor(
                    out=yo[:, :], in0=on1[:, :], scalar=nlam, in1=on0[:, :], op0=ALU.mult, op1=ALU.add)),
                    reads=[on0r, on1r, lt_r], writes=[yor])
                store_y(yus[0], qb, yo, yor)


def build_b():
    nc = bass.Bass("TRN2", target_bir_lowering=False)
    dt = lambda name, shape, dtype, kind: nc.dram_tensor(name, shape, dtype, kind=kind).ap()
    d = {}
    gains = dt("gains", [128, NG], F32, "ExternalInput")
    lconst = dt("lconst", [128, 2], F32, "ExternalInput")
    cmat = dt("cmat", [128, 768], F32, "ExternalInput")
    d["qT"] = dt("qT", [QROWS, T], BF16, "ExternalInput")
    d["kT"] = dt("kT", [KROWS, T], BF16, "ExternalInput")
    d["v"] = dt("v", [NVU, 128, 16, 65], BF16, "ExternalInput")
    d["kTg"] = dt("kTg", [4 * KROWS, T], BF16, "ExternalInput")
    d["vg"] = dt("vg", [4 * NVU * 128, 16 * 65], BF16, "ExternalInput")
    d["btl"] = dt("btl", [NTILES, 128, 512], F32, "ExternalInput")
    d["t5c"] = dt("t5c", [128, 1024], F32, "ExternalInput")
    d["dlam"] = dt("dlam", [64, 128], F32, "ExternalInput")
    d["y"] = dt("y", [16, 64, T], F32, "ExternalOutput")
    d["y_r"] = Res("y")
    with contextlib.ExitStack() as st:
        S = Sched(nc)
        c = Ctx()
        common_setup(c, st, nc, S, gains, lconst, cmat)
        emit_phase_b(c, st, nc, S, d)
        S.add("sp", None, reads=[d["y_r"]])
        S.emit(st)
    return nc


def t5_bucket_np(rel):
    half = 16
    max_exact = 8
    n = np.abs(rel)
    large = max_exact + (np.log(np.maximum(n, 1).astype(np.float32) / max_exact)
                         / np.float32(math.log(128 / max_exact)) * (half - max_exact)).astype(np.int32)
    large = np.minimum(large, half - 1)
    return np.where(rel > 0, half, 0) + np.where(n < max_exact, n, large)


def bias_tiles_host(inp, l, core):
    r = core % 4
    rpb = np.asarray(inp["na_rpb"][l], np.float32)
    t5 = np.asarray(inp["t5_bias"], np.float32)
    btl = np.empty((NTILES, 128, 512), np.float32)
    kk = np.arange(128)[:, None]
    qq = np.arange(512)[None, :]
    NEG = np.float32(-1e30)
    for qb in range(4):
        qbg = 4 * r + qb
        qtok = qbg * 512 + qq
        qr, qc = qtok // 64, qtok % 64
        rs = np.clip(qr - 4, 0, 120)
        cs = np.clip(qc - 8, 0, 48)
        for i, (src, ch) in enumerate(near_entries("A", qb)):
            g = 16 * r + ch if src == "own" else ch
            ktok = g * 128 + kk
            kr, kcol = ktok // 64, ktok % 64
            valid = (kr >= rs) & (kr < rs + 8) & (kcol >= cs) & (kcol < cs + 16)
            dr = np.clip(kr - qr + 7, 0, 14)
            dc = np.clip(kcol - qc + 15, 0, 30)
            flat = np.where(valid, dr * 31 + dc, 15 * 31)
            for h in range(4):
                tab = np.concatenate([rpb[h].ravel(), np.array([NEG], np.float32)])
                btl[TILE_IDS[("A", h, qb, i)]] = tab[flat]
        for i, (src, ch) in enumerate(near_entries("C", qb)):
            g = 16 * r + ch if src == "own" else ch
            if 4 * qbg - 1 <= g <= 4 * qbg + 4:
                bk = t5_bucket_np((g * 128 + kk) - qtok)
                for h in range(4):
                    btl[TILE_IDS[("C", h, qb, i)]] = t5[:, h][bk]
            else:
                for h in range(4):
                    btl[TILE_IDS[("C", h, qb, i)]] = NEG
    t5c = np.zeros((128, 1024), np.float32)
    for qb in range(4):
        qbg = 4 * r + qb
        for kc in range(64):
            for h in range(4):
                if 4 * qbg - 1 <= kc <= 4 * qbg + 4:
                    val = NEG
                else:
                    val = t5[31, h] if kc > 4 * qbg + 3 else t5[15, h]
                t5c[:, (h * 4 + qb) * 64 + kc] = val
    return btl, t5c


_PROGS = {}


def _prog(name):
    if name not in _PROGS:
        _PROGS[name] = {"a": build_a, "b": build_b, "c": build_c, "f": build_fused}[name]()
    return _PROGS[name]


def to_featmajor(xc):
    return np.ascontiguousarray(xc.T.reshape(8, 128, T).transpose(1, 0, 2).reshape(128, 8 * T))


def from_featmajor(xT):
    return xT.reshape(128, 8, T).transpose(1, 0, 2).reshape(D, T).T


def kernel_unfused(**inp):
    x = np.asarray(inp["x"], np.float32).reshape(BATCH * SEQ, D)
    cm = const_mats()
    xTs = [to_featmajor(x[c * T:(c + 1) * T]) for c in range(NCORE)]
    tabs = [rope_tables(c) for c in range(NCORE)]
    cores = list(range(NCORE))
    for l in range(DEPTH):
        H = layer_host(inp, l)
        common = {"gains": H["gains"], "lconst": H["lconst"], "cmat": cm}
        maps = [dict(common, xT=xTs[c], wg=H["ffn1_wg"], wu=H["ffn1_wu"], wd=H["ffn1_wd"], winf=H["winf"],
                     winv=H["winv"], wsm=H["wsm"], tabs=tabs[c]) for c in cores]
        ra = run_bass_kernel_spmd(_prog("a"), maps, core_ids=cores).results
        maps = []
        dl = np.ascontiguousarray(np.broadcast_to(np.asarray(inp["diff_lambda"][l], np.float32).reshape(1, 128), (64, 128)))
        for c in cores:
            b = c // 4
            kTg = np.zeros((4 * KROWS, T), np.asarray(ra[0]["kT"]).dtype)
            vg = np.zeros((4 * NVU * 128, 1040), np.asarray(ra[0]["v"]).dtype)
            for j in range(4):
                kj = np.asarray(ra[4 * b + j]["kT"])
                vj = np.asarray(ra[4 * b + j]["v"]).reshape(NVU * 128, 1040)
                for a, bb in K_PARTS:
                    kTg[4 * a + j * (bb - a): 4 * a + (j + 1) * (bb - a)] = kj[a:bb]
                for a, bb in V_PARTS:
                    vg[(4 * a + j * (bb - a)) * 128: (4 * a + (j + 1) * (bb - a)) * 128] = vj[a * 128:bb * 128]
            btl, t5c = bias_tiles_host(inp, l, c)
            maps.append(dict(common, qT=np.asarray(ra[c]["qT"]), kT=np.asarray(ra[c]["kT"]), v=np.asarray(ra[c]["v"]),
                             kTg=kTg, vg=vg, btl=btl, t5c=t5c, dlam=dl))
        rb = run_bass_kernel_spmd(_prog("b"), maps, core_ids=cores).results
        wout = np.ascontiguousarray(np.asarray(inp["w_out"][l], np.float32).reshape(16, 64, 8, 128).transpose(2, 1, 0, 3).reshape(8, 64, 2048))
        maps = [dict(common, xT=np.asarray(ra[c]["x1T"]), y=np.asarray(rb[c]["y"]), wg=H["ffn2_wg"], wu=H["ffn2_wu"],
                     wd=H["ffn2_wd"], wout=wout) for c in cores]
        rc = run_bass_kernel_spmd(_prog("c"), maps, core_ids=cores).results
        xTs = [np.asarray(rc[c]["xoT"]) for c in cores]
    out = np.concatenate([from_featmajor(xTs[c]) for c in cores], axis=0)
    return np.ascontiguousarray(out.reshape(BATCH, SEQ, D).astype(np.float32))


RG = [[0, 1, 2, 3], [4, 5, 6, 7]]
K_PARTS = [(0, 256), (256, 448), (448, 640), (640, 896), (896, 1024)]
V_PARTS = [(0, 3), (3, 6), (6, 9), (9, 12), (12, 14)]


def kg_row(j, row):
    for a, b in K_PARTS:
        if a <= row < b:
            return 4 * a + j * (b - a) + (row - a)
    raise AssertionError(row)


def vg_row(j, u):
    for a, b in V_PARTS:
        if a <= u < b:
            return (4 * a + j * (b - a) + (u - a)) * 128
    raise AssertionError(u)


def build_fused():
    nc = bass.Bass("TRN2", target_bir_lowering=False)
    dt = lambda name, shape, dtype, kind, **kw: nc.dram_tensor(name, shape, dtype, kind=kind, **kw).ap()
    xT_d = dt("xT", [128, 8 * T], F32, "ExternalInput")
    cmat_d = dt("cmat", [128, 768], F32, "ExternalInput")
    tabs_d = dt("tabs", [4, 128, T], F32, "ExternalInput")
    t5c_d = dt("t5c", [128, 1024], F32, "ExternalInput")
    xo_d = dt("xoT", [128, 8 * T], F32, "ExternalOutput")
    LD = []
    for l in range(DEPTH):
        sfx = "_%d" % l
        d = {}
        d["gains"] = dt("gains" + sfx, [128, NG], F32, "ExternalInput")
        d["lconst"] = dt("lconst" + sfx, [128, 2], F32, "ExternalInput")
        for nm in ("wg1", "wu1", "wg2", "wu2"):
            d[nm] = dt(nm + sfx, [11, 128, 2048], F32, "ExternalInput")
        for nm in ("wd1", "wd2"):
            d[nm] = dt(nm + sfx, [8, 128, 2816], F32, "ExternalInput")
        d["winf"] = dt("winf" + sfx, [7, 128, 2048], F32, "ExternalInput")
        d["winv"] = dt("winv" + sfx, [2, 128, 2560], F32, "ExternalInput")
        d["wsm"] = dt("wsm" + sfx, [128, WSM], F32, "ExternalInput")
        d["wout"] = dt("wout" + sfx, [8, 64, 2048], F32, "ExternalInput")
        d["btl"] = dt("btl" + sfx, [NTILES, 128, 512], F32, "ExternalInput")
        d["dlam"] = dt("dlam" + sfx, [64, 128], F32, "ExternalInput")
        d["qT"] = dt("s_qT" + sfx, [QROWS, T], BF16, "Internal")
        d["kT"] = dt("s_kT" + sfx, [KROWS, T], BF16, "Internal")
        d["v"] = dt("s_v" + sfx, [NVU, 128, 16, 65], BF16, "Internal")
        d["kTg"] = dt("s_kTg" + sfx, [4 * KROWS, T], BF16, "Internal", addr_space="Local")
        d["vg"] = dt("s_vg" + sfx, [4 * NVU * 128, 16 * 65], BF16, "Internal", addr_space="Local")
        d["y"] = dt("s_y" + sfx, [16, 64, T], F32, "Internal")
        LD.append(d)
    with contextlib.ExitStack() as top:
        ss = SemState(top)
        c = Ctx()
        c.xT = _sb(top, nc, "xT", [128, 8 * T], F32)

        def fresh_x():
            c.x_r = [Res("x%d" % t) for t in range(NT)]

        with contextlib.ExitStack() as st:
            S = Sched(nc, ss)
            fresh_x()
            for k in range(8):
                S.add("sp", (lambda e, k=k: e.dma_start(out=c.xT[:, k * T:(k + 1) * T], in_=xT_d[:, k * T:(k + 1) * T])),
                      writes=c.x_r, dma_key="xT")
            S.drain()
            S.emit(st)
        for l in range(DEPTH):
            d = LD[l]
            with contextlib.ExitStack() as st:
                S = Sched(nc, ss)
                fresh_x()
                common_setup(c, st, nc, S, d["gains"], d["lconst"], cmat_d)
                da = dict(wg=d["wg1"], wu=d["wu1"], wd=d["wd1"], winf=d["winf"], winv=d["winv"], wsm=d["wsm"], tabs=tabs_d,
                          qT=d["qT"], kT=d["kT"], v=d["v"], q_r=Res("q"), k_r=Res("k"), v_r=Res("v"))
                emit_phase_a(c, st, nc, S, da)
                S.drain()
                S.emit(st)
            with contextlib.ExitStack() as st:
                S = Sched(nc, ss)
                v2 = d["v"].rearrange("u p c d -> (u p) (c d)")
                for a, b in K_PARTS:
                    S.add("pool", (lambda e, d=d, a=a, b=b: e.collective_compute(
                        "AllGather", ALU.bypass, replica_groups=RG, ins=[d["kT"][a:b, :]], outs=[d["kTg"][4 * a:4 * b, :]])),
                        dma_key="cc", inc=1)
                for a, b in V_PARTS:
                    S.add("pool", (lambda e, d=d, a=a, b=b, v2=v2: e.collective_compute(
                        "AllGather", ALU.bypass, replica_groups=RG, ins=[v2[a * 128:b * 128, :]],
                        outs=[d["vg"][4 * a * 128:4 * b * 128, :]])), dma_key="cc", inc=1)
                S.drain()
                S.emit(st)
            with contextlib.ExitStack() as st:
                S = Sched(nc, ss)
                common_setup(c, st, nc, S, d["gains"], d["lconst"], cmat_d)
                db = dict(qT=d["qT"], kT=d["kT"], v=d["v"], kTg=d["kTg"], vg=d["vg"], btl=d["btl"], t5c=t5c_d,
                          dlam=d["dlam"], y=d["y"], y_r=Res("y"))
                emit_phase_b(c, st, nc, S, db)
                S.drain()
                S.emit(st)
            with contextlib.ExitStack() as st:
                S = Sched(nc, ss)
                fresh_x()
                common_setup(c, st, nc, S, d["gains"], d["lconst"], cmat_d)
                dc = dict(wg=d["wg2"], wu=d["wu2"], wd=d["wd2"], wout=d["wout"], y=d["y"], y_r=Res("y"))
                emit_phase_c(c, st, nc, S, dc)
                S.drain()
                S.emit(st)
        with contextlib.ExitStack() as st:
            S = Sched(nc, ss)
            fresh_x()
            xo_r = Res("xo")
            store_x(c, S, xo_d, xo_r)
            S.add("sp", None, reads=[xo_r])
            S.drain()
            S.emit(st)
    return nc


def kernel(**inp):
    x = np.asarray(inp["x"], np.float32).reshape(BATCH * SEQ, D)
    cm = const_mats()
    cores = list(range(NCORE))
    shared = {}
    for l in range(DEPTH):
        H = layer_host(inp, l)
        sfx = "_%d" % l
        shared.update({"gains" + sfx: H["gains"], "lconst" + sfx: H["lconst"], "wg1" + sfx: H["ffn1_wg"], "wu1" + sfx: H["ffn1_wu"],
                       "wd1" + sfx: H["ffn1_wd"], "wg2" + sfx: H["ffn2_wg"], "wu2" + sfx: H["ffn2_wu"], "wd2" + sfx: H["ffn2_wd"],
                       "winf" + sfx: H["winf"], "winv" + sfx: H["winv"], "wsm" + sfx: H["wsm"],
                       "wout" + sfx: np.ascontiguousarray(np.asarray(inp["w_out"][l], np.float32).reshape(16, 64, 8, 128)
                                                         .transpose(2, 1, 0, 3).reshape(8, 64, 2048)),
                       "dlam" + sfx: np.ascontiguousarray(np.broadcast_to(
                           np.asarray(inp["diff_lambda"][l], np.float32).reshape(1, 128), (64, 128)))})
    maps = []
    for c in cores:
        m = dict(shared, xT=to_featmajor(x[c * T:(c + 1) * T]), cmat=cm, tabs=rope_tables(c))
        for l in range(DEPTH):
            btl, t5c = bias_tiles_host(inp, l, c)
            m["btl_%d" % l] = btl
            m["t5c"] = t5c
        maps.append(m)
    res = run_bass_kernel_spmd(_prog("f"), maps, core_ids=cores).results
    out = np.concatenate([from_featmajor(np.asarray(res[c]["xoT"])) for c in cores], axis=0)
    return np.ascontiguousarray(out.reshape(BATCH, SEQ, D).astype(np.float32))
```

```python
import contextlib
import math
import numpy as np
import ml_dtypes
import concourse.bass as bass
import concourse.mybir as mybir
from concourse.bass_utils import run_bass_kernel_spmd

F32 = mybir.dt.float32
BF16 = mybir.dt.bfloat16
AF = mybir.ActivationFunctionType
ALU = mybir.AluOpType

_UID = [0]


def _uid():
    _UID[0] += 1
    return _UID[0]

ENGS = ("pe", "act", "dve", "pool", "sp")
EPOCH = 16000
SAME_ENGINE_SYNC = True


class Res:
    __slots__ = ("name", "writer", "readers")

    def __init__(self, name=""):
        self.name = name
        self.writer = None
        self.readers = {}


class Op:
    __slots__ = ("eng", "fn", "deps", "dma_key", "signaled", "count", "semid", "idx", "inc")


class SemState:
    def __init__(self, stack):
        self.stack = stack
        self.cnt = {e: 0 for e in ENGS}
        self.dcnt = {}
        self.sems = {}


class Sched:
    def __init__(self, nc, semstate=None):
        self.nc = nc
        self.ops = []
        self.by_eng = {e: [] for e in ENGS}
        self.semstate = semstate
        self.last_dma = {}

    def drain(self):
        op = self.add("sp", None)
        op.deps = list(self.last_dma.values())
        return op

    def add(self, eng, fn, reads=(), writes=(), dma_key=None, inc=16):
        op = Op()
        op.inc = inc
        op.eng = eng
        op.fn = fn
        op.dma_key = dma_key
        op.signaled = dma_key is not None
        op.count = 0
        op.semid = None
        op.idx = len(self.ops)
        deps = {}
        for r in reads:
            if r.writer is not None:
                deps[r.writer.idx] = r.writer
        for w in writes:
            if w.writer is not None:
                deps[w.writer.idx] = w.writer
            for rd in w.readers.values():
                deps[rd.idx] = rd
        dl = []
        for d in deps.values():
            if d.dma_key is not None or dma_key is not None:
                need = True
            elif d.eng == eng:
                need = (eng != "pe") and SAME_ENGINE_SYNC
            else:
                need = True
            if need:
                d.signaled = True
                dl.append(d)
        op.deps = dl
        for r in reads:
            key = eng if dma_key is None else ("dma", op.idx)
            r.readers[key] = op
        for w in writes:
            w.writer = op
            w.readers = {}
        self.ops.append(op)
        self.by_eng[eng].append(op)
        if dma_key is not None:
            self.last_dma[dma_key] = op
        return op

    def emit(self, stack):
        nc = self.nc
        ss = self.semstate if self.semstate is not None else SemState(stack)
        cnt = ss.cnt
        dcnt = ss.dcnt
        for op in self.ops:
            if op.dma_key is not None:
                dcnt[op.dma_key] = dcnt.get(op.dma_key, 0) + op.inc
                op.count = dcnt[op.dma_key]
                op.semid = ("d", op.dma_key)
            elif op.signaled:
                c = cnt[op.eng]
                cnt[op.eng] = c + 1
                op.semid = ("e", op.eng, c // EPOCH)
                op.count = (c % EPOCH) + 1
        self.check(ss)
        sems = ss.sems
        for op in self.ops:
            if op.semid is not None and op.semid not in sems:
                sems[op.semid] = ss.stack.enter_context(nc.semaphore("s%d" % len(sems)))
        self.nsems = len(sems)
        block = stack.enter_context(nc.Block())
        engobj = {"pe": block.tensor, "act": block.scalar, "dve": block.vector,
                  "pool": block.gpsimd, "sp": block.sync}

        def make(ename):
            ops = self.by_eng[ename]

            def body(e):
                waited = {}
                wepoch = {}
                for op in ops:
                    for d in op.deps:
                        sid = d.semid
                        c = d.count
                        if sid[0] == "e" and wepoch.get(sid[1], -1) > sid[2]:
                            continue
                        if waited.get(sid, 0) >= c:
                            continue
                        e.wait_ge(sems[sid], c)
                        waited[sid] = c
                        if sid[0] == "e":
                            wepoch[sid[1]] = max(wepoch.get(sid[1], -1), sid[2])
                    if op.fn is None:
                        continue
                    ins = op.fn(e)
                    if op.dma_key is not None and op.inc == 1:
                        ins.then_inc(sems[op.semid])
                    elif op.dma_key is not None:
                        ins.then_inc(sems[op.semid], 16)
                    elif op.signaled:
                        ins.then_inc(sems[op.semid], 1)
            return body

        for ename in ENGS:
            if self.by_eng[ename]:
                engobj[ename](make(ename))


def _sched_check(self, ss):
    val = dict(getattr(ss, "val", {}))
    pos = {e: 0 for e in ENGS}
    progress = True
    while progress:
        progress = False
        for e in ENGS:
            ops = self.by_eng[e]
            while pos[e] < len(ops):
                op = ops[pos[e]]
                if all(val.get(d.semid, 0) >= d.count and (d.semid[0] != "e" or True) for d in op.deps):
                    if op.semid is not None:
                        if op.semid[0] == "d":
                            val[op.semid] = val.get(op.semid, 0) + op.inc
                        else:
                            val[op.semid] = val.get(op.semid, 0) + 1
                        assert val[op.semid] == op.count or op.semid[0] == "d", (op.semid, val[op.semid], op.count)
                    pos[e] += 1
                    progress = True
                else:
                    break
    stuck = {e: pos[e] for e in ENGS if pos[e] < len(self.by_eng[e])}
    assert not stuck, "deadlock in schedule: %r" % stuck
    ss.val = val
    print("sched ok: ops=%d per-eng=%r" % (len(self.ops), {e: len(self.by_eng[e]) for e in ENGS}))


Sched.check = _sched_check


class Ring:
    def __init__(self, stack, nc, name, n, shape, dtype, psum=False):
        self.bufs = []
        for i in range(n):
            nm = "r_%s%d" % (name, i)
            tn = "%s_%d" % (nm, _uid())
            if psum:
                t = stack.enter_context(nc.psum_tensor(tn, shape, dtype))
            else:
                t = stack.enter_context(nc.sbuf_tensor(tn, shape, dtype))
            self.bufs.append((t, Res(nm), nm))
        self.i = 0

    def next(self):
        b = self.bufs[self.i % len(self.bufs)]
        self.i += 1
        return b


D = 1024
SEQ = 8192
BATCH = 2
DEPTH = 2
DFF = 2816
NCORE = 8
T = 2048
NT = 4
TS = 512
EPS = 1e-6
NG = 59
G_FFN1, G_MIX, G_NAQ, G_NAK, G_QLAT, G_KVLAT, G_MQ, G_MK, G_DQ, G_DK, G_GQ, G_GK = \
    0, 8, 16, 17, 18, 20, 21, 22, 23, 24, 25, 26
G_BETA, G_FFN2, G_FIN = 27, 43, 51
WS_UQ, WS_KN, WS_V, WS_KR = 0, 768, 1152, 1408
WSM = 2176
C_ONES, C_BD64, C_BD32, C_RB, C_RD, C_ID = 0, 1, 2, 3, 4, 5
QROWS = 1152
KROWS = 1024
NVU = 14


def lambda_init(l):
    return 0.8 - 0.6 * math.exp(-0.3 * l)


class Ctx:
    pass


def _sb(st, nc, name, shape, dt):
    return st.enter_context(nc.sbuf_tensor("sb_%s_%d" % (name, _uid()), shape, dt))


class WStream:
    def __init__(self, S, ring, pf):
        self.S = S
        self.ring = ring
        self.pf = pf
        self.items = []
        self.issued = 0
        self.handles = []

    def plan(self, src_ap, ncols, part=128):
        self.items.append((src_ap, ncols, part))
        return len(self.items) - 1

    def get(self, i):
        while self.issued < min(len(self.items), i + 1 + self.pf):
            src, ncols, part = self.items[self.issued]
            buf, res, nm = self.ring.next()
            self.S.add("pool", (lambda e, buf=buf, src=src, ncols=ncols, part=part:
                                e.dma_start(out=buf[0:part, 0:ncols], in_=src)),
                       writes=[res], dma_key=nm)
            self.handles.append((buf, res))
            self.issued += 1
        return self.handles[i]


def rms_feat(c, srcs, P, ones_idx, n, gain_cols, outs):
    S = c.S
    ones = c.cmat[0:P, ones_idx * 128: ones_idx * 128 + P]
    ps, psr, _ = c.PS.next()
    last = len(srcs) - 1
    sc = float(n) ** -0.5
    for i, (src, sres) in enumerate(srcs):
        sq, sqr, _ = c.SQ.next()
        S.add("act", (lambda e, sq=sq, src=src: e.activation(out=sq[0:P, :], in_=src, func=AF.Square, scale=sc)),
              reads=[sres], writes=[sqr])
        S.add("pe", (lambda e, sq=sq, i=i: e.matmul(ps[0:P, :], ones, sq[0:P, :], start=(i == 0), stop=(i == last))),
              reads=[sqr, c.cmat_r], writes=[psr])
    rs, rsr, _ = c.RS.next()
    S.add("act", lambda e: e.activation(out=rs[0:P, :], in_=ps[0:P, :], func=AF.Ln, bias=c.eps[0:P, 0:1]),
          reads=[psr, c.eps_r], writes=[rsr])
    S.add("act", lambda e: e.activation(out=rs[0:P, :], in_=rs[0:P, :], func=AF.Exp, scale=-0.5),
          reads=[rsr], writes=[rsr])
    for i, ((src, sres), (out, ores)) in enumerate(zip(srcs, outs)):
        g = c.gains[0:P, gain_cols[i]:gain_cols[i] + 1]
        S.add("dve", (lambda e, src=src, out=out, g=g: e.scalar_tensor_tensor(
            out=out, in0=src, scalar=g, in1=rs[0:P, :], op0=ALU.mult, op1=ALU.mult)),
            reads=[sres, rsr, c.gains_r], writes=[ores])


def rope_feat(c, qn, qnr, P, r_idx, cos_ap, sin_ap, tab_r, out, outr):
    S = c.S
    R = c.cmat[0:P, r_idx * 128: r_idx * 128 + P]
    ps, psr, _ = c.PS.next()
    S.add("pe", lambda e: e.matmul(ps[0:P, :], R, qn, start=True, stop=True), reads=[qnr, c.cmat_r], writes=[psr])
    t1, t1r, _ = c.TF.next()
    t2, t2r, _ = c.TF.next()
    S.add("pool", lambda e: e.tensor_tensor(out=t1[0:P, :], in0=qn, in1=cos_ap, op=ALU.mult), reads=[qnr, tab_r], writes=[t1r])
    S.add("dve", lambda e: e.tensor_tensor(out=t2[0:P, :], in0=ps[0:P, :], in1=sin_ap, op=ALU.mult), reads=[psr, tab_r], writes=[t2r])
    S.add("pool", lambda e: e.tensor_tensor(out=out, in0=t1[0:P, :], in1=t2[0:P, :], op=ALU.add), reads=[t1r, t2r], writes=[outr])


def norm_x_tile(c, t, gcol0):
    hT, hTr, _ = c.HT.next()
    srcs = [(c.xT[:, k * T + t * TS: k * T + (t + 1) * TS], c.x_r[t]) for k in range(8)]
    outs = [(hT[:, k * TS:(k + 1) * TS], hTr) for k in range(8)]
    rms_feat(c, srcs, 128, C_ONES, D, [gcol0 + k for k in range(8)], outs)
    return hT, hTr


def ffn_tile(c, t, hT, hTr, ws, base):
    S = c.S
    for b in range(11):
        wg, wgr = ws.get(base + 2 * b)
        wu, wur = ws.get(base + 2 * b + 1)
        for jj in range(2):
            j = 2 * b + jj
            pg, pgr, _ = c.PS.next()
            pu, pur, _ = c.PS.next()
            for k in range(8):
                S.add("pe", (lambda e, k=k, pg=pg, wg=wg, jj=jj: e.matmul(
                    pg[:, :], wg[:, k * 256 + jj * 128: k * 256 + jj * 128 + 128], hT[:, k * TS:(k + 1) * TS],
                    start=(k == 0), stop=(k == 7))), reads=[wgr, hTr], writes=[pgr])
            for k in range(8):
                S.add("pe", (lambda e, k=k, pu=pu, wu=wu, jj=jj: e.matmul(
                    pu[:, :], wu[:, k * 256 + jj * 128: k * 256 + jj * 128 + 128], hT[:, k * TS:(k + 1) * TS],
                    start=(k == 0), stop=(k == 7))), reads=[wur, hTr], writes=[pur])
            sg, sgr, _ = c.TF.next()
            S.add("act", (lambda e, sg=sg, pg=pg: e.activation(out=sg[:, :], in_=pg[:, :], func=AF.Silu)),
                  reads=[pgr], writes=[sgr])
            S.add("dve", (lambda e, sg=sg, pu=pu, j=j: e.tensor_tensor(
                out=c.actT[:, j * TS:(j + 1) * TS], in0=sg[:, :], in1=pu[:, :], op=ALU.mult)),
                reads=[sgr, pur], writes=[c.act_r[j]])
    for m in range(8):
        wd, wdr = ws.get(base + 22 + m)
        pd, pdr, _ = c.PS.next()
        for j in range(22):
            S.add("pe", (lambda e, j=j, pd=pd, wd=wd: e.matmul(
                pd[:, :], wd[:, j * 128:(j + 1) * 128], c.actT[:, j * TS:(j + 1) * TS],
                start=(j == 0), stop=(j == 21))), reads=[wdr, c.act_r[j]], writes=[pdr])
        xs = c.xT[:, m * T + t * TS: m * T + (t + 1) * TS]
        S.add("dve", (lambda e, pd=pd, xs=xs: e.scalar_tensor_tensor(
            out=xs, in0=pd[:, :], scalar=0.5, in1=xs, op0=ALU.mult, op1=ALU.add)),
            reads=[pdr, c.x_r[t]], writes=[c.x_r[t]])


def plan_ffn(ws, wg, wu, wd):
    base = len(ws.items)
    for b in range(11):
        ws.plan(wg[b], 2048)
        ws.plan(wu[b], 2048)
    for m in range(8):
        ws.plan(wd[m], 2816)
    return base


def common_setup(c, st, nc, S, gains_d, lconst_d, cmat_d):
    c.S = S
    c.nc = nc
    c.gains = _sb(st, nc, "gains", [128, NG], F32)
    c.gains_r = Res("gains")
    c.lconst = _sb(st, nc, "lconst", [128, 2], F32)
    c.lconst_r = Res("lconst")
    c.cmat = _sb(st, nc, "cmat", [128, 6 * 128], BF16)
    c.cmat_r = Res("cmat")
    c.eps = _sb(st, nc, "eps", [128, 1], F32)
    c.eps_r = Res("eps")
    S.add("sp", lambda e: e.dma_start(out=c.gains[:, :], in_=gains_d), writes=[c.gains_r], dma_key="gains")
    S.add("sp", lambda e: e.dma_start(out=c.lconst[:, :], in_=lconst_d), writes=[c.lconst_r], dma_key="lconst")
    S.add("pool", lambda e: e.dma_start(out=c.cmat[:, :], in_=cmat_d), writes=[c.cmat_r], dma_key="cmat")
    S.add("dve", lambda e: e.memset(c.eps[:, :], EPS), writes=[c.eps_r])


def load_x(c, st, nc, S, x_d):
    c.xT = _sb(st, nc, "xT", [128, 8 * T], F32)
    c.x_r = [Res("x%d" % t) for t in range(NT)]
    for k in range(8):
        S.add("sp", (lambda e, k=k: e.dma_start(out=c.xT[:, k * T:(k + 1) * T], in_=x_d[:, k * T:(k + 1) * T])),
              writes=c.x_r, dma_key="xT")


def store_x(c, S, xo_d, xo_r):
    for k in range(8):
        S.add("sp", (lambda e, k=k: e.dma_start(out=xo_d[:, k * T:(k + 1) * T], in_=c.xT[:, k * T:(k + 1) * T])),
              reads=c.x_r, writes=[xo_r], dma_key="xT")


def emit_phase_a(c, st, nc, S, d):
    c.PS = Ring(st, nc, "ps", 8, [128, 512], F32, psum=True)
    c.SQ = Ring(st, nc, "sq", 3, [128, 512], BF16)
    c.RS = Ring(st, nc, "rs", 2, [128, 512], F32)
    c.TF = Ring(st, nc, "tf", 4, [128, 512], F32)
    c.HT = Ring(st, nc, "hT", 2, [128, 8 * TS], BF16)
    c.actT = _sb(st, nc, "actT", [128, 22 * TS], BF16)
    c.act_r = [Res("act%d" % j) for j in range(22)]
    wring = Ring(st, nc, "wb", 6, [128, 2816], BF16)
    ws = WStream(S, wring, 4)
    wsm = _sb(st, nc, "wsm", [128, WSM], BF16)
    wsm_r = Res("wsm")
    S.add("pool", lambda e: e.dma_start(out=wsm[:, :], in_=d["wsm"]), writes=[wsm_r], dma_key="wsm")
    tabs = _sb(st, nc, "tabs", [128, 4 * T], F32)
    tab_r = Res("tabs")
    for i in range(4):
        S.add("sp", (lambda e, i=i: e.dma_start(out=tabs[:, i * T:(i + 1) * T], in_=d["tabs"][i])),
              writes=[tab_r], dma_key="tabs")
    for col, sc in ((G_NAQ, 64 ** -0.5), (G_MQ, 96 ** -0.5), (G_DQ, 32 ** -0.5), (G_GQ, 64 ** -0.5)):
        S.add("dve", (lambda e, col=col, sc=sc: e.tensor_scalar(
            out=c.gains[:, col:col + 1], in0=c.gains[:, col:col + 1], scalar1=float(sc), scalar2=None, op0=ALU.mult)),
            reads=[c.gains_r], writes=[c.gains_r])

    bases_ffn = [plan_ffn(ws, d["wg"], d["wu"], d["wd"]) for t in range(NT)]
    bases_in = []
    for t in range(NT):
        bases_in.append(len(ws.items))
        for b in range(7):
            ws.plan(d["winf"][b], 2048)
        for vb in range(2):
            ws.plan(d["winv"][vb], 2560)

    import os
    n1 = int(os.environ.get("KA_N1", NT))
    n2 = int(os.environ.get("KA_N2", NT))
    for t in range(n1):
        hT, hTr = norm_x_tile(c, t, G_FFN1)
        ffn_tile(c, t, hT, hTr, ws, bases_ffn[t])

    STG = Ring(st, nc, "stg", 6, [128, 512], BF16)
    QN = Ring(st, nc, "qn", 3, [128, 512], BF16)
    cqn = _sb(st, nc, "cqn", [128, 2 * TS], BF16)
    cqn_r = Res("cqn")
    ckvn = _sb(st, nc, "ckvn", [128, TS], BF16)
    ckvn_r = Res("ckvn")
    vst = _sb(st, nc, "vst", [128, 4, NVU, 65], BF16)
    vst_r = Res("vst")
    S.add("pool", lambda e: e.memset(vst[:, :, :, :], 1.0), writes=[vst_r])
    q_r, k_r, v_r = d["q_r"], d["k_r"], d["v_r"]

    def store_rows(dst, dres, row0, P, t, src, sres, key):
        S.add("sp", lambda e: e.dma_start(out=dst[row0:row0 + P, t * TS:(t + 1) * TS], in_=src),
              reads=[sres], writes=[dres], dma_key=key)

    def stage2_tile(t, hT, hTr):

        def proj_chunk(wb, wbr, jj):
            p, pr, _ = c.PS.next()
            for k in range(8):
                S.add("pe", (lambda e, k=k: e.matmul(
                    p[:, :], wb[:, k * 256 + jj * 128: k * 256 + jj * 128 + 128], hT[:, k * TS:(k + 1) * TS],
                    start=(k == 0), stop=(k == 7))), reads=[wbr, hTr], writes=[pr])
            return p, pr

        def simple_head_chunk(wb, wbr, jj, ones_idx, n, gcol, dst, dres, row0, rope):
            p, pr = proj_chunk(wb, wbr, jj)
            if rope:
                qn, qnr, _ = QN.next()
                rms_feat(c, [(p[:, :], pr)], 128, ones_idx, n, [gcol], [(qn[:, :], qnr)])
                sg, sgr, key = STG.next()
                rope_feat(c, qn[:, :], qnr, 128, C_RD, tabs[:, 2 * T + t * TS: 2 * T + (t + 1) * TS],
                          tabs[:, 3 * T + t * TS: 3 * T + (t + 1) * TS], tab_r, sg[:, :], sgr)
            else:
                sg, sgr, key = STG.next()
                rms_feat(c, [(p[:, :], pr)], 128, ones_idx, n, [gcol], [(sg[:, :], sgr)])
            store_rows(dst, dres, row0, 128, t, sg[:, :], sgr, key)

        b0 = bases_in[t]
        wb, wbr = ws.get(b0 + 0)
        for jj in range(2):
            simple_head_chunk(wb, wbr, jj, C_BD64, 64, G_NAQ, d["qT"], q_r, 0 + jj * 128, False)
        wb, wbr = ws.get(b0 + 1)
        for jj in range(2):
            simple_head_chunk(wb, wbr, jj, C_BD64, 64, G_NAK, d["kT"], k_r, 0 + jj * 128, False)
        wb, wbr = ws.get(b0 + 2)
        p0, p0r = proj_chunk(wb, wbr, 0)
        p1, p1r = proj_chunk(wb, wbr, 1)
        rms_feat(c, [(p0[:, :], p0r), (p1[:, :], p1r)], 128, C_ONES, 256, [G_QLAT, G_QLAT + 1],
                 [(cqn[:, 0:TS], cqn_r), (cqn[:, TS:2 * TS], cqn_r)])
        for h in range(4):
            p, pr, _ = c.PS.next()
            for k in range(2):
                S.add("pe", (lambda e, k=k, h=h, p=p: e.matmul(
                    p[0:96, :], wsm[:, WS_UQ + k * 384 + h * 96: WS_UQ + k * 384 + h * 96 + 96],
                    cqn[:, k * TS:(k + 1) * TS], start=(k == 0), stop=(k == 1))),
                    reads=[wsm_r, cqn_r], writes=[pr])
            qn, qnr, _ = QN.next()
            rms_feat(c, [(p[0:96, :], pr)], 96, C_ONES, 96, [G_MQ], [(qn[0:96, :], qnr)])
            sg, sgr, key = STG.next()
            rope_feat(c, qn[0:96, :], qnr, 96, C_RB, tabs[0:96, 0 * T + t * TS: 0 * T + (t + 1) * TS],
                      tabs[0:96, 1 * T + t * TS: 1 * T + (t + 1) * TS], tab_r, sg[0:96, :], sgr)
            store_rows(d["qT"], q_r, 256 + h * 96, 96, t, sg[0:96, :], sgr, key)
        wb, wbr = ws.get(b0 + 3)
        p, pr = proj_chunk(wb, wbr, 0)
        rms_feat(c, [(p[:, :], pr)], 128, C_ONES, 128, [G_KVLAT], [(ckvn[:, :], ckvn_r)])
        for h in range(4):
            p, pr, _ = c.PS.next()
            S.add("pe", (lambda e, h=h, p=p: e.matmul(
                p[0:96, :], wsm[:, WS_KN + h * 96: WS_KN + h * 96 + 96], ckvn[:, :], start=True, stop=False)),
                reads=[wsm_r, ckvn_r], writes=[pr])
            for k in range(8):
                S.add("pe", (lambda e, k=k, p=p: e.matmul(
                    p[0:96, :], wsm[:, WS_KR + k * 96: WS_KR + k * 96 + 96], hT[:, k * TS:(k + 1) * TS],
                    start=False, stop=(k == 7))), reads=[wsm_r, hTr], writes=[pr])
            qn, qnr, _ = QN.next()
            rms_feat(c, [(p[0:96, :], pr)], 96, C_ONES, 96, [G_MK], [(qn[0:96, :], qnr)])
            sg, sgr, key = STG.next()
            rope_feat(c, qn[0:96, :], qnr, 96, C_RB, tabs[0:96, 0 * T + t * TS: 0 * T + (t + 1) * TS],
                      tabs[0:96, 1 * T + t * TS: 1 * T + (t + 1) * TS], tab_r, sg[0:96, :], sgr)
            store_rows(d["kT"], k_r, 256 + h * 96, 96, t, sg[0:96, :], sgr, key)
        for s in range(4):
            p, pr, _ = c.PS.next()
            S.add("pe", (lambda e, s=s, p=p: e.matmul(
                p[:, 0:256], ckvn[:, s * 128:(s + 1) * 128], wsm[:, WS_V:WS_V + 256], start=True, stop=True)),
                reads=[wsm_r, ckvn_r], writes=[pr])
            S.add("act", (lambda e, s=s, p=p: e.activation(out=vst[:, s, 10:14, 0:64], in_=p[:, 0:256].rearrange("p (u d) -> p u d", d=64), func=AF.Copy)),
                  reads=[pr], writes=[vst_r])
        simple_head_chunk(wb, wbr, 1, C_BD64, 64, G_GK, d["kT"], k_r, 896, True)
        wb, wbr = ws.get(b0 + 4)
        for jj in range(2):
            simple_head_chunk(wb, wbr, jj, C_BD32, 32, G_DQ, d["qT"], q_r, 640 + jj * 128, False)
        wb, wbr = ws.get(b0 + 5)
        for jj in range(2):
            simple_head_chunk(wb, wbr, jj, C_BD32, 32, G_DK, d["kT"], k_r, 640 + jj * 128, False)
        wb, wbr = ws.get(b0 + 6)
        for jj in range(2):
            simple_head_chunk(wb, wbr, jj, C_BD64, 64, G_GQ, d["qT"], q_r, 896 + jj * 128, True)
        for vb in range(2):
            wv, wvr = ws.get(b0 + 7 + vb)
            for s in range(4):
                p, pr, _ = c.PS.next()
                for k in range(8):
                    S.add("pe", (lambda e, k=k, s=s, p=p, wv=wv: e.matmul(
                        p[:, 0:320], hT[:, k * TS + s * 128: k * TS + (s + 1) * 128], wv[:, k * 320:(k + 1) * 320],
                        start=(k == 0), stop=(k == 7))), reads=[wvr, hTr], writes=[pr])
                S.add("dve", (lambda e, s=s, p=p, vb=vb: e.tensor_copy(out=vst[:, s, vb * 5:(vb + 1) * 5, 0:64], in_=p[:, 0:320].rearrange("p (u d) -> p u d", d=64))),
                      reads=[pr], writes=[vst_r])
        for u in range(NVU):
            S.add("sp", (lambda e, u=u: e.dma_start(out=d["v"][u, :, t * 4:(t + 1) * 4, :], in_=vst[:, :, u, :])),
                  reads=[vst_r], writes=[v_r], dma_key="vst")

    for t in range(n2):
        hT, hTr = norm_x_tile(c, t, G_MIX)
        stage2_tile(t, hT, hTr)


def build_a():
    nc = bass.Bass("TRN2", target_bir_lowering=False)
    dt = lambda name, shape, dtype, kind: nc.dram_tensor(name, shape, dtype, kind=kind).ap()
    d = {}
    xT = dt("xT", [128, 8 * T], F32, "ExternalInput")
    gains = dt("gains", [128, NG], F32, "ExternalInput")
    lconst = dt("lconst", [128, 2], F32, "ExternalInput")
    cmat = dt("cmat", [128, 768], F32, "ExternalInput")
    d["wg"] = dt("wg", [11, 128, 2048], F32, "ExternalInput")
    d["wu"] = dt("wu", [11, 128, 2048], F32, "ExternalInput")
    d["wd"] = dt("wd", [8, 128, 2816], F32, "ExternalInput")
    d["winf"] = dt("winf", [7, 128, 2048], F32, "ExternalInput")
    d["winv"] = dt("winv", [2, 128, 2560], F32, "ExternalInput")
    d["wsm"] = dt("wsm", [128, WSM], F32, "ExternalInput")
    d["tabs"] = dt("tabs", [4, 128, T], F32, "ExternalInput")
    xo = dt("x1T", [128, 8 * T], F32, "ExternalOutput")
    d["qT"] = dt("qT", [QROWS, T], BF16, "ExternalOutput")
    d["kT"] = dt("kT", [KROWS, T], BF16, "ExternalOutput")
    d["v"] = dt("v", [NVU, 128, 16, 65], BF16, "ExternalOutput")
    d["q_r"], d["k_r"], d["v_r"] = Res("qT"), Res("kT"), Res("v")
    xo_r = Res("xo")
    with contextlib.ExitStack() as st:
        S = Sched(nc)
        c = Ctx()
        common_setup(c, st, nc, S, gains, lconst, cmat)
        load_x(c, st, nc, S, xT)
        emit_phase_a(c, st, nc, S, d)
        store_x(c, S, xo, xo_r)
        S.add("sp", None, reads=[xo_r, d["q_r"], d["k_r"], d["v_r"]])
        S.emit(st)
    return nc


def blk_cols(W, cpb):
    K, N = W.shape
    kc = K // 128
    nb = N // cpb
    return np.ascontiguousarray(W.reshape(kc, 128, nb, cpb).transpose(2, 1, 0, 3).reshape(nb, 128, kc * cpb))


def const_mats():
    m = np.zeros((6, 128, 128), np.float32)
    m[C_ONES] = 1.0
    for b in range(2):
        m[C_BD64, b * 64:(b + 1) * 64, b * 64:(b + 1) * 64] = 1.0
    for b in range(4):
        m[C_BD32, b * 32:(b + 1) * 32, b * 32:(b + 1) * 32] = 1.0
    for i in range(16):
        m[C_RB, 80 + i, 64 + i] = -1.0
        m[C_RB, 64 + i, 80 + i] = 1.0
    for hb in (0, 64):
        for sb in (0, 32):
            for i in range(16):
                a = hb + sb + i
                b = a + 16
                m[C_RD, b, a] = -1.0
                m[C_RD, a, b] = 1.0
    m[C_ID] = np.eye(128, dtype=np.float32)
    return np.ascontiguousarray(m.transpose(1, 0, 2).reshape(128, 768))


def rope_tables(core):
    r = core % 4
    t = np.arange(r * T, (r + 1) * T)
    inv = np.exp(-math.log(10000.0) * np.arange(0, 32, 2, dtype=np.float32) / 32).astype(np.float32)
    a_seq = t.astype(np.float32)[:, None] * inv[None, :]
    a_row = (t // 64).astype(np.float32)[:, None] * inv[None, :]
    a_col = (t % 64).astype(np.float32)[:, None] * inv[None, :]
    tabs = np.zeros((4, 128, T), np.float32)
    tabs[0, 0:64] = 1.0
    for p in range(64, 96):
        tabs[0, p] = np.cos(a_seq[:, (p - 64) % 16])
        tabs[1, p] = np.sin(a_seq[:, (p - 64) % 16])
    for p in range(128):
        f = p % 64
        a = a_row if f < 32 else a_col
        tabs[2, p] = np.cos(a[:, f % 16])
        tabs[3, p] = np.sin(a[:, f % 16])
    return tabs


def tile64(v):
    return np.tile(v, 128 // v.shape[0])


def layer_host(inp, l):
    f = lambda k: np.asarray(inp[k][l], np.float32)
    h = {}
    for nm in ("ffn1", "ffn2"):
        h[nm + "_wg"] = blk_cols(f(nm + "_w_gate"), 256)
        h[nm + "_wu"] = blk_cols(f(nm + "_w_up"), 256)
        h[nm + "_wd"] = blk_cols(f(nm + "_w_down"), 128)
    win = f("w_in")
    cols_f = np.concatenate([np.arange(0, 256), np.arange(256, 512), np.arange(768, 1024), np.arange(1024, 1152),
                             np.arange(2208, 2336), np.arange(1184, 1440), np.arange(1440, 1696), np.arange(1952, 2208)])
    cols_v = np.concatenate([np.arange(512, 768), np.arange(1696, 1952), np.arange(2336, 2464)])
    h["winf"] = blk_cols(win[:, cols_f], 256)
    h["winv"] = blk_cols(win[:, cols_v], 320)
    wsm = np.zeros((128, WSM), np.float32)
    wuq = f("mla_w_uq")
    wsm[:, WS_UQ:WS_UQ + 768] = wuq.reshape(2, 128, 384).transpose(1, 0, 2).reshape(128, 768)
    wukv = f("mla_w_ukv")
    for hh in range(4):
        wsm[:, WS_KN + hh * 96: WS_KN + hh * 96 + 64] = wukv[:, hh * 128: hh * 128 + 64]
        wsm[:, WS_V + hh * 64: WS_V + (hh + 1) * 64] = wukv[:, hh * 128 + 64: hh * 128 + 128]
    kr = win[:, 1152:1184].reshape(8, 128, 32)
    for k in range(8):
        wsm[:, WS_KR + k * 96 + 64: WS_KR + k * 96 + 96] = kr[k]
    h["wsm"] = wsm
    g = np.zeros((128, NG), np.float32)
    g[:, G_FFN1:G_FFN1 + 8] = f("ffn1_norm").reshape(8, 128).T
    g[:, G_MIX:G_MIX + 8] = f("mix_norm").reshape(8, 128).T
    g[:, G_NAQ] = tile64(f("na_q_norm"))
    g[:, G_NAK] = tile64(f("na_k_norm"))
    g[:, G_QLAT:G_QLAT + 2] = f("mla_q_lat_norm").reshape(2, 128).T
    g[:, G_KVLAT] = f("mla_kv_lat_norm")
    g[0:96, G_MQ] = f("mla_q_norm")
    g[0:96, G_MK] = f("mla_k_norm")
    g[:, G_DQ] = tile64(f("diff_q_norm"))
    g[:, G_DK] = tile64(f("diff_k_norm"))
    g[:, G_GQ] = tile64(f("gqa_q_norm"))
    g[:, G_GK] = tile64(f("gqa_k_norm"))
    for hh in range(4):
        g[0:64, G_BETA + 0 + hh] = f("na_beta")[hh * 64:(hh + 1) * 64]
        g[0:64, G_BETA + 4 + hh] = f("mla_beta")[hh * 64:(hh + 1) * 64]
        g[0:64, G_BETA + 8 + hh] = f("diff_subln")
        g[0:64, G_BETA + 12 + hh] = f("gqa_beta")[hh * 64:(hh + 1) * 64]
    g[:, G_FFN2:G_FFN2 + 8] = f("ffn2_norm").reshape(8, 128).T
    g[:, G_FIN:G_FIN + 8] = f("final_norm").reshape(8, 128).T
    h["gains"] = g
    lc = np.zeros((128, 2), np.float32)
    lc[:, 0] = lambda_init(l)
    lc[:, 1] = 1.0 - lambda_init(l)
    h["lconst"] = lc
    return h


def emit_phase_c(c, st, nc, S, d):
    c.PS = Ring(st, nc, "ps", 8, [128, 512], F32, psum=True)
    c.SQ = Ring(st, nc, "sq", 3, [128, 512], BF16)
    c.RS = Ring(st, nc, "rs", 2, [128, 512], F32)
    c.TF = Ring(st, nc, "tf", 4, [128, 512], F32)
    c.HT = Ring(st, nc, "hT", 2, [128, 8 * TS], BF16)
    c.actT = _sb(st, nc, "actT", [128, 22 * TS], BF16)
    c.act_r = [Res("cact%d" % j) for j in range(22)]
    wring = Ring(st, nc, "wb", 6, [128, 2816], BF16)
    ws = WStream(S, wring, 4)
    YT = Ring(st, nc, "yt", 1, [64, 16 * TS], F32)
    yn = _sb(st, nc, "yn", [64, 16 * TS], BF16)
    yn_r = Res("yn")
    S.add("dve", lambda e: e.tensor_scalar(out=c.gains[0:64, G_BETA + 8:G_BETA + 12], in0=c.gains[0:64, G_BETA + 8:G_BETA + 12],
                                            scalar1=c.lconst[0:64, 1:2], scalar2=None, op0=ALU.mult),
          reads=[c.gains_r, c.lconst_r], writes=[c.gains_r])
    bases = []
    for t in range(NT):
        b0 = len(ws.items)
        for m in range(8):
            ws.plan(d["wout"][m], 2048, part=64)
        plan_ffn(ws, d["wg"], d["wu"], d["wd"])
        bases.append(b0)

    def tile_c(t, yt, ytr, key):
        for u in range(16):
            S.add("sp", (lambda e, u=u: e.dma_start(out=yt[:, u * TS:(u + 1) * TS], in_=d["y"][u, :, t * TS:(t + 1) * TS])),
                  reads=[d["y_r"]], writes=[ytr], dma_key=key)
        for g in range(4):
            if g == 2:
                for h in range(4):
                    u = 8 + h
                    rms_feat(c, [(yt[:, u * TS:(u + 1) * TS], ytr)], 64, C_ONES, 64, [G_BETA + u],
                             [(yn[:, u * TS:(u + 1) * TS], yn_r)])
            else:
                us = [4 * g + h for h in range(4)]
                rms_feat(c, [(yt[:, u * TS:(u + 1) * TS], ytr) for u in us], 64, C_ONES, 256, [G_BETA + u for u in us],
                         [(yn[:, u * TS:(u + 1) * TS], yn_r) for u in us])
        for m in range(8):
            wb, wbr = ws.get(bases[t] + m)
            p, pr, _ = c.PS.next()
            for hk in range(16):
                S.add("pe", (lambda e, hk=hk, p=p, wb=wb: e.matmul(
                    p[:, :], wb[0:64, hk * 128:(hk + 1) * 128], yn[:, hk * TS:(hk + 1) * TS],
                    start=(hk == 0), stop=(hk == 15))), reads=[wbr, yn_r], writes=[pr])
            xs = c.xT[:, m * T + t * TS: m * T + (t + 1) * TS]
            S.add("dve", (lambda e, p=p, xs=xs: e.tensor_tensor(out=xs, in0=p[:, :], in1=xs, op=ALU.add)),
                  reads=[pr, c.x_r[t]], writes=[c.x_r[t]])
        hT, hTr = norm_x_tile(c, t, G_FFN2)
        ffn_tile(c, t, hT, hTr, ws, bases[t] + 8)
        srcs = [(c.xT[:, k * T + t * TS: k * T + (t + 1) * TS], c.x_r[t]) for k in range(8)]
        rms_feat(c, srcs, 128, C_ONES, D, [G_FIN + k for k in range(8)], srcs)

    for t in range(NT):
        yt, ytr, key = YT.next()
        tile_c(t, yt, ytr, key)


def build_c():
    nc = bass.Bass("TRN2", target_bir_lowering=False)
    dt = lambda name, shape, dtype, kind: nc.dram_tensor(name, shape, dtype, kind=kind).ap()
    d = {}
    xT = dt("xT", [128, 8 * T], F32, "ExternalInput")
    gains = dt("gains", [128, NG], F32, "ExternalInput")
    lconst = dt("lconst", [128, 2], F32, "ExternalInput")
    cmat = dt("cmat", [128, 768], F32, "ExternalInput")
    d["wg"] = dt("wg", [11, 128, 2048], F32, "ExternalInput")
    d["wu"] = dt("wu", [11, 128, 2048], F32, "ExternalInput")
    d["wd"] = dt("wd", [8, 128, 2816], F32, "ExternalInput")
    d["wout"] = dt("wout", [8, 64, 2048], F32, "ExternalInput")
    d["y"] = dt("y", [16, 64, T], F32, "ExternalInput")
    d["y_r"] = Res("y")
    xo = dt("xoT", [128, 8 * T], F32, "ExternalOutput")
    xo_r = Res("xo")
    with contextlib.ExitStack() as st:
        S = Sched(nc)
        c = Ctx()
        common_setup(c, st, nc, S, gains, lconst, cmat)
        load_x(c, st, nc, S, xT)
        emit_phase_c(c, st, nc, S, d)
        store_x(c, S, xo, xo_r)
        S.add("sp", None, reads=[xo_r])
        S.emit(st)
    return nc


LA = 2


def near_entries(kind, qb):
    ent = []
    if kind == "A":
        lo, hi, halo = 4 * qb - 2, 4 * qb + 5, 2
    else:
        lo, hi, halo = 4 * qb - 1, 4 * qb + 4, 1
    for lc in range(max(lo, 0), min(hi, 15) + 1):
        ent.append(("own", lc))
    if qb == 0:
        for j in range(4):
            for x in range(16 - halo, 16):
                ent.append(("gath", 16 * j + x))
    if qb == 3:
        for j in range(4):
            for x in range(halo):
                ent.append(("gath", 16 * j + x))
    return ent


def tile_ids():
    ids = {}
    n = 0
    for kind in ("A", "C"):
        for h in range(4):
            for qb in range(4):
                for i, _e in enumerate(near_entries(kind, qb)):
                    ids[(kind, h, qb, i)] = n
                    n += 1
    return ids, n


TILE_IDS, NTILES = tile_ids()


def emit_phase_b(c, st, nc, S, d):
    PSS = Ring(st, nc, "pss", 3, [128, 1024], F32, psum=True)
    PO = Ring(st, nc, "po", 1, [128, 512], F32, psum=True)
    PBC = Ring(st, nc, "pbc", 1, [128, 512], F32, psum=True)
    PT = Ring(st, nc, "pt", 4, [128, 1024], BF16)
    KT = Ring(st, nc, "kt", 2, [128, SEQ], BF16)
    VT = Ring(st, nc, "vt", 2, [128, 64, 65], BF16)
    KTO = Ring(st, nc, "kto", 2, [128, T], BF16)
    VTO = Ring(st, nc, "vto", 2, [128, 16, 65], BF16)
    QT = Ring(st, nc, "qt", 2, [128, T], BF16)
    BT = Ring(st, nc, "bt", 12, [128, 512], BF16)
    VS = Ring(st, nc, "vs", 2, [128, 64, 65], BF16)
    SC = Ring(st, nc, "sc", 2, [128, 64], F32)
    OSB = Ring(st, nc, "osb", 2, [128, 512], F32)
    ON = Ring(st, nc, "on", 4, [64, 512], F32)
    RR = Ring(st, nc, "rr", 2, [128, 512], F32)
    bts = WStream(S, BT, 8)
    t5c = _sb(st, nc, "t5c", [128, 1024], F32)
    t5c_r = Res("t5c")
    S.add("sp", lambda e: e.dma_start(out=t5c[:, :], in_=d["t5c"]), writes=[t5c_r], dma_key="t5c")
    onesf = _sb(st, nc, "onesf", [128, 64], F32)
    onesf_r = Res("onesf")
    S.add("dve", lambda e: e.memset(onesf[:, :], 1.0), writes=[onesf_r])
    dl = _sb(st, nc, "dl", [64, 128], F32)
    dl_r = Res("dl")
    S.add("sp", lambda e: e.dma_start(out=dl[:, :], in_=d["dlam"]), writes=[dl_r], dma_key="dl")
    lt = _sb(st, nc, "lt", [64, 72], F32)
    lt_r = Res("lt")
    S.add("dve", lambda e: e.tensor_tensor(out=lt[:, 0:32], in0=dl[:, 0:32], in1=dl[:, 32:64], op=ALU.mult), reads=[dl_r], writes=[lt_r])
    S.add("dve", lambda e: e.tensor_tensor(out=lt[:, 32:64], in0=dl[:, 64:96], in1=dl[:, 96:128], op=ALU.mult), reads=[dl_r, lt_r], writes=[lt_r])
    S.add("dve", lambda e: e.reduce_sum(lt[:, 64:65], lt[:, 0:32], axis=mybir.AxisListType.X), reads=[lt_r], writes=[lt_r])
    S.add("dve", lambda e: e.reduce_sum(lt[:, 65:66], lt[:, 32:64], axis=mybir.AxisListType.X), reads=[lt_r], writes=[lt_r])
    S.add("act", lambda e: e.activation(out=lt[:, 66:68], in_=lt[:, 64:66], func=AF.Exp), reads=[lt_r], writes=[lt_r])
    S.add("dve", lambda e: e.tensor_tensor(out=lt[:, 68:69], in0=lt[:, 67:68], in1=lt[:, 66:67], op=ALU.subtract), reads=[lt_r], writes=[lt_r])
    S.add("dve", lambda e: e.tensor_scalar(out=lt[:, 69:70], in0=lt[:, 68:69], scalar1=c.lconst[0:64, 0:1], scalar2=None, op0=ALU.subtract),
          reads=[lt_r, c.lconst_r], writes=[lt_r])
    nlam = lt[:, 69:70]
    ident = c.cmat[:, C_ID * 128:(C_ID + 1) * 128]
    y_r = d["y_r"]

    def attend(qt, qtr, qp0, dq, qb, chunks):
        O, Or, _ = PO.next()
        steps = []
        i = 0
        isc = lambda ch: ch[5] is not None and ch[5][0] == "const"
        while i < len(chunks):
            if i + 1 < len(chunks) and not isc(chunks[i]) and not isc(chunks[i + 1]):
                steps.append([chunks[i], chunks[i + 1]])
                i += 2
            else:
                steps.append([chunks[i]])
                i += 1
        n = len(steps)
        total = len(chunks)
        pend = []
        pvi = [0]
        for i in range(n + LA):
            if i < n:
                st_ = steps[i]
                s_, sr, _ = PSS.next()
                w = 512 * len(st_)
                for hf, (kt, ktr, vt, vtr, kc, mode) in enumerate(st_):
                    sl = s_[:, hf * 512:(hf + 1) * 512]
                    if mode is not None and mode[0] == "tile":
                        bt, btr = bts.get(mode[1])
                        S.add("pe", (lambda e, sl=sl, bt=bt: e.matmul(sl, ident, bt[:, :], start=True, stop=False)),
                              reads=[btr, c.cmat_r], writes=[sr])
                        first = False
                    else:
                        first = True
                    S.add("pe", (lambda e, sl=sl, kc=kc, first=first, kt=kt: e.matmul(
                        sl, kt[qp0:qp0 + dq, kc * 128:(kc + 1) * 128], qt[qp0:qp0 + dq, qb * TS:(qb + 1) * TS],
                        start=first, stop=True)), reads=[ktr, qtr], writes=[sr])
                p_, pr, _ = PT.next()
                mode = st_[0][5]
                if len(st_) == 1 and mode is not None and mode[0] == "const":
                    bcol = mode[1]
                    S.add("act", (lambda e, s_=s_, p_=p_, bcol=bcol: e.activation(out=p_[:, 0:512], in_=s_[:, 0:512], func=AF.Exp, bias=bcol)),
                          reads=[sr, t5c_r], writes=[pr])
                else:
                    S.add("act", (lambda e, s_=s_, p_=p_, w=w: e.activation(out=p_[:, 0:w], in_=s_[:, 0:w], func=AF.Exp)),
                          reads=[sr], writes=[pr])
                pend.append((st_, p_, pr))
            j = i - LA
            if j >= 0:
                st_, p_, pr = pend[j]
                for hf, (kt, ktr, vt, vtr, kc, mode) in enumerate(st_):
                    k_ = pvi[0]
                    pvi[0] += 1
                    S.add("pe", (lambda e, kc=kc, p_=p_, k_=k_, vt=vt, hf=hf: e.matmul(
                        O[0:65, :], vt[:, kc, :], p_[:, hf * 512:(hf + 1) * 512], start=(k_ == 0), stop=(k_ == total - 1))),
                        reads=[vtr, pr], writes=[Or])
        rr, rrr, _ = RR.next()
        osb, osbr, _ = OSB.next()
        S.add("dve", lambda e: e.tensor_copy(out=osb[0:65, :], in_=O[0:65, :]), reads=[Or], writes=[osbr])
        S.add("dve", lambda e: e.reciprocal(out=rr[64:65, :], in_=osb[64:65, :]), reads=[osbr], writes=[rrr])
        bc, bcr, _ = PBC.next()
        S.add("pe", lambda e: e.matmul(bc[0:64, :], onesf[64:65, 0:64], rr[64:65, :], start=True, stop=True),
              reads=[rrr, onesf_r], writes=[bcr])
        on, onr, _ = ON.next()
        S.add("dve", lambda e: e.tensor_tensor(out=on[:, :], in0=osb[0:64, :], in1=bc[0:64, :], op=ALU.mult),
              reads=[osbr, bcr], writes=[onr])
        return on, onr

    def store_y(u, qb, on, onr):
        S.add("sp", lambda e: e.dma_start(out=d["y"][u, :, qb * TS:(qb + 1) * TS], in_=on[:, :]),
              reads=[onr], writes=[y_r], dma_key="ystore")

    plan = []
    for h in range(4):
        plan.append(("A", h * 64, 64, h, h * 64, 64, [h]))
    for h in range(4):
        plan.append(("B", 256 + h * 96, 96, 10 + h, 256 + h * 96, 96, [4 + h]))
    for h in range(4):
        plan.append(("C", 640 + h * 64, 64, 4 + h, 640 + h * 64, 64, [8 + h]))
    for g in range(2):
        plan.append(("D", 896 + g * 64, 64, 8 + g, 896 + 2 * g * 64, 128, [12 + 2 * g, 13 + 2 * g]))
    import os
    units = [int(x) for x in os.environ.get("KB_UNITS", ",".join(str(i) for i in range(len(plan)))).split(",")]
    bt_plan = {}
    for ui in units:
        kind = plan[ui][0]
        if kind in ("A", "C"):
            h = ui % 4
            for qb in range(4):
                for m in range(2 if kind == "C" else 1):
                    for i, _e in enumerate(near_entries(kind, qb)):
                        bt_plan[(kind, h, qb, m, i)] = bts.plan(d["btl"][TILE_IDS[(kind, h, qb, i)]], 512)

    loaded = {}

    def load_unit(ui):
        kind, krow, kd, vu, qrow, qd, yus = plan[ui]
        kt, ktr, kkey = KT.next()
        vt, vtr, vkey = VT.next()
        qt, qtr, qkey = QT.next()
        for j in range(4):
            kr0 = kg_row(j, krow)
            S.add("sp", (lambda e, j=j, kr0=kr0: e.dma_start(out=kt[0:kd, j * T:(j + 1) * T], in_=d["kTg"][kr0:kr0 + kd, :])),
                  writes=[ktr], dma_key=kkey)
            if kind == "D":
                S.add("sp", (lambda e, j=j, kr0=kr0: e.dma_start(out=kt[64:128, j * T:(j + 1) * T], in_=d["kTg"][kr0:kr0 + kd, :])),
                      writes=[ktr], dma_key=kkey)
            vr0 = vg_row(j, vu)
            S.add("sp", (lambda e, j=j, vr0=vr0: e.dma_start(out=vt[:, 16 * j:16 * (j + 1), :],
                                                              in_=d["vg"][vr0:vr0 + 128, :].rearrange("p (c d) -> p c d", d=65))),
                  writes=[vtr], dma_key=vkey)
        S.add("sp", lambda e: e.dma_start(out=qt[0:qd, :], in_=d["qT"][qrow:qrow + qd, :]), writes=[qtr], dma_key=qkey)
        own = None
        if kind in ("A", "C"):
            kto, ktor, kokey = KTO.next()
            vto, vtor, vokey = VTO.next()
            S.add("sp", lambda e: e.dma_start(out=kto[0:kd, :], in_=d["kT"][krow:krow + kd, :]), writes=[ktor], dma_key=kokey)
            S.add("sp", lambda e: e.dma_start(out=vto[:, :, :], in_=d["v"][vu]), writes=[vtor], dma_key=vokey)
            own = (kto, ktor, vto, vtor)
        loaded[ui] = (kt, ktr, vt, vtr, qt, qtr, own)

    load_unit(units[0])
    for n_, ui in enumerate(units):
        if n_ + 1 < len(units):
            load_unit(units[n_ + 1])
        kind, krow, kd, vu, qrow, qd, yus = plan[ui]
        kt, ktr, vt, vtr, qt, qtr, own = loaded.pop(ui)
        h = ui % 4
        dense = [(kt, ktr, vt, vtr, kc, None) for kc in range(64)]

        def near_list(kind, qb, m):
            out = []
            for i, (src, ch) in enumerate(near_entries(kind, qb)):
                mode = ("tile", bt_plan[(kind, h, qb, m, i)])
                if src == "own":
                    out.append((own[0], own[1], own[2], own[3], ch, mode))
                else:
                    out.append((kt, ktr, vt, vtr, ch, mode))
            return out

        for qb in range(4):
            if kind == "A":
                on, onr = attend(qt, qtr, 0, 64, qb, near_list("A", qb, 0))
                store_y(yus[0], qb, on, onr)
            elif kind == "B":
                on, onr = attend(qt, qtr, 0, 96, qb, dense)
                store_y(yus[0], qb, on, onr)
            elif kind == "D":
                for hh in range(2):
                    on, onr = attend(qt, qtr, hh * 64, 64, qb, dense)
                    store_y(yus[hh], qb, on, onr)
            else:
                sc, scr, _ = SC.next()
                S.add("act", (lambda e, sc=sc, qb=qb, h=h: e.activation(out=sc[:, :], in_=t5c[:, (h * 4 + qb) * 64:(h * 4 + qb + 1) * 64], func=AF.Exp)),
                      reads=[t5c_r], writes=[scr])
                vs, vsr, _ = VS.next()
                S.add("dve", (lambda e, sc=sc, vs=vs, vt=vt: e.tensor_tensor(
                    out=vs[:, :, :], in0=vt[:, :, :], in1=sc[:, :].unsqueeze(2).broadcast_to([128, 64, 65]), op=ALU.mult)),
                    reads=[vtr, scr], writes=[vsr])
                ons = []
                for m in range(2):
                    ch = [(kt, ktr, vs, vsr, kc, None) for kc in range(64)] + near_list("C", qb, m)
                    ons.append(attend(qt, qtr, 32 * m, 32, qb, ch))
                (on0, on0r), (on1, on1r) = ons
                yo, yor, _ = ON.next()
                S.add("dve", (lambda e, yo=yo, on0=on0, on1=on1: e.scalar_tensor_tensor(
                    out=yo[:, :], in0=on1[:, :], scalar=nlam, in1=on0[:, :], op0=ALU.mult, op1=ALU.add)),
                    reads=[on0r, on1r, lt_r], writes=[yor])
                store_y(yus[0], qb, yo, yor)


def build_b():
    nc = bass.Bass("TRN2", target_bir_lowering=False)
    dt = lambda name, shape, dtype, kind: nc.dram_tensor(name, shape, dtype, kind=kind).ap()
    d = {}
    gains = dt("gains", [128, NG], F32, "ExternalInput")
    lconst = dt("lconst", [128, 2], F32, "ExternalInput")
    cmat = dt("cmat", [128, 768], F32, "ExternalInput")
    d["qT"] = dt("qT", [QROWS, T], BF16, "ExternalInput")
    d["kT"] = dt("kT", [KROWS, T], BF16, "ExternalInput")
    d["v"] = dt("v", [NVU, 128, 16, 65], BF16, "ExternalInput")
    d["kTg"] = dt("kTg", [4 * KROWS, T], BF16, "ExternalInput")
    d["vg"] = dt("vg", [4 * NVU * 128, 16 * 65], BF16, "ExternalInput")
    d["btl"] = dt("btl", [NTILES, 128, 512], F32, "ExternalInput")
    d["t5c"] = dt("t5c", [128, 1024], F32, "ExternalInput")
    d["dlam"] = dt("dlam", [64, 128], F32, "ExternalInput")
    d["y"] = dt("y", [16, 64, T], F32, "ExternalOutput")
    d["y_r"] = Res("y")
    with contextlib.ExitStack() as st:
        S = Sched(nc)
        c = Ctx()
        common_setup(c, st, nc, S, gains, lconst, cmat)
        emit_phase_b(c, st, nc, S, d)
        S.add("sp", None, reads=[d["y_r"]])
        S.emit(st)
    return nc


def t5_bucket_np(rel):
    half = 16
    max_exact = 8
    n = np.abs(rel)
    large = max_exact + (np.log(np.maximum(n, 1).astype(np.float32) / max_exact)
                         / np.float32(math.log(128 / max_exact)) * (half - max_exact)).astype(np.int32)
    large = np.minimum(large, half - 1)
    return np.where(rel > 0, half, 0) + np.where(n < max_exact, n, large)


def bias_tiles_host(inp, l, core):
    r = core % 4
    rpb = np.asarray(inp["na_rpb"][l], np.float32)
    t5 = np.asarray(inp["t5_bias"], np.float32)
    btl = np.empty((NTILES, 128, 512), np.float32)
    kk = np.arange(128)[:, None]
    qq = np.arange(512)[None, :]
    NEG = np.float32(-1e30)
    for qb in range(4):
        qbg = 4 * r + qb
        qtok = qbg * 512 + qq
        qr, qc = qtok // 64, qtok % 64
        rs = np.clip(qr - 4, 0, 120)
        cs = np.clip(qc - 8, 0, 48)
        for i, (src, ch) in enumerate(near_entries("A", qb)):
            g = 16 * r + ch if src == "own" else ch
            ktok = g * 128 + kk
            kr, kcol = ktok // 64, ktok % 64
            valid = (kr >= rs) & (kr < rs + 8) & (kcol >= cs) & (kcol < cs + 16)
            dr = np.clip(kr - qr + 7, 0, 14)
            dc = np.clip(kcol - qc + 15, 0, 30)
            flat = np.where(valid, dr * 31 + dc, 15 * 31)
            for h in range(4):
                tab = np.concatenate([rpb[h].ravel(), np.array([NEG], np.float32)])
                btl[TILE_IDS[("A", h, qb, i)]] = tab[flat]
        for i, (src, ch) in enumerate(near_entries("C", qb)):
            g = 16 * r + ch if src == "own" else ch
            if 4 * qbg - 1 <= g <= 4 * qbg + 4:
                bk = t5_bucket_np((g * 128 + kk) - qtok)
                for h in range(4):
                    btl[TILE_IDS[("C", h, qb, i)]] = t5[:, h][bk]
            else:
                for h in range(4):
                    btl[TILE_IDS[("C", h, qb, i)]] = NEG
    t5c = np.zeros((128, 1024), np.float32)
    for qb in range(4):
        qbg = 4 * r + qb
        for kc in range(64):
            for h in range(4):
                if 4 * qbg - 1 <= kc <= 4 * qbg + 4:
                    val = NEG
                else:
                    val = t5[31, h] if kc > 4 * qbg + 3 else t5[15, h]
                t5c[:, (h * 4 + qb) * 64 + kc] = val
    return btl, t5c


_PROGS = {}


def _prog(name):
    if name not in _PROGS:
        _PROGS[name] = {"a": build_a, "b": build_b, "c": build_c, "f": build_fused}[name]()
    return _PROGS[name]


def to_featmajor(xc):
    return np.ascontiguousarray(xc.T.reshape(8, 128, T).transpose(1, 0, 2).reshape(128, 8 * T))


def from_featmajor(xT):
    return xT.reshape(128, 8, T).transpose(1, 0, 2).reshape(D, T).T


def kernel_unfused(**inp):
    x = np.asarray(inp["x"], np.float32).reshape(BATCH * SEQ, D)
    cm = const_mats()
    xTs = [to_featmajor(x[c * T:(c + 1) * T]) for c in range(NCORE)]
    tabs = [rope_tables(c) for c in range(NCORE)]
    cores = list(range(NCORE))
    for l in range(DEPTH):
        H = layer_host(inp, l)
        common = {"gains": H["gains"], "lconst": H["lconst"], "cmat": cm}
        maps = [dict(common, xT=xTs[c], wg=H["ffn1_wg"], wu=H["ffn1_wu"], wd=H["ffn1_wd"], winf=H["winf"],
                     winv=H["winv"], wsm=H["wsm"], tabs=tabs[c]) for c in cores]
        ra = run_bass_kernel_spmd(_prog("a"), maps, core_ids=cores).results
        maps = []
        dl = np.ascontiguousarray(np.broadcast_to(np.asarray(inp["diff_lambda"][l], np.float32).reshape(1, 128), (64, 128)))
        for c in cores:
            b = c // 4
            kTg = np.zeros((4 * KROWS, T), np.asarray(ra[0]["kT"]).dtype)
            vg = np.zeros((4 * NVU * 128, 1040), np.asarray(ra[0]["v"]).dtype)
            for j in range(4):
                kj = np.asarray(ra[4 * b + j]["kT"])
                vj = np.asarray(ra[4 * b + j]["v"]).reshape(NVU * 128, 1040)
                for a, bb in K_PARTS:
                    kTg[4 * a + j * (bb - a): 4 * a + (j + 1) * (bb - a)] = kj[a:bb]
                for a, bb in V_PARTS:
                    vg[(4 * a + j * (bb - a)) * 128: (4 * a + (j + 1) * (bb - a)) * 128] = vj[a * 128:bb * 128]
            btl, t5c = bias_tiles_host(inp, l, c)
            maps.append(dict(common, qT=np.asarray(ra[c]["qT"]), kT=np.asarray(ra[c]["kT"]), v=np.asarray(ra[c]["v"]),
                             kTg=kTg, vg=vg, btl=btl, t5c=t5c, dlam=dl))
        rb = run_bass_kernel_spmd(_prog("b"), maps, core_ids=cores).results
        wout = np.ascontiguousarray(np.asarray(inp["w_out"][l], np.float32).reshape(16, 64, 8, 128).transpose(2, 1, 0, 3).reshape(8, 64, 2048))
        maps = [dict(common, xT=np.asarray(ra[c]["x1T"]), y=np.asarray(rb[c]["y"]), wg=H["ffn2_wg"], wu=H["ffn2_wu"],
                     wd=H["ffn2_wd"], wout=wout) for c in cores]
        rc = run_bass_kernel_spmd(_prog("c"), maps, core_ids=cores).results
        xTs = [np.asarray(rc[c]["xoT"]) for c in cores]
    out = np.concatenate([from_featmajor(xTs[c]) for c in cores], axis=0)
    return np.ascontiguousarray(out.reshape(BATCH, SEQ, D).astype(np.float32))


RG = [[0, 1, 2, 3], [4, 5, 6, 7]]
K_PARTS = [(0, 256), (256, 448), (448, 640), (640, 896), (896, 1024)]
V_PARTS = [(0, 3), (3, 6), (6, 9), (9, 12), (12, 14)]


def kg_row(j, row):
    for a, b in K_PARTS:
        if a <= row < b:
            return 4 * a + j * (b - a) + (row - a)
    raise AssertionError(row)


def vg_row(j, u):
    for a, b in V_PARTS:
        if a <= u < b:
            return (4 * a + j * (b - a) + (u - a)) * 128
    raise AssertionError(u)


def build_fused():
    nc = bass.Bass("TRN2", target_bir_lowering=False)
    dt = lambda name, shape, dtype, kind, **kw: nc.dram_tensor(name, shape, dtype, kind=kind, **kw).ap()
    xT_d = dt("xT", [128, 8 * T], F32, "ExternalInput")
    cmat_d = dt("cmat", [128, 768], F32, "ExternalInput")
    tabs_d = dt("tabs", [4, 128, T], F32, "ExternalInput")
    t5c_d = dt("t5c", [128, 1024], F32, "ExternalInput")
    xo_d = dt("xoT", [128, 8 * T], F32, "ExternalOutput")
    LD = []
    for l in range(DEPTH):
        sfx = "_%d" % l
        d = {}
        d["gains"] = dt("gains" + sfx, [128, NG], F32, "ExternalInput")
        d["lconst"] = dt("lconst" + sfx, [128, 2], F32, "ExternalInput")
        for nm in ("wg1", "wu1", "wg2", "wu2"):
            d[nm] = dt(nm + sfx, [11, 128, 2048], F32, "ExternalInput")
        for nm in ("wd1", "wd2"):
            d[nm] = dt(nm + sfx, [8, 128, 2816], F32, "ExternalInput")
        d["winf"] = dt("winf" + sfx, [7, 128, 2048], F32, "ExternalInput")
        d["winv"] = dt("winv" + sfx, [2, 128, 2560], F32, "ExternalInput")
        d["wsm"] = dt("wsm" + sfx, [128, WSM], F32, "ExternalInput")
        d["wout"] = dt("wout" + sfx, [8, 64, 2048], F32, "ExternalInput")
        d["btl"] = dt("btl" + sfx, [NTILES, 128, 512], F32, "ExternalInput")
        d["dlam"] = dt("dlam" + sfx, [64, 128], F32, "ExternalInput")
        d["qT"] = dt("s_qT" + sfx, [QROWS, T], BF16, "Internal")
        d["kT"] = dt("s_kT" + sfx, [KROWS, T], BF16, "Internal")
        d["v"] = dt("s_v" + sfx, [NVU, 128, 16, 65], BF16, "Internal")
        d["kTg"] = dt("s_kTg" + sfx, [4 * KROWS, T], BF16, "Internal", addr_space="Local")
        d["vg"] = dt("s_vg" + sfx, [4 * NVU * 128, 16 * 65], BF16, "Internal", addr_space="Local")
        d["y"] = dt("s_y" + sfx, [16, 64, T], F32, "Internal")
        LD.append(d)
    with contextlib.ExitStack() as top:
        ss = SemState(top)
        c = Ctx()
        c.xT = _sb(top, nc, "xT", [128, 8 * T], F32)

        def fresh_x():
            c.x_r = [Res("x%d" % t) for t in range(NT)]

        with contextlib.ExitStack() as st:
            S = Sched(nc, ss)
            fresh_x()
            for k in range(8):
                S.add("sp", (lambda e, k=k: e.dma_start(out=c.xT[:, k * T:(k + 1) * T], in_=xT_d[:, k * T:(k + 1) * T])),
                      writes=c.x_r, dma_key="xT")
            S.drain()
            S.emit(st)
        for l in range(DEPTH):
            d = LD[l]
            with contextlib.ExitStack() as st:
                S = Sched(nc, ss)
                fresh_x()
                common_setup(c, st, nc, S, d["gains"], d["lconst"], cmat_d)
                da = dict(wg=d["wg1"], wu=d["wu1"], wd=d["wd1"], winf=d["winf"], winv=d["winv"], wsm=d["wsm"], tabs=tabs_d,
                          qT=d["qT"], kT=d["kT"], v=d["v"], q_r=Res("q"), k_r=Res("k"), v_r=Res("v"))
                emit_phase_a(c, st, nc, S, da)
                S.drain()
                S.emit(st)
            with contextlib.ExitStack() as st:
                S = Sched(nc, ss)
                v2 = d["v"].rearrange("u p c d -> (u p) (c d)")
                for a, b in K_PARTS:
                    S.add("pool", (lambda e, d=d, a=a, b=b: e.collective_compute(
                        "AllGather", ALU.bypass, replica_groups=RG, ins=[d["kT"][a:b, :]], outs=[d["kTg"][4 * a:4 * b, :]])),
                        dma_key="cc", inc=1)
                for a, b in V_PARTS:
                    S.add("pool", (lambda e, d=d, a=a, b=b, v2=v2: e.collective_compute(
                        "AllGather", ALU.bypass, replica_groups=RG, ins=[v2[a * 128:b * 128, :]],
                        outs=[d["vg"][4 * a * 128:4 * b * 128, :]])), dma_key="cc", inc=1)
                S.drain()
                S.emit(st)
            with contextlib.ExitStack() as st:
                S = Sched(nc, ss)
                common_setup(c, st, nc, S, d["gains"], d["lconst"], cmat_d)
                db = dict(qT=d["qT"], kT=d["kT"], v=d["v"], kTg=d["kTg"], vg=d["vg"], btl=d["btl"], t5c=t5c_d,
                          dlam=d["dlam"], y=d["y"], y_r=Res("y"))
                emit_phase_b(c, st, nc, S, db)
                S.drain()
                S.emit(st)
            with contextlib.ExitStack() as st:
                S = Sched(nc, ss)
                fresh_x()
                common_setup(c, st, nc, S, d["gains"], d["lconst"], cmat_d)
                dc = dict(wg=d["wg2"], wu=d["wu2"], wd=d["wd2"], wout=d["wout"], y=d["y"], y_r=Res("y"))
                emit_phase_c(c, st, nc, S, dc)
                S.drain()
                S.emit(st)
        with contextlib.ExitStack() as st:
            S = Sched(nc, ss)
            fresh_x()
            xo_r = Res("xo")
            store_x(c, S, xo_d, xo_r)
            S.add("sp", None, reads=[xo_r])
            S.drain()
            S.emit(st)
    return nc


def kernel(**inp):
    x = np.asarray(inp["x"], np.float32).reshape(BATCH * SEQ, D)
    cm = const_mats()
    cores = list(range(NCORE))
    shared = {}
    for l in range(DEPTH):
        H = layer_host(inp, l)
        sfx = "_%d" % l
        shared.update({"gains" + sfx: H["gains"], "lconst" + sfx: H["lconst"], "wg1" + sfx: H["ffn1_wg"], "wu1" + sfx: H["ffn1_wu"],
                       "wd1" + sfx: H["ffn1_wd"], "wg2" + sfx: H["ffn2_wg"], "wu2" + sfx: H["ffn2_wu"], "wd2" + sfx: H["ffn2_wd"],
                       "winf" + sfx: H["winf"], "winv" + sfx: H["winv"], "wsm" + sfx: H["wsm"],
                       "wout" + sfx: np.ascontiguousarray(np.asarray(inp["w_out"][l], np.float32).reshape(16, 64, 8, 128)
                                                         .transpose(2, 1, 0, 3).reshape(8, 64, 2048)),
                       "dlam" + sfx: np.ascontiguousarray(np.broadcast_to(
                           np.asarray(inp["diff_lambda"][l], np.float32).reshape(1, 128), (64, 128)))})
    maps = []
    for c in cores:
        m = dict(shared, xT=to_featmajor(x[c * T:(c + 1) * T]), cmat=cm, tabs=rope_tables(c))
        for l in range(DEPTH):
            btl, t5c = bias_tiles_host(inp, l, c)
            m["btl_%d" % l] = btl
            m["t5c"] = t5c
        maps.append(m)
    res = run_bass_kernel_spmd(_prog("f"), maps, core_ids=cores).results
    out = np.concatenate([from_featmajor(np.asarray(res[c]["xoT"])) for c in cores], axis=0)
    return np.ascontiguousarray(out.reshape(BATCH, SEQ, D).astype(np.float32))
```

```python
import contextlib
import math
import numpy as np
import ml_dtypes
import concourse.bass as bass
import concourse.mybir as mybir
from concourse.bass_utils import run_bass_kernel_spmd

F32 = mybir.dt.float32
BF16 = mybir.dt.bfloat16
AF = mybir.ActivationFunctionType
ALU = mybir.AluOpType

_UID = [0]


def _uid():
    _UID[0] += 1
    return _UID[0]

ENGS = ("pe", "act", "dve", "pool", "sp")
EPOCH = 16000
SAME_ENGINE_SYNC = True


class Res:
    __slots__ = ("name", "writer", "readers")

    def __init__(self, name=""):
        self.name = name
        self.writer = None
        self.readers = {}


class Op:
    __slots__ = ("eng", "fn", "deps", "dma_key", "signaled", "count", "semid", "idx", "inc")


class SemState:
    def __init__(self, stack):
        self.stack = stack
        self.cnt = {e: 0 for e in ENGS}
        self.dcnt = {}
        self.sems = {}


class Sched:
    def __init__(self, nc, semstate=None):
        self.nc = nc
        self.ops = []
        self.by_eng = {e: [] for e in ENGS}
        self.semstate = semstate
        self.last_dma = {}

    def drain(self):
        op = self.add("sp", None)
        op.deps = list(self.last_dma.values())
        return op

    def add(self, eng, fn, reads=(), writes=(), dma_key=None, inc=16):
        op = Op()
        op.inc = inc
        op.eng = eng
        op.fn = fn
        op.dma_key = dma_key
        op.signaled = dma_key is not None
        op.count = 0
        op.semid = None
        op.idx = len(self.ops)
        deps = {}
        for r in reads:
            if r.writer is not None:
                deps[r.writer.idx] = r.writer
        for w in writes:
            if w.writer is not None:
                deps[w.writer.idx] = w.writer
            for rd in w.readers.values():
                deps[rd.idx] = rd
        dl = []
        for d in deps.values():
            if d.dma_key is not None or dma_key is not None:
                need = True
            elif d.eng == eng:
                need = (eng != "pe") and SAME_ENGINE_SYNC
            else:
                need = True
            if need:
                d.signaled = True
                dl.append(d)
        op.deps = dl
        for r in reads:
            key = eng if dma_key is None else ("dma", op.idx)
            r.readers[key] = op
        for w in writes:
            w.writer = op
            w.readers = {}
        self.ops.append(op)
        self.by_eng[eng].append(op)
        if dma_key is not None:
            self.last_dma[dma_key] = op
        return op

    def emit(self, stack):
        nc = self.nc
        ss = self.semstate if self.semstate is not None else SemState(stack)
        cnt = ss.cnt
        dcnt = ss.dcnt
        for op in self.ops:
            if op.dma_key is not None:
                dcnt[op.dma_key] = dcnt.get(op.dma_key, 0) + op.inc
                op.count = dcnt[op.dma_key]
                op.semid = ("d", op.dma_key)
            elif op.signaled:
                c = cnt[op.eng]
                cnt[op.eng] = c + 1
                op.semid = ("e", op.eng, c // EPOCH)
                op.count = (c % EPOCH) + 1
        self.check(ss)
        sems = ss.sems
        for op in self.ops:
            if op.semid is not None and op.semid not in sems:
                sems[op.semid] = ss.stack.enter_context(nc.semaphore("s%d" % len(sems)))
        self.nsems = len(sems)
        block = stack.enter_context(nc.Block())
        engobj = {"pe": block.tensor, "act": block.scalar, "dve": block.vector,
                  "pool": block.gpsimd, "sp": block.sync}

        def make(ename):
            ops = self.by_eng[ename]

            def body(e):
                waited = {}
                wepoch = {}
                for op in ops:
                    for d in op.deps:
                        sid = d.semid
                        c = d.count
                        if sid[0] == "e" and wepoch.get(sid[1], -1) > sid[2]:
                            continue
                        if waited.get(sid, 0) >= c:
                            continue
                        e.wait_ge(sems[sid], c)
                        waited[sid] = c
                        if sid[0] == "e":
                            wepoch[sid[1]] = max(wepoch.get(sid[1], -1), sid[2])
                    if op.fn is None:
                        continue
                    ins = op.fn(e)
                    if op.dma_key is not None and op.inc == 1:
                        ins.then_inc(sems[op.semid])
                    elif op.dma_key is not None:
                        ins.then_inc(sems[op.semid], 16)
                    elif op.signaled:
                        ins.then_inc(sems[op.semid], 1)
            return body

        for ename in ENGS:
            if self.by_eng[ename]:
                engobj[ename](make(ename))


def _sched_check(self, ss):
    val = dict(getattr(ss, "val", {}))
    pos = {e: 0 for e in ENGS}
    progress = True
    while progress:
        progress = False
        for e in ENGS:
            ops = self.by_eng[e]
            while pos[e] < len(ops):
                op = ops[pos[e]]
                if all(val.get(d.semid, 0) >= d.count and (d.semid[0] != "e" or True) for d in op.deps):
                    if op.semid is not None:
                        if op.semid[0] == "d":
                            val[op.semid] = val.get(op.semid, 0) + op.inc
                        else:
                            val[op.semid] = val.get(op.semid, 0) + 1
                        assert val[op.semid] == op.count or op.semid[0] == "d", (op.semid, val[op.semid], op.count)
                    pos[e] += 1
                    progress = True
                else:
                    break
    stuck = {e: pos[e] for e in ENGS if pos[e] < len(self.by_eng[e])}
    assert not stuck, "deadlock in schedule: %r" % stuck
    ss.val = val
    print("sched ok: ops=%d per-eng=%r" % (len(self.ops), {e: len(self.by_eng[e]) for e in ENGS}))


Sched.check = _sched_check


class Ring:
    def __init__(self, stack, nc, name, n, shape, dtype, psum=False):
        self.bufs = []
        for i in range(n):
            nm = "r_%s%d" % (name, i)
            tn = "%s_%d" % (nm, _uid())
            if psum:
                t = stack.enter_context(nc.psum_tensor(tn, shape, dtype))
            else:
                t = stack.enter_context(nc.sbuf_tensor(tn, shape, dtype))
            self.bufs.append((t, Res(nm), nm))
        self.i = 0

    def next(self):
        b = self.bufs[self.i % len(self.bufs)]
        self.i += 1
        return b


D = 1024
SEQ = 8192
BATCH = 2
DEPTH = 2
DFF = 2816
NCORE = 8
T = 2048
NT = 4
TS = 512
EPS = 1e-6
NG = 59
G_FFN1, G_MIX, G_NAQ, G_NAK, G_QLAT, G_KVLAT, G_MQ, G_MK, G_DQ, G_DK, G_GQ, G_GK = \
    0, 8, 16, 17, 18, 20, 21, 22, 23, 24, 25, 26
G_BETA, G_FFN2, G_FIN = 27, 43, 51
WS_UQ, WS_KN, WS_V, WS_KR = 0, 768, 1152, 1408
WSM = 2176
C_ONES, C_BD64, C_BD32, C_RB, C_RD, C_ID = 0, 1, 2, 3, 4, 5
QROWS = 1152
KROWS = 1024
NVU = 14


def lambda_init(l):
    return 0.8 - 0.6 * math.exp(-0.3 * l)


class Ctx:
    pass


def _sb(st, nc, name, shape, dt):
    return st.enter_context(nc.sbuf_tensor("sb_%s_%d" % (name, _uid()), shape, dt))


class WStream:
    def __init__(self, S, ring, pf):
        self.S = S
        self.ring = ring
        self.pf = pf
        self.items = []
        self.issued = 0
        self.handles = []

    def plan(self, src_ap, ncols, part=128):
        self.items.append((src_ap, ncols, part))
        return len(self.items) - 1

    def get(self, i):
        while self.issued < min(len(self.items), i + 1 + self.pf):
            src, ncols, part = self.items[self.issued]
            buf, res, nm = self.ring.next()
            self.S.add("pool", (lambda e, buf=buf, src=src, ncols=ncols, part=part:
                                e.dma_start(out=buf[0:part, 0:ncols], in_=src)),
                       writes=[res], dma_key=nm)
            self.handles.append((buf, res))
            self.issued += 1
        return self.handles[i]


def rms_feat(c, srcs, P, ones_idx, n, gain_cols, outs):
    S = c.S
    ones = c.cmat[0:P, ones_idx * 128: ones_idx * 128 + P]
    ps, psr, _ = c.PS.next()
    last = len(srcs) - 1
    sc = float(n) ** -0.5
    for i, (src, sres) in enumerate(srcs):
        sq, sqr, _ = c.SQ.next()
        S.add("act", (lambda e, sq=sq, src=src: e.activation(out=sq[0:P, :], in_=src, func=AF.Square, scale=sc)),
              reads=[sres], writes=[sqr])
        S.add("pe", (lambda e, sq=sq, i=i: e.matmul(ps[0:P, :], ones, sq[0:P, :], start=(i == 0), stop=(i == last))),
              reads=[sqr, c.cmat_r], writes=[psr])
    rs, rsr, _ = c.RS.next()
    S.add("act", lambda e: e.activation(out=rs[0:P, :], in_=ps[0:P, :], func=AF.Ln, bias=c.eps[0:P, 0:1]),
          reads=[psr, c.eps_r], writes=[rsr])
    S.add("act", lambda e: e.activation(out=rs[0:P, :], in_=rs[0:P, :], func=AF.Exp, scale=-0.5),
          reads=[rsr], writes=[rsr])
    for i, ((src, sres), (out, ores)) in enumerate(zip(srcs, outs)):
        g = c.gains[0:P, gain_cols[i]:gain_cols[i] + 1]
        S.add("dve", (lambda e, src=src, out=out, g=g: e.scalar_tensor_tensor(
            out=out, in0=src, scalar=g, in1=rs[0:P, :], op0=ALU.mult, op1=ALU.mult)),
            reads=[sres, rsr, c.gains_r], writes=[ores])


def rope_feat(c, qn, qnr, P, r_idx, cos_ap, sin_ap, tab_r, out, outr):
    S = c.S
    R = c.cmat[0:P, r_idx * 128: r_idx * 128 + P]
    ps, psr, _ = c.PS.next()
    S.add("pe", lambda e: e.matmul(ps[0:P, :], R, qn, start=True, stop=True), reads=[qnr, c.cmat_r], writes=[psr])
    t1, t1r, _ = c.TF.next()
    t2, t2r, _ = c.TF.next()
    S.add("pool", lambda e: e.tensor_tensor(out=t1[0:P, :], in0=qn, in1=cos_ap, op=ALU.mult), reads=[qnr, tab_r], writes=[t1r])
    S.add("dve", lambda e: e.tensor_tensor(out=t2[0:P, :], in0=ps[0:P, :], in1=sin_ap, op=ALU.mult), reads=[psr, tab_r], writes=[t2r])
    S.add("pool", lambda e: e.tensor_tensor(out=out, in0=t1[0:P, :], in1=t2[0:P, :], op=ALU.add), reads=[t1r, t2r], writes=[outr])


def norm_x_tile(c, t, gcol0):
    hT, hTr, _ = c.HT.next()
    srcs = [(c.xT[:, k * T + t * TS: k * T + (t + 1) * TS], c.x_r[t]) for k in range(8)]
    outs = [(hT[:, k * TS:(k + 1) * TS], hTr) for k in range(8)]
    rms_feat(c, srcs, 128, C_ONES, D, [gcol0 + k for k in range(8)], outs)
    return hT, hTr


def ffn_tile(c, t, hT, hTr, ws, base):
    S = c.S
    for b in range(11):
        wg, wgr = ws.get(base + 2 * b)
        wu, wur = ws.get(base + 2 * b + 1)
        for jj in range(2):
            j = 2 * b + jj
            pg, pgr, _ = c.PS.next()
            pu, pur, _ = c.PS.next()
            for k in range(8):
                S.add("pe", (lambda e, k=k, pg=pg, wg=wg, jj=jj: e.matmul(
                    pg[:, :], wg[:, k * 256 + jj * 128: k * 256 + jj * 128 + 128], hT[:, k * TS:(k + 1) * TS],
                    start=(k == 0), stop=(k == 7))), reads=[wgr, hTr], writes=[pgr])
            for k in range(8):
                S.add("pe", (lambda e, k=k, pu=pu, wu=wu, jj=jj: e.matmul(
                    pu[:, :], wu[:, k * 256 + jj * 128: k * 256 + jj * 128 + 128], hT[:, k * TS:(k + 1) * TS],
                    start=(k == 0), stop=(k == 7))), reads=[wur, hTr], writes=[pur])
            sg, sgr, _ = c.TF.next()
            S.add("act", (lambda e, sg=sg, pg=pg: e.activation(out=sg[:, :], in_=pg[:, :], func=AF.Silu)),
                  reads=[pgr], writes=[sgr])
            S.add("dve", (lambda e, sg=sg, pu=pu, j=j: e.tensor_tensor(
                out=c.actT[:, j * TS:(j + 1) * TS], in0=sg[:, :], in1=pu[:, :], op=ALU.mult)),
                reads=[sgr, pur], writes=[c.act_r[j]])
    for m in range(8):
        wd, wdr = ws.get(base + 22 + m)
        pd, pdr, _ = c.PS.next()
        for j in range(22):
            S.add("pe", (lambda e, j=j, pd=pd, wd=wd: e.matmul(
                pd[:, :], wd[:, j * 128:(j + 1) * 128], c.actT[:, j * TS:(j + 1) * TS],
                start=(j == 0), stop=(j == 21))), reads=[wdr, c.act_r[j]], writes=[pdr])
        xs = c.xT[:, m * T + t * TS: m * T + (t + 1) * TS]
        S.add("dve", (lambda e, pd=pd, xs=xs: e.scalar_tensor_tensor(
            out=xs, in0=pd[:, :], scalar=0.5, in1=xs, op0=ALU.mult, op1=ALU.add)),
            reads=[pdr, c.x_r[t]], writes=[c.x_r[t]])


def plan_ffn(ws, wg, wu, wd):
    base = len(ws.items)
    for b in range(11):
        ws.plan(wg[b], 2048)
        ws.plan(wu[b], 2048)
    for m in range(8):
        ws.plan(wd[m], 2816)
    return base


def common_setup(c, st, nc, S, gains_d, lconst_d, cmat_d):
    c.S = S
    c.nc = nc
    c.gains = _sb(st, nc, "gains", [128, NG], F32)
    c.gains_r = Res("gains")
    c.lconst = _sb(st, nc, "lconst", [128, 2], F32)
    c.lconst_r = Res("lconst")
    c.cmat = _sb(st, nc, "cmat", [128, 6 * 128], BF16)
    c.cmat_r = Res("cmat")
    c.eps = _sb(st, nc, "eps", [128, 1], F32)
    c.eps_r = Res("eps")
    S.add("sp", lambda e: e.dma_start(out=c.gains[:, :], in_=gains_d), writes=[c.gains_r], dma_key="gains")
    S.add("sp", lambda e: e.dma_start(out=c.lconst[:, :], in_=lconst_d), writes=[c.lconst_r], dma_key="lconst")
    S.add("pool", lambda e: e.dma_start(out=c.cmat[:, :], in_=cmat_d), writes=[c.cmat_r], dma_key="cmat")
    S.add("dve", lambda e: e.memset(c.eps[:, :], EPS), writes=[c.eps_r])


def load_x(c, st, nc, S, x_d):
    c.xT = _sb(st, nc, "xT", [128, 8 * T], F32)
    c.x_r = [Res("x%d" % t) for t in range(NT)]
    for k in range(8):
        S.add("sp", (lambda e, k=k: e.dma_start(out=c.xT[:, k * T:(k + 1) * T], in_=x_d[:, k * T:(k + 1) * T])),
              writes=c.x_r, dma_key="xT")


def store_x(c, S, xo_d, xo_r):
    for k in range(8):
        S.add("sp", (lambda e, k=k: e.dma_start(out=xo_d[:, k * T:(k + 1) * T], in_=c.xT[:, k * T:(k + 1) * T])),
              reads=c.x_r, writes=[xo_r], dma_key="xT")


def emit_phase_a(c, st, nc, S, d):
    c.PS = Ring(st, nc, "ps", 8, [128, 512], F32, psum=True)
    c.SQ = Ring(st, nc, "sq", 3, [128, 512], BF16)
    c.RS = Ring(st, nc, "rs", 2, [128, 512], F32)
    c.TF = Ring(st, nc, "tf", 4, [128, 512], F32)
    c.HT = Ring(st, nc, "hT", 2, [128, 8 * TS], BF16)
    c.actT = _sb(st, nc, "actT", [128, 22 * TS], BF16)
    c.act_r = [Res("act%d" % j) for j in range(22)]
    wring = Ring(st, nc, "wb", 6, [128, 2816], BF16)
    ws = WStream(S, wring, 4)
    wsm = _sb(st, nc, "wsm", [128, WSM], BF16)
    wsm_r = Res("wsm")
    S.add("pool", lambda e: e.dma_start(out=wsm[:, :], in_=d["wsm"]), writes=[wsm_r], dma_key="wsm")
    tabs = _sb(st, nc, "tabs", [128, 4 * T], F32)
    tab_r = Res("tabs")
    for i in range(4):
        S.add("sp", (lambda e, i=i: e.dma_start(out=tabs[:, i * T:(i + 1) * T], in_=d["tabs"][i])),
              writes=[tab_r], dma_key="tabs")
    for col, sc in ((G_NAQ, 64 ** -0.5), (G_MQ, 96 ** -0.5), (G_DQ, 32 ** -0.5), (G_GQ, 64 ** -0.5)):
        S.add("dve", (lambda e, col=col, sc=sc: e.tensor_scalar(
            out=c.gains[:, col:col + 1], in0=c.gains[:, col:col + 1], scalar1=float(sc), scalar2=None, op0=ALU.mult)),
            reads=[c.gains_r], writes=[c.gains_r])

    bases_ffn = [plan_ffn(ws, d["wg"], d["wu"], d["wd"]) for t in range(NT)]
    bases_in = []
    for t in range(NT):
        bases_in.append(len(ws.items))
        for b in range(7):
            ws.plan(d["winf"][b], 2048)
        for vb in range(2):
            ws.plan(d["winv"][vb], 2560)

    import os
    n1 = int(os.environ.get("KA_N1", NT))
    n2 = int(os.environ.get("KA_N2", NT))
    for t in range(n1):
        hT, hTr = norm_x_tile(c, t, G_FFN1)
        ffn_tile(c, t, hT, hTr, ws, bases_ffn[t])

    STG = Ring(st, nc, "stg", 6, [128, 512], BF16)
    QN = Ring(st, nc, "qn", 3, [128, 512], BF16)
    cqn = _sb(st, nc, "cqn", [128, 2 * TS], BF16)
    cqn_r = Res("cqn")
    ckvn = _sb(st, nc, "ckvn", [128, TS], BF16)
    ckvn_r = Res("ckvn")
    vst = _sb(st, nc, "vst", [128, 4, NVU, 65], BF16)
    vst_r = Res("vst")
    S.add("pool", lambda e: e.memset(vst[:, :, :, :], 1.0), writes=[vst_r])
    q_r, k_r, v_r = d["q_r"], d["k_r"], d["v_r"]

    def store_rows(dst, dres, row0, P, t, src, sres, key):
        S.add("sp", lambda e: e.dma_start(out=dst[row0:row0 + P, t * TS:(t + 1) * TS], in_=src),
              reads=[sres], writes=[dres], dma_key=key)

    def stage2_tile(t, hT, hTr):

        def proj_chunk(wb, wbr, jj):
            p, pr, _ = c.PS.next()
            for k in range(8):
                S.add("pe", (lambda e, k=k: e.matmul(
                    p[:, :], wb[:, k * 256 + jj * 128: k * 256 + jj * 128 + 128], hT[:, k * TS:(k + 1) * TS],
                    start=(k == 0), stop=(k == 7))), reads=[wbr, hTr], writes=[pr])
            return p, pr

        def simple_head_chunk(wb, wbr, jj, ones_idx, n, gcol, dst, dres, row0, rope):
            p, pr = proj_chunk(wb, wbr, jj)
            if rope:
                qn, qnr, _ = QN.next()
                rms_feat(c, [(p[:, :], pr)], 128, ones_idx, n, [gcol], [(qn[:, :], qnr)])
                sg, sgr, key = STG.next()
                rope_feat(c, qn[:, :], qnr, 128, C_RD, tabs[:, 2 * T + t * TS: 2 * T + (t + 1) * TS],
                          tabs[:, 3 * T + t * TS: 3 * T + (t + 1) * TS], tab_r, sg[:, :], sgr)
            else:
                sg, sgr, key = STG.next()
                rms_feat(c, [(p[:, :], pr)], 128, ones_idx, n, [gcol], [(sg[:, :], sgr)])
            store_rows(dst, dres, row0, 128, t, sg[:, :], sgr, key)

        b0 = bases_in[t]
        wb, wbr = ws.get(b0 + 0)
        for jj in range(2):
            simple_head_chunk(wb, wbr, jj, C_BD64, 64, G_NAQ, d["qT"], q_r, 0 + jj * 128, False)
        wb, wbr = ws.get(b0 + 1)
        for jj in range(2):
            simple_head_chunk(wb, wbr, jj, C_BD64, 64, G_NAK, d["kT"], k_r, 0 + jj * 128, False)
        wb, wbr = ws.get(b0 + 2)
        p0, p0r = proj_chunk(wb, wbr, 0)
        p1, p1r = proj_chunk(wb, wbr, 1)
        rms_feat(c, [(p0[:, :], p0r), (p1[:, :], p1r)], 128, C_ONES, 256, [G_QLAT, G_QLAT + 1],
                 [(cqn[:, 0:TS], cqn_r), (cqn[:, TS:2 * TS], cqn_r)])
        for h in range(4):
            p, pr, _ = c.PS.next()
            for k in range(2):
                S.add("pe", (lambda e, k=k, h=h, p=p: e.matmul(
                    p[0:96, :], wsm[:, WS_UQ + k * 384 + h * 96: WS_UQ + k * 384 + h * 96 + 96],
                    cqn[:, k * TS:(k + 1) * TS], start=(k == 0), stop=(k == 1))),
                    reads=[wsm_r, cqn_r], writes=[pr])
            qn, qnr, _ = QN.next()
            rms_feat(c, [(p[0:96, :], pr)], 96, C_ONES, 96, [G_MQ], [(qn[0:96, :], qnr)])
            sg, sgr, key = STG.next()
            rope_feat(c, qn[0:96, :], qnr, 96, C_RB, tabs[0:96, 0 * T + t * TS: 0 * T + (t + 1) * TS],
                      tabs[0:96, 1 * T + t * TS: 1 * T + (t + 1) * TS], tab_r, sg[0:96, :], sgr)
            store_rows(d["qT"], q_r, 256 + h * 96, 96, t, sg[0:96, :], sgr, key)
        wb, wbr = ws.get(b0 + 3)
        p, pr = proj_chunk(wb, wbr, 0)
        rms_feat(c, [(p[:, :], pr)], 128, C_ONES, 128, [G_KVLAT], [(ckvn[:, :], ckvn_r)])
        for h in range(4):
            p, pr, _ = c.PS.next()
            S.add("pe", (lambda e, h=h, p=p: e.matmul(
                p[0:96, :], wsm[:, WS_KN + h * 96: WS_KN + h * 96 + 96], ckvn[:, :], start=True, stop=False)),
                reads=[wsm_r, ckvn_r], writes=[pr])
            for k in range(8):
                S.add("pe", (lambda e, k=k, p=p: e.matmul(
                    p[0:96, :], wsm[:, WS_KR + k * 96: WS_KR + k * 96 + 96], hT[:, k * TS:(k + 1) * TS],
                    start=False, stop=(k == 7))), reads=[wsm_r, hTr], writes=[pr])
            qn, qnr, _ = QN.next()
            rms_feat(c, [(p[0:96, :], pr)], 96, C_ONES, 96, [G_MK], [(qn[0:96, :], qnr)])
            sg, sgr, key = STG.next()
            rope_feat(c, qn[0:96, :], qnr, 96, C_RB, tabs[0:96, 0 * T + t * TS: 0 * T + (t + 1) * TS],
                      tabs[0:96, 1 * T + t * TS: 1 * T + (t + 1) * TS], tab_r, sg[0:96, :], sgr)
            store_rows(d["kT"], k_r, 256 + h * 96, 96, t, sg[0:96, :], sgr, key)
        for s in range(4):
            p, pr, _ = c.PS.next()
            S.add("pe", (lambda e, s=s, p=p: e.matmul(
                p[:, 0:256], ckvn[:, s * 128:(s + 1) * 128], wsm[:, WS_V:WS_V + 256], start=True, stop=True)),
                reads=[wsm_r, ckvn_r], writes=[pr])
            S.add("act", (lambda e, s=s, p=p: e.activation(out=vst[:, s, 10:14, 0:64], in_=p[:, 0:256].rearrange("p (u d) -> p u d", d=64), func=AF.Copy)),
                  reads=[pr], writes=[vst_r])
        simple_head_chunk(wb, wbr, 1, C_BD64, 64, G_GK, d["kT"], k_r, 896, True)
        wb, wbr = ws.get(b0 + 4)
        for jj in range(2):
            simple_head_chunk(wb, wbr, jj, C_BD32, 32, G_DQ, d["qT"], q_r, 640 + jj * 128, False)
        wb, wbr = ws.get(b0 + 5)
        for jj in range(2):
            simple_head_chunk(wb, wbr, jj, C_BD32, 32, G_DK, d["kT"], k_r, 640 + jj * 128, False)
        wb, wbr = ws.get(b0 + 6)
        for jj in range(2):
            simple_head_chunk(wb, wbr, jj, C_BD64, 64, G_GQ, d["qT"], q_r, 896 + jj * 128, True)
        for vb in range(2):
            wv, wvr = ws.get(b0 + 7 + vb)
            for s in range(4):
                p, pr, _ = c.PS.next()
                for k in range(8):
                    S.add("pe", (lambda e, k=k, s=s, p=p, wv=wv: e.matmul(
                        p[:, 0:320], hT[:, k * TS + s * 128: k * TS + (s + 1) * 128], wv[:, k * 320:(k + 1) * 320],
                        start=(k == 0), stop=(k == 7))), reads=[wvr, hTr], writes=[pr])
                S.add("dve", (lambda e, s=s, p=p, vb=vb: e.tensor_copy(out=vst[:, s, vb * 5:(vb + 1) * 5, 0:64], in_=p[:, 0:320].rearrange("p (u d) -> p u d", d=64))),
                      reads=[pr], writes=[vst_r])
        for u in range(NVU):
            S.add("sp", (lambda e, u=u: e.dma_start(out=d["v"][u, :, t * 4:(t + 1) * 4, :], in_=vst[:, :, u, :])),
                  reads=[vst_r], writes=[v_r], dma_key="vst")

    for t in range(n2):
        hT, hTr = norm_x_tile(c, t, G_MIX)
        stage2_tile(t, hT, hTr)


def build_a():
    nc = bass.Bass("TRN2", target_bir_lowering=False)
    dt = lambda name, shape, dtype, kind: nc.dram_tensor(name, shape, dtype, kind=kind).ap()
    d = {}
    xT = dt("xT", [128, 8 * T], F32, "ExternalInput")
    gains = dt("gains", [128, NG], F32, "ExternalInput")
    lconst = dt("lconst", [128, 2], F32, "ExternalInput")
    cmat = dt("cmat", [128, 768], F32, "ExternalInput")
    d["wg"] = dt("wg", [11, 128, 2048], F32, "ExternalInput")
    d["wu"] = dt("wu", [11, 128, 2048], F32, "ExternalInput")
    d["wd"] = dt("wd", [8, 128, 2816], F32, "ExternalInput")
    d["winf"] = dt("winf", [7, 128, 2048], F32, "ExternalInput")
    d["winv"] = dt("winv", [2, 128, 2560], F32, "ExternalInput")
    d["wsm"] = dt("wsm", [128, WSM], F32, "ExternalInput")
    d["tabs"] = dt("tabs", [4, 128, T], F32, "ExternalInput")
    xo = dt("x1T", [128, 8 * T], F32, "ExternalOutput")
    d["qT"] = dt("qT", [QROWS, T], BF16, "ExternalOutput")
    d["kT"] = dt("kT", [KROWS, T], BF16, "ExternalOutput")
    d["v"] = dt("v", [NVU, 128, 16, 65], BF16, "ExternalOutput")
    d["q_r"], d["k_r"], d["v_r"] = Res("qT"), Res("kT"), Res("v")
    xo_r = Res("xo")
    with contextlib.ExitStack() as st:
        S = Sched(nc)
        c = Ctx()
        common_setup(c, st, nc, S, gains, lconst, cmat)
        load_x(c, st, nc, S, xT)
        emit_phase_a(c, st, nc, S, d)
        store_x(c, S, xo, xo_r)
        S.add("sp", None, reads=[xo_r, d["q_r"], d["k_r"], d["v_r"]])
        S.emit(st)
    return nc


def blk_cols(W, cpb):
    K, N = W.shape
    kc = K // 128
    nb = N // cpb
    return np.ascontiguousarray(W.reshape(kc, 128, nb, cpb).transpose(2, 1, 0, 3).reshape(nb, 128, kc * cpb))


def const_mats():
    m = np.zeros((6, 128, 128), np.float32)
    m[C_ONES] = 1.0
    for b in range(2):
        m[C_BD64, b * 64:(b + 1) * 64, b * 64:(b + 1) * 64] = 1.0
    for b in range(4):
        m[C_BD32, b * 32:(b + 1) * 32, b * 32:(b + 1) * 32] = 1.0
    for i in range(16):
        m[C_RB, 80 + i, 64 + i] = -1.0
        m[C_RB, 64 + i, 80 + i] = 1.0
    for hb in (0, 64):
        for sb in (0, 32):
            for i in range(16):
                a = hb + sb + i
                b = a + 16
                m[C_RD, b, a] = -1.0
                m[C_RD, a, b] = 1.0
    m[C_ID] = np.eye(128, dtype=np.float32)
    return np.ascontiguousarray(m.transpose(1, 0, 2).reshape(128, 768))


def rope_tables(core):
    r = core % 4
    t = np.arange(r * T, (r + 1) * T)
    inv = np.exp(-math.log(10000.0) * np.arange(0, 32, 2, dtype=np.float32) / 32).astype(np.float32)
    a_seq = t.astype(np.float32)[:, None] * inv[None, :]
    a_row = (t // 64).astype(np.float32)[:, None] * inv[None, :]
    a_col = (t % 64).astype(np.float32)[:, None] * inv[None, :]
    tabs = np.zeros((4, 128, T), np.float32)
    tabs[0, 0:64] = 1.0
    for p in range(64, 96):
        tabs[0, p] = np.cos(a_seq[:, (p - 64) % 16])
        tabs[1, p] = np.sin(a_seq[:, (p - 64) % 16])
    for p in range(128):
        f = p % 64
        a = a_row if f < 32 else a_col
        tabs[2, p] = np.cos(a[:, f % 16])
        tabs[3, p] = np.sin(a[:, f % 16])
    return tabs


def tile64(v):
    return np.tile(v, 128 // v.shape[0])


def layer_host(inp, l):
    f = lambda k: np.asarray(inp[k][l], np.float32)
    h = {}
    for nm in ("ffn1", "ffn2"):
        h[nm + "_wg"] = blk_cols(f(nm + "_w_gate"), 256)
        h[nm + "_wu"] = blk_cols(f(nm + "_w_up"), 256)
        h[nm + "_wd"] = blk_cols(f(nm + "_w_down"), 128)
    win = f("w_in")
    cols_f = np.concatenate([np.arange(0, 256), np.arange(256, 512), np.arange(768, 1024), np.arange(1024, 1152),
                             np.arange(2208, 2336), np.arange(1184, 1440), np.arange(1440, 1696), np.arange(1952, 2208)])
    cols_v = np.concatenate([np.arange(512, 768), np.arange(1696, 1952), np.arange(2336, 2464)])
    h["winf"] = blk_cols(win[:, cols_f], 256)
    h["winv"] = blk_cols(win[:, cols_v], 320)
    wsm = np.zeros((128, WSM), np.float32)
    wuq = f("mla_w_uq")
    wsm[:, WS_UQ:WS_UQ + 768] = wuq.reshape(2, 128, 384).transpose(1, 0, 2).reshape(128, 768)
    wukv = f("mla_w_ukv")
    for hh in range(4):
        wsm[:, WS_KN + hh * 96: WS_KN + hh * 96 + 64] = wukv[:, hh * 128: hh * 128 + 64]
        wsm[:, WS_V + hh * 64: WS_V + (hh + 1) * 64] = wukv[:, hh * 128 + 64: hh * 128 + 128]
    kr = win[:, 1152:1184].reshape(8, 128, 32)
    for k in range(8):
        wsm[:, WS_KR + k * 96 + 64: WS_KR + k * 96 + 96] = kr[k]
    h["wsm"] = wsm
    g = np.zeros((128, NG), np.float32)
    g[:, G_FFN1:G_FFN1 + 8] = f("ffn1_norm").reshape(8, 128).T
    g[:, G_MIX:G_MIX + 8] = f("mix_norm").reshape(8, 128).T
    g[:, G_NAQ] = tile64(f("na_q_norm"))
    g[:, G_NAK] = tile64(f("na_k_norm"))
    g[:, G_QLAT:G_QLAT + 2] = f("mla_q_lat_norm").reshape(2, 128).T
    g[:, G_KVLAT] = f("mla_kv_lat_norm")
    g[0:96, G_MQ] = f("mla_q_norm")
    g[0:96, G_MK] = f("mla_k_norm")
    g[:, G_DQ] = tile64(f("diff_q_norm"))
    g[:, G_DK] = tile64(f("diff_k_norm"))
    g[:, G_GQ] = tile64(f("gqa_q_norm"))
    g[:, G_GK] = tile64(f("gqa_k_norm"))
    for hh in range(4):
        g[0:64, G_BETA + 0 + hh] = f("na_beta")[hh * 64:(hh + 1) * 64]
        g[0:64, G_BETA + 4 + hh] = f("mla_beta")[hh * 64:(hh + 1) * 64]
        g[0:64, G_BETA + 8 + hh] = f("diff_subln")
        g[0:64, G_BETA + 12 + hh] = f("gqa_beta")[hh * 64:(hh + 1) * 64]
    g[:, G_FFN2:G_FFN2 + 8] = f("ffn2_norm").reshape(8, 128).T
    g[:, G_FIN:G_FIN + 8] = f("final_norm").reshape(8, 128).T
    h["gains"] = g
    lc = np.zeros((128, 2), np.float32)
    lc[:, 0] = lambda_init(l)
    lc[:, 1] = 1.0 - lambda_init(l)
    h["lconst"] = lc
    return h


def emit_phase_c(c, st, nc, S, d):
    c.PS = Ring(st, nc, "ps", 8, [128, 512], F32, psum=True)
    c.SQ = Ring(st, nc, "sq", 3, [128, 512], BF16)
    c.RS = Ring(st, nc, "rs", 2, [128, 512], F32)
    c.TF = Ring(st, nc, "tf", 4, [128, 512], F32)
    c.HT = Ring(st, nc, "hT", 2, [128, 8 * TS], BF16)
    c.actT = _sb(st, nc, "actT", [128, 22 * TS], BF16)
    c.act_r = [Res("cact%d" % j) for j in range(22)]
    wring = Ring(st, nc, "wb", 6, [128, 2816], BF16)
    ws = WStream(S, wring, 4)
    YT = Ring(st, nc, "yt", 1, [64, 16 * TS], F32)
    yn = _sb(st, nc, "yn", [64, 16 * TS], BF16)
    yn_r = Res("yn")
    S.add("dve", lambda e: e.tensor_scalar(out=c.gains[0:64, G_BETA + 8:G_BETA + 12], in0=c.gains[0:64, G_BETA + 8:G_BETA + 12],
                                            scalar1=c.lconst[0:64, 1:2], scalar2=None, op0=ALU.mult),
          reads=[c.gains_r, c.lconst_r], writes=[c.gains_r])
    bases = []
    for t in range(NT):
        b0 = len(ws.items)
        for m in range(8):
            ws.plan(d["wout"][m], 2048, part=64)
        plan_ffn(ws, d["wg"], d["wu"], d["wd"])
        bases.append(b0)

    def tile_c(t, yt, ytr, key):
        for u in range(16):
            S.add("sp", (lambda e, u=u: e.dma_start(out=yt[:, u * TS:(u + 1) * TS], in_=d["y"][u, :, t * TS:(t + 1) * TS])),
                  reads=[d["y_r"]], writes=[ytr], dma_key=key)
        for g in range(4):
            if g == 2:
                for h in range(4):
                    u = 8 + h
                    rms_feat(c, [(yt[:, u * TS:(u + 1) * TS], ytr)], 64, C_ONES, 64, [G_BETA + u],
                             [(yn[:, u * TS:(u + 1) * TS], yn_r)])
            else:
                us = [4 * g + h for h in range(4)]
                rms_feat(c, [(yt[:, u * TS:(u + 1) * TS], ytr) for u in us], 64, C_ONES, 256, [G_BETA + u for u in us],
                         [(yn[:, u * TS:(u + 1) * TS], yn_r) for u in us])
        for m in range(8):
            wb, wbr = ws.get(bases[t] + m)
            p, pr, _ = c.PS.next()
            for hk in range(16):
                S.add("pe", (lambda e, hk=hk, p=p, wb=wb: e.matmul(
                    p[:, :], wb[0:64, hk * 128:(hk + 1) * 128], yn[:, hk * TS:(hk + 1) * TS],
                    start=(hk == 0), stop=(hk == 15))), reads=[wbr, yn_r], writes=[pr])
            xs = c.xT[:, m * T + t * TS: m * T + (t + 1) * TS]
            S.add("dve", (lambda e, p=p, xs=xs: e.tensor_tensor(out=xs, in0=p[:, :], in1=xs, op=ALU.add)),
                  reads=[pr, c.x_r[t]], writes=[c.x_r[t]])
        hT, hTr = norm_x_tile(c, t, G_FFN2)
        ffn_tile(c, t, hT, hTr, ws, bases[t] + 8)
        srcs = [(c.xT[:, k * T + t * TS: k * T + (t + 1) * TS], c.x_r[t]) for k in range(8)]
        rms_feat(c, srcs, 128, C_ONES, D, [G_FIN + k for k in range(8)], srcs)

    for t in range(NT):
        yt, ytr, key = YT.next()
        tile_c(t, yt, ytr, key)


def build_c():
    nc = bass.Bass("TRN2", target_bir_lowering=False)
    dt = lambda name, shape, dtype, kind: nc.dram_tensor(name, shape, dtype, kind=kind).ap()
    d = {}
    xT = dt("xT", [128, 8 * T], F32, "ExternalInput")
    gains = dt("gains", [128, NG], F32, "ExternalInput")
    lconst = dt("lconst", [128, 2], F32, "ExternalInput")
    cmat = dt("cmat", [128, 768], F32, "ExternalInput")
    d["wg"] = dt("wg", [11, 128, 2048], F32, "ExternalInput")
    d["wu"] = dt("wu", [11, 128, 2048], F32, "ExternalInput")
    d["wd"] = dt("wd", [8, 128, 2816], F32, "ExternalInput")
    d["wout"] = dt("wout", [8, 64, 2048], F32, "ExternalInput")
    d["y"] = dt("y", [16, 64, T], F32, "ExternalInput")
    d["y_r"] = Res("y")
    xo = dt("xoT", [128, 8 * T], F32, "ExternalOutput")
    xo_r = Res("xo")
    with contextlib.ExitStack() as st:
        S = Sched(nc)
        c = Ctx()
        common_setup(c, st, nc, S, gains, lconst, cmat)
        load_x(c, st, nc, S, xT)
        emit_phase_c(c, st, nc, S, d)
        store_x(c, S, xo, xo_r)
        S.add("sp", None, reads=[xo_r])
        S.emit(st)
    return nc


LA = 2


def near_entries(kind, qb):
    ent = []
    if kind == "A":
        lo, hi, halo = 4 * qb - 2, 4 * qb + 5, 2
    else:
        lo, hi, halo = 4 * qb - 1, 4 * qb + 4, 1
    for lc in range(max(lo, 0), min(hi, 15) + 1):
        ent.append(("own", lc))
    if qb == 0:
        for j in range(4):
            for x in range(16 - halo, 16):
                ent.append(("gath", 16 * j + x))
    if qb == 3:
        for j in range(4):
            for x in range(halo):
                ent.append(("gath", 16 * j + x))
    return ent


def tile_ids():
    ids = {}
    n = 0
    for kind in ("A", "C"):
        for h in range(4):
            for qb in range(4):
                for i, _e in enumerate(near_entries(kind, qb)):
                    ids[(kind, h, qb, i)] = n
                    n += 1
    return ids, n


TILE_IDS, NTILES = tile_ids()


def emit_phase_b(c, st, nc, S, d):
    PSS = Ring(st, nc, "pss", 3, [128, 1024], F32, psum=True)
    PO = Ring(st, nc, "po", 1, [128, 512], F32, psum=True)
    PBC = Ring(st, nc, "pbc", 1, [128, 512], F32, psum=True)
    PT = Ring(st, nc, "pt", 4, [128, 1024], BF16)
    KT = Ring(st, nc, "kt", 2, [128, SEQ], BF16)
    VT = Ring(st, nc, "vt", 2, [128, 64, 65], BF16)
    KTO = Ring(st, nc, "kto", 2, [128, T], BF16)
    VTO = Ring(st, nc, "vto", 2, [128, 16, 65], BF16)
    QT = Ring(st, nc, "qt", 4, [128, T], BF16)
    BT = Ring(st, nc, "bt", 12, [128, 512], BF16)
    VS = Ring(st, nc, "vs", 2, [128, 64, 65], BF16)
    SC = Ring(st, nc, "sc", 2, [128, 64], F32)
    OSB = Ring(st, nc, "osb", 2, [128, 512], F32)
    ON = Ring(st, nc, "on", 4, [64, 512], F32)
    RR = Ring(st, nc, "rr", 2, [128, 512], F32)
    for ring_ in (KT, KTO):
        for buf_, res_, _k in ring_.bufs:
            S.add("pool", (lambda e, buf_=buf_: e.memset(buf_[:, :], 0.0)), writes=[res_])
    bts = WStream(S, BT, 8)
    t5c = _sb(st, nc, "t5c", [128, 1024], F32)
    t5c_r = Res("t5c")
    S.add("sp", lambda e: e.dma_start(out=t5c[:, :], in_=d["t5c"]), writes=[t5c_r], dma_key="t5c")
    onesf = _sb(st, nc, "onesf", [128, 64], F32)
    onesf_r = Res("onesf")
    S.add("dve", lambda e: e.memset(onesf[:, :], 1.0), writes=[onesf_r])
    dl = _sb(st, nc, "dl", [64, 128], F32)
    dl_r = Res("dl")
    S.add("sp", lambda e: e.dma_start(out=dl[:, :], in_=d["dlam"]), writes=[dl_r], dma_key="dl")
    lt = _sb(st, nc, "lt", [64, 72], F32)
    lt_r = Res("lt")
    S.add("dve", lambda e: e.tensor_tensor(out=lt[:, 0:32], in0=dl[:, 0:32], in1=dl[:, 32:64], op=ALU.mult), reads=[dl_r], writes=[lt_r])
    S.add("dve", lambda e: e.tensor_tensor(out=lt[:, 32:64], in0=dl[:, 64:96], in1=dl[:, 96:128], op=ALU.mult), reads=[dl_r, lt_r], writes=[lt_r])
    S.add("dve", lambda e: e.reduce_sum(lt[:, 64:65], lt[:, 0:32], axis=mybir.AxisListType.X), reads=[lt_r], writes=[lt_r])
    S.add("dve", lambda e: e.reduce_sum(lt[:, 65:66], lt[:, 32:64], axis=mybir.AxisListType.X), reads=[lt_r], writes=[lt_r])
    S.add("act", lambda e: e.activation(out=lt[:, 66:68], in_=lt[:, 64:66], func=AF.Exp), reads=[lt_r], writes=[lt_r])
    S.add("dve", lambda e: e.tensor_tensor(out=lt[:, 68:69], in0=lt[:, 67:68], in1=lt[:, 66:67], op=ALU.subtract), reads=[lt_r], writes=[lt_r])
    S.add("dve", lambda e: e.tensor_scalar(out=lt[:, 69:70], in0=lt[:, 68:69], scalar1=c.lconst[0:64, 0:1], scalar2=None, op0=ALU.subtract),
          reads=[lt_r, c.lconst_r], writes=[lt_r])
    nlam = lt[:, 69:70]
    ident = c.cmat[:, C_ID * 128:(C_ID + 1) * 128]
    y_r = d["y_r"]

    def attend(qt, qtr, qp0, dq, qb, chunks):
        O, Or, _ = PO.next()
        steps = []
        i = 0
        isc = lambda ch: ch[5] is not None and ch[5][0] == "const"
        while i < len(chunks):
            if i + 1 < len(chunks) and not isc(chunks[i]) and not isc(chunks[i + 1]):
                steps.append([chunks[i], chunks[i + 1]])
                i += 2
            else:
                steps.append([chunks[i]])
                i += 1
        n = len(steps)
        total = len(chunks)
        pend = []
        pvi = [0]
        for i in range(n + LA):
            if i < n:
                st_ = steps[i]
                s_, sr, _ = PSS.next()
                w = 512 * len(st_)
                for hf, (kt, ktr, vt, vtr, kc, mode) in enumerate(st_):
                    sl = s_[:, hf * 512:(hf + 1) * 512]
                    if mode is not None and mode[0] == "tile":
                        bt, btr = bts.get(mode[1])
                        S.add("pe", (lambda e, sl=sl, bt=bt: e.matmul(sl, ident, bt[:, :], start=True, stop=False)),
                              reads=[btr, c.cmat_r], writes=[sr])
                        first = False
                    else:
                        first = True
                    S.add("pe", (lambda e, sl=sl, kc=kc, first=first, kt=kt: e.matmul(
                        sl, kt[qp0:qp0 + dq, kc * 128:(kc + 1) * 128], qt[qp0:qp0 + dq, qb * TS:(qb + 1) * TS],
                        start=first, stop=True)), reads=[ktr, qtr], writes=[sr])
                p_, pr, _ = PT.next()
                mode = st_[0][5]
                if len(st_) == 1 and mode is not None and mode[0] == "const":
                    bcol = mode[1]
                    S.add("act", (lambda e, s_=s_, p_=p_, bcol=bcol: e.activation(out=p_[:, 0:512], in_=s_[:, 0:512], func=AF.Exp, bias=bcol)),
                          reads=[sr, t5c_r], writes=[pr])
                else:
                    S.add("act", (lambda e, s_=s_, p_=p_, w=w: e.activation(out=p_[:, 0:w], in_=s_[:, 0:w], func=AF.Exp)),
                          reads=[sr], writes=[pr])
                pend.append((st_, p_, pr))
            j = i - LA
            if j >= 0:
                st_, p_, pr = pend[j]
                for hf, (kt, ktr, vt, vtr, kc, mode) in enumerate(st_):
                    k_ = pvi[0]
                    pvi[0] += 1
                    S.add("pe", (lambda e, kc=kc, p_=p_, k_=k_, vt=vt, hf=hf: e.matmul(
                        O[0:65, :], vt[:, kc, :], p_[:, hf * 512:(hf + 1) * 512], start=(k_ == 0), stop=(k_ == total - 1))),
                        reads=[vtr, pr], writes=[Or])
        rr, rrr, _ = RR.next()
        osb, osbr, _ = OSB.next()
        S.add("dve", lambda e: e.tensor_copy(out=osb[0:65, :], in_=O[0:65, :]), reads=[Or], writes=[osbr])
        S.add("dve", lambda e: e.reciprocal(out=rr[64:65, :], in_=osb[64:65, :]), reads=[osbr], writes=[rrr])
        bc, bcr, _ = PBC.next()
        S.add("pe", lambda e: e.matmul(bc[0:64, :], onesf[64:65, 0:64], rr[64:65, :], start=True, stop=True),
              reads=[rrr, onesf_r], writes=[bcr])
        on, onr, _ = ON.next()
        S.add("dve", lambda e: e.tensor_tensor(out=on[:, :], in0=osb[0:64, :], in1=bc[0:64, :], op=ALU.mult),
              reads=[osbr, bcr], writes=[onr])
        return on, onr

    def store_y(u, qb, on, onr):
        S.add("sp", lambda e: e.dma_start(out=d["y"][u, :, qb * TS:(qb + 1) * TS], in_=on[:, :]),
              reads=[onr], writes=[y_r], dma_key="ystore")

    plan = []
    for h in range(4):
        plan.append(("A", h * 64, 64, h, h * 64, 64, [h]))
    for h in range(4):
        plan.append(("B", 256 + h * 96, 96, 10 + h, 256 + h * 96, 96, [4 + h]))
    for h in range(4):
        plan.append(("C", 640 + h * 64, 64, 4 + h, 640 + h * 64, 64, [8 + h]))
    for g in range(2):
        plan.append(("D", 896 + g * 64, 64, 8 + g, 896 + 2 * g * 64, 128, [12 + 2 * g, 13 + 2 * g]))
    import os
    units = [int(x) for x in os.environ.get("KB_UNITS", ",".join(str(i) for i in range(len(plan)))).split(",")]
    bt_plan = {}
    for ui in units:
        kind = plan[ui][0]
        if kind in ("A", "C"):
            h = ui % 4
            for qb in range(4):
                for m in range(2 if kind == "C" else 1):
                    for i, _e in enumerate(near_entries(kind, qb)):
                        bt_plan[(kind, h, qb, m, i)] = bts.plan(d["btl"][TILE_IDS[(kind, h, qb, i)]], 512)

    loaded = {}

    def load_unit(ui):
        kind, krow, kd, vu, qrow, qd, yus = plan[ui]
        kt, ktr, kkey = KT.next()
        vt, vtr, vkey = VT.next()
        qt, qtr, qkey = QT.next()
        for j in range(4):
            kr0 = kg_row(j, krow)
            S.add("sp", (lambda e, j=j, kr0=kr0: e.dma_start(out=kt[0:kd, j * T:(j + 1) * T], in_=d["kTg"][kr0:kr0 + kd, :])),
                  writes=[ktr], dma_key=kkey)
            if kind == "D":
                S.add("sp", (lambda e, j=j, kr0=kr0: e.dma_start(out=kt[64:128, j * T:(j + 1) * T], in_=d["kTg"][kr0:kr0 + kd, :])),
                      writes=[ktr], dma_key=kkey)
            vr0 = vg_row(j, vu)
            S.add("sp", (lambda e, j=j, vr0=vr0: e.dma_start(out=vt[:, 16 * j:16 * (j + 1), :],
                                                              in_=d["vg"][vr0:vr0 + 128, :].rearrange("p (c d) -> p c d", d=65))),
                  writes=[vtr], dma_key=vkey)
        if kind in ("A", "B"):
            pieces = [(0, qd, qrow)]
        elif kind == "C":
            pieces = [(0, 32, qrow), (32, 32, qrow + 32)]
        else:
            pieces = [(0, 64, qrow), (64, 64, qrow + 64)]
        qtiles = []
        for p0, pn, r0 in pieces:
            if qtiles or True:
                qt, qtr, qkey = (qt, qtr, qkey) if not qtiles else QT.next()
            S.add("pool", (lambda e, qt=qt: e.memset(qt[:, :], 0.0)), writes=[qtr])
            S.add("sp", (lambda e, qt=qt, p0=p0, pn=pn, r0=r0: e.dma_start(out=qt[p0:p0 + pn, :], in_=d["qT"][r0:r0 + pn, :])),
                  writes=[qtr], dma_key=qkey)
            qtiles.append((qt, qtr))
        own = None
        if kind in ("A", "C"):
            kto, ktor, kokey = KTO.next()
            vto, vtor, vokey = VTO.next()
            S.add("sp", lambda e: e.dma_start(out=kto[0:kd, :], in_=d["kT"][krow:krow + kd, :]), writes=[ktor], dma_key=kokey)
            S.add("sp", lambda e: e.dma_start(out=vto[:, :, :], in_=d["v"][vu]), writes=[vtor], dma_key=vokey)
            own = (kto, ktor, vto, vtor)
        loaded[ui] = (kt, ktr, vt, vtr, qtiles, own)

    load_unit(units[0])
    for n_, ui in enumerate(units):
        if n_ + 1 < len(units):
            load_unit(units[n_ + 1])
        kind, krow, kd, vu, qrow, qd, yus = plan[ui]
        kt, ktr, vt, vtr, qtiles, own = loaded.pop(ui)
        h = ui % 4
        dense = [(kt, ktr, vt, vtr, kc, None) for kc in range(64)]

        def near_list(kind, qb, m):
            out = []
            for i, (src, ch) in enumerate(near_entries(kind, qb)):
                mode = ("tile", bt_plan[(kind, h, qb, m, i)])
                if src == "own":
                    out.append((own[0], own[1], own[2], own[3], ch, mode))
                else:
                    out.append((kt, ktr, vt, vtr, ch, mode))
            return out

        for qb in range(4):
            if kind == "A":
                on, onr = attend(qtiles[0][0], qtiles[0][1], 0, 128, qb, near_list("A", qb, 0))
                store_y(yus[0], qb, on, onr)
            elif kind == "B":
                on, onr = attend(qtiles[0][0], qtiles[0][1], 0, 128, qb, dense)
                store_y(yus[0], qb, on, onr)
            elif kind == "D":
                for hh in range(2):
                    on, onr = attend(qtiles[hh][0], qtiles[hh][1], 0, 128, qb, dense)
                    store_y(yus[hh], qb, on, onr)
            else:
                sc, scr, _ = SC.next()
                S.add("act", (lambda e, sc=sc, qb=qb, h=h: e.activation(out=sc[:, :], in_=t5c[:, (h * 4 + qb) * 64:(h * 4 + qb + 1) * 64], func=AF.Exp)),
                      reads=[t5c_r], writes=[scr])
                vs, vsr, _ = VS.next()
                S.add("dve", (lambda e, sc=sc, vs=vs, vt=vt: e.tensor_tensor(
                    out=vs[:, :, :], in0=vt[:, :, :], in1=sc[:, :].unsqueeze(2).broadcast_to([128, 64, 65]), op=ALU.mult)),
                    reads=[vtr, scr], writes=[vsr])
                ons = []
                for m in range(2):
                    ch = [(kt, ktr, vs, vsr, kc, None) for kc in range(64)] + near_list("C", qb, m)
                    ons.append(attend(qtiles[m][0], qtiles[m][1], 0, 128, qb, ch))
                (on0, on0r), (on1, on1r) = ons
                yo, yor, _ = ON.next()
                S.add("dve", (lambda e, yo=yo, on0=on0, on1=on1: e.scalar_tensor_tensor(
                    out=yo[:, :], in0=on1[:, :], scalar=nlam, in1=on0[:, :], op0=ALU.mult, op1=ALU.add)),
                    reads=[on0r, on1r, lt_r], writes=[yor])
                store_y(yus[0], qb, yo, yor)


def build_b():
    nc = bass.Bass("TRN2", target_bir_lowering=False)
    dt = lambda name, shape, dtype, kind: nc.dram_tensor(name, shape, dtype, kind=kind).ap()
    d = {}
    gains = dt("gains", [128, NG], F32, "ExternalInput")
    lconst = dt("lconst", [128, 2], F32, "ExternalInput")
    cmat = dt("cmat", [128, 768], F32, "ExternalInput")
    d["qT"] = dt("qT", [QROWS, T], BF16, "ExternalInput")
    d["kT"] = dt("kT", [KROWS, T], BF16, "ExternalInput")
    d["v"] = dt("v", [NVU, 128, 16, 65], BF16, "ExternalInput")
    d["kTg"] = dt("kTg", [4 * KROWS, T], BF16, "ExternalInput")
    d["vg"] = dt("vg", [4 * NVU * 128, 16 * 65], BF16, "ExternalInput")
    d["btl"] = dt("btl", [NTILES, 128, 512], F32, "ExternalInput")
    d["t5c"] = dt("t5c", [128, 1024], F32, "ExternalInput")
    d["dlam"] = dt("dlam", [64, 128], F32, "ExternalInput")
    d["y"] = dt("y", [16, 64, T], F32, "ExternalOutput")
    d["y_r"] = Res("y")
    with contextlib.ExitStack() as st:
        S = Sched(nc)
        c = Ctx()
        common_setup(c, st, nc, S, gains, lconst, cmat)
        emit_phase_b(c, st, nc, S, d)
        S.add("sp", None, reads=[d["y_r"]])
        S.emit(st)
    return nc


def t5_bucket_np(rel):
    half = 16
    max_exact = 8
    n = np.abs(rel)
    large = max_exact + (np.log(np.maximum(n, 1).astype(np.float32) / max_exact)
                         / np.float32(math.log(128 / max_exact)) * (half - max_exact)).astype(np.int32)
    large = np.minimum(large, half - 1)
    return np.where(rel > 0, half, 0) + np.where(n < max_exact, n, large)


def bias_tiles_host(inp, l, core):
    r = core % 4
    rpb = np.asarray(inp["na_rpb"][l], np.float32)
    t5 = np.asarray(inp["t5_bias"], np.float32)
    btl = np.empty((NTILES, 128, 512), np.float32)
    kk = np.arange(128)[:, None]
    qq = np.arange(512)[None, :]
    NEG = np.float32(-1e30)
    for qb in range(4):
        qbg = 4 * r + qb
        qtok = qbg * 512 + qq
        qr, qc = qtok // 64, qtok % 64
        rs = np.clip(qr - 4, 0, 120)
        cs = np.clip(qc - 8, 0, 48)
        for i, (src, ch) in enumerate(near_entries("A", qb)):
            g = 16 * r + ch if src == "own" else ch
            ktok = g * 128 + kk
            kr, kcol = ktok // 64, ktok % 64
            valid = (kr >= rs) & (kr < rs + 8) & (kcol >= cs) & (kcol < cs + 16)
            dr = np.clip(kr - qr + 7, 0, 14)
            dc = np.clip(kcol - qc + 15, 0, 30)
            flat = np.where(valid, dr * 31 + dc, 15 * 31)
            for h in range(4):
                tab = np.concatenate([rpb[h].ravel(), np.array([NEG], np.float32)])
                btl[TILE_IDS[("A", h, qb, i)]] = tab[flat]
        for i, (src, ch) in enumerate(near_entries("C", qb)):
            g = 16 * r + ch if src == "own" else ch
            if 4 * qbg - 1 <= g <= 4 * qbg + 4:
                bk = t5_bucket_np((g * 128 + kk) - qtok)
                for h in range(4):
                    btl[TILE_IDS[("C", h, qb, i)]] = t5[:, h][bk]
            else:
                for h in range(4):
                    btl[TILE_IDS[("C", h, qb, i)]] = NEG
    t5c = np.zeros((128, 1024), np.float32)
    for qb in range(4):
        qbg = 4 * r + qb
        for kc in range(64):
            for h in range(4):
                if 4 * qbg - 1 <= kc <= 4 * qbg + 4:
                    val = NEG
                else:
                    val = t5[31, h] if kc > 4 * qbg + 3 else t5[15, h]
                t5c[:, (h * 4 + qb) * 64 + kc] = val
    return btl, t5c


_PROGS = {}


def _prog(name):
    if name not in _PROGS:
        _PROGS[name] = {"a": build_a, "b": build_b, "c": build_c, "f": build_fused}[name]()
    return _PROGS[name]


def to_featmajor(xc):
    return np.ascontiguousarray(xc.T.reshape(8, 128, T).transpose(1, 0, 2).reshape(128, 8 * T))


def from_featmajor(xT):
    return xT.reshape(128, 8, T).transpose(1, 0, 2).reshape(D, T).T


def kernel_unfused(**inp):
    x = np.asarray(inp["x"], np.float32).reshape(BATCH * SEQ, D)
    cm = const_mats()
    xTs = [to_featmajor(x[c * T:(c + 1) * T]) for c in range(NCORE)]
    tabs = [rope_tables(c) for c in range(NCORE)]
    cores = list(range(NCORE))
    for l in range(DEPTH):
        H = layer_host(inp, l)
        common = {"gains": H["gains"], "lconst": H["lconst"], "cmat": cm}
        maps = [dict(common, xT=xTs[c], wg=H["ffn1_wg"], wu=H["ffn1_wu"], wd=H["ffn1_wd"], winf=H["winf"],
                     winv=H["winv"], wsm=H["wsm"], tabs=tabs[c]) for c in cores]
        ra = run_bass_kernel_spmd(_prog("a"), maps, core_ids=cores).results
        maps = []
        dl = np.ascontiguousarray(np.broadcast_to(np.asarray(inp["diff_lambda"][l], np.float32).reshape(1, 128), (64, 128)))
        for c in cores:
            b = c // 4
            kTg = np.zeros((4 * KROWS, T), np.asarray(ra[0]["kT"]).dtype)
            vg = np.zeros((4 * NVU * 128, 1040), np.asarray(ra[0]["v"]).dtype)
            for j in range(4):
                kj = np.asarray(ra[4 * b + j]["kT"])
                vj = np.asarray(ra[4 * b + j]["v"]).reshape(NVU * 128, 1040)
                for a, bb in K_PARTS:
                    kTg[4 * a + j * (bb - a): 4 * a + (j + 1) * (bb - a)] = kj[a:bb]
                for a, bb in V_PARTS:
                    vg[(4 * a + j * (bb - a)) * 128: (4 * a + (j + 1) * (bb - a)) * 128] = vj[a * 128:bb * 128]
            btl, t5c = bias_tiles_host(inp, l, c)
            maps.append(dict(common, qT=np.asarray(ra[c]["qT"]), kT=np.asarray(ra[c]["kT"]), v=np.asarray(ra[c]["v"]),
                             kTg=kTg, vg=vg, btl=btl, t5c=t5c, dlam=dl))
        rb = run_bass_kernel_spmd(_prog("b"), maps, core_ids=cores).results
        wout = np.ascontiguousarray(np.asarray(inp["w_out"][l], np.float32).reshape(16, 64, 8, 128).transpose(2, 1, 0, 3).reshape(8, 64, 2048))
        maps = [dict(common, xT=np.asarray(ra[c]["x1T"]), y=np.asarray(rb[c]["y"]), wg=H["ffn2_wg"], wu=H["ffn2_wu"],
                     wd=H["ffn2_wd"], wout=wout) for c in cores]
        rc = run_bass_kernel_spmd(_prog("c"), maps, core_ids=cores).results
        xTs = [np.asarray(rc[c]["xoT"]) for c in cores]
    out = np.concatenate([from_featmajor(xTs[c]) for c in cores], axis=0)
    return np.ascontiguousarray(out.reshape(BATCH, SEQ, D).astype(np.float32))


RG = [[0, 1, 2, 3], [4, 5, 6, 7]]
K_PARTS = [(0, 256), (256, 448), (448, 640), (640, 896), (896, 1024)]
V_PARTS = [(0, 3), (3, 6), (6, 9), (9, 12), (12, 14)]


def kg_row(j, row):
    for a, b in K_PARTS:
        if a <= row < b:
            return 4 * a + j * (b - a) + (row - a)
    raise AssertionError(row)


def vg_row(j, u):
    for a, b in V_PARTS:
        if a <= u < b:
            return (4 * a + j * (b - a) + (u - a)) * 128
    raise AssertionError(u)


def build_fused():
    nc = bass.Bass("TRN2", target_bir_lowering=False)
    dt = lambda name, shape, dtype, kind, **kw: nc.dram_tensor(name, shape, dtype, kind=kind, **kw).ap()
    xT_d = dt("xT", [128, 8 * T], F32, "ExternalInput")
    cmat_d = dt("cmat", [128, 768], F32, "ExternalInput")
    tabs_d = dt("tabs", [4, 128, T], F32, "ExternalInput")
    t5c_d = dt("t5c", [128, 1024], F32, "ExternalInput")
    xo_d = dt("xoT", [128, 8 * T], F32, "ExternalOutput")
    LD = []
    for l in range(DEPTH):
        sfx = "_%d" % l
        d = {}
        d["gains"] = dt("gains" + sfx, [128, NG], F32, "ExternalInput")
        d["lconst"] = dt("lconst" + sfx, [128, 2], F32, "ExternalInput")
        for nm in ("wg1", "wu1", "wg2", "wu2"):
            d[nm] = dt(nm + sfx, [11, 128, 2048], F32, "ExternalInput")
        for nm in ("wd1", "wd2"):
            d[nm] = dt(nm + sfx, [8, 128, 2816], F32, "ExternalInput")
        d["winf"] = dt("winf" + sfx, [7, 128, 2048], F32, "ExternalInput")
        d["winv"] = dt("winv" + sfx, [2, 128, 2560], F32, "ExternalInput")
        d["wsm"] = dt("wsm" + sfx, [128, WSM], F32, "ExternalInput")
        d["wout"] = dt("wout" + sfx, [8, 64, 2048], F32, "ExternalInput")
        d["btl"] = dt("btl" + sfx, [NTILES, 128, 512], F32, "ExternalInput")
        d["dlam"] = dt("dlam" + sfx, [64, 128], F32, "ExternalInput")
        d["qT"] = dt("s_qT" + sfx, [QROWS, T], BF16, "Internal")
        d["kT"] = dt("s_kT" + sfx, [KROWS, T], BF16, "Internal")
        d["v"] = dt("s_v" + sfx, [NVU, 128, 16, 65], BF16, "Internal")
        d["kTg"] = dt("s_kTg" + sfx, [4 * KROWS, T], BF16, "Internal", addr_space="Local")
        d["vg"] = dt("s_vg" + sfx, [4 * NVU * 128, 16 * 65], BF16, "Internal", addr_space="Local")
        d["y"] = dt("s_y" + sfx, [16, 64, T], F32, "Internal")
        LD.append(d)
    with contextlib.ExitStack() as top:
        ss = SemState(top)
        c = Ctx()
        c.xT = _sb(top, nc, "xT", [128, 8 * T], F32)

        def fresh_x():
            c.x_r = [Res("x%d" % t) for t in range(NT)]

        with contextlib.ExitStack() as st:
            S = Sched(nc, ss)
            fresh_x()
            for k in range(8):
                S.add("sp", (lambda e, k=k: e.dma_start(out=c.xT[:, k * T:(k + 1) * T], in_=xT_d[:, k * T:(k + 1) * T])),
                      writes=c.x_r, dma_key="xT")
            S.drain()
            S.emit(st)
        for l in range(DEPTH):
            d = LD[l]
            with contextlib.ExitStack() as st:
                S = Sched(nc, ss)
                fresh_x()
                common_setup(c, st, nc, S, d["gains"], d["lconst"], cmat_d)
                da = dict(wg=d["wg1"], wu=d["wu1"], wd=d["wd1"], winf=d["winf"], winv=d["winv"], wsm=d["wsm"], tabs=tabs_d,
                          qT=d["qT"], kT=d["kT"], v=d["v"], q_r=Res("q"), k_r=Res("k"), v_r=Res("v"))
                emit_phase_a(c, st, nc, S, da)
                S.drain()
                S.emit(st)
            with contextlib.ExitStack() as st:
                S = Sched(nc, ss)
                v2 = d["v"].rearrange("u p c d -> (u p) (c d)")
                for a, b in K_PARTS:
                    S.add("pool", (lambda e, d=d, a=a, b=b: e.collective_compute(
                        "AllGather", ALU.bypass, replica_groups=RG, ins=[d["kT"][a:b, :]], outs=[d["kTg"][4 * a:4 * b, :]])),
                        dma_key="cc", inc=1)
                for a, b in V_PARTS:
                    S.add("pool", (lambda e, d=d, a=a, b=b, v2=v2: e.collective_compute(
                        "AllGather", ALU.bypass, replica_groups=RG, ins=[v2[a * 128:b * 128, :]],
                        outs=[d["vg"][4 * a * 128:4 * b * 128, :]])), dma_key="cc", inc=1)
                S.drain()
                S.emit(st)
            with contextlib.ExitStack() as st:
                S = Sched(nc, ss)
                common_setup(c, st, nc, S, d["gains"], d["lconst"], cmat_d)
                db = dict(qT=d["qT"], kT=d["kT"], v=d["v"], kTg=d["kTg"], vg=d["vg"], btl=d["btl"], t5c=t5c_d,
                          dlam=d["dlam"], y=d["y"], y_r=Res("y"))
                emit_phase_b(c, st, nc, S, db)
                S.drain()
                S.emit(st)
            with contextlib.ExitStack() as st:
                S = Sched(nc, ss)
                fresh_x()
                common_setup(c, st, nc, S, d["gains"], d["lconst"], cmat_d)
                dc = dict(wg=d["wg2"], wu=d["wu2"], wd=d["wd2"], wout=d["wout"], y=d["y"], y_r=Res("y"))
                emit_phase_c(c, st, nc, S, dc)
                S.drain()
                S.emit(st)
        with contextlib.ExitStack() as st:
            S = Sched(nc, ss)
            fresh_x()
            xo_r = Res("xo")
            store_x(c, S, xo_d, xo_r)
            S.add("sp", None, reads=[xo_r])
            S.drain()
            S.emit(st)
    return nc


def kernel(**inp):
    x = np.asarray(inp["x"], np.float32).reshape(BATCH * SEQ, D)
    cm = const_mats()
    cores = list(range(NCORE))
    shared = {}
    for l in range(DEPTH):
        H = layer_host(inp, l)
        sfx = "_%d" % l
        shared.update({"gains" + sfx: H["gains"], "lconst" + sfx: H["lconst"], "wg1" + sfx: H["ffn1_wg"], "wu1" + sfx: H["ffn1_wu"],
                       "wd1" + sfx: H["ffn1_wd"], "wg2" + sfx: H["ffn2_wg"], "wu2" + sfx: H["ffn2_wu"], "wd2" + sfx: H["ffn2_wd"],
                       "winf" + sfx: H["winf"], "winv" + sfx: H["winv"], "wsm" + sfx: H["wsm"],
                       "wout" + sfx: np.ascontiguousarray(np.asarray(inp["w_out"][l], np.float32).reshape(16, 64, 8, 128)
                                                         .transpose(2, 1, 0, 3).reshape(8, 64, 2048)),
                       "dlam" + sfx: np.ascontiguousarray(np.broadcast_to(
                           np.asarray(inp["diff_lambda"][l], np.float32).reshape(1, 128), (64, 128)))})
    maps = []
    for c in cores:
        m = dict(shared, xT=to_featmajor(x[c * T:(c + 1) * T]), cmat=cm, tabs=rope_tables(c))
        for l in range(DEPTH):
            btl, t5c = bias_tiles_host(inp, l, c)
            m["btl_%d" % l] = btl
            m["t5c"] = t5c
        maps.append(m)
    res = run_bass_kernel_spmd(_prog("f"), maps, core_ids=cores).results
    out = np.concatenate([from_featmajor(np.asarray(res[c]["xoT"])) for c in cores], axis=0)
    return np.ascontiguousarray(out.reshape(BATCH, SEQ, D).astype(np.float32))
```

```python
import contextlib
import math
import numpy as np
import ml_dtypes
import concourse.bass as bass
import concourse.mybir as mybir
from concourse.bass_utils import run_bass_kernel_spmd

F32 = mybir.dt.float32
BF16 = mybir.dt.bfloat16
AF = mybir.ActivationFunctionType
ALU = mybir.AluOpType

_UID = [0]


def _uid():
    _UID[0] += 1
    return _UID[0]

ENGS = ("pe", "act", "dve", "pool", "sp")
EPOCH = 16000
SAME_ENGINE_SYNC = True


class Res:
    __slots__ = ("name", "writer", "readers")

    def __init__(self, name=""):
        self.name = name
        self.writer = None
        self.readers = {}


class Op:
    __slots__ = ("eng", "fn", "deps", "dma_key", "signaled", "count", "semid", "idx", "inc")


class SemState:
    def __init__(self, stack):
        self.stack = stack
        self.cnt = {e: 0 for e in ENGS}
        self.dcnt = {}
        self.sems = {}


class Sched:
    def __init__(self, nc, semstate=None):
        self.nc = nc
        self.ops = []
        self.by_eng = {e: [] for e in ENGS}
        self.semstate = semstate
        self.last_dma = {}

    def drain(self):
        op = self.add("sp", None)
        op.deps = list(self.last_dma.values())
        return op

    def add(self, eng, fn, reads=(), writes=(), dma_key=None, inc=16):
        op = Op()
        op.inc = inc
        op.eng = eng
        op.fn = fn
        op.dma_key = dma_key
        op.signaled = dma_key is not None
        op.count = 0
        op.semid = None
        op.idx = len(self.ops)
        deps = {}
        for r in reads:
            if r.writer is not None:
                deps[r.writer.idx] = r.writer
        for w in writes:
            if w.writer is not None:
                deps[w.writer.idx] = w.writer
            for rd in w.readers.values():
                deps[rd.idx] = rd
        dl = []
        for d in deps.values():
            if d.dma_key is not None or dma_key is not None:
                need = True
            elif d.eng == eng:
                need = (eng != "pe") and SAME_ENGINE_SYNC
            else:
                need = True
            if need:
                d.signaled = True
                dl.append(d)
        op.deps = dl
        for r in reads:
            key = eng if dma_key is None else ("dma", op.idx)
            r.readers[key] = op
        for w in writes:
            w.writer = op
            w.readers = {}
        self.ops.append(op)
        self.by_eng[eng].append(op)
        if dma_key is not None:
            self.last_dma[dma_key] = op
        return op

    def emit(self, stack):
        nc = self.nc
        ss = self.semstate if self.semstate is not None else SemState(stack)
        cnt = ss.cnt
        dcnt = ss.dcnt
        for op in self.ops:
            if op.dma_key is not None:
                dcnt[op.dma_key] = dcnt.get(op.dma_key, 0) + op.inc
                op.count = dcnt[op.dma_key]
                op.semid = ("d", op.dma_key)
            elif op.signaled:
                c = cnt[op.eng]
                cnt[op.eng] = c + 1
                op.semid = ("e", op.eng, c // EPOCH)
                op.count = (c % EPOCH) + 1
        self.check(ss)
        sems = ss.sems
        for op in self.ops:
            if op.semid is not None and op.semid not in sems:
                sems[op.semid] = ss.stack.enter_context(nc.semaphore("s%d" % len(sems)))
        self.nsems = len(sems)
        block = stack.enter_context(nc.Block())
        engobj = {"pe": block.tensor, "act": block.scalar, "dve": block.vector,
                  "pool": block.gpsimd, "sp": block.sync}

        def make(ename):
            ops = self.by_eng[ename]

            def body(e):
                waited = {}
                wepoch = {}
                for op in ops:
                    for d in op.deps:
                        sid = d.semid
                        c = d.count
                        if sid[0] == "e" and wepoch.get(sid[1], -1) > sid[2]:
                            continue
                        if waited.get(sid, 0) >= c:
                            continue
                        e.wait_ge(sems[sid], c)
                        waited[sid] = c
                        if sid[0] == "e":
                            wepoch[sid[1]] = max(wepoch.get(sid[1], -1), sid[2])
                    if op.fn is None:
                        continue
                    ins = op.fn(e)
                    if op.dma_key is not None and op.inc == 1:
                        ins.then_inc(sems[op.semid])
                    elif op.dma_key is not None:
                        ins.then_inc(sems[op.semid], 16)
                    elif op.signaled:
                        ins.then_inc(sems[op.semid], 1)
            return body

        for ename in ENGS:
            if self.by_eng[ename]:
                engobj[ename](make(ename))


def _sched_check(self, ss):
    val = dict(getattr(ss, "val", {}))
    pos = {e: 0 for e in ENGS}
    progress = True
    while progress:
        progress = False
        for e in ENGS:
            ops = self.by_eng[e]
            while pos[e] < len(ops):
                op = ops[pos[e]]
                if all(val.get(d.semid, 0) >= d.count and (d.semid[0] != "e" or True) for d in op.deps):
                    if op.semid is not None:
                        if op.semid[0] == "d":
                            val[op.semid] = val.get(op.semid, 0) + op.inc
                        else:
                            val[op.semid] = val.get(op.semid, 0) + 1
                        assert val[op.semid] == op.count or op.semid[0] == "d", (op.semid, val[op.semid], op.count)
                    pos[e] += 1
                    progress = True
                else:
                    break
    stuck = {e: pos[e] for e in ENGS if pos[e] < len(self.by_eng[e])}
    assert not stuck, "deadlock in schedule: %r" % stuck
    ss.val = val
    print("sched ok: ops=%d per-eng=%r" % (len(self.ops), {e: len(self.by_eng[e]) for e in ENGS}))


Sched.check = _sched_check


class Ring:
    def __init__(self, stack, nc, name, n, shape, dtype, psum=False):
        self.bufs = []
        for i in range(n):
            nm = "r_%s%d" % (name, i)
            tn = "%s_%d" % (nm, _uid())
            if psum:
                t = stack.enter_context(nc.psum_tensor(tn, shape, dtype))
            else:
                t = stack.enter_context(nc.sbuf_tensor(tn, shape, dtype))
            self.bufs.append((t, Res(nm), nm))
        self.i = 0

    def next(self):
        b = self.bufs[self.i % len(self.bufs)]
        self.i += 1
        return b


D = 1024
SEQ = 8192
BATCH = 2
DEPTH = 2
DFF = 2816
NCORE = 8
T = 2048
NT = 4
TS = 512
EPS = 1e-6
NG = 59
G_FFN1, G_MIX, G_NAQ, G_NAK, G_QLAT, G_KVLAT, G_MQ, G_MK, G_DQ, G_DK, G_GQ, G_GK = \
    0, 8, 16, 17, 18, 20, 21, 22, 23, 24, 25, 26
G_BETA, G_FFN2, G_FIN = 27, 43, 51
WS_UQ, WS_KN, WS_V, WS_KR = 0, 768, 1152, 1408
WSM = 2176
C_ONES, C_BD64, C_BD32, C_RB, C_RD, C_ID = 0, 1, 2, 3, 4, 5
QROWS = 1152
KROWS = 1024
NVU = 14


def lambda_init(l):
    return 0.8 - 0.6 * math.exp(-0.3 * l)


class Ctx:
    pass


def _sb(st, nc, name, shape, dt):
    return st.enter_context(nc.sbuf_tensor("sb_%s_%d" % (name, _uid()), shape, dt))


class WStream:
    def __init__(self, S, ring, pf):
        self.S = S
        self.ring = ring
        self.pf = pf
        self.items = []
        self.issued = 0
        self.handles = []

    def plan(self, src_ap, ncols, part=128):
        self.items.append((src_ap, ncols, part))
        return len(self.items) - 1

    def get(self, i):
        while self.issued < min(len(self.items), i + 1 + self.pf):
            src, ncols, part = self.items[self.issued]
            buf, res, nm = self.ring.next()
            self.S.add("pool", (lambda e, buf=buf, src=src, ncols=ncols, part=part:
                                e.dma_start(out=buf[0:part, 0:ncols], in_=src)),
                       writes=[res], dma_key=nm)
            self.handles.append((buf, res))
            self.issued += 1
        return self.handles[i]


def rms_feat(c, srcs, P, ones_idx, n, gain_cols, outs):
    S = c.S
    ones = c.cmat[0:P, ones_idx * 128: ones_idx * 128 + P]
    ps, psr, _ = c.PS.next()
    last = len(srcs) - 1
    sc = float(n) ** -0.5
    for i, (src, sres) in enumerate(srcs):
        sq, sqr, _ = c.SQ.next()
        S.add("act", (lambda e, sq=sq, src=src: e.activation(out=sq[0:P, :], in_=src, func=AF.Square, scale=sc)),
              reads=[sres], writes=[sqr])
        S.add("pe", (lambda e, sq=sq, i=i: e.matmul(ps[0:P, :], ones, sq[0:P, :], start=(i == 0), stop=(i == last))),
              reads=[sqr, c.cmat_r], writes=[psr])
    rs, rsr, _ = c.RS.next()
    S.add("act", lambda e: e.activation(out=rs[0:P, :], in_=ps[0:P, :], func=AF.Ln, bias=c.eps[0:P, 0:1]),
          reads=[psr, c.eps_r], writes=[rsr])
    S.add("act", lambda e: e.activation(out=rs[0:P, :], in_=rs[0:P, :], func=AF.Exp, scale=-0.5),
          reads=[rsr], writes=[rsr])
    for i, ((src, sres), (out, ores)) in enumerate(zip(srcs, outs)):
        g = c.gains[0:P, gain_cols[i]:gain_cols[i] + 1]
        S.add("dve", (lambda e, src=src, out=out, g=g: e.scalar_tensor_tensor(
            out=out, in0=src, scalar=g, in1=rs[0:P, :], op0=ALU.mult, op1=ALU.mult)),
            reads=[sres, rsr, c.gains_r], writes=[ores])


def rope_feat(c, qn, qnr, P, r_idx, cos_ap, sin_ap, tab_r, out, outr):
    S = c.S
    R = c.cmat[0:P, r_idx * 128: r_idx * 128 + P]
    ps, psr, _ = c.PS.next()
    S.add("pe", lambda e: e.matmul(ps[0:P, :], R, qn, start=True, stop=True), reads=[qnr, c.cmat_r], writes=[psr])
    t1, t1r, _ = c.TF.next()
    t2, t2r, _ = c.TF.next()
    S.add("pool", lambda e: e.tensor_tensor(out=t1[0:P, :], in0=qn, in1=cos_ap, op=ALU.mult), reads=[qnr, tab_r], writes=[t1r])
    S.add("dve", lambda e: e.tensor_tensor(out=t2[0:P, :], in0=ps[0:P, :], in1=sin_ap, op=ALU.mult), reads=[psr, tab_r], writes=[t2r])
    S.add("pool", lambda e: e.tensor_tensor(out=out, in0=t1[0:P, :], in1=t2[0:P, :], op=ALU.add), reads=[t1r, t2r], writes=[outr])


def norm_x_tile(c, t, gcol0):
    hT, hTr, _ = c.HT.next()
    srcs = [(c.xT[:, k * T + t * TS: k * T + (t + 1) * TS], c.x_r[t]) for k in range(8)]
    outs = [(hT[:, k * TS:(k + 1) * TS], hTr) for k in range(8)]
    rms_feat(c, srcs, 128, C_ONES, D, [gcol0 + k for k in range(8)], outs)
    return hT, hTr


def ffn_tile(c, t, hT, hTr, ws, base):
    S = c.S
    for b in range(11):
        wg, wgr = ws.get(base + 2 * b)
        wu, wur = ws.get(base + 2 * b + 1)
        for jj in range(2):
            j = 2 * b + jj
            pg, pgr, _ = c.PS.next()
            pu, pur, _ = c.PS.next()
            for k in range(8):
                S.add("pe", (lambda e, k=k, pg=pg, wg=wg, jj=jj: e.matmul(
                    pg[:, :], wg[:, k * 256 + jj * 128: k * 256 + jj * 128 + 128], hT[:, k * TS:(k + 1) * TS],
                    start=(k == 0), stop=(k == 7))), reads=[wgr, hTr], writes=[pgr])
            for k in range(8):
                S.add("pe", (lambda e, k=k, pu=pu, wu=wu, jj=jj: e.matmul(
                    pu[:, :], wu[:, k * 256 + jj * 128: k * 256 + jj * 128 + 128], hT[:, k * TS:(k + 1) * TS],
                    start=(k == 0), stop=(k == 7))), reads=[wur, hTr], writes=[pur])
            sg, sgr, _ = c.TF.next()
            S.add("act", (lambda e, sg=sg, pg=pg: e.activation(out=sg[:, :], in_=pg[:, :], func=AF.Silu)),
                  reads=[pgr], writes=[sgr])
            S.add("dve", (lambda e, sg=sg, pu=pu, j=j: e.tensor_tensor(
                out=c.actT[:, j * TS:(j + 1) * TS], in0=sg[:, :], in1=pu[:, :], op=ALU.mult)),
                reads=[sgr, pur], writes=[c.act_r[j]])
    for m in range(8):
        wd, wdr = ws.get(base + 22 + m)
        pd, pdr, _ = c.PS.next()
        for j in range(22):
            S.add("pe", (lambda e, j=j, pd=pd, wd=wd: e.matmul(
                pd[:, :], wd[:, j * 128:(j + 1) * 128], c.actT[:, j * TS:(j + 1) * TS],
                start=(j == 0), stop=(j == 21))), reads=[wdr, c.act_r[j]], writes=[pdr])
        xs = c.xT[:, m * T + t * TS: m * T + (t + 1) * TS]
        S.add("dve", (lambda e, pd=pd, xs=xs: e.scalar_tensor_tensor(
            out=xs, in0=pd[:, :], scalar=0.5, in1=xs, op0=ALU.mult, op1=ALU.add)),
            reads=[pdr, c.x_r[t]], writes=[c.x_r[t]])


def plan_ffn(ws, wg, wu, wd):
    base = len(ws.items)
    for b in range(11):
        ws.plan(wg[b], 2048)
        ws.plan(wu[b], 2048)
    for m in range(8):
        ws.plan(wd[m], 2816)
    return base


def common_setup(c, st, nc, S, gains_d, lconst_d, cmat_d):
    c.S = S
    c.nc = nc
    c.gains = _sb(st, nc, "gains", [128, NG], F32)
    c.gains_r = Res("gains")
    c.lconst = _sb(st, nc, "lconst", [128, 2], F32)
    c.lconst_r = Res("lconst")
    c.cmat = _sb(st, nc, "cmat", [128, 6 * 128], BF16)
    c.cmat_r = Res("cmat")
    c.eps = _sb(st, nc, "eps", [128, 1], F32)
    c.eps_r = Res("eps")
    S.add("sp", lambda e: e.dma_start(out=c.gains[:, :], in_=gains_d), writes=[c.gains_r], dma_key="gains")
    S.add("sp", lambda e: e.dma_start(out=c.lconst[:, :], in_=lconst_d), writes=[c.lconst_r], dma_key="lconst")
    S.add("pool", lambda e: e.dma_start(out=c.cmat[:, :], in_=cmat_d), writes=[c.cmat_r], dma_key="cmat")
    S.add("dve", lambda e: e.memset(c.eps[:, :], EPS), writes=[c.eps_r])


def load_x(c, st, nc, S, x_d):
    c.xT = _sb(st, nc, "xT", [128, 8 * T], F32)
    c.x_r = [Res("x%d" % t) for t in range(NT)]
    for k in range(8):
        S.add("sp", (lambda e, k=k: e.dma_start(out=c.xT[:, k * T:(k + 1) * T], in_=x_d[:, k * T:(k + 1) * T])),
              writes=c.x_r, dma_key="xT")


def store_x(c, S, xo_d, xo_r):
    for k in range(8):
        S.add("sp", (lambda e, k=k: e.dma_start(out=xo_d[:, k * T:(k + 1) * T], in_=c.xT[:, k * T:(k + 1) * T])),
              reads=c.x_r, writes=[xo_r], dma_key="xT")


def emit_phase_a(c, st, nc, S, d):
    c.PS = Ring(st, nc, "ps", 8, [128, 512], F32, psum=True)
    c.SQ = Ring(st, nc, "sq", 3, [128, 512], BF16)
    c.RS = Ring(st, nc, "rs", 2, [128, 512], F32)
    c.TF = Ring(st, nc, "tf", 4, [128, 512], F32)
    c.HT = Ring(st, nc, "hT", 2, [128, 8 * TS], BF16)
    c.actT = _sb(st, nc, "actT", [128, 22 * TS], BF16)
    c.act_r = [Res("act%d" % j) for j in range(22)]
    wring = Ring(st, nc, "wb", 6, [128, 2816], BF16)
    ws = WStream(S, wring, 4)
    wsm = _sb(st, nc, "wsm", [128, WSM], BF16)
    wsm_r = Res("wsm")
    S.add("pool", lambda e: e.dma_start(out=wsm[:, :], in_=d["wsm"]), writes=[wsm_r], dma_key="wsm")
    tabs = _sb(st, nc, "tabs", [128, 4 * T], F32)
    tab_r = Res("tabs")
    for i in range(4):
        S.add("sp", (lambda e, i=i: e.dma_start(out=tabs[:, i * T:(i + 1) * T], in_=d["tabs"][i])),
              writes=[tab_r], dma_key="tabs")
    for col, sc in ((G_NAQ, 64 ** -0.5), (G_MQ, 96 ** -0.5), (G_DQ, 32 ** -0.5), (G_GQ, 64 ** -0.5)):
        S.add("dve", (lambda e, col=col, sc=sc: e.tensor_scalar(
            out=c.gains[:, col:col + 1], in0=c.gains[:, col:col + 1], scalar1=float(sc), scalar2=None, op0=ALU.mult)),
            reads=[c.gains_r], writes=[c.gains_r])

    bases_ffn = [plan_ffn(ws, d["wg"], d["wu"], d["wd"]) for t in range(NT)]
    bases_in = []
    for t in range(NT):
        bases_in.append(len(ws.items))
        for b in range(7):
            ws.plan(d["winf"][b], 2048)
        for vb in range(2):
            ws.plan(d["winv"][vb], 2560)

    import os
    n1 = int(os.environ.get("KA_N1", NT))
    n2 = int(os.environ.get("KA_N2", NT))
    for t in range(n1):
        hT, hTr = norm_x_tile(c, t, G_FFN1)
        ffn_tile(c, t, hT, hTr, ws, bases_ffn[t])

    STG = Ring(st, nc, "stg", 6, [128, 512], BF16)
    QN = Ring(st, nc, "qn", 3, [128, 512], BF16)
    cqn = _sb(st, nc, "cqn", [128, 2 * TS], BF16)
    cqn_r = Res("cqn")
    ckvn = _sb(st, nc, "ckvn", [128, TS], BF16)
    ckvn_r = Res("ckvn")
    vst = _sb(st, nc, "vst", [128, 4, NVU, 65], BF16)
    vst_r = Res("vst")
    S.add("pool", lambda e: e.memset(vst[:, :, :, :], 1.0), writes=[vst_r])
    q_r, k_r, v_r = d["q_r"], d["k_r"], d["v_r"]

    def store_rows(dst, dres, row0, P, t, src, sres, key):
        S.add("sp", lambda e: e.dma_start(out=dst[row0:row0 + P, t * TS:(t + 1) * TS], in_=src),
              reads=[sres], writes=[dres], dma_key=key)

    def stage2_tile(t, hT, hTr):

        def proj_chunk(wb, wbr, jj):
            p, pr, _ = c.PS.next()
            for k in range(8):
                S.add("pe", (lambda e, k=k: e.matmul(
                    p[:, :], wb[:, k * 256 + jj * 128: k * 256 + jj * 128 + 128], hT[:, k * TS:(k + 1) * TS],
                    start=(k == 0), stop=(k == 7))), reads=[wbr, hTr], writes=[pr])
            return p, pr

        def simple_head_chunk(wb, wbr, jj, ones_idx, n, gcol, dst, dres, row0, rope):
            p, pr = proj_chunk(wb, wbr, jj)
            if rope:
                qn, qnr, _ = QN.next()
                rms_feat(c, [(p[:, :], pr)], 128, ones_idx, n, [gcol], [(qn[:, :], qnr)])
                sg, sgr, key = STG.next()
                rope_feat(c, qn[:, :], qnr, 128, C_RD, tabs[:, 2 * T + t * TS: 2 * T + (t + 1) * TS],
                          tabs[:, 3 * T + t * TS: 3 * T + (t + 1) * TS], tab_r, sg[:, :], sgr)
            else:
                sg, sgr, key = STG.next()
                rms_feat(c, [(p[:, :], pr)], 128, ones_idx, n, [gcol], [(sg[:, :], sgr)])
            store_rows(dst, dres, row0, 128, t, sg[:, :], sgr, key)

        b0 = bases_in[t]
        wb, wbr = ws.get(b0 + 0)
        for jj in range(2):
            simple_head_chunk(wb, wbr, jj, C_BD64, 64, G_NAQ, d["qT"], q_r, 0 + jj * 128, False)
        wb, wbr = ws.get(b0 + 1)
        for jj in range(2):
            simple_head_chunk(wb, wbr, jj, C_BD64, 64, G_NAK, d["kT"], k_r, 0 + jj * 128, False)
        wb, wbr = ws.get(b0 + 2)
        p0, p0r = proj_chunk(wb, wbr, 0)
        p1, p1r = proj_chunk(wb, wbr, 1)
        rms_feat(c, [(p0[:, :], p0r), (p1[:, :], p1r)], 128, C_ONES, 256, [G_QLAT, G_QLAT + 1],
                 [(cqn[:, 0:TS], cqn_r), (cqn[:, TS:2 * TS], cqn_r)])
        for h in range(4):
            p, pr, _ = c.PS.next()
            for k in range(2):
                S.add("pe", (lambda e, k=k, h=h, p=p: e.matmul(
                    p[0:96, :], wsm[:, WS_UQ + k * 384 + h * 96: WS_UQ + k * 384 + h * 96 + 96],
                    cqn[:, k * TS:(k + 1) * TS], start=(k == 0), stop=(k == 1))),
                    reads=[wsm_r, cqn_r], writes=[pr])
            qn, qnr, _ = QN.next()
            rms_feat(c, [(p[0:96, :], pr)], 96, C_ONES, 96, [G_MQ], [(qn[0:96, :], qnr)])
            sg, sgr, key = STG.next()
            rope_feat(c, qn[0:96, :], qnr, 96, C_RB, tabs[0:96, 0 * T + t * TS: 0 * T + (t + 1) * TS],
                      tabs[0:96, 1 * T + t * TS: 1 * T + (t + 1) * TS], tab_r, sg[0:96, :], sgr)
            store_rows(d["qT"], q_r, 256 + h * 96, 96, t, sg[0:96, :], sgr, key)
        wb, wbr = ws.get(b0 + 3)
        p, pr = proj_chunk(wb, wbr, 0)
        rms_feat(c, [(p[:, :], pr)], 128, C_ONES, 128, [G_KVLAT], [(ckvn[:, :], ckvn_r)])
        for h in range(4):
            p, pr, _ = c.PS.next()
            S.add("pe", (lambda e, h=h, p=p: e.matmul(
                p[0:96, :], wsm[:, WS_KN + h * 96: WS_KN + h * 96 + 96], ckvn[:, :], start=True, stop=False)),
                reads=[wsm_r, ckvn_r], writes=[pr])
            for k in range(8):
                S.add("pe", (lambda e, k=k, p=p: e.matmul(
                    p[0:96, :], wsm[:, WS_KR + k * 96: WS_KR + k * 96 + 96], hT[:, k * TS:(k + 1) * TS],
                    start=False, stop=(k == 7))), reads=[wsm_r, hTr], writes=[pr])
            qn, qnr, _ = QN.next()
            rms_feat(c, [(p[0:96, :], pr)], 96, C_ONES, 96, [G_MK], [(qn[0:96, :], qnr)])
            sg, sgr, key = STG.next()
            rope_feat(c, qn[0:96, :], qnr, 96, C_RB, tabs[0:96, 0 * T + t * TS: 0 * T + (t + 1) * TS],
                      tabs[0:96, 1 * T + t * TS: 1 * T + (t + 1) * TS], tab_r, sg[0:96, :], sgr)
            store_rows(d["kT"], k_r, 256 + h * 96, 96, t, sg[0:96, :], sgr, key)
        for s in range(4):
            p, pr, _ = c.PS.next()
            S.add("pe", (lambda e, s=s, p=p: e.matmul(
                p[:, 0:256], ckvn[:, s * 128:(s + 1) * 128], wsm[:, WS_V:WS_V + 256], start=True, stop=True)),
                reads=[wsm_r, ckvn_r], writes=[pr])
            S.add("act", (lambda e, s=s, p=p: e.activation(out=vst[:, s, 10:14, 0:64], in_=p[:, 0:256].rearrange("p (u d) -> p u d", d=64), func=AF.Copy)),
                  reads=[pr], writes=[vst_r])
        simple_head_chunk(wb, wbr, 1, C_BD64, 64, G_GK, d["kT"], k_r, 896, True)
        wb, wbr = ws.get(b0 + 4)
        for jj in range(2):
            simple_head_chunk(wb, wbr, jj, C_BD32, 32, G_DQ, d["qT"], q_r, 640 + jj * 128, False)
        wb, wbr = ws.get(b0 + 5)
        for jj in range(2):
            simple_head_chunk(wb, wbr, jj, C_BD32, 32, G_DK, d["kT"], k_r, 640 + jj * 128, False)
        wb, wbr = ws.get(b0 + 6)
        for jj in range(2):
            simple_head_chunk(wb, wbr, jj, C_BD64, 64, G_GQ, d["qT"], q_r, 896 + jj * 128, True)
        for vb in range(2):
            wv, wvr = ws.get(b0 + 7 + vb)
            for s in range(4):
                p, pr, _ = c.PS.next()
                for k in range(8):
                    S.add("pe", (lambda e, k=k, s=s, p=p, wv=wv: e.matmul(
                        p[:, 0:320], hT[:, k * TS + s * 128: k * TS + (s + 1) * 128], wv[:, k * 320:(k + 1) * 320],
                        start=(k == 0), stop=(k == 7))), reads=[wvr, hTr], writes=[pr])
                S.add("dve", (lambda e, s=s, p=p, vb=vb: e.tensor_copy(out=vst[:, s, vb * 5:(vb + 1) * 5, 0:64], in_=p[:, 0:320].rearrange("p (u d) -> p u d", d=64))),
                      reads=[pr], writes=[vst_r])
        for u in range(NVU):
            S.add("sp", (lambda e, u=u: e.dma_start(out=d["v"][u, :, t * 4:(t + 1) * 4, :], in_=vst[:, :, u, :])),
                  reads=[vst_r], writes=[v_r], dma_key="vst")

    for t in range(n2):
        hT, hTr = norm_x_tile(c, t, G_MIX)
        stage2_tile(t, hT, hTr)


def build_a():
    nc = bass.Bass("TRN2", target_bir_lowering=False)
    dt = lambda name, shape, dtype, kind: nc.dram_tensor(name, shape, dtype, kind=kind).ap()
    d = {}
    xT = dt("xT", [128, 8 * T], F32, "ExternalInput")
    gains = dt("gains", [128, NG], F32, "ExternalInput")
    lconst = dt("lconst", [128, 2], F32, "ExternalInput")
    cmat = dt("cmat", [128, 768], F32, "ExternalInput")
    d["wg"] = dt("wg", [11, 128, 2048], F32, "ExternalInput")
    d["wu"] = dt("wu", [11, 128, 2048], F32, "ExternalInput")
    d["wd"] = dt("wd", [8, 128, 2816], F32, "ExternalInput")
    d["winf"] = dt("winf", [7, 128, 2048], F32, "ExternalInput")
    d["winv"] = dt("winv", [2, 128, 2560], F32, "ExternalInput")
    d["wsm"] = dt("wsm", [128, WSM], F32, "ExternalInput")
    d["tabs"] = dt("tabs", [4, 128, T], F32, "ExternalInput")
    xo = dt("x1T", [128, 8 * T], F32, "ExternalOutput")
    d["qT"] = dt("qT", [QROWS, T], BF16, "ExternalOutput")
    d["kT"] = dt("kT", [KROWS, T], BF16, "ExternalOutput")
    d["v"] = dt("v", [NVU, 128, 16, 65], BF16, "ExternalOutput")
    d["q_r"], d["k_r"], d["v_r"] = Res("qT"), Res("kT"), Res("v")
    xo_r = Res("xo")
    with contextlib.ExitStack() as st:
        S = Sched(nc)
        c = Ctx()
        common_setup(c, st, nc, S, gains, lconst, cmat)
        load_x(c, st, nc, S, xT)
        emit_phase_a(c, st, nc, S, d)
        store_x(c, S, xo, xo_r)
        S.add("sp", None, reads=[xo_r, d["q_r"], d["k_r"], d["v_r"]])
        S.emit(st)
    return nc


def blk_cols(W, cpb):
    K, N = W.shape
    kc = K // 128
    nb = N // cpb
    return np.ascontiguousarray(W.reshape(kc, 128, nb, cpb).transpose(2, 1, 0, 3).reshape(nb, 128, kc * cpb))


def const_mats():
    m = np.zeros((6, 128, 128), np.float32)
    m[C_ONES] = 1.0
    for b in range(2):
        m[C_BD64, b * 64:(b + 1) * 64, b * 64:(b + 1) * 64] = 1.0
    for b in range(4):
        m[C_BD32, b * 32:(b + 1) * 32, b * 32:(b + 1) * 32] = 1.0
    for i in range(16):
        m[C_RB, 80 + i, 64 + i] = -1.0
        m[C_RB, 64 + i, 80 + i] = 1.0
    for hb in (0, 64):
        for sb in (0, 32):
            for i in range(16):
                a = hb + sb + i
                b = a + 16
                m[C_RD, b, a] = -1.0
                m[C_RD, a, b] = 1.0
    m[C_ID] = np.eye(128, dtype=np.float32)
    return np.ascontiguousarray(m.transpose(1, 0, 2).reshape(128, 768))


def rope_tables(core):
    r = core % 4
    t = np.arange(r * T, (r + 1) * T)
    inv = np.exp(-math.log(10000.0) * np.arange(0, 32, 2, dtype=np.float32) / 32).astype(np.float32)
    a_seq = t.astype(np.float32)[:, None] * inv[None, :]
    a_row = (t // 64).astype(np.float32)[:, None] * inv[None, :]
    a_col = (t % 64).astype(np.float32)[:, None] * inv[None, :]
    tabs = np.zeros((4, 128, T), np.float32)
    tabs[0, 0:64] = 1.0
    for p in range(64, 96):
        tabs[0, p] = np.cos(a_seq[:, (p - 64) % 16])
        tabs[1, p] = np.sin(a_seq[:, (p - 64) % 16])
    for p in range(128):
        f = p % 64
        a = a_row if f < 32 else a_col
        tabs[2, p] = np.cos(a[:, f % 16])
        tabs[3, p] = np.sin(a[:, f % 16])
    return tabs


def tile64(v):
    return np.tile(v, 128 // v.shape[0])


def layer_host(inp, l):
    f = lambda k: np.asarray(inp[k][l], np.float32)
    h = {}
    for nm in ("ffn1", "ffn2"):
        h[nm + "_wg"] = blk_cols(f(nm + "_w_gate"), 256)
        h[nm + "_wu"] = blk_cols(f(nm + "_w_up"), 256)
        h[nm + "_wd"] = blk_cols(f(nm + "_w_down"), 128)
    win = f("w_in")
    cols_f = np.concatenate([np.arange(0, 256), np.arange(256, 512), np.arange(768, 1024), np.arange(1024, 1152),
                             np.arange(2208, 2336), np.arange(1184, 1440), np.arange(1440, 1696), np.arange(1952, 2208)])
    cols_v = np.concatenate([np.arange(512, 768), np.arange(1696, 1952), np.arange(2336, 2464)])
    h["winf"] = blk_cols(win[:, cols_f], 256)
    h["winv"] = blk_cols(win[:, cols_v], 320)
    wsm = np.zeros((128, WSM), np.float32)
    wuq = f("mla_w_uq")
    wsm[:, WS_UQ:WS_UQ + 768] = wuq.reshape(2, 128, 384).transpose(1, 0, 2).reshape(128, 768)
    wukv = f("mla_w_ukv")
    for hh in range(4):
        wsm[:, WS_KN + hh * 96: WS_KN + hh * 96 + 64] = wukv[:, hh * 128: hh * 128 + 64]
        wsm[:, WS_V + hh * 64: WS_V + (hh + 1) * 64] = wukv[:, hh * 128 + 64: hh * 128 + 128]
    kr = win[:, 1152:1184].reshape(8, 128, 32)
    for k in range(8):
        wsm[:, WS_KR + k * 96 + 64: WS_KR + k * 96 + 96] = kr[k]
    h["wsm"] = wsm
    g = np.zeros((128, NG), np.float32)
    g[:, G_FFN1:G_FFN1 + 8] = f("ffn1_norm").reshape(8, 128).T
    g[:, G_MIX:G_MIX + 8] = f("mix_norm").reshape(8, 128).T
    g[:, G_NAQ] = tile64(f("na_q_norm"))
    g[:, G_NAK] = tile64(f("na_k_norm"))
    g[:, G_QLAT:G_QLAT + 2] = f("mla_q_lat_norm").reshape(2, 128).T
    g[:, G_KVLAT] = f("mla_kv_lat_norm")
    g[0:96, G_MQ] = f("mla_q_norm")
    g[0:96, G_MK] = f("mla_k_norm")
    g[:, G_DQ] = tile64(f("diff_q_norm"))
    g[:, G_DK] = tile64(f("diff_k_norm"))
    g[:, G_GQ] = tile64(f("gqa_q_norm"))
    g[:, G_GK] = tile64(f("gqa_k_norm"))
    betas = np.concatenate([f("na_beta"), f("mla_beta"), np.tile(f("diff_subln"), 4), f("gqa_beta")])
    g[:, G_BETA:G_BETA + 8] = betas.reshape(8, 128).T
    g[:, G_FFN2:G_FFN2 + 8] = f("ffn2_norm").reshape(8, 128).T
    g[:, G_FIN:G_FIN + 8] = f("final_norm").reshape(8, 128).T
    h["gains"] = g
    lc = np.zeros((128, 2), np.float32)
    lc[:, 0] = lambda_init(l)
    lc[:, 1] = 1.0 - lambda_init(l)
    h["lconst"] = lc
    return h


def emit_phase_c(c, st, nc, S, d):
    c.PS = Ring(st, nc, "ps", 8, [128, 512], F32, psum=True)
    c.SQ = Ring(st, nc, "sq", 3, [128, 512], BF16)
    c.RS = Ring(st, nc, "rs", 2, [128, 512], F32)
    c.TF = Ring(st, nc, "tf", 4, [128, 512], F32)
    c.HT = Ring(st, nc, "hT", 2, [128, 8 * TS], BF16)
    c.actT = _sb(st, nc, "actT", [128, 22 * TS], BF16)
    c.act_r = [Res("cact%d" % j) for j in range(22)]
    wring = Ring(st, nc, "wb", 6, [128, 2816], BF16)
    ws = WStream(S, wring, 4)
    YT = Ring(st, nc, "yt", 1, [128, 8 * TS], F32)
    yn = _sb(st, nc, "yn", [128, 8 * TS], BF16)
    yn_r = Res("yn")
    S.add("dve", lambda e: e.tensor_scalar(out=c.gains[:, G_BETA + 4:G_BETA + 6], in0=c.gains[:, G_BETA + 4:G_BETA + 6],
                                            scalar1=c.lconst[:, 1:2], scalar2=None, op0=ALU.mult),
          reads=[c.gains_r, c.lconst_r], writes=[c.gains_r])
    bases = []
    for t in range(NT):
        b0 = len(ws.items)
        for m in range(8):
            ws.plan(d["wout"][m], 1024)
        plan_ffn(ws, d["wg"], d["wu"], d["wd"])
        bases.append(b0)

    def tile_c(t, yt, ytr, key):
        for u in range(16):
            cc, hf = u // 2, u % 2
            S.add("sp", (lambda e, u=u, cc=cc, hf=hf: e.dma_start(out=yt[hf * 64:(hf + 1) * 64, cc * TS:(cc + 1) * TS],
                                                                  in_=d["y"][u, :, t * TS:(t + 1) * TS])),
                  reads=[d["y_r"]], writes=[ytr], dma_key=key)
        for g in range(4):
            ccs = [2 * g, 2 * g + 1]
            if g == 2:
                for cc in ccs:
                    rms_feat(c, [(yt[:, cc * TS:(cc + 1) * TS], ytr)], 128, C_BD64, 64, [G_BETA + cc],
                             [(yn[:, cc * TS:(cc + 1) * TS], yn_r)])
            else:
                rms_feat(c, [(yt[:, cc * TS:(cc + 1) * TS], ytr) for cc in ccs], 128, C_ONES, 256, [G_BETA + cc for cc in ccs],
                         [(yn[:, cc * TS:(cc + 1) * TS], yn_r) for cc in ccs])
        for m in range(8):
            wb, wbr = ws.get(bases[t] + m)
            p, pr, _ = c.PS.next()
            for k in range(8):
                S.add("pe", (lambda e, k=k, p=p, wb=wb: e.matmul(
                    p[:, :], wb[:, k * 128:(k + 1) * 128], yn[:, k * TS:(k + 1) * TS],
                    start=(k == 0), stop=(k == 7))), reads=[wbr, yn_r], writes=[pr])
            xs = c.xT[:, m * T + t * TS: m * T + (t + 1) * TS]
            S.add("dve", (lambda e, p=p, xs=xs: e.tensor_tensor(out=xs, in0=p[:, :], in1=xs, op=ALU.add)),
                  reads=[pr, c.x_r[t]], writes=[c.x_r[t]])
        hT, hTr = norm_x_tile(c, t, G_FFN2)
        ffn_tile(c, t, hT, hTr, ws, bases[t] + 8)
        srcs = [(c.xT[:, k * T + t * TS: k * T + (t + 1) * TS], c.x_r[t]) for k in range(8)]
        rms_feat(c, srcs, 128, C_ONES, D, [G_FIN + k for k in range(8)], srcs)

    for t in range(NT):
        yt, ytr, key = YT.next()
        tile_c(t, yt, ytr, key)


def build_c():
    nc = bass.Bass("TRN2", target_bir_lowering=False)
    dt = lambda name, shape, dtype, kind: nc.dram_tensor(name, shape, dtype, kind=kind).ap()
    d = {}
    xT = dt("xT", [128, 8 * T], F32, "ExternalInput")
    gains = dt("gains", [128, NG], F32, "ExternalInput")
    lconst = dt("lconst", [128, 2], F32, "ExternalInput")
    cmat = dt("cmat", [128, 768], F32, "ExternalInput")
    d["wg"] = dt("wg", [11, 128, 2048], F32, "ExternalInput")
    d["wu"] = dt("wu", [11, 128, 2048], F32, "ExternalInput")
    d["wd"] = dt("wd", [8, 128, 2816], F32, "ExternalInput")
    d["wout"] = dt("wout", [8, 128, 1024], F32, "ExternalInput")
    d["y"] = dt("y", [16, 64, T], F32, "ExternalInput")
    d["y_r"] = Res("y")
    xo = dt("xoT", [128, 8 * T], F32, "ExternalOutput")
    xo_r = Res("xo")
    with contextlib.ExitStack() as st:
        S = Sched(nc)
        c = Ctx()
        common_setup(c, st, nc, S, gains, lconst, cmat)
        load_x(c, st, nc, S, xT)
        emit_phase_c(c, st, nc, S, d)
        store_x(c, S, xo, xo_r)
        S.add("sp", None, reads=[xo_r])
        S.emit(st)
    return nc


LA = 2


def near_entries(kind, qb):
    ent = []
    if kind == "A":
        lo, hi, halo = 4 * qb - 2, 4 * qb + 5, 2
    else:
        lo, hi, halo = 4 * qb - 1, 4 * qb + 4, 1
    for lc in range(max(lo, 0), min(hi, 15) + 1):
        ent.append(("own", lc))
    if qb == 0:
        for j in range(4):
            for x in range(16 - halo, 16):
                ent.append(("gath", 16 * j + x))
    if qb == 3:
        for j in range(4):
            for x in range(halo):
                ent.append(("gath", 16 * j + x))
    return ent


def tile_ids():
    ids = {}
    n = 0
    for kind in ("A", "C"):
        for h in range(4):
            for qb in range(4):
                for i, _e in enumerate(near_entries(kind, qb)):
                    ids[(kind, h, qb, i)] = n
                    n += 1
    return ids, n


TILE_IDS, NTILES = tile_ids()


def emit_phase_b(c, st, nc, S, d):
    PSS = Ring(st, nc, "pss", 3, [128, 1024], F32, psum=True)
    PO = Ring(st, nc, "po", 1, [128, 512], F32, psum=True)
    PBC = Ring(st, nc, "pbc", 1, [128, 512], F32, psum=True)
    PT = Ring(st, nc, "pt", 4, [128, 1024], BF16)
    KT = Ring(st, nc, "kt", 2, [128, SEQ], BF16)
    VT = Ring(st, nc, "vt", 2, [128, 64, 65], BF16)
    KTO = Ring(st, nc, "kto", 2, [128, T], BF16)
    VTO = Ring(st, nc, "vto", 2, [128, 16, 65], BF16)
    QT = Ring(st, nc, "qt", 4, [128, T], BF16)
    BT = Ring(st, nc, "bt", 12, [128, 512], BF16)
    VS = Ring(st, nc, "vs", 2, [128, 64, 65], BF16)
    SC = Ring(st, nc, "sc", 2, [128, 64], F32)
    OSB = Ring(st, nc, "osb", 2, [128, 512], F32)
    ON = Ring(st, nc, "on", 4, [64, 512], F32)
    RR = Ring(st, nc, "rr", 2, [128, 512], F32)
    for ring_ in (KT, KTO):
        for buf_, res_, _k in ring_.bufs:
            S.add("pool", (lambda e, buf_=buf_: e.memset(buf_[:, :], 0.0)), writes=[res_])
    bts = WStream(S, BT, 8)
    t5c = _sb(st, nc, "t5c", [128, 1024], F32)
    t5c_r = Res("t5c")
    S.add("sp", lambda e: e.dma_start(out=t5c[:, :], in_=d["t5c"]), writes=[t5c_r], dma_key="t5c")
    onesf = _sb(st, nc, "onesf", [128, 64], F32)
    onesf_r = Res("onesf")
    S.add("dve", lambda e: e.memset(onesf[:, :], 1.0), writes=[onesf_r])
    dl = _sb(st, nc, "dl", [64, 128], F32)
    dl_r = Res("dl")
    S.add("sp", lambda e: e.dma_start(out=dl[:, :], in_=d["dlam"]), writes=[dl_r], dma_key="dl")
    lt = _sb(st, nc, "lt", [64, 72], F32)
    lt_r = Res("lt")
    S.add("dve", lambda e: e.tensor_tensor(out=lt[:, 0:32], in0=dl[:, 0:32], in1=dl[:, 32:64], op=ALU.mult), reads=[dl_r], writes=[lt_r])
    S.add("dve", lambda e: e.tensor_tensor(out=lt[:, 32:64], in0=dl[:, 64:96], in1=dl[:, 96:128], op=ALU.mult), reads=[dl_r, lt_r], writes=[lt_r])
    S.add("dve", lambda e: e.reduce_sum(lt[:, 64:65], lt[:, 0:32], axis=mybir.AxisListType.X), reads=[lt_r], writes=[lt_r])
    S.add("dve", lambda e: e.reduce_sum(lt[:, 65:66], lt[:, 32:64], axis=mybir.AxisListType.X), reads=[lt_r], writes=[lt_r])
    S.add("act", lambda e: e.activation(out=lt[:, 66:68], in_=lt[:, 64:66], func=AF.Exp), reads=[lt_r], writes=[lt_r])
    S.add("dve", lambda e: e.tensor_tensor(out=lt[:, 68:69], in0=lt[:, 67:68], in1=lt[:, 66:67], op=ALU.subtract), reads=[lt_r], writes=[lt_r])
    S.add("dve", lambda e: e.tensor_scalar(out=lt[:, 69:70], in0=lt[:, 68:69], scalar1=c.lconst[0:64, 0:1], scalar2=None, op0=ALU.subtract),
          reads=[lt_r, c.lconst_r], writes=[lt_r])
    nlam = lt[:, 69:70]
    ident = c.cmat[:, C_ID * 128:(C_ID + 1) * 128]
    y_r = d["y_r"]

    def attend(qt, qtr, qp0, dq, qb, chunks):
        O, Or, _ = PO.next()
        steps = []
        i = 0
        isc = lambda ch: ch[5] is not None and ch[5][0] == "const"
        while i < len(chunks):
            if i + 1 < len(chunks) and not isc(chunks[i]) and not isc(chunks[i + 1]):
                steps.append([chunks[i], chunks[i + 1]])
                i += 2
            else:
                steps.append([chunks[i]])
                i += 1
        n = len(steps)
        total = len(chunks)
        pend = []
        pvi = [0]
        for i in range(n + LA):
            if i < n:
                st_ = steps[i]
                s_, sr, _ = PSS.next()
                w = 512 * len(st_)
                for hf, (kt, ktr, vt, vtr, kc, mode) in enumerate(st_):
                    sl = s_[:, hf * 512:(hf + 1) * 512]
                    if mode is not None and mode[0] == "tile":
                        bt, btr = bts.get(mode[1])
                        S.add("pe", (lambda e, sl=sl, bt=bt: e.matmul(sl, ident, bt[:, :], start=True, stop=False)),
                              reads=[btr, c.cmat_r], writes=[sr])
                        first = False
                    else:
                        first = True
                    S.add("pe", (lambda e, sl=sl, kc=kc, first=first, kt=kt: e.matmul(
                        sl, kt[qp0:qp0 + dq, kc * 128:(kc + 1) * 128], qt[qp0:qp0 + dq, qb * TS:(qb + 1) * TS],
                        start=first, stop=True)), reads=[ktr, qtr], writes=[sr])
                p_, pr, _ = PT.next()
                mode = st_[0][5]
                if len(st_) == 1 and mode is not None and mode[0] == "const":
                    bcol = mode[1]
                    S.add("act", (lambda e, s_=s_, p_=p_, bcol=bcol: e.activation(out=p_[:, 0:512], in_=s_[:, 0:512], func=AF.Exp, bias=bcol)),
                          reads=[sr, t5c_r], writes=[pr])
                else:
                    S.add("act", (lambda e, s_=s_, p_=p_, w=w: e.activation(out=p_[:, 0:w], in_=s_[:, 0:w], func=AF.Exp)),
                          reads=[sr], writes=[pr])
                pend.append((st_, p_, pr))
            j = i - LA
            if j >= 0:
                st_, p_, pr = pend[j]
                for hf, (kt, ktr, vt, vtr, kc, mode) in enumerate(st_):
                    k_ = pvi[0]
                    pvi[0] += 1
                    S.add("pe", (lambda e, kc=kc, p_=p_, k_=k_, vt=vt, hf=hf: e.matmul(
                        O[0:65, :], vt[:, kc, :], p_[:, hf * 512:(hf + 1) * 512], start=(k_ == 0), stop=(k_ == total - 1))),
                        reads=[vtr, pr], writes=[Or])
        rr, rrr, _ = RR.next()
        osb, osbr, _ = OSB.next()
        S.add("dve", lambda e: e.tensor_copy(out=osb[0:65, :], in_=O[0:65, :]), reads=[Or], writes=[osbr])
        S.add("dve", lambda e: e.reciprocal(out=rr[64:65, :], in_=osb[64:65, :]), reads=[osbr], writes=[rrr])
        bc, bcr, _ = PBC.next()
        S.add("pe", lambda e: e.matmul(bc[0:64, :], onesf[64:65, 0:64], rr[64:65, :], start=True, stop=True),
              reads=[rrr, onesf_r], writes=[bcr])
        on, onr, _ = ON.next()
        S.add("dve", lambda e: e.tensor_tensor(out=on[:, :], in0=osb[0:64, :], in1=bc[0:64, :], op=ALU.mult),
              reads=[osbr, bcr], writes=[onr])
        return on, onr

    def store_y(u, qb, on, onr):
        S.add("sp", lambda e: e.dma_start(out=d["y"][u, :, qb * TS:(qb + 1) * TS], in_=on[:, :]),
              reads=[onr], writes=[y_r], dma_key="ystore")

    plan = []
    for h in range(4):
        plan.append(("A", h * 64, 64, h, h * 64, 64, [h]))
    for h in range(4):
        plan.append(("B", 256 + h * 96, 96, 10 + h, 256 + h * 96, 96, [4 + h]))
    for h in range(4):
        plan.append(("C", 640 + h * 64, 64, 4 + h, 640 + h * 64, 64, [8 + h]))
    for g in range(2):
        plan.append(("D", 896 + g * 64, 64, 8 + g, 896 + 2 * g * 64, 128, [12 + 2 * g, 13 + 2 * g]))
    import os
    units = [int(x) for x in os.environ.get("KB_UNITS", ",".join(str(i) for i in range(len(plan)))).split(",")]
    bt_plan = {}
    for ui in units:
        kind = plan[ui][0]
        if kind in ("A", "C"):
            h = ui % 4
            for qb in range(4):
                for m in range(2 if kind == "C" else 1):
                    for i, _e in enumerate(near_entries(kind, qb)):
                        bt_plan[(kind, h, qb, m, i)] = bts.plan(d["btl"][TILE_IDS[(kind, h, qb, i)]], 512)

    loaded = {}

    def load_unit(ui):
        kind, krow, kd, vu, qrow, qd, yus = plan[ui]
        kt, ktr, kkey = KT.next()
        vt, vtr, vkey = VT.next()
        qt, qtr, qkey = QT.next()
        for j in range(4):
            kr0 = kg_row(j, krow)
            S.add("sp", (lambda e, j=j, kr0=kr0: e.dma_start(out=kt[0:kd, j * T:(j + 1) * T], in_=d["kTg"][kr0:kr0 + kd, :])),
                  writes=[ktr], dma_key=kkey)
            if kind == "D":
                S.add("sp", (lambda e, j=j, kr0=kr0: e.dma_start(out=kt[64:128, j * T:(j + 1) * T], in_=d["kTg"][kr0:kr0 + kd, :])),
                      writes=[ktr], dma_key=kkey)
            vr0 = vg_row(j, vu)
            S.add("sp", (lambda e, j=j, vr0=vr0: e.dma_start(out=vt[:, 16 * j:16 * (j + 1), :],
                                                              in_=d["vg"][vr0:vr0 + 128, :].rearrange("p (c d) -> p c d", d=65))),
                  writes=[vtr], dma_key=vkey)
        if kind in ("A", "B"):
            pieces = [(0, qd, qrow)]
        elif kind == "C":
            pieces = [(0, 32, qrow), (32, 32, qrow + 32)]
        else:
            pieces = [(0, 64, qrow), (64, 64, qrow + 64)]
        qtiles = []
        for p0, pn, r0 in pieces:
            if qtiles or True:
                qt, qtr, qkey = (qt, qtr, qkey) if not qtiles else QT.next()
            S.add("pool", (lambda e, qt=qt: e.memset(qt[:, :], 0.0)), writes=[qtr])
            S.add("sp", (lambda e, qt=qt, p0=p0, pn=pn, r0=r0: e.dma_start(out=qt[p0:p0 + pn, :], in_=d["qT"][r0:r0 + pn, :])),
                  writes=[qtr], dma_key=qkey)
            qtiles.append((qt, qtr))
        own = None
        if kind in ("A", "C"):
            kto, ktor, kokey = KTO.next()
            vto, vtor, vokey = VTO.next()
            S.add("sp", lambda e: e.dma_start(out=kto[0:kd, :], in_=d["kT"][krow:krow + kd, :]), writes=[ktor], dma_key=kokey)
            S.add("sp", lambda e: e.dma_start(out=vto[:, :, :], in_=d["v"][vu]), writes=[vtor], dma_key=vokey)
            own = (kto, ktor, vto, vtor)
        loaded[ui] = (kt, ktr, vt, vtr, qtiles, own)

    load_unit(units[0])
    for n_, ui in enumerate(units):
        if n_ + 1 < len(units):
            load_unit(units[n_ + 1])
        kind, krow, kd, vu, qrow, qd, yus = plan[ui]
        kt, ktr, vt, vtr, qtiles, own = loaded.pop(ui)
        h = ui % 4
        dense = [(kt, ktr, vt, vtr, kc, None) for kc in range(64)]

        def near_list(kind, qb, m):
            out = []
            for i, (src, ch) in enumerate(near_entries(kind, qb)):
                mode = ("tile", bt_plan[(kind, h, qb, m, i)])
                if src == "own":
                    out.append((own[0], own[1], own[2], own[3], ch, mode))
                else:
                    out.append((kt, ktr, vt, vtr, ch, mode))
            return out

        for qb in range(4):
            if kind == "A":
                on, onr = attend(qtiles[0][0], qtiles[0][1], 0, 128, qb, near_list("A", qb, 0))
                store_y(yus[0], qb, on, onr)
            elif kind == "B":
                on, onr = attend(qtiles[0][0], qtiles[0][1], 0, 128, qb, dense)
                store_y(yus[0], qb, on, onr)
            elif kind == "D":
                for hh in range(2):
                    on, onr = attend(qtiles[hh][0], qtiles[hh][1], 0, 128, qb, dense)
                    store_y(yus[hh], qb, on, onr)
            else:
                sc, scr, _ = SC.next()
                S.add("act", (lambda e, sc=sc, qb=qb, h=h: e.activation(out=sc[:, :], in_=t5c[:, (h * 4 + qb) * 64:(h * 4 + qb + 1) * 64], func=AF.Exp)),
                      reads=[t5c_r], writes=[scr])
                vs, vsr, _ = VS.next()
                S.add("dve", (lambda e, sc=sc, vs=vs, vt=vt: e.tensor_tensor(
                    out=vs[:, :, :], in0=vt[:, :, :], in1=sc[:, :].unsqueeze(2).broadcast_to([128, 64, 65]), op=ALU.mult)),
                    reads=[vtr, scr], writes=[vsr])
                ons = []
                for m in range(2):
                    ch = [(kt, ktr, vs, vsr, kc, None) for kc in range(64)] + near_list("C", qb, m)
                    ons.append(attend(qtiles[m][0], qtiles[m][1], 0, 128, qb, ch))
                (on0, on0r), (on1, on1r) = ons
                yo, yor, _ = ON.next()
                S.add("dve", (lambda e, yo=yo, on0=on0, on1=on1: e.scalar_tensor_tensor(
                    out=yo[:, :], in0=on1[:, :], scalar=nlam, in1=on0[:, :], op0=ALU.mult, op1=ALU.add)),
                    reads=[on0r, on1r, lt_r], writes=[yor])
                store_y(yus[0], qb, yo, yor)


def build_b():
    nc = bass.Bass("TRN2", target_bir_lowering=False)
    dt = lambda name, shape, dtype, kind: nc.dram_tensor(name, shape, dtype, kind=kind).ap()
    d = {}
    gains = dt("gains", [128, NG], F32, "ExternalInput")
    lconst = dt("lconst", [128, 2], F32, "ExternalInput")
    cmat = dt("cmat", [128, 768], F32, "ExternalInput")
    d["qT"] = dt("qT", [QROWS, T], BF16, "ExternalInput")
    d["kT"] = dt("kT", [KROWS, T], BF16, "ExternalInput")
    d["v"] = dt("v", [NVU, 128, 16, 65], BF16, "ExternalInput")
    d["kTg"] = dt("kTg", [4 * KROWS, T], BF16, "ExternalInput")
    d["vg"] = dt("vg", [4 * NVU * 128, 16 * 65], BF16, "ExternalInput")
    d["btl"] = dt("btl", [NTILES, 128, 512], F32, "ExternalInput")
    d["t5c"] = dt("t5c", [128, 1024], F32, "ExternalInput")
    d["dlam"] = dt("dlam", [64, 128], F32, "ExternalInput")
    d["y"] = dt("y", [16, 64, T], F32, "ExternalOutput")
    d["y_r"] = Res("y")
    with contextlib.ExitStack() as st:
        S = Sched(nc)
        c = Ctx()
        common_setup(c, st, nc, S, gains, lconst, cmat)
        emit_phase_b(c, st, nc, S, d)
        S.add("sp", None, reads=[d["y_r"]])
        S.emit(st)
    return nc


def t5_bucket_np(rel):
    half = 16
    max_exact = 8
    n = np.abs(rel)
    large = max_exact + (np.log(np.maximum(n, 1).astype(np.float32) / max_exact)
                         / np.float32(math.log(128 / max_exact)) * (half - max_exact)).astype(np.int32)
    large = np.minimum(large, half - 1)
    return np.where(rel > 0, half, 0) + np.where(n < max_exact, n, large)


def bias_tiles_host(inp, l, core):
    r = core % 4
    rpb = np.asarray(inp["na_rpb"][l], np.float32)
    t5 = np.asarray(inp["t5_bias"], np.float32)
    btl = np.empty((NTILES, 128, 512), np.float32)
    kk = np.arange(128)[:, None]
    qq = np.arange(512)[None, :]
    NEG = np.float32(-1e30)
    for qb in range(4):
        qbg = 4 * r + qb
        qtok = qbg * 512 + qq
        qr, qc = qtok // 64, qtok % 64
        rs = np.clip(qr - 4, 0, 120)
        cs = np.clip(qc - 8, 0, 48)
        for i, (src, ch) in enumerate(near_entries("A", qb)):
            g = 16 * r + ch if src == "own" else ch
            ktok = g * 128 + kk
            kr, kcol = ktok // 64, ktok % 64
            valid = (kr >= rs) & (kr < rs + 8) & (kcol >= cs) & (kcol < cs + 16)
            dr = np.clip(kr - qr + 7, 0, 14)
            dc = np.clip(kcol - qc + 15, 0, 30)
            flat = np.where(valid, dr * 31 + dc, 15 * 31)
            for h in range(4):
                tab = np.concatenate([rpb[h].ravel(), np.array([NEG], np.float32)])
                btl[TILE_IDS[("A", h, qb, i)]] = tab[flat]
        for i, (src, ch) in enumerate(near_entries("C", qb)):
            g = 16 * r + ch if src == "own" else ch
            if 4 * qbg - 1 <= g <= 4 * qbg + 4:
                bk = t5_bucket_np((g * 128 + kk) - qtok)
                for h in range(4):
                    btl[TILE_IDS[("C", h, qb, i)]] = t5[:, h][bk]
            else:
                for h in range(4):
                    btl[TILE_IDS[("C", h, qb, i)]] = NEG
    t5c = np.zeros((128, 1024), np.float32)
    for qb in range(4):
        qbg = 4 * r + qb
        for kc in range(64):
            for h in range(4):
                if 4 * qbg - 1 <= kc <= 4 * qbg + 4:
                    val = NEG
                else:
                    val = t5[31, h] if kc > 4 * qbg + 3 else t5[15, h]
                t5c[:, (h * 4 + qb) * 64 + kc] = val
    return btl, t5c


_PROGS = {}


def _prog(name):
    if name not in _PROGS:
        _PROGS[name] = {"a": build_a, "b": build_b, "c": build_c, "f": build_fused}[name]()
    return _PROGS[name]


def to_featmajor(xc):
    return np.ascontiguousarray(xc.T.reshape(8, 128, T).transpose(1, 0, 2).reshape(128, 8 * T))


def from_featmajor(xT):
    return xT.reshape(128, 8, T).transpose(1, 0, 2).reshape(D, T).T


def kernel_unfused(**inp):
    x = np.asarray(inp["x"], np.float32).reshape(BATCH * SEQ, D)
    cm = const_mats()
    xTs = [to_featmajor(x[c * T:(c + 1) * T]) for c in range(NCORE)]
    tabs = [rope_tables(c) for c in range(NCORE)]
    cores = list(range(NCORE))
    for l in range(DEPTH):
        H = layer_host(inp, l)
        common = {"gains": H["gains"], "lconst": H["lconst"], "cmat": cm}
        maps = [dict(common, xT=xTs[c], wg=H["ffn1_wg"], wu=H["ffn1_wu"], wd=H["ffn1_wd"], winf=H["winf"],
                     winv=H["winv"], wsm=H["wsm"], tabs=tabs[c]) for c in cores]
        ra = run_bass_kernel_spmd(_prog("a"), maps, core_ids=cores).results
        maps = []
        dl = np.ascontiguousarray(np.broadcast_to(np.asarray(inp["diff_lambda"][l], np.float32).reshape(1, 128), (64, 128)))
        for c in cores:
            b = c // 4
            kTg = np.zeros((4 * KROWS, T), np.asarray(ra[0]["kT"]).dtype)
            vg = np.zeros((4 * NVU * 128, 1040), np.asarray(ra[0]["v"]).dtype)
            for j in range(4):
                kj = np.asarray(ra[4 * b + j]["kT"])
                vj = np.asarray(ra[4 * b + j]["v"]).reshape(NVU * 128, 1040)
                for a, bb in K_PARTS:
                    kTg[4 * a + j * (bb - a): 4 * a + (j + 1) * (bb - a)] = kj[a:bb]
                for a, bb in V_PARTS:
                    vg[(4 * a + j * (bb - a)) * 128: (4 * a + (j + 1) * (bb - a)) * 128] = vj[a * 128:bb * 128]
            btl, t5c = bias_tiles_host(inp, l, c)
            maps.append(dict(common, qT=np.asarray(ra[c]["qT"]), kT=np.asarray(ra[c]["kT"]), v=np.asarray(ra[c]["v"]),
                             kTg=kTg, vg=vg, btl=btl, t5c=t5c, dlam=dl))
        rb = run_bass_kernel_spmd(_prog("b"), maps, core_ids=cores).results
        wout = blk_cols(np.asarray(inp["w_out"][l], np.float32), 128)
        maps = [dict(common, xT=np.asarray(ra[c]["x1T"]), y=np.asarray(rb[c]["y"]), wg=H["ffn2_wg"], wu=H["ffn2_wu"],
                     wd=H["ffn2_wd"], wout=wout) for c in cores]
        rc = run_bass_kernel_spmd(_prog("c"), maps, core_ids=cores).results
        xTs = [np.asarray(rc[c]["xoT"]) for c in cores]
    out = np.concatenate([from_featmajor(xTs[c]) for c in cores], axis=0)
    return np.ascontiguousarray(out.reshape(BATCH, SEQ, D).astype(np.float32))


RG = [[0, 1, 2, 3], [4, 5, 6, 7]]
K_PARTS = [(0, 256), (256, 448), (448, 640), (640, 896), (896, 1024)]
V_PARTS = [(0, 3), (3, 6), (6, 9), (9, 12), (12, 14)]


def kg_row(j, row):
    for a, b in K_PARTS:
        if a <= row < b:
            return 4 * a + j * (b - a) + (row - a)
    raise AssertionError(row)


def vg_row(j, u):
    for a, b in V_PARTS:
        if a <= u < b:
            return (4 * a + j * (b - a) + (u - a)) * 128
    raise AssertionError(u)


def build_fused():
    nc = bass.Bass("TRN2", target_bir_lowering=False)
    dt = lambda name, shape, dtype, kind, **kw: nc.dram_tensor(name, shape, dtype, kind=kind, **kw).ap()
    xT_d = dt("xT", [128, 8 * T], F32, "ExternalInput")
    cmat_d = dt("cmat", [128, 768], F32, "ExternalInput")
    tabs_d = dt("tabs", [4, 128, T], F32, "ExternalInput")
    t5c_d = dt("t5c", [128, 1024], F32, "ExternalInput")
    xo_d = dt("xoT", [128, 8 * T], F32, "ExternalOutput")
    LD = []
    for l in range(DEPTH):
        sfx = "_%d" % l
        d = {}
        d["gains"] = dt("gains" + sfx, [128, NG], F32, "ExternalInput")
        d["lconst"] = dt("lconst" + sfx, [128, 2], F32, "ExternalInput")
        for nm in ("wg1", "wu1", "wg2", "wu2"):
            d[nm] = dt(nm + sfx, [11, 128, 2048], F32, "ExternalInput")
        for nm in ("wd1", "wd2"):
            d[nm] = dt(nm + sfx, [8, 128, 2816], F32, "ExternalInput")
        d["winf"] = dt("winf" + sfx, [7, 128, 2048], F32, "ExternalInput")
        d["winv"] = dt("winv" + sfx, [2, 128, 2560], F32, "ExternalInput")
        d["wsm"] = dt("wsm" + sfx, [128, WSM], F32, "ExternalInput")
        d["wout"] = dt("wout" + sfx, [8, 128, 1024], F32, "ExternalInput")
        d["btl"] = dt("btl" + sfx, [NTILES, 128, 512], F32, "ExternalInput")
        d["dlam"] = dt("dlam" + sfx, [64, 128], F32, "ExternalInput")
        d["qT"] = dt("s_qT" + sfx, [QROWS, T], BF16, "Internal")
        d["kT"] = dt("s_kT" + sfx, [KROWS, T], BF16, "Internal")
        d["v"] = dt("s_v" + sfx, [NVU, 128, 16, 65], BF16, "Internal")
        d["kTg"] = dt("s_kTg" + sfx, [4 * KROWS, T], BF16, "Internal", addr_space="Local")
        d["vg"] = dt("s_vg" + sfx, [4 * NVU * 128, 16 * 65], BF16, "Internal", addr_space="Local")
        d["y"] = dt("s_y" + sfx, [16, 64, T], F32, "Internal")
        LD.append(d)
    with contextlib.ExitStack() as top:
        ss = SemState(top)
        c = Ctx()
        c.xT = _sb(top, nc, "xT", [128, 8 * T], F32)

        def fresh_x():
            c.x_r = [Res("x%d" % t) for t in range(NT)]

        with contextlib.ExitStack() as st:
            S = Sched(nc, ss)
            fresh_x()
            for k in range(8):
                S.add("sp", (lambda e, k=k: e.dma_start(out=c.xT[:, k * T:(k + 1) * T], in_=xT_d[:, k * T:(k + 1) * T])),
                      writes=c.x_r, dma_key="xT")
            S.drain()
            S.emit(st)
        for l in range(DEPTH):
            d = LD[l]
            with contextlib.ExitStack() as st:
                S = Sched(nc, ss)
                fresh_x()
                common_setup(c, st, nc, S, d["gains"], d["lconst"], cmat_d)
                da = dict(wg=d["wg1"], wu=d["wu1"], wd=d["wd1"], winf=d["winf"], winv=d["winv"], wsm=d["wsm"], tabs=tabs_d,
                          qT=d["qT"], kT=d["kT"], v=d["v"], q_r=Res("q"), k_r=Res("k"), v_r=Res("v"))
                emit_phase_a(c, st, nc, S, da)
                S.drain()
                S.emit(st)
            with contextlib.ExitStack() as st:
                S = Sched(nc, ss)
                v2 = d["v"].rearrange("u p c d -> (u p) (c d)")
                for a, b in K_PARTS:
                    S.add("pool", (lambda e, d=d, a=a, b=b: e.collective_compute(
                        "AllGather", ALU.bypass, replica_groups=RG, ins=[d["kT"][a:b, :]], outs=[d["kTg"][4 * a:4 * b, :]])),
                        dma_key="cc", inc=1)
                for a, b in V_PARTS:
                    S.add("pool", (lambda e, d=d, a=a, b=b, v2=v2: e.collective_compute(
                        "AllGather", ALU.bypass, replica_groups=RG, ins=[v2[a * 128:b * 128, :]],
                        outs=[d["vg"][4 * a * 128:4 * b * 128, :]])), dma_key="cc", inc=1)
                S.drain()
                S.emit(st)
            with contextlib.ExitStack() as st:
                S = Sched(nc, ss)
                common_setup(c, st, nc, S, d["gains"], d["lconst"], cmat_d)
                db = dict(qT=d["qT"], kT=d["kT"], v=d["v"], kTg=d["kTg"], vg=d["vg"], btl=d["btl"], t5c=t5c_d,
                          dlam=d["dlam"], y=d["y"], y_r=Res("y"))
                emit_phase_b(c, st, nc, S, db)
                S.drain()
                S.emit(st)
            with contextlib.ExitStack() as st:
                S = Sched(nc, ss)
                fresh_x()
                common_setup(c, st, nc, S, d["gains"], d["lconst"], cmat_d)
                dc = dict(wg=d["wg2"], wu=d["wu2"], wd=d["wd2"], wout=d["wout"], y=d["y"], y_r=Res("y"))
                emit_phase_c(c, st, nc, S, dc)
                S.drain()
                S.emit(st)
        with contextlib.ExitStack() as st:
            S = Sched(nc, ss)
            fresh_x()
            xo_r = Res("xo")
            store_x(c, S, xo_d, xo_r)
            S.add("sp", None, reads=[xo_r])
            S.drain()
            S.emit(st)
    return nc


def kernel(**inp):
    x = np.asarray(inp["x"], np.float32).reshape(BATCH * SEQ, D)
    cm = const_mats()
    cores = list(range(NCORE))
    shared = {}
    for l in range(DEPTH):
        H = layer_host(inp, l)
        sfx = "_%d" % l
        shared.update({"gains" + sfx: H["gains"], "lconst" + sfx: H["lconst"], "wg1" + sfx: H["ffn1_wg"], "wu1" + sfx: H["ffn1_wu"],
                       "wd1" + sfx: H["ffn1_wd"], "wg2" + sfx: H["ffn2_wg"], "wu2" + sfx: H["ffn2_wu"], "wd2" + sfx: H["ffn2_wd"],
                       "winf" + sfx: H["winf"], "winv" + sfx: H["winv"], "wsm" + sfx: H["wsm"],
                       "wout" + sfx: blk_cols(np.asarray(inp["w_out"][l], np.float32), 128),
                       "dlam" + sfx: np.ascontiguousarray(np.broadcast_to(
                           np.asarray(inp["diff_lambda"][l], np.float32).reshape(1, 128), (64, 128)))})
    maps = []
    for c in cores:
        m = dict(shared, xT=to_featmajor(x[c * T:(c + 1) * T]), cmat=cm, tabs=rope_tables(c))
        for l in range(DEPTH):
            btl, t5c = bias_tiles_host(inp, l, c)
            m["btl_%d" % l] = btl
            m["t5c"] = t5c
        maps.append(m)
    res = run_bass_kernel_spmd(_prog("f"), maps, core_ids=cores).results
    out = np.concatenate([from_featmajor(np.asarray(res[c]["xoT"])) for c in cores], axis=0)
    return np.ascontiguousarray(out.reshape(BATCH, SEQ, D).astype(np.float32))
```

```python
import contextlib
import math
import numpy as np
import ml_dtypes
import concourse.bass as bass
import concourse.mybir as mybir
from concourse.bass_utils import run_bass_kernel_spmd

F32 = mybir.dt.float32
BF16 = mybir.dt.bfloat16
AF = mybir.ActivationFunctionType
ALU = mybir.AluOpType

_UID = [0]


def _uid():
    _UID[0] += 1
    return _UID[0]

ENGS = ("pe", "act", "dve", "pool", "sp")
EPOCH = 16000
SAME_ENGINE_SYNC = True


class Res:
    __slots__ = ("name", "writer", "readers")

    def __init__(self, name=""):
        self.name = name
        self.writer = None
        self.readers = {}


class Op:
    __slots__ = ("eng", "fn", "deps", "dma_key", "signaled", "count", "semid", "idx", "inc")


class SemState:
    def __init__(self, stack):
        self.stack = stack
        self.cnt = {e: 0 for e in ENGS}
        self.dcnt = {}
        self.sems = {}


class Sched:
    def __init__(self, nc, semstate=None):
        self.nc = nc
        self.ops = []
        self.by_eng = {e: [] for e in ENGS}
        self.semstate = semstate
        self.last_dma = {}

    def drain(self):
        op = self.add("sp", None)
        op.deps = list(self.last_dma.values())
        return op

    def add(self, eng, fn, reads=(), writes=(), dma_key=None, inc=16):
        op = Op()
        op.inc = inc
        op.eng = eng
        op.fn = fn
        op.dma_key = dma_key
        op.signaled = dma_key is not None
        op.count = 0
        op.semid = None
        op.idx = len(self.ops)
        deps = {}
        for r in reads:
            if r.writer is not None:
                deps[r.writer.idx] = r.writer
        for w in writes:
            if w.writer is not None:
                deps[w.writer.idx] = w.writer
            for rd in w.readers.values():
                deps[rd.idx] = rd
        dl = []
        for d in deps.values():
            if d.dma_key is not None or dma_key is not None:
                need = True
            elif d.eng == eng:
                need = (eng != "pe") and SAME_ENGINE_SYNC
            else:
                need = True
            if need:
                d.signaled = True
                dl.append(d)
        op.deps = dl
        for r in reads:
            key = eng if dma_key is None else ("dma", op.idx)
            r.readers[key] = op
        for w in writes:
            w.writer = op
            w.readers = {}
        self.ops.append(op)
        self.by_eng[eng].append(op)
        if dma_key is not None:
            self.last_dma[dma_key] = op
        return op

    def emit(self, stack):
        nc = self.nc
        ss = self.semstate if self.semstate is not None else SemState(stack)
        cnt = ss.cnt
        dcnt = ss.dcnt
        for op in self.ops:
            if op.dma_key is not None:
                dcnt[op.dma_key] = dcnt.get(op.dma_key, 0) + op.inc
                op.count = dcnt[op.dma_key]
                op.semid = ("d", op.dma_key)
            elif op.signaled:
                c = cnt[op.eng]
                cnt[op.eng] = c + 1
                op.semid = ("e", op.eng, c // EPOCH)
                op.count = (c % EPOCH) + 1
        self.check(ss)
        sems = ss.sems
        for op in self.ops:
            if op.semid is not None and op.semid not in sems:
                sems[op.semid] = ss.stack.enter_context(nc.semaphore("s%d" % len(sems)))
        self.nsems = len(sems)
        block = stack.enter_context(nc.Block())
        engobj = {"pe": block.tensor, "act": block.scalar, "dve": block.vector,
                  "pool": block.gpsimd, "sp": block.sync}

        def make(ename):
            ops = self.by_eng[ename]

            def body(e):
                waited = {}
                wepoch = {}
                for op in ops:
                    for d in op.deps:
                        sid = d.semid
                        c = d.count
                        if sid[0] == "e" and wepoch.get(sid[1], -1) > sid[2]:
                            continue
                        if waited.get(sid, 0) >= c:
                            continue
                        e.wait_ge(sems[sid], c)
                        waited[sid] = c
                        if sid[0] == "e":
                            wepoch[sid[1]] = max(wepoch.get(sid[1], -1), sid[2])
                    if op.fn is None:
                        continue
                    ins = op.fn(e)
                    if op.dma_key is not None and op.inc == 1:
                        ins.then_inc(sems[op.semid])
                    elif op.dma_key is not None:
                        ins.then_inc(sems[op.semid], 16)
                    elif op.signaled:
                        ins.then_inc(sems[op.semid], 1)
            return body

        for ename in ENGS:
            if self.by_eng[ename]:
                engobj[ename](make(ename))


def _sched_check(self, ss):
    val = dict(getattr(ss, "val", {}))
    pos = {e: 0 for e in ENGS}
    progress = True
    while progress:
        progress = False
        for e in ENGS:
            ops = self.by_eng[e]
            while pos[e] < len(ops):
                op = ops[pos[e]]
                if all(val.get(d.semid, 0) >= d.count and (d.semid[0] != "e" or True) for d in op.deps):
                    if op.semid is not None:
                        if op.semid[0] == "d":
                            val[op.semid] = val.get(op.semid, 0) + op.inc
                        else:
                            val[op.semid] = val.get(op.semid, 0) + 1
                        assert val[op.semid] == op.count or op.semid[0] == "d", (op.semid, val[op.semid], op.count)
                    pos[e] += 1
                    progress = True
                else:
                    break
    stuck = {e: pos[e] for e in ENGS if pos[e] < len(self.by_eng[e])}
    assert not stuck, "deadlock in schedule: %r" % stuck
    ss.val = val
    print("sched ok: ops=%d per-eng=%r" % (len(self.ops), {e: len(self.by_eng[e]) for e in ENGS}))


Sched.check = _sched_check


class Ring:
    def __init__(self, stack, nc, name, n, shape, dtype, psum=False):
        self.bufs = []
        for i in range(n):
            nm = "r_%s%d" % (name, i)
            tn = "%s_%d" % (nm, _uid())
            if psum:
                t = stack.enter_context(nc.psum_tensor(tn, shape, dtype))
            else:
                t = stack.enter_context(nc.sbuf_tensor(tn, shape, dtype))
            self.bufs.append((t, Res(nm), nm))
        self.i = 0

    def next(self):
        b = self.bufs[self.i % len(self.bufs)]
        self.i += 1
        return b


D = 1024
SEQ = 8192
BATCH = 2
DEPTH = 2
DFF = 2816
NCORE = 8
T = 2048
NT = 4
TS = 512
EPS = 1e-6
NG = 59
G_FFN1, G_MIX, G_NAQ, G_NAK, G_QLAT, G_KVLAT, G_MQ, G_MK, G_DQ, G_DK, G_GQ, G_GK = \
    0, 8, 16, 17, 18, 20, 21, 22, 23, 24, 25, 26
G_BETA, G_FFN2, G_FIN = 27, 43, 51
WS_UQ, WS_KN, WS_V, WS_KR = 0, 768, 1152, 1408
WSM = 2176
C_ONES, C_BD64, C_BD32, C_RB, C_RD, C_ID = 0, 1, 2, 3, 4, 5
QROWS = 1152
KROWS = 1024
NVU = 14


def lambda_init(l):
    return 0.8 - 0.6 * math.exp(-0.3 * l)


class Ctx:
    pass


def _sb(st, nc, name, shape, dt):
    return st.enter_context(nc.sbuf_tensor("sb_%s_%d" % (name, _uid()), shape, dt))


class WStream:
    def __init__(self, S, ring, pf):
        self.S = S
        self.ring = ring
        self.pf = pf
        self.items = []
        self.issued = 0
        self.handles = []

    def plan(self, src_ap, ncols, part=128):
        self.items.append((src_ap, ncols, part))
        return len(self.items) - 1

    def get(self, i):
        while self.issued < min(len(self.items), i + 1 + self.pf):
            src, ncols, part = self.items[self.issued]
            buf, res, nm = self.ring.next()
            self.S.add("pool", (lambda e, buf=buf, src=src, ncols=ncols, part=part:
                                e.dma_start(out=buf[0:part, 0:ncols], in_=src)),
                       writes=[res], dma_key=nm)
            self.handles.append((buf, res))
            self.issued += 1
        return self.handles[i]


def rms_feat(c, srcs, P, ones_idx, n, gain_cols, outs):
    S = c.S
    ones = c.cmat[0:P, ones_idx * 128: ones_idx * 128 + P]
    ps, psr, _ = c.PS.next()
    last = len(srcs) - 1
    sc = float(n) ** -0.5
    for i, (src, sres) in enumerate(srcs):
        sq, sqr, _ = c.SQ.next()
        S.add("act", (lambda e, sq=sq, src=src: e.activation(out=sq[0:P, :], in_=src, func=AF.Square, scale=sc)),
              reads=[sres], writes=[sqr])
        S.add("pe", (lambda e, sq=sq, i=i: e.matmul(ps[0:P, :], ones, sq[0:P, :], start=(i == 0), stop=(i == last))),
              reads=[sqr, c.cmat_r], writes=[psr])
    rs, rsr, _ = c.RS.next()
    S.add("act", lambda e: e.activation(out=rs[0:P, :], in_=ps[0:P, :], func=AF.Ln, bias=c.eps[0:P, 0:1]),
          reads=[psr, c.eps_r], writes=[rsr])
    S.add("act", lambda e: e.activation(out=rs[0:P, :], in_=rs[0:P, :], func=AF.Exp, scale=-0.5),
          reads=[rsr], writes=[rsr])
    for i, ((src, sres), (out, ores)) in enumerate(zip(srcs, outs)):
        g = c.gains[0:P, gain_cols[i]:gain_cols[i] + 1]
        S.add("dve", (lambda e, src=src, out=out, g=g: e.scalar_tensor_tensor(
            out=out, in0=src, scalar=g, in1=rs[0:P, :], op0=ALU.mult, op1=ALU.mult)),
            reads=[sres, rsr, c.gains_r], writes=[ores])


def rope_feat(c, qn, qnr, P, r_idx, cos_ap, sin_ap, tab_r, out, outr):
    S = c.S
    R = c.cmat[0:P, r_idx * 128: r_idx * 128 + P]
    ps, psr, _ = c.PS.next()
    S.add("pe", lambda e: e.matmul(ps[0:P, :], R, qn, start=True, stop=True), reads=[qnr, c.cmat_r], writes=[psr])
    t1, t1r, _ = c.TF.next()
    t2, t2r, _ = c.TF.next()
    S.add("pool", lambda e: e.tensor_tensor(out=t1[0:P, :], in0=qn, in1=cos_ap, op=ALU.mult), reads=[qnr, tab_r], writes=[t1r])
    S.add("dve", lambda e: e.tensor_tensor(out=t2[0:P, :], in0=ps[0:P, :], in1=sin_ap, op=ALU.mult), reads=[psr, tab_r], writes=[t2r])
    S.add("pool", lambda e: e.tensor_tensor(out=out, in0=t1[0:P, :], in1=t2[0:P, :], op=ALU.add), reads=[t1r, t2r], writes=[outr])


def norm_x_tile(c, t, gcol0):
    hT, hTr, _ = c.HT.next()
    srcs = [(c.xT[:, k * T + t * TS: k * T + (t + 1) * TS], c.x_r[t]) for k in range(8)]
    outs = [(hT[:, k * TS:(k + 1) * TS], hTr) for k in range(8)]
    rms_feat(c, srcs, 128, C_ONES, D, [gcol0 + k for k in range(8)], outs)
    return hT, hTr


def ffn_tile(c, t, hT, hTr, ws, base):
    S = c.S
    for b in range(11):
        wg, wgr = ws.get(base + 2 * b)
        wu, wur = ws.get(base + 2 * b + 1)
        for jj in range(2):
            j = 2 * b + jj
            pg, pgr, _ = c.PS.next()
            pu, pur, _ = c.PS.next()
            for k in range(8):
                S.add("pe", (lambda e, k=k, pg=pg, wg=wg, jj=jj: e.matmul(
                    pg[:, :], wg[:, k * 256 + jj * 128: k * 256 + jj * 128 + 128], hT[:, k * TS:(k + 1) * TS],
                    start=(k == 0), stop=(k == 7))), reads=[wgr, hTr], writes=[pgr])
            for k in range(8):
                S.add("pe", (lambda e, k=k, pu=pu, wu=wu, jj=jj: e.matmul(
                    pu[:, :], wu[:, k * 256 + jj * 128: k * 256 + jj * 128 + 128], hT[:, k * TS:(k + 1) * TS],
                    start=(k == 0), stop=(k == 7))), reads=[wur, hTr], writes=[pur])
            sg, sgr, _ = c.TF.next()
            S.add("act", (lambda e, sg=sg, pg=pg: e.activation(out=sg[:, :], in_=pg[:, :], func=AF.Silu)),
                  reads=[pgr], writes=[sgr])
            S.add("dve", (lambda e, sg=sg, pu=pu, j=j: e.tensor_tensor(
                out=c.actT[:, j * TS:(j + 1) * TS], in0=sg[:, :], in1=pu[:, :], op=ALU.mult)),
                reads=[sgr, pur], writes=[c.act_r[j]])
    for m in range(8):
        wd, wdr = ws.get(base + 22 + m)
        pd, pdr, _ = c.PS.next()
        for j in range(22):
            S.add("pe", (lambda e, j=j, pd=pd, wd=wd: e.matmul(
                pd[:, :], wd[:, j * 128:(j + 1) * 128], c.actT[:, j * TS:(j + 1) * TS],
                start=(j == 0), stop=(j == 21))), reads=[wdr, c.act_r[j]], writes=[pdr])
        xs = c.xT[:, m * T + t * TS: m * T + (t + 1) * TS]
        S.add("dve", (lambda e, pd=pd, xs=xs: e.scalar_tensor_tensor(
            out=xs, in0=pd[:, :], scalar=0.5, in1=xs, op0=ALU.mult, op1=ALU.add)),
            reads=[pdr, c.x_r[t]], writes=[c.x_r[t]])


def plan_ffn(ws, wg, wu, wd):
    base = len(ws.items)
    for b in range(11):
        ws.plan(wg[b], 2048)
        ws.plan(wu[b], 2048)
    for m in range(8):
        ws.plan(wd[m], 2816)
    return base


def common_setup(c, st, nc, S, gains_d, lconst_d, cmat_d):
    c.S = S
    c.nc = nc
    c.gains = _sb(st, nc, "gains", [128, NG], F32)
    c.gains_r = Res("gains")
    c.lconst = _sb(st, nc, "lconst", [128, 2], F32)
    c.lconst_r = Res("lconst")
    c.cmat = _sb(st, nc, "cmat", [128, 6 * 128], BF16)
    c.cmat_r = Res("cmat")
    c.eps = _sb(st, nc, "eps", [128, 1], F32)
    c.eps_r = Res("eps")
    S.add("sp", lambda e: e.dma_start(out=c.gains[:, :], in_=gains_d), writes=[c.gains_r], dma_key="gains")
    S.add("sp", lambda e: e.dma_start(out=c.lconst[:, :], in_=lconst_d), writes=[c.lconst_r], dma_key="lconst")
    S.add("pool", lambda e: e.dma_start(out=c.cmat[:, :], in_=cmat_d), writes=[c.cmat_r], dma_key="cmat")
    S.add("dve", lambda e: e.memset(c.eps[:, :], EPS), writes=[c.eps_r])


def load_x(c, st, nc, S, x_d):
    c.xT = _sb(st, nc, "xT", [128, 8 * T], F32)
    c.x_r = [Res("x%d" % t) for t in range(NT)]
    for k in range(8):
        S.add("sp", (lambda e, k=k: e.dma_start(out=c.xT[:, k * T:(k + 1) * T], in_=x_d[:, k * T:(k + 1) * T])),
              writes=c.x_r, dma_key="xT")


def store_x(c, S, xo_d, xo_r):
    for k in range(8):
        S.add("sp", (lambda e, k=k: e.dma_start(out=xo_d[:, k * T:(k + 1) * T], in_=c.xT[:, k * T:(k + 1) * T])),
              reads=c.x_r, writes=[xo_r], dma_key="xT")


def emit_phase_a(c, st, nc, S, d):
    c.PS = Ring(st, nc, "ps", 8, [128, 512], F32, psum=True)
    c.SQ = Ring(st, nc, "sq", 3, [128, 512], BF16)
    c.RS = Ring(st, nc, "rs", 2, [128, 512], F32)
    c.TF = Ring(st, nc, "tf", 4, [128, 512], F32)
    c.HT = Ring(st, nc, "hT", 2, [128, 8 * TS], BF16)
    c.actT = _sb(st, nc, "actT", [128, 22 * TS], BF16)
    c.act_r = [Res("act%d" % j) for j in range(22)]
    wring = Ring(st, nc, "wb", 6, [128, 2816], BF16)
    ws = WStream(S, wring, 4)
    wsm = _sb(st, nc, "wsm", [128, WSM], BF16)
    wsm_r = Res("wsm")
    S.add("pool", lambda e: e.dma_start(out=wsm[:, :], in_=d["wsm"]), writes=[wsm_r], dma_key="wsm")
    tabs = _sb(st, nc, "tabs", [128, 4 * T], F32)
    tab_r = Res("tabs")
    for i in range(4):
        S.add("sp", (lambda e, i=i: e.dma_start(out=tabs[:, i * T:(i + 1) * T], in_=d["tabs"][i])),
              writes=[tab_r], dma_key="tabs")
    for col, sc in ((G_NAQ, 64 ** -0.5), (G_MQ, 96 ** -0.5), (G_DQ, 32 ** -0.5), (G_GQ, 64 ** -0.5)):
        S.add("dve", (lambda e, col=col, sc=sc: e.tensor_scalar(
            out=c.gains[:, col:col + 1], in0=c.gains[:, col:col + 1], scalar1=float(sc), scalar2=None, op0=ALU.mult)),
            reads=[c.gains_r], writes=[c.gains_r])

    bases_ffn = [plan_ffn(ws, d["wg"], d["wu"], d["wd"]) for t in range(NT)]
    bases_in = []
    for t in range(NT):
        bases_in.append(len(ws.items))
        for b in range(7):
            ws.plan(d["winf"][b], 2048)
        for vb in range(2):
            ws.plan(d["winv"][vb], 2560)

    import os
    n1 = int(os.environ.get("KA_N1", NT))
    n2 = int(os.environ.get("KA_N2", NT))
    for t in range(n1):
        hT, hTr = norm_x_tile(c, t, G_FFN1)
        ffn_tile(c, t, hT, hTr, ws, bases_ffn[t])

    STG = Ring(st, nc, "stg", 6, [128, 512], BF16)
    QN = Ring(st, nc, "qn", 3, [128, 512], BF16)
    cqn = _sb(st, nc, "cqn", [128, 2 * TS], BF16)
    cqn_r = Res("cqn")
    ckvn = _sb(st, nc, "ckvn", [128, TS], BF16)
    ckvn_r = Res("ckvn")
    vst = _sb(st, nc, "vst", [128, 4, NVU, 65], BF16)
    vst_r = Res("vst")
    S.add("pool", lambda e: e.memset(vst[:, :, :, :], 1.0), writes=[vst_r])
    q_r, k_r, v_r = d["q_r"], d["k_r"], d["v_r"]

    def store_rows(dst, dres, row0, P, t, src, sres, key):
        S.add("sp", lambda e: e.dma_start(out=dst[row0:row0 + P, t * TS:(t + 1) * TS], in_=src),
              reads=[sres], writes=[dres], dma_key=key)

    def stage2_tile(t, hT, hTr):

        def proj_chunk(wb, wbr, jj):
            p, pr, _ = c.PS.next()
            for k in range(8):
                S.add("pe", (lambda e, k=k: e.matmul(
                    p[:, :], wb[:, k * 256 + jj * 128: k * 256 + jj * 128 + 128], hT[:, k * TS:(k + 1) * TS],
                    start=(k == 0), stop=(k == 7))), reads=[wbr, hTr], writes=[pr])
            return p, pr

        def simple_head_chunk(wb, wbr, jj, ones_idx, n, gcol, dst, dres, row0, rope):
            p, pr = proj_chunk(wb, wbr, jj)
            if rope:
                qn, qnr, _ = QN.next()
                rms_feat(c, [(p[:, :], pr)], 128, ones_idx, n, [gcol], [(qn[:, :], qnr)])
                sg, sgr, key = STG.next()
                rope_feat(c, qn[:, :], qnr, 128, C_RD, tabs[:, 2 * T + t * TS: 2 * T + (t + 1) * TS],
                          tabs[:, 3 * T + t * TS: 3 * T + (t + 1) * TS], tab_r, sg[:, :], sgr)
            else:
                sg, sgr, key = STG.next()
                rms_feat(c, [(p[:, :], pr)], 128, ones_idx, n, [gcol], [(sg[:, :], sgr)])
            store_rows(dst, dres, row0, 128, t, sg[:, :], sgr, key)

        b0 = bases_in[t]
        wb, wbr = ws.get(b0 + 0)
        for jj in range(2):
            simple_head_chunk(wb, wbr, jj, C_BD64, 64, G_NAQ, d["qT"], q_r, 0 + jj * 128, False)
        wb, wbr = ws.get(b0 + 1)
        for jj in range(2):
            simple_head_chunk(wb, wbr, jj, C_BD64, 64, G_NAK, d["kT"], k_r, 0 + jj * 128, False)
        wb, wbr = ws.get(b0 + 2)
        p0, p0r = proj_chunk(wb, wbr, 0)
        p1, p1r = proj_chunk(wb, wbr, 1)
        rms_feat(c, [(p0[:, :], p0r), (p1[:, :], p1r)], 128, C_ONES, 256, [G_QLAT, G_QLAT + 1],
                 [(cqn[:, 0:TS], cqn_r), (cqn[:, TS:2 * TS], cqn_r)])
        for h in range(4):
            p, pr, _ = c.PS.next()
            for k in range(2):
                S.add("pe", (lambda e, k=k, h=h, p=p: e.matmul(
                    p[0:96, :], wsm[:, WS_UQ + k * 384 + h * 96: WS_UQ + k * 384 + h * 96 + 96],
                    cqn[:, k * TS:(k + 1) * TS], start=(k == 0), stop=(k == 1))),
                    reads=[wsm_r, cqn_r], writes=[pr])
            qn, qnr, _ = QN.next()
            rms_feat(c, [(p[0:96, :], pr)], 96, C_ONES, 96, [G_MQ], [(qn[0:96, :], qnr)])
            sg, sgr, key = STG.next()
            rope_feat(c, qn[0:96, :], qnr, 96, C_RB, tabs[0:96, 0 * T + t * TS: 0 * T + (t + 1) * TS],
                      tabs[0:96, 1 * T + t * TS: 1 * T + (t + 1) * TS], tab_r, sg[0:96, :], sgr)
            store_rows(d["qT"], q_r, 256 + h * 96, 96, t, sg[0:96, :], sgr, key)
        wb, wbr = ws.get(b0 + 3)
        p, pr = proj_chunk(wb, wbr, 0)
        rms_feat(c, [(p[:, :], pr)], 128, C_ONES, 128, [G_KVLAT], [(ckvn[:, :], ckvn_r)])
        for h in range(4):
            p, pr, _ = c.PS.next()
            S.add("pe", (lambda e, h=h, p=p: e.matmul(
                p[0:96, :], wsm[:, WS_KN + h * 96: WS_KN + h * 96 + 96], ckvn[:, :], start=True, stop=False)),
                reads=[wsm_r, ckvn_r], writes=[pr])
            for k in range(8):
                S.add("pe", (lambda e, k=k, p=p: e.matmul(
                    p[0:96, :], wsm[:, WS_KR + k * 96: WS_KR + k * 96 + 96], hT[:, k * TS:(k + 1) * TS],
                    start=False, stop=(k == 7))), reads=[wsm_r, hTr], writes=[pr])
            qn, qnr, _ = QN.next()
            rms_feat(c, [(p[0:96, :], pr)], 96, C_ONES, 96, [G_MK], [(qn[0:96, :], qnr)])
            sg, sgr, key = STG.next()
            rope_feat(c, qn[0:96, :], qnr, 96, C_RB, tabs[0:96, 0 * T + t * TS: 0 * T + (t + 1) * TS],
                      tabs[0:96, 1 * T + t * TS: 1 * T + (t + 1) * TS], tab_r, sg[0:96, :], sgr)
            store_rows(d["kT"], k_r, 256 + h * 96, 96, t, sg[0:96, :], sgr, key)
        for s in range(4):
            p, pr, _ = c.PS.next()
            S.add("pe", (lambda e, s=s, p=p: e.matmul(
                p[:, 0:256], ckvn[:, s * 128:(s + 1) * 128], wsm[:, WS_V:WS_V + 256], start=True, stop=True)),
                reads=[wsm_r, ckvn_r], writes=[pr])
            S.add("act", (lambda e, s=s, p=p: e.activation(out=vst[:, s, 10:14, 0:64], in_=p[:, 0:256].rearrange("p (u d) -> p u d", d=64), func=AF.Copy)),
                  reads=[pr], writes=[vst_r])
        simple_head_chunk(wb, wbr, 1, C_BD64, 64, G_GK, d["kT"], k_r, 896, True)
        wb, wbr = ws.get(b0 + 4)
        for jj in range(2):
            simple_head_chunk(wb, wbr, jj, C_BD32, 32, G_DQ, d["qT"], q_r, 640 + jj * 128, False)
        wb, wbr = ws.get(b0 + 5)
        for jj in range(2):
            simple_head_chunk(wb, wbr, jj, C_BD32, 32, G_DK, d["kT"], k_r, 640 + jj * 128, False)
        wb, wbr = ws.get(b0 + 6)
        for jj in range(2):
            simple_head_chunk(wb, wbr, jj, C_BD64, 64, G_GQ, d["qT"], q_r, 896 + jj * 128, True)
        for vb in range(2):
            wv, wvr = ws.get(b0 + 7 + vb)
            for s in range(4):
                p, pr, _ = c.PS.next()
                for k in range(8):
                    S.add("pe", (lambda e, k=k, s=s, p=p, wv=wv: e.matmul(
                        p[:, 0:320], hT[:, k * TS + s * 128: k * TS + (s + 1) * 128], wv[:, k * 320:(k + 1) * 320],
                        start=(k == 0), stop=(k == 7))), reads=[wvr, hTr], writes=[pr])
                S.add("dve", (lambda e, s=s, p=p, vb=vb: e.tensor_copy(out=vst[:, s, vb * 5:(vb + 1) * 5, 0:64], in_=p[:, 0:320].rearrange("p (u d) -> p u d", d=64))),
                      reads=[pr], writes=[vst_r])
        for u in range(NVU):
            S.add("sp", (lambda e, u=u: e.dma_start(out=d["v"][u, :, t * 4:(t + 1) * 4, :], in_=vst[:, :, u, :])),
                  reads=[vst_r], writes=[v_r], dma_key="vst")

    for t in range(n2):
        hT, hTr = norm_x_tile(c, t, G_MIX)
        stage2_tile(t, hT, hTr)


def build_a():
    nc = bass.Bass("TRN2", target_bir_lowering=False)
    dt = lambda name, shape, dtype, kind: nc.dram_tensor(name, shape, dtype, kind=kind).ap()
    d = {}
    xT = dt("xT", [128, 8 * T], F32, "ExternalInput")
    gains = dt("gains", [128, NG], F32, "ExternalInput")
    lconst = dt("lconst", [128, 2], F32, "ExternalInput")
    cmat = dt("cmat", [128, 768], F32, "ExternalInput")
    d["wg"] = dt("wg", [11, 128, 2048], F32, "ExternalInput")
    d["wu"] = dt("wu", [11, 128, 2048], F32, "ExternalInput")
    d["wd"] = dt("wd", [8, 128, 2816], F32, "ExternalInput")
    d["winf"] = dt("winf", [7, 128, 2048], F32, "ExternalInput")
    d["winv"] = dt("winv", [2, 128, 2560], F32, "ExternalInput")
    d["wsm"] = dt("wsm", [128, WSM], F32, "ExternalInput")
    d["tabs"] = dt("tabs", [4, 128, T], F32, "ExternalInput")
    xo = dt("x1T", [128, 8 * T], F32, "ExternalOutput")
    d["qT"] = dt("qT", [QROWS, T], BF16, "ExternalOutput")
    d["kT"] = dt("kT", [KROWS, T], BF16, "ExternalOutput")
    d["v"] = dt("v", [NVU, 128, 16, 65], BF16, "ExternalOutput")
    d["q_r"], d["k_r"], d["v_r"] = Res("qT"), Res("kT"), Res("v")
    xo_r = Res("xo")
    with contextlib.ExitStack() as st:
        S = Sched(nc)
        c = Ctx()
        common_setup(c, st, nc, S, gains, lconst, cmat)
        load_x(c, st, nc, S, xT)
        emit_phase_a(c, st, nc, S, d)
        store_x(c, S, xo, xo_r)
        S.add("sp", None, reads=[xo_r, d["q_r"], d["k_r"], d["v_r"]])
        S.emit(st)
    return nc


def blk_cols(W, cpb):
    K, N = W.shape
    kc = K // 128
    nb = N // cpb
    return np.ascontiguousarray(W.reshape(kc, 128, nb, cpb).transpose(2, 1, 0, 3).reshape(nb, 128, kc * cpb))


def const_mats():
    m = np.zeros((6, 128, 128), np.float32)
    m[C_ONES] = 1.0
    for b in range(2):
        m[C_BD64, b * 64:(b + 1) * 64, b * 64:(b + 1) * 64] = 1.0
    for b in range(4):
        m[C_BD32, b * 32:(b + 1) * 32, b * 32:(b + 1) * 32] = 1.0
    for i in range(16):
        m[C_RB, 80 + i, 64 + i] = -1.0
        m[C_RB, 64 + i, 80 + i] = 1.0
    for hb in (0, 64):
        for sb in (0, 32):
            for i in range(16):
                a = hb + sb + i
                b = a + 16
                m[C_RD, b, a] = -1.0
                m[C_RD, a, b] = 1.0
    m[C_ID] = np.eye(128, dtype=np.float32)
    return np.ascontiguousarray(m.transpose(1, 0, 2).reshape(128, 768))


def rope_tables(core):
    r = core % 4
    t = np.arange(r * T, (r + 1) * T)
    inv = np.exp(-math.log(10000.0) * np.arange(0, 32, 2, dtype=np.float32) / 32).astype(np.float32)
    a_seq = t.astype(np.float32)[:, None] * inv[None, :]
    a_row = (t // 64).astype(np.float32)[:, None] * inv[None, :]
    a_col = (t % 64).astype(np.float32)[:, None] * inv[None, :]
    tabs = np.zeros((4, 128, T), np.float32)
    tabs[0, 0:64] = 1.0
    for p in range(64, 96):
        tabs[0, p] = np.cos(a_seq[:, (p - 64) % 16])
        tabs[1, p] = np.sin(a_seq[:, (p - 64) % 16])
    for p in range(128):
        f = p % 64
        a = a_row if f < 32 else a_col
        tabs[2, p] = np.cos(a[:, f % 16])
        tabs[3, p] = np.sin(a[:, f % 16])
    return tabs


def tile64(v):
    return np.tile(v, 128 // v.shape[0])


def layer_host(inp, l):
    f = lambda k: np.asarray(inp[k][l], np.float32)
    h = {}
    for nm in ("ffn1", "ffn2"):
        h[nm + "_wg"] = blk_cols(f(nm + "_w_gate"), 256)
        h[nm + "_wu"] = blk_cols(f(nm + "_w_up"), 256)
        h[nm + "_wd"] = blk_cols(f(nm + "_w_down"), 128)
    win = f("w_in")
    cols_f = np.concatenate([np.arange(0, 256), np.arange(256, 512), np.arange(768, 1024), np.arange(1024, 1152),
                             np.arange(2208, 2336), np.arange(1184, 1440), np.arange(1440, 1696), np.arange(1952, 2208)])
    cols_v = np.concatenate([np.arange(512, 768), np.arange(1696, 1952), np.arange(2336, 2464)])
    h["winf"] = blk_cols(win[:, cols_f], 256)
    h["winv"] = blk_cols(win[:, cols_v], 320)
    wsm = np.zeros((128, WSM), np.float32)
    wuq = f("mla_w_uq")
    wsm[:, WS_UQ:WS_UQ + 768] = wuq.reshape(2, 128, 384).transpose(1, 0, 2).reshape(128, 768)
    wukv = f("mla_w_ukv")
    for hh in range(4):
        wsm[:, WS_KN + hh * 96: WS_KN + hh * 96 + 64] = wukv[:, hh * 128: hh * 128 + 64]
        wsm[:, WS_V + hh * 64: WS_V + (hh + 1) * 64] = wukv[:, hh * 128 + 64: hh * 128 + 128]
    kr = win[:, 1152:1184].reshape(8, 128, 32)
    for k in range(8):
        wsm[:, WS_KR + k * 96 + 64: WS_KR + k * 96 + 96] = kr[k]
    h["wsm"] = wsm
    g = np.zeros((128, NG), np.float32)
    g[:, G_FFN1:G_FFN1 + 8] = f("ffn1_norm").reshape(8, 128).T
    g[:, G_MIX:G_MIX + 8] = f("mix_norm").reshape(8, 128).T
    g[:, G_NAQ] = tile64(f("na_q_norm"))
    g[:, G_NAK] = tile64(f("na_k_norm"))
    g[:, G_QLAT:G_QLAT + 2] = f("mla_q_lat_norm").reshape(2, 128).T
    g[:, G_KVLAT] = f("mla_kv_lat_norm")
    g[0:96, G_MQ] = f("mla_q_norm")
    g[0:96, G_MK] = f("mla_k_norm")
    g[:, G_DQ] = tile64(f("diff_q_norm"))
    g[:, G_DK] = tile64(f("diff_k_norm"))
    g[:, G_GQ] = tile64(f("gqa_q_norm"))
    g[:, G_GK] = tile64(f("gqa_k_norm"))
    betas = np.concatenate([f("na_beta"), f("mla_beta"), np.tile(f("diff_subln"), 4), f("gqa_beta")])
    g[:, G_BETA:G_BETA + 8] = betas.reshape(8, 128).T
    g[:, G_FFN2:G_FFN2 + 8] = f("ffn2_norm").reshape(8, 128).T
    g[:, G_FIN:G_FIN + 8] = f("final_norm").reshape(8, 128).T
    h["gains"] = g
    lc = np.zeros((128, 2), np.float32)
    lc[:, 0] = lambda_init(l)
    lc[:, 1] = 1.0 - lambda_init(l)
    h["lconst"] = lc
    return h


def emit_phase_c(c, st, nc, S, d):
    c.PS = Ring(st, nc, "ps", 8, [128, 512], F32, psum=True)
    c.SQ = Ring(st, nc, "sq", 3, [128, 512], BF16)
    c.RS = Ring(st, nc, "rs", 2, [128, 512], F32)
    c.TF = Ring(st, nc, "tf", 4, [128, 512], F32)
    c.HT = Ring(st, nc, "hT", 2, [128, 8 * TS], BF16)
    c.actT = _sb(st, nc, "actT", [128, 22 * TS], BF16)
    c.act_r = [Res("cact%d" % j) for j in range(22)]
    wring = Ring(st, nc, "wb", 6, [128, 2816], BF16)
    ws = WStream(S, wring, 4)
    YT = Ring(st, nc, "yt", 1, [128, 8 * TS], F32)
    yn = _sb(st, nc, "yn", [128, 8 * TS], BF16)
    yn_r = Res("yn")
    S.add("dve", lambda e: e.tensor_scalar(out=c.gains[:, G_BETA + 4:G_BETA + 6], in0=c.gains[:, G_BETA + 4:G_BETA + 6],
                                            scalar1=c.lconst[:, 1:2], scalar2=None, op0=ALU.mult),
          reads=[c.gains_r, c.lconst_r], writes=[c.gains_r])
    bases = []
    for t in range(NT):
        b0 = len(ws.items)
        for m in range(8):
            ws.plan(d["wout"][m], 1024)
        plan_ffn(ws, d["wg"], d["wu"], d["wd"])
        bases.append(b0)

    def tile_c(t, yt, ytr, key):
        for u in range(16):
            cc, hf = u // 2, u % 2
            S.add("sp", (lambda e, u=u, cc=cc, hf=hf: e.dma_start(out=yt[hf * 64:(hf + 1) * 64, cc * TS:(cc + 1) * TS],
                                                                  in_=d["y"][u, :, t * TS:(t + 1) * TS])),
                  reads=[d["y_r"]], writes=[ytr], dma_key=key)
        for g in range(4):
            ccs = [2 * g, 2 * g + 1]
            if g == 2:
                for cc in ccs:
                    rms_feat(c, [(yt[:, cc * TS:(cc + 1) * TS], ytr)], 128, C_BD64, 64, [G_BETA + cc],
                             [(yn[:, cc * TS:(cc + 1) * TS], yn_r)])
            else:
                rms_feat(c, [(yt[:, cc * TS:(cc + 1) * TS], ytr) for cc in ccs], 128, C_ONES, 256, [G_BETA + cc for cc in ccs],
                         [(yn[:, cc * TS:(cc + 1) * TS], yn_r) for cc in ccs])
        for m in range(8):
            wb, wbr = ws.get(bases[t] + m)
            p, pr, _ = c.PS.next()
            for k in range(8):
                S.add("pe", (lambda e, k=k, p=p, wb=wb: e.matmul(
                    p[:, :], wb[:, k * 128:(k + 1) * 128], yn[:, k * TS:(k + 1) * TS],
                    start=(k == 0), stop=(k == 7))), reads=[wbr, yn_r], writes=[pr])
            xs = c.xT[:, m * T + t * TS: m * T + (t + 1) * TS]
            S.add("dve", (lambda e, p=p, xs=xs: e.tensor_tensor(out=xs, in0=p[:, :], in1=xs, op=ALU.add)),
                  reads=[pr, c.x_r[t]], writes=[c.x_r[t]])
        hT, hTr = norm_x_tile(c, t, G_FFN2)
        ffn_tile(c, t, hT, hTr, ws, bases[t] + 8)
        srcs = [(c.xT[:, k * T + t * TS: k * T + (t + 1) * TS], c.x_r[t]) for k in range(8)]
        rms_feat(c, srcs, 128, C_ONES, D, [G_FIN + k for k in range(8)], srcs)

    for t in range(NT):
        yt, ytr, key = YT.next()
        tile_c(t, yt, ytr, key)


def build_c():
    nc = bass.Bass("TRN2", target_bir_lowering=False)
    dt = lambda name, shape, dtype, kind: nc.dram_tensor(name, shape, dtype, kind=kind).ap()
    d = {}
    xT = dt("xT", [128, 8 * T], F32, "ExternalInput")
    gains = dt("gains", [128, NG], F32, "ExternalInput")
    lconst = dt("lconst", [128, 2], F32, "ExternalInput")
    cmat = dt("cmat", [128, 768], F32, "ExternalInput")
    d["wg"] = dt("wg", [11, 128, 2048], F32, "ExternalInput")
    d["wu"] = dt("wu", [11, 128, 2048], F32, "ExternalInput")
    d["wd"] = dt("wd", [8, 128, 2816], F32, "ExternalInput")
    d["wout"] = dt("wout", [8, 128, 1024], F32, "ExternalInput")
    d["y"] = dt("y", [16, 64, T], F32, "ExternalInput")
    d["y_r"] = Res("y")
    xo = dt("xoT", [128, 8 * T], F32, "ExternalOutput")
    xo_r = Res("xo")
    with contextlib.ExitStack() as st:
        S = Sched(nc)
        c = Ctx()
        common_setup(c, st, nc, S, gains, lconst, cmat)
        load_x(c, st, nc, S, xT)
        emit_phase_c(c, st, nc, S, d)
        store_x(c, S, xo, xo_r)
        S.add("sp", None, reads=[xo_r])
        S.emit(st)
    return nc


LA = 2


def near_entries(kind, qb):
    ent = []
    if kind == "A":
        lo, hi, halo = 4 * qb - 2, 4 * qb + 5, 2
    else:
        lo, hi, halo = 4 * qb - 1, 4 * qb + 4, 1
    for lc in range(max(lo, 0), min(hi, 15) + 1):
        ent.append(("own", lc))
    if qb == 0:
        for j in range(4):
            for x in range(16 - halo, 16):
                ent.append(("gath", 16 * j + x))
    if qb == 3:
        for j in range(4):
            for x in range(halo):
                ent.append(("gath", 16 * j + x))
    return ent


def tile_ids():
    ids = {}
    n = 0
    for kind in ("A", "C"):
        for h in range(4):
            for qb in range(4):
                for i, _e in enumerate(near_entries(kind, qb)):
                    ids[(kind, h, qb, i)] = n
                    n += 1
    return ids, n


TILE_IDS, NTILES = tile_ids()


def emit_phase_b(c, st, nc, S, d):
    PSS = Ring(st, nc, "pss", 3, [128, 1024], F32, psum=True)
    PO = Ring(st, nc, "po", 1, [128, 512], F32, psum=True)
    PBC = Ring(st, nc, "pbc", 1, [128, 512], F32, psum=True)
    PT = Ring(st, nc, "pt", 4, [128, 1024], BF16)
    KT = Ring(st, nc, "kt", 2, [128, SEQ], BF16)
    VT = Ring(st, nc, "vt", 2, [128, 64, 65], BF16)
    KTO = Ring(st, nc, "kto", 2, [128, T], BF16)
    VTO = Ring(st, nc, "vto", 2, [128, 16, 65], BF16)
    QT = Ring(st, nc, "qt", 4, [128, T], BF16)
    BT = Ring(st, nc, "bt", 12, [128, 512], BF16)
    VS = Ring(st, nc, "vs", 2, [128, 64, 65], BF16)
    SC = Ring(st, nc, "sc", 2, [128, 64], F32)
    OSB = Ring(st, nc, "osb", 2, [128, 512], F32)
    ON = Ring(st, nc, "on", 4, [64, 512], F32)
    RR = Ring(st, nc, "rr", 2, [128, 512], F32)
    for ring_ in (KT, KTO):
        for buf_, res_, _k in ring_.bufs:
            S.add("pool", (lambda e, buf_=buf_: e.memset(buf_[:, :], 0.0)), writes=[res_])
    bts = WStream(S, BT, 8)
    t5c = _sb(st, nc, "t5c", [128, 1024], F32)
    t5c_r = Res("t5c")
    S.add("sp", lambda e: e.dma_start(out=t5c[:, :], in_=d["t5c"]), writes=[t5c_r], dma_key="t5c")
    onesf = _sb(st, nc, "onesf", [128, 64], F32)
    onesf_r = Res("onesf")
    S.add("dve", lambda e: e.memset(onesf[:, :], 1.0), writes=[onesf_r])
    dl = _sb(st, nc, "dl", [64, 128], F32)
    dl_r = Res("dl")
    S.add("sp", lambda e: e.dma_start(out=dl[:, :], in_=d["dlam"]), writes=[dl_r], dma_key="dl")
    lt = _sb(st, nc, "lt", [64, 72], F32)
    lt_r = Res("lt")
    S.add("dve", lambda e: e.tensor_tensor(out=lt[:, 0:32], in0=dl[:, 0:32], in1=dl[:, 32:64], op=ALU.mult), reads=[dl_r], writes=[lt_r])
    S.add("dve", lambda e: e.tensor_tensor(out=lt[:, 32:64], in0=dl[:, 64:96], in1=dl[:, 96:128], op=ALU.mult), reads=[dl_r, lt_r], writes=[lt_r])
    S.add("dve", lambda e: e.reduce_sum(lt[:, 64:65], lt[:, 0:32], axis=mybir.AxisListType.X), reads=[lt_r], writes=[lt_r])
    S.add("dve", lambda e: e.reduce_sum(lt[:, 65:66], lt[:, 32:64], axis=mybir.AxisListType.X), reads=[lt_r], writes=[lt_r])
    S.add("act", lambda e: e.activation(out=lt[:, 66:68], in_=lt[:, 64:66], func=AF.Exp), reads=[lt_r], writes=[lt_r])
    S.add("dve", lambda e: e.tensor_tensor(out=lt[:, 68:69], in0=lt[:, 67:68], in1=lt[:, 66:67], op=ALU.subtract), reads=[lt_r], writes=[lt_r])
    S.add("dve", lambda e: e.tensor_scalar(out=lt[:, 69:70], in0=lt[:, 68:69], scalar1=c.lconst[0:64, 0:1], scalar2=None, op0=ALU.subtract),
          reads=[lt_r, c.lconst_r], writes=[lt_r])
    nlam = lt[:, 69:70]
    ident = c.cmat[:, C_ID * 128:(C_ID + 1) * 128]
    y_r = d["y_r"]

    pending = []

    def attend(qt, qtr, qp0, dq, qb, chunks, cont):
        O, Or, _ = PO.next()
        steps = []
        i = 0
        isc = lambda ch: ch[5] is not None and ch[5][0] == "const"
        while i < len(chunks):
            if i + 1 < len(chunks) and not isc(chunks[i]) and not isc(chunks[i + 1]):
                steps.append([chunks[i], chunks[i + 1]])
                i += 2
            else:
                steps.append([chunks[i]])
                i += 1
        n = len(steps)
        total = len(chunks)
        pend = []
        pvi = [0]
        for i in range(n + LA):
            if i < n:
                st_ = steps[i]
                s_, sr, _ = PSS.next()
                w = 512 * len(st_)
                for hf, (kt, ktr, vt, vtr, kc, mode) in enumerate(st_):
                    sl = s_[:, hf * 512:(hf + 1) * 512]
                    if mode is not None and mode[0] == "tile":
                        bt, btr = bts.get(mode[1])
                        S.add("pe", (lambda e, sl=sl, bt=bt: e.matmul(sl, ident, bt[:, :], start=True, stop=False)),
                              reads=[btr, c.cmat_r], writes=[sr])
                        first = False
                    else:
                        first = True
                    S.add("pe", (lambda e, sl=sl, kc=kc, first=first, kt=kt: e.matmul(
                        sl, kt[qp0:qp0 + dq, kc * 128:(kc + 1) * 128], qt[qp0:qp0 + dq, qb * TS:(qb + 1) * TS],
                        start=first, stop=True)), reads=[ktr, qtr], writes=[sr])
                p_, pr, _ = PT.next()
                mode = st_[0][5]
                if len(st_) == 1 and mode is not None and mode[0] == "const":
                    bcol = mode[1]
                    S.add("act", (lambda e, s_=s_, p_=p_, bcol=bcol: e.activation(out=p_[:, 0:512], in_=s_[:, 0:512], func=AF.Exp, bias=bcol)),
                          reads=[sr, t5c_r], writes=[pr])
                else:
                    S.add("act", (lambda e, s_=s_, p_=p_, w=w: e.activation(out=p_[:, 0:w], in_=s_[:, 0:w], func=AF.Exp)),
                          reads=[sr], writes=[pr])
                pend.append((st_, p_, pr))
            j = i - LA
            if j >= 0:
                st_, p_, pr = pend[j]
                for hf, (kt, ktr, vt, vtr, kc, mode) in enumerate(st_):
                    k_ = pvi[0]
                    pvi[0] += 1
                    S.add("pe", (lambda e, kc=kc, p_=p_, k_=k_, vt=vt, hf=hf: e.matmul(
                        O[0:65, :], vt[:, kc, :], p_[:, hf * 512:(hf + 1) * 512], start=(k_ == 0), stop=(k_ == total - 1))),
                        reads=[vtr, pr], writes=[Or])
            if i == min(3, n + LA - 1) and pending:
                pending.pop(0)()
        osb, osbr, _ = OSB.next()
        S.add("dve", lambda e: e.tensor_copy(out=osb[0:65, :], in_=O[0:65, :]), reads=[Or], writes=[osbr])

        def fin():
            rr, rrr, _ = RR.next()
            S.add("dve", lambda e: e.reciprocal(out=rr[64:65, :], in_=osb[64:65, :]), reads=[osbr], writes=[rrr])
            bc, bcr, _ = PBC.next()
            S.add("pe", lambda e: e.matmul(bc[0:64, :], onesf[64:65, 0:64], rr[64:65, :], start=True, stop=True),
                  reads=[rrr, onesf_r], writes=[bcr])
            on, onr, _ = ON.next()
            S.add("dve", lambda e: e.tensor_tensor(out=on[:, :], in0=osb[0:64, :], in1=bc[0:64, :], op=ALU.mult),
                  reads=[osbr, bcr], writes=[onr])
            cont(on, onr)
        pending.append(fin)

    def store_y(u, qb, on, onr):
        S.add("sp", lambda e: e.dma_start(out=d["y"][u, :, qb * TS:(qb + 1) * TS], in_=on[:, :]),
              reads=[onr], writes=[y_r], dma_key="ystore")

    plan = []
    for h in range(4):
        plan.append(("A", h * 64, 64, h, h * 64, 64, [h]))
    for h in range(4):
        plan.append(("B", 256 + h * 96, 96, 10 + h, 256 + h * 96, 96, [4 + h]))
    for h in range(4):
        plan.append(("C", 640 + h * 64, 64, 4 + h, 640 + h * 64, 64, [8 + h]))
    for g in range(2):
        plan.append(("D", 896 + g * 64, 64, 8 + g, 896 + 2 * g * 64, 128, [12 + 2 * g, 13 + 2 * g]))
    import os
    units = [int(x) for x in os.environ.get("KB_UNITS", ",".join(str(i) for i in range(len(plan)))).split(",")]
    bt_plan = {}
    for ui in units:
        kind = plan[ui][0]
        if kind in ("A", "C"):
            h = ui % 4
            for qb in range(4):
                for m in range(2 if kind == "C" else 1):
                    for i, _e in enumerate(near_entries(kind, qb)):
                        bt_plan[(kind, h, qb, m, i)] = bts.plan(d["btl"][TILE_IDS[(kind, h, qb, i)]], 512)

    loaded = {}

    def load_unit(ui):
        kind, krow, kd, vu, qrow, qd, yus = plan[ui]
        kt, ktr, kkey = KT.next()
        vt, vtr, vkey = VT.next()
        qt, qtr, qkey = QT.next()
        for j in range(4):
            kr0 = kg_row(j, krow)
            S.add("sp", (lambda e, j=j, kr0=kr0: e.dma_start(out=kt[0:kd, j * T:(j + 1) * T], in_=d["kTg"][kr0:kr0 + kd, :])),
                  writes=[ktr], dma_key=kkey)
            if kind == "D":
                S.add("sp", (lambda e, j=j, kr0=kr0: e.dma_start(out=kt[64:128, j * T:(j + 1) * T], in_=d["kTg"][kr0:kr0 + kd, :])),
                      writes=[ktr], dma_key=kkey)
            vr0 = vg_row(j, vu)
            S.add("sp", (lambda e, j=j, vr0=vr0: e.dma_start(out=vt[:, 16 * j:16 * (j + 1), :],
                                                              in_=d["vg"][vr0:vr0 + 128, :].rearrange("p (c d) -> p c d", d=65))),
                  writes=[vtr], dma_key=vkey)
        if kind in ("A", "B"):
            pieces = [(0, qd, qrow)]
        elif kind == "C":
            pieces = [(0, 32, qrow), (32, 32, qrow + 32)]
        else:
            pieces = [(0, 64, qrow), (64, 64, qrow + 64)]
        qtiles = []
        for p0, pn, r0 in pieces:
            if qtiles or True:
                qt, qtr, qkey = (qt, qtr, qkey) if not qtiles else QT.next()
            S.add("pool", (lambda e, qt=qt: e.memset(qt[:, :], 0.0)), writes=[qtr])
            S.add("sp", (lambda e, qt=qt, p0=p0, pn=pn, r0=r0: e.dma_start(out=qt[p0:p0 + pn, :], in_=d["qT"][r0:r0 + pn, :])),
                  writes=[qtr], dma_key=qkey)
            qtiles.append((qt, qtr))
        own = None
        if kind in ("A", "C"):
            kto, ktor, kokey = KTO.next()
            vto, vtor, vokey = VTO.next()
            S.add("sp", lambda e: e.dma_start(out=kto[0:kd, :], in_=d["kT"][krow:krow + kd, :]), writes=[ktor], dma_key=kokey)
            S.add("sp", lambda e: e.dma_start(out=vto[:, :, :], in_=d["v"][vu]), writes=[vtor], dma_key=vokey)
            own = (kto, ktor, vto, vtor)
        loaded[ui] = (kt, ktr, vt, vtr, qtiles, own)

    load_unit(units[0])
    for n_, ui in enumerate(units):
        if n_ + 1 < len(units):
            load_unit(units[n_ + 1])
        kind, krow, kd, vu, qrow, qd, yus = plan[ui]
        kt, ktr, vt, vtr, qtiles, own = loaded.pop(ui)
        h = ui % 4
        dense = [(kt, ktr, vt, vtr, kc, None) for kc in range(64)]

        def near_list(kind, qb, m):
            out = []
            for i, (src, ch) in enumerate(near_entries(kind, qb)):
                mode = ("tile", bt_plan[(kind, h, qb, m, i)])
                if src == "own":
                    out.append((own[0], own[1], own[2], own[3], ch, mode))
                else:
                    out.append((kt, ktr, vt, vtr, ch, mode))
            return out

        for qb in range(4):
            if kind == "A":
                attend(qtiles[0][0], qtiles[0][1], 0, 128, qb, near_list("A", qb, 0),
                       (lambda on, onr, u=yus[0], qb=qb: store_y(u, qb, on, onr)))
            elif kind == "B":
                attend(qtiles[0][0], qtiles[0][1], 0, 128, qb, dense,
                       (lambda on, onr, u=yus[0], qb=qb: store_y(u, qb, on, onr)))
            elif kind == "D":
                for hh in range(2):
                    attend(qtiles[hh][0], qtiles[hh][1], 0, 128, qb, dense,
                           (lambda on, onr, u=yus[hh], qb=qb: store_y(u, qb, on, onr)))
            else:
                sc, scr, _ = SC.next()
                S.add("act", (lambda e, sc=sc, qb=qb, h=h: e.activation(out=sc[:, :], in_=t5c[:, (h * 4 + qb) * 64:(h * 4 + qb + 1) * 64], func=AF.Exp)),
                      reads=[t5c_r], writes=[scr])
                vs, vsr, _ = VS.next()
                S.add("dve", (lambda e, sc=sc, vs=vs, vt=vt: e.tensor_tensor(
                    out=vs[:, :, :], in0=vt[:, :, :], in1=sc[:, :].unsqueeze(2).broadcast_to([128, 64, 65]), op=ALU.mult)),
                    reads=[vtr, scr], writes=[vsr])
                state = {}

                def cont0(on, onr, state=state):
                    state["on0"] = (on, onr)

                def cont1(on1, on1r, state=state, u=yus[0], qb=qb):
                    on0, on0r = state["on0"]
                    yo, yor, _ = ON.next()
                    S.add("dve", (lambda e, yo=yo, on0=on0, on1=on1: e.scalar_tensor_tensor(
                        out=yo[:, :], in0=on1[:, :], scalar=nlam, in1=on0[:, :], op0=ALU.mult, op1=ALU.add)),
                        reads=[on0r, on1r, lt_r], writes=[yor])
                    store_y(u, qb, yo, yor)
                for m in range(2):
                    ch = [(kt, ktr, vs, vsr, kc, None) for kc in range(64)] + near_list("C", qb, m)
                    attend(qtiles[m][0], qtiles[m][1], 0, 128, qb, ch, cont0 if m == 0 else cont1)
    while pending:
        pending.pop(0)()


def build_b():
    nc = bass.Bass("TRN2", target_bir_lowering=False)
    dt = lambda name, shape, dtype, kind: nc.dram_tensor(name, shape, dtype, kind=kind).ap()
    d = {}
    gains = dt("gains", [128, NG], F32, "ExternalInput")
    lconst = dt("lconst", [128, 2], F32, "ExternalInput")
    cmat = dt("cmat", [128, 768], F32, "ExternalInput")
    d["qT"] = dt("qT", [QROWS, T], BF16, "ExternalInput")
    d["kT"] = dt("kT", [KROWS, T], BF16, "ExternalInput")
    d["v"] = dt("v", [NVU, 128, 16, 65], BF16, "ExternalInput")
    d["kTg"] = dt("kTg", [4 * KROWS, T], BF16, "ExternalInput")
    d["vg"] = dt("vg", [4 * NVU * 128, 16 * 65], BF16, "ExternalInput")
    d["btl"] = dt("btl", [NTILES, 128, 512], F32, "ExternalInput")
    d["t5c"] = dt("t5c", [128, 1024], F32, "ExternalInput")
    d["dlam"] = dt("dlam", [64, 128], F32, "ExternalInput")
    d["y"] = dt("y", [16, 64, T], F32, "ExternalOutput")
    d["y_r"] = Res("y")
    with contextlib.ExitStack() as st:
        S = Sched(nc)
        c = Ctx()
        common_setup(c, st, nc, S, gains, lconst, cmat)
        emit_phase_b(c, st, nc, S, d)
        S.add("sp", None, reads=[d["y_r"]])
        S.emit(st)
    return nc


def t5_bucket_np(rel):
    half = 16
    max_exact = 8
    n = np.abs(rel)
    large = max_exact + (np.log(np.maximum(n, 1).astype(np.float32) / max_exact)
                         / np.float32(math.log(128 / max_exact)) * (half - max_exact)).astype(np.int32)
    large = np.minimum(large, half - 1)
    return np.where(rel > 0, half, 0) + np.where(n < max_exact, n, large)


def bias_tiles_host(inp, l, core):
    r = core % 4
    rpb = np.asarray(inp["na_rpb"][l], np.float32)
    t5 = np.asarray(inp["t5_bias"], np.float32)
    btl = np.empty((NTILES, 128, 512), np.float32)
    kk = np.arange(128)[:, None]
    qq = np.arange(512)[None, :]
    NEG = np.float32(-1e30)
    for qb in range(4):
        qbg = 4 * r + qb
        qtok = qbg * 512 + qq
        qr, qc = qtok // 64, qtok % 64
        rs = np.clip(qr - 4, 0, 120)
        cs = np.clip(qc - 8, 0, 48)
        for i, (src, ch) in enumerate(near_entries("A", qb)):
            g = 16 * r + ch if src == "own" else ch
            ktok = g * 128 + kk
            kr, kcol = ktok // 64, ktok % 64
            valid = (kr >= rs) & (kr < rs + 8) & (kcol >= cs) & (kcol < cs + 16)
            dr = np.clip(kr - qr + 7, 0, 14)
            dc = np.clip(kcol - qc + 15, 0, 30)
            flat = np.where(valid, dr * 31 + dc, 15 * 31)
            for h in range(4):
                tab = np.concatenate([rpb[h].ravel(), np.array([NEG], np.float32)])
                btl[TILE_IDS[("A", h, qb, i)]] = tab[flat]
        for i, (src, ch) in enumerate(near_entries("C", qb)):
            g = 16 * r + ch if src == "own" else ch
            if 4 * qbg - 1 <= g <= 4 * qbg + 4:
                bk = t5_bucket_np((g * 128 + kk) - qtok)
                for h in range(4):
                    btl[TILE_IDS[("C", h, qb, i)]] = t5[:, h][bk]
            else:
                for h in range(4):
                    btl[TILE_IDS[("C", h, qb, i)]] = NEG
    t5c = np.zeros((128, 1024), np.float32)
    for qb in range(4):
        qbg = 4 * r + qb
        for kc in range(64):
            for h in range(4):
                if 4 * qbg - 1 <= kc <= 4 * qbg + 4:
                    val = NEG
                else:
                    val = t5[31, h] if kc > 4 * qbg + 3 else t5[15, h]
                t5c[:, (h * 4 + qb) * 64 + kc] = val
    return btl, t5c


_PROGS = {}


def _prog(name):
    if name not in _PROGS:
        _PROGS[name] = {"a": build_a, "b": build_b, "c": build_c, "f": build_fused}[name]()
    return _PROGS[name]


def to_featmajor(xc):
    return np.ascontiguousarray(xc.T.reshape(8, 128, T).transpose(1, 0, 2).reshape(128, 8 * T))


def from_featmajor(xT):
    return xT.reshape(128, 8, T).transpose(1, 0, 2).reshape(D, T).T


def kernel_unfused(**inp):
    x = np.asarray(inp["x"], np.float32).reshape(BATCH * SEQ, D)
    cm = const_mats()
    xTs = [to_featmajor(x[c * T:(c + 1) * T]) for c in range(NCORE)]
    tabs = [rope_tables(c) for c in range(NCORE)]
    cores = list(range(NCORE))
    for l in range(DEPTH):
        H = layer_host(inp, l)
        common = {"gains": H["gains"], "lconst": H["lconst"], "cmat": cm}
        maps = [dict(common, xT=xTs[c], wg=H["ffn1_wg"], wu=H["ffn1_wu"], wd=H["ffn1_wd"], winf=H["winf"],
                     winv=H["winv"], wsm=H["wsm"], tabs=tabs[c]) for c in cores]
        ra = run_bass_kernel_spmd(_prog("a"), maps, core_ids=cores).results
        maps = []
        dl = np.ascontiguousarray(np.broadcast_to(np.asarray(inp["diff_lambda"][l], np.float32).reshape(1, 128), (64, 128)))
        for c in cores:
            b = c // 4
            kTg = np.zeros((4 * KROWS, T), np.asarray(ra[0]["kT"]).dtype)
            vg = np.zeros((4 * NVU * 128, 1040), np.asarray(ra[0]["v"]).dtype)
            for j in range(4):
                kj = np.asarray(ra[4 * b + j]["kT"])
                vj = np.asarray(ra[4 * b + j]["v"]).reshape(NVU * 128, 1040)
                for a, bb in K_PARTS:
                    kTg[4 * a + j * (bb - a): 4 * a + (j + 1) * (bb - a)] = kj[a:bb]
                for a, bb in V_PARTS:
                    vg[(4 * a + j * (bb - a)) * 128: (4 * a + (j + 1) * (bb - a)) * 128] = vj[a * 128:bb * 128]
            btl, t5c = bias_tiles_host(inp, l, c)
            maps.append(dict(common, qT=np.asarray(ra[c]["qT"]), kT=np.asarray(ra[c]["kT"]), v=np.asarray(ra[c]["v"]),
                             kTg=kTg, vg=vg, btl=btl, t5c=t5c, dlam=dl))
        rb = run_bass_kernel_spmd(_prog("b"), maps, core_ids=cores).results
        wout = blk_cols(np.asarray(inp["w_out"][l], np.float32), 128)
        maps = [dict(common, xT=np.asarray(ra[c]["x1T"]), y=np.asarray(rb[c]["y"]), wg=H["ffn2_wg"], wu=H["ffn2_wu"],
                     wd=H["ffn2_wd"], wout=wout) for c in cores]
        rc = run_bass_kernel_spmd(_prog("c"), maps, core_ids=cores).results
        xTs = [np.asarray(rc[c]["xoT"]) for c in cores]
    out = np.concatenate([from_featmajor(xTs[c]) for c in cores], axis=0)
    return np.ascontiguousarray(out.reshape(BATCH, SEQ, D).astype(np.float32))


RG = [[0, 1, 2, 3], [4, 5, 6, 7]]
K_PARTS = [(0, 256), (256, 448), (448, 640), (640, 896), (896, 1024)]
V_PARTS = [(0, 3), (3, 6), (6, 9), (9, 12), (12, 14)]


def kg_row(j, row):
    for a, b in K_PARTS:
        if a <= row < b:
            return 4 * a + j * (b - a) + (row - a)
    raise AssertionError(row)


def vg_row(j, u):
    for a, b in V_PARTS:
        if a <= u < b:
            return (4 * a + j * (b - a) + (u - a)) * 128
    raise AssertionError(u)


def build_fused():
    nc = bass.Bass("TRN2", target_bir_lowering=False)
    dt = lambda name, shape, dtype, kind, **kw: nc.dram_tensor(name, shape, dtype, kind=kind, **kw).ap()
    xT_d = dt("xT", [128, 8 * T], F32, "ExternalInput")
    cmat_d = dt("cmat", [128, 768], F32, "ExternalInput")
    tabs_d = dt("tabs", [4, 128, T], F32, "ExternalInput")
    t5c_d = dt("t5c", [128, 1024], F32, "ExternalInput")
    xo_d = dt("xoT", [128, 8 * T], F32, "ExternalOutput")
    LD = []
    for l in range(DEPTH):
        sfx = "_%d" % l
        d = {}
        d["gains"] = dt("gains" + sfx, [128, NG], F32, "ExternalInput")
        d["lconst"] = dt("lconst" + sfx, [128, 2], F32, "ExternalInput")
        for nm in ("wg1", "wu1", "wg2", "wu2"):
            d[nm] = dt(nm + sfx, [11, 128, 2048], F32, "ExternalInput")
        for nm in ("wd1", "wd2"):
            d[nm] = dt(nm + sfx, [8, 128, 2816], F32, "ExternalInput")
        d["winf"] = dt("winf" + sfx, [7, 128, 2048], F32, "ExternalInput")
        d["winv"] = dt("winv" + sfx, [2, 128, 2560], F32, "ExternalInput")
        d["wsm"] = dt("wsm" + sfx, [128, WSM], F32, "ExternalInput")
        d["wout"] = dt("wout" + sfx, [8, 128, 1024], F32, "ExternalInput")
        d["btl"] = dt("btl" + sfx, [NTILES, 128, 512], F32, "ExternalInput")
        d["dlam"] = dt("dlam" + sfx, [64, 128], F32, "ExternalInput")
        d["qT"] = dt("s_qT" + sfx, [QROWS, T], BF16, "Internal")
        d["kT"] = dt("s_kT" + sfx, [KROWS, T], BF16, "Internal")
        d["v"] = dt("s_v" + sfx, [NVU, 128, 16, 65], BF16, "Internal")
        d["kTg"] = dt("s_kTg" + sfx, [4 * KROWS, T], BF16, "Internal", addr_space="Local")
        d["vg"] = dt("s_vg" + sfx, [4 * NVU * 128, 16 * 65], BF16, "Internal", addr_space="Local")
        d["y"] = dt("s_y" + sfx, [16, 64, T], F32, "Internal")
        LD.append(d)
    with contextlib.ExitStack() as top:
        ss = SemState(top)
        c = Ctx()
        c.xT = _sb(top, nc, "xT", [128, 8 * T], F32)

        def fresh_x():
            c.x_r = [Res("x%d" % t) for t in range(NT)]

        with contextlib.ExitStack() as st:
            S = Sched(nc, ss)
            fresh_x()
            for k in range(8):
                S.add("sp", (lambda e, k=k: e.dma_start(out=c.xT[:, k * T:(k + 1) * T], in_=xT_d[:, k * T:(k + 1) * T])),
                      writes=c.x_r, dma_key="xT")
            S.drain()
            S.emit(st)
        for l in range(DEPTH):
            d = LD[l]
            with contextlib.ExitStack() as st:
                S = Sched(nc, ss)
                fresh_x()
                common_setup(c, st, nc, S, d["gains"], d["lconst"], cmat_d)
                da = dict(wg=d["wg1"], wu=d["wu1"], wd=d["wd1"], winf=d["winf"], winv=d["winv"], wsm=d["wsm"], tabs=tabs_d,
                          qT=d["qT"], kT=d["kT"], v=d["v"], q_r=Res("q"), k_r=Res("k"), v_r=Res("v"))
                emit_phase_a(c, st, nc, S, da)
                S.drain()
                S.emit(st)
            with contextlib.ExitStack() as st:
                S = Sched(nc, ss)
                v2 = d["v"].rearrange("u p c d -> (u p) (c d)")
                for a, b in K_PARTS:
                    S.add("pool", (lambda e, d=d, a=a, b=b: e.collective_compute(
                        "AllGather", ALU.bypass, replica_groups=RG, ins=[d["kT"][a:b, :]], outs=[d["kTg"][4 * a:4 * b, :]])),
                        dma_key="cc", inc=1)
                for a, b in V_PARTS:
                    S.add("pool", (lambda e, d=d, a=a, b=b, v2=v2: e.collective_compute(
                        "AllGather", ALU.bypass, replica_groups=RG, ins=[v2[a * 128:b * 128, :]],
                        outs=[d["vg"][4 * a * 128:4 * b * 128, :]])), dma_key="cc", inc=1)
                S.drain()
                S.emit(st)
            with contextlib.ExitStack() as st:
                S = Sched(nc, ss)
                common_setup(c, st, nc, S, d["gains"], d["lconst"], cmat_d)
                db = dict(qT=d["qT"], kT=d["kT"], v=d["v"], kTg=d["kTg"], vg=d["vg"], btl=d["btl"], t5c=t5c_d,
                          dlam=d["dlam"], y=d["y"], y_r=Res("y"))
                emit_phase_b(c, st, nc, S, db)
                S.drain()
                S.emit(st)
            with contextlib.ExitStack() as st:
                S = Sched(nc, ss)
                fresh_x()
                common_setup(c, st, nc, S, d["gains"], d["lconst"], cmat_d)
                dc = dict(wg=d["wg2"], wu=d["wu2"], wd=d["wd2"], wout=d["wout"], y=d["y"], y_r=Res("y"))
                emit_phase_c(c, st, nc, S, dc)
                S.drain()
                S.emit(st)
        with contextlib.ExitStack() as st:
            S = Sched(nc, ss)
            fresh_x()
            xo_r = Res("xo")
            store_x(c, S, xo_d, xo_r)
            S.add("sp", None, reads=[xo_r])
            S.drain()
            S.emit(st)
    return nc


def kernel(**inp):
    x = np.asarray(inp["x"], np.float32).reshape(BATCH * SEQ, D)
    cm = const_mats()
    cores = list(range(NCORE))
    shared = {}
    for l in range(DEPTH):
        H = layer_host(inp, l)
        sfx = "_%d" % l
        shared.update({"gains" + sfx: H["gains"], "lconst" + sfx: H["lconst"], "wg1" + sfx: H["ffn1_wg"], "wu1" + sfx: H["ffn1_wu"],
                       "wd1" + sfx: H["ffn1_wd"], "wg2" + sfx: H["ffn2_wg"], "wu2" + sfx: H["ffn2_wu"], "wd2" + sfx: H["ffn2_wd"],
                       "winf" + sfx: H["winf"], "winv" + sfx: H["winv"], "wsm" + sfx: H["wsm"],
                       "wout" + sfx: blk_cols(np.asarray(inp["w_out"][l], np.float32), 128),
                       "dlam" + sfx: np.ascontiguousarray(np.broadcast_to(
                           np.asarray(inp["diff_lambda"][l], np.float32).reshape(1, 128), (64, 128)))})
    maps = []
    for c in cores:
        m = dict(shared, xT=to_featmajor(x[c * T:(c + 1) * T]), cmat=cm, tabs=rope_tables(c))
        for l in range(DEPTH):
            btl, t5c = bias_tiles_host(inp, l, c)
            m["btl_%d" % l] = btl
            m["t5c"] = t5c
        maps.append(m)
    res = run_bass_kernel_spmd(_prog("f"), maps, core_ids=cores).results
    out = np.concatenate([from_featmajor(np.asarray(res[c]["xoT"])) for c in cores], axis=0)
    return np.ascontiguousarray(out.reshape(BATCH, SEQ, D).astype(np.float32))
```

```python
import contextlib
import math
import numpy as np
import ml_dtypes
import concourse.bass as bass
import concourse.mybir as mybir
from concourse.bass_utils import run_bass_kernel_spmd

F32 = mybir.dt.float32
BF16 = mybir.dt.bfloat16
AF = mybir.ActivationFunctionType
ALU = mybir.AluOpType

_UID = [0]


def _uid():
    _UID[0] += 1
    return _UID[0]

ENGS = ("pe", "act", "dve", "pool", "sp")
EPOCH = 16000
SAME_ENGINE_SYNC = True


class Res:
    __slots__ = ("name", "writer", "readers")

    def __init__(self, name=""):
        self.name = name
        self.writer = None
        self.readers = {}


class Op:
    __slots__ = ("eng", "fn", "deps", "dma_key", "signaled", "count", "semid", "idx", "inc")


class SemState:
    def __init__(self, stack):
        self.stack = stack
        self.cnt = {e: 0 for e in ENGS}
        self.dcnt = {}
        self.sems = {}


class Sched:
    def __init__(self, nc, semstate=None):
        self.nc = nc
        self.ops = []
        self.by_eng = {e: [] for e in ENGS}
        self.semstate = semstate
        self.last_dma = {}

    def drain(self):
        op = self.add("sp", None)
        op.deps = list(self.last_dma.values())
        return op

    def add(self, eng, fn, reads=(), writes=(), dma_key=None, inc=16):
        op = Op()
        op.inc = inc
        op.eng = eng
        op.fn = fn
        op.dma_key = dma_key
        op.signaled = dma_key is not None
        op.count = 0
        op.semid = None
        op.idx = len(self.ops)
        deps = {}
        for r in reads:
            if r.writer is not None:
                deps[r.writer.idx] = r.writer
        for w in writes:
            if w.writer is not None:
                deps[w.writer.idx] = w.writer
            for rd in w.readers.values():
                deps[rd.idx] = rd
        dl = []
        for d in deps.values():
            if d.dma_key is not None or dma_key is not None:
                need = True
            elif d.eng == eng:
                need = (eng != "pe") and SAME_ENGINE_SYNC
            else:
                need = True
            if need:
                d.signaled = True
                dl.append(d)
        op.deps = dl
        for r in reads:
            key = eng if dma_key is None else ("dma", op.idx)
            r.readers[key] = op
        for w in writes:
            w.writer = op
            w.readers = {}
        self.ops.append(op)
        self.by_eng[eng].append(op)
        if dma_key is not None:
            self.last_dma[dma_key] = op
        return op

    def emit(self, stack):
        nc = self.nc
        ss = self.semstate if self.semstate is not None else SemState(stack)
        cnt = ss.cnt
        dcnt = ss.dcnt
        for op in self.ops:
            if op.dma_key is not None:
                dcnt[op.dma_key] = dcnt.get(op.dma_key, 0) + op.inc
                op.count = dcnt[op.dma_key]
                op.semid = ("d", op.dma_key)
            elif op.signaled:
                c = cnt[op.eng]
                cnt[op.eng] = c + 1
                op.semid = ("e", op.eng, c // EPOCH)
                op.count = (c % EPOCH) + 1
        self.check(ss)
        sems = ss.sems
        for op in self.ops:
            if op.semid is not None and op.semid not in sems:
                sems[op.semid] = ss.stack.enter_context(nc.semaphore("s%d" % len(sems)))
        self.nsems = len(sems)
        block = stack.enter_context(nc.Block())
        engobj = {"pe": block.tensor, "act": block.scalar, "dve": block.vector,
                  "pool": block.gpsimd, "sp": block.sync}

        def make(ename):
            ops = self.by_eng[ename]

            def body(e):
                waited = {}
                wepoch = {}
                for op in ops:
                    for d in op.deps:
                        sid = d.semid
                        c = d.count
                        if sid[0] == "e" and wepoch.get(sid[1], -1) > sid[2]:
                            continue
                        if waited.get(sid, 0) >= c:
                            continue
                        e.wait_ge(sems[sid], c)
                        waited[sid] = c
                        if sid[0] == "e":
                            wepoch[sid[1]] = max(wepoch.get(sid[1], -1), sid[2])
                    if op.fn is None:
                        continue
                    ins = op.fn(e)
                    if op.dma_key is not None and op.inc == 1:
                        ins.then_inc(sems[op.semid])
                    elif op.dma_key is not None:
                        ins.then_inc(sems[op.semid], 16)
                    elif op.signaled:
                        ins.then_inc(sems[op.semid], 1)
            return body

        for ename in ENGS:
            if self.by_eng[ename]:
                engobj[ename](make(ename))


def _sched_check(self, ss):
    val = dict(getattr(ss, "val", {}))
    pos = {e: 0 for e in ENGS}
    progress = True
    while progress:
        progress = False
        for e in ENGS:
            ops = self.by_eng[e]
            while pos[e] < len(ops):
                op = ops[pos[e]]
                if all(val.get(d.semid, 0) >= d.count and (d.semid[0] != "e" or True) for d in op.deps):
                    if op.semid is not None:
                        if op.semid[0] == "d":
                            val[op.semid] = val.get(op.semid, 0) + op.inc
                        else:
                            val[op.semid] = val.get(op.semid, 0) + 1
                        assert val[op.semid] == op.count or op.semid[0] == "d", (op.semid, val[op.semid], op.count)
                    pos[e] += 1
                    progress = True
                else:
                    break
    stuck = {e: pos[e] for e in ENGS if pos[e] < len(self.by_eng[e])}
    assert not stuck, "deadlock in schedule: %r" % stuck
    ss.val = val
    print("sched ok: ops=%d per-eng=%r" % (len(self.ops), {e: len(self.by_eng[e]) for e in ENGS}))


Sched.check = _sched_check


class Ring:
    def __init__(self, stack, nc, name, n, shape, dtype, psum=False):
        self.bufs = []
        for i in range(n):
            nm = "r_%s%d" % (name, i)
            tn = "%s_%d" % (nm, _uid())
            if psum:
                t = stack.enter_context(nc.psum_tensor(tn, shape, dtype))
            else:
                t = stack.enter_context(nc.sbuf_tensor(tn, shape, dtype))
            self.bufs.append((t, Res(nm), nm))
        self.i = 0

    def next(self):
        b = self.bufs[self.i % len(self.bufs)]
        self.i += 1
        return b


D = 1024
SEQ = 8192
BATCH = 2
DEPTH = 2
DFF = 2816
NCORE = 8
T = 2048
NT = 4
TS = 512
EPS = 1e-6
NG = 59
G_FFN1, G_MIX, G_NAQ, G_NAK, G_QLAT, G_KVLAT, G_MQ, G_MK, G_DQ, G_DK, G_GQ, G_GK = \
    0, 8, 16, 17, 18, 20, 21, 22, 23, 24, 25, 26
G_BETA, G_FFN2, G_FIN = 27, 43, 51
WS_UQ, WS_KN, WS_V, WS_KR = 0, 768, 1152, 1408
WSM = 2176
C_ONES, C_BD64, C_BD32, C_RB, C_RD, C_ID = 0, 1, 2, 3, 4, 5
QROWS = 1152
KROWS = 1024
NVU = 14


def lambda_init(l):
    return 0.8 - 0.6 * math.exp(-0.3 * l)


class Ctx:
    pass


def _sb(st, nc, name, shape, dt):
    return st.enter_context(nc.sbuf_tensor("sb_%s_%d" % (name, _uid()), shape, dt))


class WStream:
    def __init__(self, S, ring, pf):
        self.S = S
        self.ring = ring
        self.pf = pf
        self.items = []
        self.issued = 0
        self.handles = []

    def plan(self, src_ap, ncols, part=128):
        self.items.append((src_ap, ncols, part))
        return len(self.items) - 1

    def get(self, i):
        while self.issued < min(len(self.items), i + 1 + self.pf):
            src, ncols, part = self.items[self.issued]
            buf, res, nm = self.ring.next()
            self.S.add("pool", (lambda e, buf=buf, src=src, ncols=ncols, part=part:
                                e.dma_start(out=buf[0:part, 0:ncols], in_=src)),
                       writes=[res], dma_key=nm)
            self.handles.append((buf, res))
            self.issued += 1
        return self.handles[i]


def rms_feat(c, srcs, P, ones_idx, n, gain_cols, outs):
    S = c.S
    ones = c.cmat[0:P, ones_idx * 128: ones_idx * 128 + P]
    ps, psr, _ = c.PS.next()
    last = len(srcs) - 1
    sc = float(n) ** -0.5
    for i, (src, sres) in enumerate(srcs):
        sq, sqr, _ = c.SQ.next()
        S.add("act", (lambda e, sq=sq, src=src: e.activation(out=sq[0:P, :], in_=src, func=AF.Square, scale=sc)),
              reads=[sres], writes=[sqr])
        S.add("pe", (lambda e, sq=sq, i=i: e.matmul(ps[0:P, :], ones, sq[0:P, :], start=(i == 0), stop=(i == last))),
              reads=[sqr, c.cmat_r], writes=[psr])
    rs, rsr, _ = c.RS.next()
    S.add("act", lambda e: e.activation(out=rs[0:P, :], in_=ps[0:P, :], func=AF.Ln, bias=c.eps[0:P, 0:1]),
          reads=[psr, c.eps_r], writes=[rsr])
    S.add("act", lambda e: e.activation(out=rs[0:P, :], in_=rs[0:P, :], func=AF.Exp, scale=-0.5),
          reads=[rsr], writes=[rsr])
    for i, ((src, sres), (out, ores)) in enumerate(zip(srcs, outs)):
        g = c.gains[0:P, gain_cols[i]:gain_cols[i] + 1]
        S.add("dve", (lambda e, src=src, out=out, g=g: e.scalar_tensor_tensor(
            out=out, in0=src, scalar=g, in1=rs[0:P, :], op0=ALU.mult, op1=ALU.mult)),
            reads=[sres, rsr, c.gains_r], writes=[ores])


def rope_feat(c, qn, qnr, P, r_idx, cos_ap, sin_ap, tab_r, out, outr):
    S = c.S
    R = c.cmat[0:P, r_idx * 128: r_idx * 128 + P]
    ps, psr, _ = c.PS.next()
    S.add("pe", lambda e: e.matmul(ps[0:P, :], R, qn, start=True, stop=True), reads=[qnr, c.cmat_r], writes=[psr])
    t1, t1r, _ = c.TF.next()
    t2, t2r, _ = c.TF.next()
    S.add("pool", lambda e: e.tensor_tensor(out=t1[0:P, :], in0=qn, in1=cos_ap, op=ALU.mult), reads=[qnr, tab_r], writes=[t1r])
    S.add("dve", lambda e: e.tensor_tensor(out=t2[0:P, :], in0=ps[0:P, :], in1=sin_ap, op=ALU.mult), reads=[psr, tab_r], writes=[t2r])
    S.add("pool", lambda e: e.tensor_tensor(out=out, in0=t1[0:P, :], in1=t2[0:P, :], op=ALU.add), reads=[t1r, t2r], writes=[outr])


def norm_x_tile(c, t, gcol0):
    hT, hTr, _ = c.HT.next()
    srcs = [(c.xT[:, k * T + t * TS: k * T + (t + 1) * TS], c.x_r[t]) for k in range(8)]
    outs = [(hT[:, k * TS:(k + 1) * TS], hTr) for k in range(8)]
    rms_feat(c, srcs, 128, C_ONES, D, [gcol0 + k for k in range(8)], outs)
    return hT, hTr


def ffn_tile(c, t, hT, hTr, ws, base):
    S = c.S
    for b in range(11):
        wg, wgr = ws.get(base + 2 * b)
        wu, wur = ws.get(base + 2 * b + 1)
        for jj in range(2):
            j = 2 * b + jj
            pg, pgr, _ = c.PS.next()
            pu, pur, _ = c.PS.next()
            for k in range(8):
                S.add("pe", (lambda e, k=k, pg=pg, wg=wg, jj=jj: e.matmul(
                    pg[:, :], wg[:, k * 256 + jj * 128: k * 256 + jj * 128 + 128], hT[:, k * TS:(k + 1) * TS],
                    start=(k == 0), stop=(k == 7))), reads=[wgr, hTr], writes=[pgr])
            for k in range(8):
                S.add("pe", (lambda e, k=k, pu=pu, wu=wu, jj=jj: e.matmul(
                    pu[:, :], wu[:, k * 256 + jj * 128: k * 256 + jj * 128 + 128], hT[:, k * TS:(k + 1) * TS],
                    start=(k == 0), stop=(k == 7))), reads=[wur, hTr], writes=[pur])
            sg, sgr, _ = c.TF.next()
            S.add("act", (lambda e, sg=sg, pg=pg: e.activation(out=sg[:, :], in_=pg[:, :], func=AF.Silu)),
                  reads=[pgr], writes=[sgr])
            S.add("dve", (lambda e, sg=sg, pu=pu, j=j: e.tensor_tensor(
                out=c.actT[:, j * TS:(j + 1) * TS], in0=sg[:, :], in1=pu[:, :], op=ALU.mult)),
                reads=[sgr, pur], writes=[c.act_r[j]])
    for m in range(8):
        wd, wdr = ws.get(base + 22 + m)
        pd, pdr, _ = c.PS.next()
        for j in range(22):
            S.add("pe", (lambda e, j=j, pd=pd, wd=wd: e.matmul(
                pd[:, :], wd[:, j * 128:(j + 1) * 128], c.actT[:, j * TS:(j + 1) * TS],
                start=(j == 0), stop=(j == 21))), reads=[wdr, c.act_r[j]], writes=[pdr])
        xs = c.xT[:, m * T + t * TS: m * T + (t + 1) * TS]
        S.add("dve", (lambda e, pd=pd, xs=xs: e.scalar_tensor_tensor(
            out=xs, in0=pd[:, :], scalar=0.5, in1=xs, op0=ALU.mult, op1=ALU.add)),
            reads=[pdr, c.x_r[t]], writes=[c.x_r[t]])


def plan_ffn(ws, wg, wu, wd):
    base = len(ws.items)
    for b in range(11):
        ws.plan(wg[b], 2048)
        ws.plan(wu[b], 2048)
    for m in range(8):
        ws.plan(wd[m], 2816)
    return base


def common_setup(c, st, nc, S, gains_d, lconst_d, cmat_d):
    c.S = S
    c.nc = nc
    c.gains = _sb(st, nc, "gains", [128, NG], F32)
    c.gains_r = Res("gains")
    c.lconst = _sb(st, nc, "lconst", [128, 2], F32)
    c.lconst_r = Res("lconst")
    c.cmat = _sb(st, nc, "cmat", [128, 6 * 128], BF16)
    c.cmat_r = Res("cmat")
    c.eps = _sb(st, nc, "eps", [128, 1], F32)
    c.eps_r = Res("eps")
    S.add("sp", lambda e: e.dma_start(out=c.gains[:, :], in_=gains_d), writes=[c.gains_r], dma_key="gains")
    S.add("sp", lambda e: e.dma_start(out=c.lconst[:, :], in_=lconst_d), writes=[c.lconst_r], dma_key="lconst")
    S.add("pool", lambda e: e.dma_start(out=c.cmat[:, :], in_=cmat_d), writes=[c.cmat_r], dma_key="cmat")
    S.add("dve", lambda e: e.memset(c.eps[:, :], EPS), writes=[c.eps_r])


def load_x(c, st, nc, S, x_d):
    c.xT = _sb(st, nc, "xT", [128, 8 * T], F32)
    c.x_r = [Res("x%d" % t) for t in range(NT)]
    for k in range(8):
        S.add("sp", (lambda e, k=k: e.dma_start(out=c.xT[:, k * T:(k + 1) * T], in_=x_d[:, k * T:(k + 1) * T])),
              writes=c.x_r, dma_key="xT")


def store_x(c, S, xo_d, xo_r):
    for k in range(8):
        S.add("sp", (lambda e, k=k: e.dma_start(out=xo_d[:, k * T:(k + 1) * T], in_=c.xT[:, k * T:(k + 1) * T])),
              reads=c.x_r, writes=[xo_r], dma_key="xT")


def emit_phase_a(c, st, nc, S, d):
    c.PS = Ring(st, nc, "ps", 8, [128, 512], F32, psum=True)
    c.SQ = Ring(st, nc, "sq", 3, [128, 512], BF16)
    c.RS = Ring(st, nc, "rs", 2, [128, 512], F32)
    c.TF = Ring(st, nc, "tf", 4, [128, 512], F32)
    c.HT = Ring(st, nc, "hT", 2, [128, 8 * TS], BF16)
    c.actT = _sb(st, nc, "actT", [128, 22 * TS], BF16)
    c.act_r = [Res("act%d" % j) for j in range(22)]
    wring = Ring(st, nc, "wb", 6, [128, 2816], BF16)
    ws = WStream(S, wring, 4)
    wsm = _sb(st, nc, "wsm", [128, WSM], BF16)
    wsm_r = Res("wsm")
    S.add("pool", lambda e: e.dma_start(out=wsm[:, :], in_=d["wsm"]), writes=[wsm_r], dma_key="wsm")
    tabs = _sb(st, nc, "tabs", [128, 4 * T], F32)
    tab_r = Res("tabs")
    for i in range(4):
        S.add("sp", (lambda e, i=i: e.dma_start(out=tabs[:, i * T:(i + 1) * T], in_=d["tabs"][i])),
              writes=[tab_r], dma_key="tabs")
    for col, sc in ((G_NAQ, 64 ** -0.5), (G_MQ, 96 ** -0.5), (G_DQ, 32 ** -0.5), (G_GQ, 64 ** -0.5)):
        S.add("dve", (lambda e, col=col, sc=sc: e.tensor_scalar(
            out=c.gains[:, col:col + 1], in0=c.gains[:, col:col + 1], scalar1=float(sc), scalar2=None, op0=ALU.mult)),
            reads=[c.gains_r], writes=[c.gains_r])

    bases_ffn = [plan_ffn(ws, d["wg"], d["wu"], d["wd"]) for t in range(NT)]
    bases_in = []
    for t in range(NT):
        bases_in.append(len(ws.items))
        for b in range(7):
            ws.plan(d["winf"][b], 2048)
        for vb in range(2):
            ws.plan(d["winv"][vb], 2560)

    import os
    n1 = int(os.environ.get("KA_N1", NT))
    n2 = int(os.environ.get("KA_N2", NT))
    for t in range(n1):
        hT, hTr = norm_x_tile(c, t, G_FFN1)
        ffn_tile(c, t, hT, hTr, ws, bases_ffn[t])

    STG = Ring(st, nc, "stg", 6, [128, 512], BF16)
    QN = Ring(st, nc, "qn", 3, [128, 512], BF16)
    cqn = _sb(st, nc, "cqn", [128, 2 * TS], BF16)
    cqn_r = Res("cqn")
    ckvn = _sb(st, nc, "ckvn", [128, TS], BF16)
    ckvn_r = Res("ckvn")
    vst = _sb(st, nc, "vst", [128, 4, NVU, 65], BF16)
    vst_r = Res("vst")
    S.add("pool", lambda e: e.memset(vst[:, :, :, :], 1.0), writes=[vst_r])
    q_r, k_r, v_r = d["q_r"], d["k_r"], d["v_r"]

    def store_rows(dst, dres, row0, P, t, src, sres, key):
        S.add("sp", lambda e: e.dma_start(out=dst[row0:row0 + P, t * TS:(t + 1) * TS], in_=src),
              reads=[sres], writes=[dres], dma_key=key)

    def stage2_tile(t, hT, hTr):

        def proj_chunk(wb, wbr, jj):
            p, pr, _ = c.PS.next()
            for k in range(8):
                S.add("pe", (lambda e, k=k: e.matmul(
                    p[:, :], wb[:, k * 256 + jj * 128: k * 256 + jj * 128 + 128], hT[:, k * TS:(k + 1) * TS],
                    start=(k == 0), stop=(k == 7))), reads=[wbr, hTr], writes=[pr])
            return p, pr

        def simple_head_chunk(wb, wbr, jj, ones_idx, n, gcol, dst, dres, row0, rope):
            p, pr = proj_chunk(wb, wbr, jj)
            if rope:
                qn, qnr, _ = QN.next()
                rms_feat(c, [(p[:, :], pr)], 128, ones_idx, n, [gcol], [(qn[:, :], qnr)])
                sg, sgr, key = STG.next()
                rope_feat(c, qn[:, :], qnr, 128, C_RD, tabs[:, 2 * T + t * TS: 2 * T + (t + 1) * TS],
                          tabs[:, 3 * T + t * TS: 3 * T + (t + 1) * TS], tab_r, sg[:, :], sgr)
            else:
                sg, sgr, key = STG.next()
                rms_feat(c, [(p[:, :], pr)], 128, ones_idx, n, [gcol], [(sg[:, :], sgr)])
            store_rows(dst, dres, row0, 128, t, sg[:, :], sgr, key)

        b0 = bases_in[t]
        wb, wbr = ws.get(b0 + 0)
        for jj in range(2):
            simple_head_chunk(wb, wbr, jj, C_BD64, 64, G_NAQ, d["qT"], q_r, 0 + jj * 128, False)
        wb, wbr = ws.get(b0 + 1)
        for jj in range(2):
            simple_head_chunk(wb, wbr, jj, C_BD64, 64, G_NAK, d["kT"], k_r, 0 + jj * 128, False)
        wb, wbr = ws.get(b0 + 2)
        p0, p0r = proj_chunk(wb, wbr, 0)
        p1, p1r = proj_chunk(wb, wbr, 1)
        rms_feat(c, [(p0[:, :], p0r), (p1[:, :], p1r)], 128, C_ONES, 256, [G_QLAT, G_QLAT + 1],
                 [(cqn[:, 0:TS], cqn_r), (cqn[:, TS:2 * TS], cqn_r)])
        for h in range(4):
            p, pr, _ = c.PS.next()
            for k in range(2):
                S.add("pe", (lambda e, k=k, h=h, p=p: e.matmul(
                    p[0:96, :], wsm[:, WS_UQ + k * 384 + h * 96: WS_UQ + k * 384 + h * 96 + 96],
                    cqn[:, k * TS:(k + 1) * TS], start=(k == 0), stop=(k == 1))),
                    reads=[wsm_r, cqn_r], writes=[pr])
            qn, qnr, _ = QN.next()
            rms_feat(c, [(p[0:96, :], pr)], 96, C_ONES, 96, [G_MQ], [(qn[0:96, :], qnr)])
            sg, sgr, key = STG.next()
            rope_feat(c, qn[0:96, :], qnr, 96, C_RB, tabs[0:96, 0 * T + t * TS: 0 * T + (t + 1) * TS],
                      tabs[0:96, 1 * T + t * TS: 1 * T + (t + 1) * TS], tab_r, sg[0:96, :], sgr)
            store_rows(d["qT"], q_r, 256 + h * 96, 96, t, sg[0:96, :], sgr, key)
        wb, wbr = ws.get(b0 + 3)
        p, pr = proj_chunk(wb, wbr, 0)
        rms_feat(c, [(p[:, :], pr)], 128, C_ONES, 128, [G_KVLAT], [(ckvn[:, :], ckvn_r)])
        for h in range(4):
            p, pr, _ = c.PS.next()
            S.add("pe", (lambda e, h=h, p=p: e.matmul(
                p[0:96, :], wsm[:, WS_KN + h * 96: WS_KN + h * 96 + 96], ckvn[:, :], start=True, stop=False)),
                reads=[wsm_r, ckvn_r], writes=[pr])
            for k in range(8):
                S.add("pe", (lambda e, k=k, p=p: e.matmul(
                    p[0:96, :], wsm[:, WS_KR + k * 96: WS_KR + k * 96 + 96], hT[:, k * TS:(k + 1) * TS],
                    start=False, stop=(k == 7))), reads=[wsm_r, hTr], writes=[pr])
            qn, qnr, _ = QN.next()
            rms_feat(c, [(p[0:96, :], pr)], 96, C_ONES, 96, [G_MK], [(qn[0:96, :], qnr)])
            sg, sgr, key = STG.next()
            rope_feat(c, qn[0:96, :], qnr, 96, C_RB, tabs[0:96, 0 * T + t * TS: 0 * T + (t + 1) * TS],
                      tabs[0:96, 1 * T + t * TS: 1 * T + (t + 1) * TS], tab_r, sg[0:96, :], sgr)
            store_rows(d["kT"], k_r, 256 + h * 96, 96, t, sg[0:96, :], sgr, key)
        for s in range(4):
            p, pr, _ = c.PS.next()
            S.add("pe", (lambda e, s=s, p=p: e.matmul(
                p[:, 0:256], ckvn[:, s * 128:(s + 1) * 128], wsm[:, WS_V:WS_V + 256], start=True, stop=True)),
                reads=[wsm_r, ckvn_r], writes=[pr])
            S.add("act", (lambda e, s=s, p=p: e.activation(out=vst[:, s, 10:14, 0:64], in_=p[:, 0:256].rearrange("p (u d) -> p u d", d=64), func=AF.Copy)),
                  reads=[pr], writes=[vst_r])
        simple_head_chunk(wb, wbr, 1, C_BD64, 64, G_GK, d["kT"], k_r, 896, True)
        wb, wbr = ws.get(b0 + 4)
        for jj in range(2):
            simple_head_chunk(wb, wbr, jj, C_BD32, 32, G_DQ, d["qT"], q_r, 640 + jj * 128, False)
        wb, wbr = ws.get(b0 + 5)
        for jj in range(2):
            simple_head_chunk(wb, wbr, jj, C_BD32, 32, G_DK, d["kT"], k_r, 640 + jj * 128, False)
        wb, wbr = ws.get(b0 + 6)
        for jj in range(2):
            simple_head_chunk(wb, wbr, jj, C_BD64, 64, G_GQ, d["qT"], q_r, 896 + jj * 128, True)
        for vb in range(2):
            wv, wvr = ws.get(b0 + 7 + vb)
            for s in range(4):
                p, pr, _ = c.PS.next()
                for k in range(8):
                    S.add("pe", (lambda e, k=k, s=s, p=p, wv=wv: e.matmul(
                        p[:, 0:320], hT[:, k * TS + s * 128: k * TS + (s + 1) * 128], wv[:, k * 320:(k + 1) * 320],
                        start=(k == 0), stop=(k == 7))), reads=[wvr, hTr], writes=[pr])
                S.add("dve", (lambda e, s=s, p=p, vb=vb: e.tensor_copy(out=vst[:, s, vb * 5:(vb + 1) * 5, 0:64], in_=p[:, 0:320].rearrange("p (u d) -> p u d", d=64))),
                      reads=[pr], writes=[vst_r])
        for u in range(NVU):
            S.add("sp", (lambda e, u=u: e.dma_start(out=d["v"][u, :, t * 4:(t + 1) * 4, :], in_=vst[:, :, u, :])),
                  reads=[vst_r], writes=[v_r], dma_key="vst")

    for t in range(n2):
        hT, hTr = norm_x_tile(c, t, G_MIX)
        stage2_tile(t, hT, hTr)


def build_a():
    nc = bass.Bass("TRN2", target_bir_lowering=False)
    dt = lambda name, shape, dtype, kind: nc.dram_tensor(name, shape, dtype, kind=kind).ap()
    d = {}
    xT = dt("xT", [128, 8 * T], F32, "ExternalInput")
    gains = dt("gains", [128, NG], F32, "ExternalInput")
    lconst = dt("lconst", [128, 2], F32, "ExternalInput")
    cmat = dt("cmat", [128, 768], F32, "ExternalInput")
    d["wg"] = dt("wg", [11, 128, 2048], F32, "ExternalInput")
    d["wu"] = dt("wu", [11, 128, 2048], F32, "ExternalInput")
    d["wd"] = dt("wd", [8, 128, 2816], F32, "ExternalInput")
    d["winf"] = dt("winf", [7, 128, 2048], F32, "ExternalInput")
    d["winv"] = dt("winv", [2, 128, 2560], F32, "ExternalInput")
    d["wsm"] = dt("wsm", [128, WSM], F32, "ExternalInput")
    d["tabs"] = dt("tabs", [4, 128, T], F32, "ExternalInput")
    xo = dt("x1T", [128, 8 * T], F32, "ExternalOutput")
    d["qT"] = dt("qT", [QROWS, T], BF16, "ExternalOutput")
    d["kT"] = dt("kT", [KROWS, T], BF16, "ExternalOutput")
    d["v"] = dt("v", [NVU, 128, 16, 65], BF16, "ExternalOutput")
    d["q_r"], d["k_r"], d["v_r"] = Res("qT"), Res("kT"), Res("v")
    xo_r = Res("xo")
    with contextlib.ExitStack() as st:
        S = Sched(nc)
        c = Ctx()
        common_setup(c, st, nc, S, gains, lconst, cmat)
        load_x(c, st, nc, S, xT)
        emit_phase_a(c, st, nc, S, d)
        store_x(c, S, xo, xo_r)
        S.add("sp", None, reads=[xo_r, d["q_r"], d["k_r"], d["v_r"]])
        S.emit(st)
    return nc


def blk_cols(W, cpb):
    K, N = W.shape
    kc = K // 128
    nb = N // cpb
    return np.ascontiguousarray(W.reshape(kc, 128, nb, cpb).transpose(2, 1, 0, 3).reshape(nb, 128, kc * cpb))


def const_mats():
    m = np.zeros((6, 128, 128), np.float32)
    m[C_ONES] = 1.0
    for b in range(2):
        m[C_BD64, b * 64:(b + 1) * 64, b * 64:(b + 1) * 64] = 1.0
    for b in range(4):
        m[C_BD32, b * 32:(b + 1) * 32, b * 32:(b + 1) * 32] = 1.0
    for i in range(16):
        m[C_RB, 80 + i, 64 + i] = -1.0
        m[C_RB, 64 + i, 80 + i] = 1.0
    for hb in (0, 64):
        for sb in (0, 32):
            for i in range(16):
                a = hb + sb + i
                b = a + 16
                m[C_RD, b, a] = -1.0
                m[C_RD, a, b] = 1.0
    m[C_ID] = np.eye(128, dtype=np.float32)
    return np.ascontiguousarray(m.transpose(1, 0, 2).reshape(128, 768))


def rope_tables(core):
    r = core % 4
    t = np.arange(r * T, (r + 1) * T)
    inv = np.exp(-math.log(10000.0) * np.arange(0, 32, 2, dtype=np.float32) / 32).astype(np.float32)
    a_seq = t.astype(np.float32)[:, None] * inv[None, :]
    a_row = (t // 64).astype(np.float32)[:, None] * inv[None, :]
    a_col = (t % 64).astype(np.float32)[:, None] * inv[None, :]
    tabs = np.zeros((4, 128, T), np.float32)
    tabs[0, 0:64] = 1.0
    for p in range(64, 96):
        tabs[0, p] = np.cos(a_seq[:, (p - 64) % 16])
        tabs[1, p] = np.sin(a_seq[:, (p - 64) % 16])
    for p in range(128):
        f = p % 64
        a = a_row if f < 32 else a_col
        tabs[2, p] = np.cos(a[:, f % 16])
        tabs[3, p] = np.sin(a[:, f % 16])
    return tabs


def tile64(v):
    return np.tile(v, 128 // v.shape[0])


def layer_host(inp, l):
    f = lambda k: np.asarray(inp[k][l], np.float32)
    h = {}
    for nm in ("ffn1", "ffn2"):
        h[nm + "_wg"] = blk_cols(f(nm + "_w_gate"), 256)
        h[nm + "_wu"] = blk_cols(f(nm + "_w_up"), 256)
        h[nm + "_wd"] = blk_cols(f(nm + "_w_down"), 128)
    win = f("w_in")
    cols_f = np.concatenate([np.arange(0, 256), np.arange(256, 512), np.arange(768, 1024), np.arange(1024, 1152),
                             np.arange(2208, 2336), np.arange(1184, 1440), np.arange(1440, 1696), np.arange(1952, 2208)])
    cols_v = np.concatenate([np.arange(512, 768), np.arange(1696, 1952), np.arange(2336, 2464)])
    h["winf"] = blk_cols(win[:, cols_f], 256)
    h["winv"] = blk_cols(win[:, cols_v], 320)
    wsm = np.zeros((128, WSM), np.float32)
    wuq = f("mla_w_uq")
    wsm[:, WS_UQ:WS_UQ + 768] = wuq.reshape(2, 128, 384).transpose(1, 0, 2).reshape(128, 768)
    wukv = f("mla_w_ukv")
    for hh in range(4):
        wsm[:, WS_KN + hh * 96: WS_KN + hh * 96 + 64] = wukv[:, hh * 128: hh * 128 + 64]
        wsm[:, WS_V + hh * 64: WS_V + (hh + 1) * 64] = wukv[:, hh * 128 + 64: hh * 128 + 128]
    kr = win[:, 1152:1184].reshape(8, 128, 32)
    for k in range(8):
        wsm[:, WS_KR + k * 96 + 64: WS_KR + k * 96 + 96] = kr[k]
    h["wsm"] = wsm
    g = np.zeros((128, NG), np.float32)
    g[:, G_FFN1:G_FFN1 + 8] = f("ffn1_norm").reshape(8, 128).T
    g[:, G_MIX:G_MIX + 8] = f("mix_norm").reshape(8, 128).T
    g[:, G_NAQ] = tile64(f("na_q_norm"))
    g[:, G_NAK] = tile64(f("na_k_norm"))
    g[:, G_QLAT:G_QLAT + 2] = f("mla_q_lat_norm").reshape(2, 128).T
    g[:, G_KVLAT] = f("mla_kv_lat_norm")
    g[0:96, G_MQ] = f("mla_q_norm")
    g[0:96, G_MK] = f("mla_k_norm")
    g[:, G_DQ] = tile64(f("diff_q_norm"))
    g[:, G_DK] = tile64(f("diff_k_norm"))
    g[:, G_GQ] = tile64(f("gqa_q_norm"))
    g[:, G_GK] = tile64(f("gqa_k_norm"))
    betas = np.concatenate([f("na_beta"), f("mla_beta"), np.tile(f("diff_subln"), 4), f("gqa_beta")])
    g[:, G_BETA:G_BETA + 8] = betas.reshape(8, 128).T
    g[:, G_FFN2:G_FFN2 + 8] = f("ffn2_norm").reshape(8, 128).T
    g[:, G_FIN:G_FIN + 8] = f("final_norm").reshape(8, 128).T
    h["gains"] = g
    lc = np.zeros((128, 2), np.float32)
    lc[:, 0] = lambda_init(l)
    lc[:, 1] = 1.0 - lambda_init(l)
    h["lconst"] = lc
    return h


def emit_phase_c(c, st, nc, S, d):
    c.PS = Ring(st, nc, "ps", 8, [128, 512], F32, psum=True)
    c.SQ = Ring(st, nc, "sq", 3, [128, 512], BF16)
    c.RS = Ring(st, nc, "rs", 2, [128, 512], F32)
    c.TF = Ring(st, nc, "tf", 4, [128, 512], F32)
    c.HT = Ring(st, nc, "hT", 2, [128, 8 * TS], BF16)
    c.actT = _sb(st, nc, "actT", [128, 22 * TS], BF16)
    c.act_r = [Res("cact%d" % j) for j in range(22)]
    wring = Ring(st, nc, "wb", 6, [128, 2816], BF16)
    ws = WStream(S, wring, 4)
    YT = Ring(st, nc, "yt", 2, [128, 8 * TS], F32)
    yn = _sb(st, nc, "yn", [128, 8 * TS], BF16)
    yn_r = Res("yn")
    S.add("dve", lambda e: e.tensor_scalar(out=c.gains[:, G_BETA + 4:G_BETA + 6], in0=c.gains[:, G_BETA + 4:G_BETA + 6],
                                            scalar1=c.lconst[:, 1:2], scalar2=None, op0=ALU.mult),
          reads=[c.gains_r, c.lconst_r], writes=[c.gains_r])
    bases = []
    for t in range(NT):
        b0 = len(ws.items)
        for m in range(8):
            ws.plan(d["wout"][m], 1024)
        plan_ffn(ws, d["wg"], d["wu"], d["wd"])
        bases.append(b0)

    def load_y(t, yt, ytr, key):
        for u in range(16):
            cc, hf = u // 2, u % 2
            S.add("sp", (lambda e, u=u, cc=cc, hf=hf: e.dma_start(out=yt[hf * 64:(hf + 1) * 64, cc * TS:(cc + 1) * TS],
                                                                  in_=d["y"][u, :, t * TS:(t + 1) * TS])),
                  reads=[d["y_r"]], writes=[ytr], dma_key=key)

    def tile_c(t, yt, ytr, key):
        for g in range(4):
            ccs = [2 * g, 2 * g + 1]
            if g == 2:
                for cc in ccs:
                    rms_feat(c, [(yt[:, cc * TS:(cc + 1) * TS], ytr)], 128, C_BD64, 64, [G_BETA + cc],
                             [(yn[:, cc * TS:(cc + 1) * TS], yn_r)])
            else:
                rms_feat(c, [(yt[:, cc * TS:(cc + 1) * TS], ytr) for cc in ccs], 128, C_ONES, 256, [G_BETA + cc for cc in ccs],
                         [(yn[:, cc * TS:(cc + 1) * TS], yn_r) for cc in ccs])
        for m in range(8):
            wb, wbr = ws.get(bases[t] + m)
            p, pr, _ = c.PS.next()
            for k in range(8):
                S.add("pe", (lambda e, k=k, p=p, wb=wb: e.matmul(
                    p[:, :], wb[:, k * 128:(k + 1) * 128], yn[:, k * TS:(k + 1) * TS],
                    start=(k == 0), stop=(k == 7))), reads=[wbr, yn_r], writes=[pr])
            xs = c.xT[:, m * T + t * TS: m * T + (t + 1) * TS]
            S.add("dve", (lambda e, p=p, xs=xs: e.tensor_tensor(out=xs, in0=p[:, :], in1=xs, op=ALU.add)),
                  reads=[pr, c.x_r[t]], writes=[c.x_r[t]])
        hT, hTr = norm_x_tile(c, t, G_FFN2)
        ffn_tile(c, t, hT, hTr, ws, bases[t] + 8)
        srcs = [(c.xT[:, k * T + t * TS: k * T + (t + 1) * TS], c.x_r[t]) for k in range(8)]
        rms_feat(c, srcs, 128, C_ONES, D, [G_FIN + k for k in range(8)], srcs)

    ybufs = [YT.next() for t in range(NT)]
    load_y(0, *ybufs[0])
    for t in range(NT):
        if t + 1 < NT:
            load_y(t + 1, *ybufs[t + 1])
        tile_c(t, *ybufs[t])


def build_c():
    nc = bass.Bass("TRN2", target_bir_lowering=False)
    dt = lambda name, shape, dtype, kind: nc.dram_tensor(name, shape, dtype, kind=kind).ap()
    d = {}
    xT = dt("xT", [128, 8 * T], F32, "ExternalInput")
    gains = dt("gains", [128, NG], F32, "ExternalInput")
    lconst = dt("lconst", [128, 2], F32, "ExternalInput")
    cmat = dt("cmat", [128, 768], F32, "ExternalInput")
    d["wg"] = dt("wg", [11, 128, 2048], F32, "ExternalInput")
    d["wu"] = dt("wu", [11, 128, 2048], F32, "ExternalInput")
    d["wd"] = dt("wd", [8, 128, 2816], F32, "ExternalInput")
    d["wout"] = dt("wout", [8, 128, 1024], F32, "ExternalInput")
    d["y"] = dt("y", [16, 64, T], F32, "ExternalInput")
    d["y_r"] = Res("y")
    xo = dt("xoT", [128, 8 * T], F32, "ExternalOutput")
    xo_r = Res("xo")
    with contextlib.ExitStack() as st:
        S = Sched(nc)
        c = Ctx()
        common_setup(c, st, nc, S, gains, lconst, cmat)
        load_x(c, st, nc, S, xT)
        emit_phase_c(c, st, nc, S, d)
        store_x(c, S, xo, xo_r)
        S.add("sp", None, reads=[xo_r])
        S.emit(st)
    return nc


LA = 2


def near_entries(kind, qb):
    ent = []
    if kind == "A":
        lo, hi, halo = 4 * qb - 2, 4 * qb + 5, 2
    else:
        lo, hi, halo = 4 * qb - 1, 4 * qb + 4, 1
    for lc in range(max(lo, 0), min(hi, 15) + 1):
        ent.append(("own", lc))
    if qb == 0:
        for j in range(4):
            for x in range(16 - halo, 16):
                ent.append(("gath", 16 * j + x))
    if qb == 3:
        for j in range(4):
            for x in range(halo):
                ent.append(("gath", 16 * j + x))
    return ent


def tile_ids():
    ids = {}
    n = 0
    for kind in ("A", "C"):
        for h in range(4):
            for qb in range(4):
                for i, _e in enumerate(near_entries(kind, qb)):
                    ids[(kind, h, qb, i)] = n
                    n += 1
    return ids, n


TILE_IDS, NTILES = tile_ids()


def emit_phase_b(c, st, nc, S, d):
    PSS = Ring(st, nc, "pss", 3, [128, 1024], F32, psum=True)
    PO = Ring(st, nc, "po", 1, [128, 512], F32, psum=True)
    PBC = Ring(st, nc, "pbc", 1, [128, 512], F32, psum=True)
    PT = Ring(st, nc, "pt", 4, [128, 1024], BF16)
    KT = Ring(st, nc, "kt", 2, [128, SEQ], BF16)
    VT = Ring(st, nc, "vt", 2, [128, 64, 65], BF16)
    KTO = Ring(st, nc, "kto", 2, [128, T], BF16)
    VTO = Ring(st, nc, "vto", 2, [128, 16, 65], BF16)
    QT = Ring(st, nc, "qt", 4, [128, T], BF16)
    BT = Ring(st, nc, "bt", 12, [128, 512], BF16)
    VS = Ring(st, nc, "vs", 2, [128, 64, 65], BF16)
    SC = Ring(st, nc, "sc", 2, [128, 64], F32)
    OSB = Ring(st, nc, "osb", 2, [128, 512], F32)
    ON = Ring(st, nc, "on", 4, [64, 512], F32)
    RR = Ring(st, nc, "rr", 2, [128, 512], F32)
    for ring_ in (KT, KTO):
        for buf_, res_, _k in ring_.bufs:
            S.add("pool", (lambda e, buf_=buf_: e.memset(buf_[:, :], 0.0)), writes=[res_])
    bts = WStream(S, BT, 8)
    t5c = _sb(st, nc, "t5c", [128, 1024], F32)
    t5c_r = Res("t5c")
    S.add("sp", lambda e: e.dma_start(out=t5c[:, :], in_=d["t5c"]), writes=[t5c_r], dma_key="t5c")
    onesf = _sb(st, nc, "onesf", [128, 64], F32)
    onesf_r = Res("onesf")
    S.add("dve", lambda e: e.memset(onesf[:, :], 1.0), writes=[onesf_r])
    dl = _sb(st, nc, "dl", [64, 128], F32)
    dl_r = Res("dl")
    S.add("sp", lambda e: e.dma_start(out=dl[:, :], in_=d["dlam"]), writes=[dl_r], dma_key="dl")
    lt = _sb(st, nc, "lt", [64, 72], F32)
    lt_r = Res("lt")
    S.add("dve", lambda e: e.tensor_tensor(out=lt[:, 0:32], in0=dl[:, 0:32], in1=dl[:, 32:64], op=ALU.mult), reads=[dl_r], writes=[lt_r])
    S.add("dve", lambda e: e.tensor_tensor(out=lt[:, 32:64], in0=dl[:, 64:96], in1=dl[:, 96:128], op=ALU.mult), reads=[dl_r, lt_r], writes=[lt_r])
    S.add("dve", lambda e: e.reduce_sum(lt[:, 64:65], lt[:, 0:32], axis=mybir.AxisListType.X), reads=[lt_r], writes=[lt_r])
    S.add("dve", lambda e: e.reduce_sum(lt[:, 65:66], lt[:, 32:64], axis=mybir.AxisListType.X), reads=[lt_r], writes=[lt_r])
    S.add("act", lambda e: e.activation(out=lt[:, 66:68], in_=lt[:, 64:66], func=AF.Exp), reads=[lt_r], writes=[lt_r])
    S.add("dve", lambda e: e.tensor_tensor(out=lt[:, 68:69], in0=lt[:, 67:68], in1=lt[:, 66:67], op=ALU.subtract), reads=[lt_r], writes=[lt_r])
    S.add("dve", lambda e: e.tensor_scalar(out=lt[:, 69:70], in0=lt[:, 68:69], scalar1=c.lconst[0:64, 0:1], scalar2=None, op0=ALU.subtract),
          reads=[lt_r, c.lconst_r], writes=[lt_r])
    nlam = lt[:, 69:70]
    ident = c.cmat[:, C_ID * 128:(C_ID + 1) * 128]
    y_r = d["y_r"]

    pending = []

    def attend(qt, qtr, qp0, dq, qb, chunks, cont):
        O, Or, _ = PO.next()
        steps = []
        i = 0
        isc = lambda ch: ch[5] is not None and ch[5][0] == "const"
        while i < len(chunks):
            if i + 1 < len(chunks) and not isc(chunks[i]) and not isc(chunks[i + 1]):
                steps.append([chunks[i], chunks[i + 1]])
                i += 2
            else:
                steps.append([chunks[i]])
                i += 1
        n = len(steps)
        total = len(chunks)
        pend = []
        pvi = [0]
        for i in range(n + LA):
            if i < n:
                st_ = steps[i]
                s_, sr, _ = PSS.next()
                w = 512 * len(st_)
                for hf, (kt, ktr, vt, vtr, kc, mode) in enumerate(st_):
                    sl = s_[:, hf * 512:(hf + 1) * 512]
                    if mode is not None and mode[0] == "tile":
                        bt, btr = bts.get(mode[1])
                        S.add("pe", (lambda e, sl=sl, bt=bt: e.matmul(sl, ident, bt[:, :], start=True, stop=False)),
                              reads=[btr, c.cmat_r], writes=[sr])
                        first = False
                    else:
                        first = True
                    S.add("pe", (lambda e, sl=sl, kc=kc, first=first, kt=kt: e.matmul(
                        sl, kt[qp0:qp0 + dq, kc * 128:(kc + 1) * 128], qt[qp0:qp0 + dq, qb * TS:(qb + 1) * TS],
                        start=first, stop=True)), reads=[ktr, qtr], writes=[sr])
                p_, pr, _ = PT.next()
                mode = st_[0][5]
                if len(st_) == 1 and mode is not None and mode[0] == "const":
                    bcol = mode[1]
                    S.add("act", (lambda e, s_=s_, p_=p_, bcol=bcol: e.activation(out=p_[:, 0:512], in_=s_[:, 0:512], func=AF.Exp, bias=bcol)),
                          reads=[sr, t5c_r], writes=[pr])
                else:
                    S.add("act", (lambda e, s_=s_, p_=p_, w=w: e.activation(out=p_[:, 0:w], in_=s_[:, 0:w], func=AF.Exp)),
                          reads=[sr], writes=[pr])
                pend.append((st_, p_, pr))
            j = i - LA
            if j >= 0:
                st_, p_, pr = pend[j]
                for hf, (kt, ktr, vt, vtr, kc, mode) in enumerate(st_):
                    k_ = pvi[0]
                    pvi[0] += 1
                    S.add("pe", (lambda e, kc=kc, p_=p_, k_=k_, vt=vt, hf=hf: e.matmul(
                        O[0:65, :], vt[:, kc, :], p_[:, hf * 512:(hf + 1) * 512], start=(k_ == 0), stop=(k_ == total - 1))),
                        reads=[vtr, pr], writes=[Or])
            if i == min(3, n + LA - 1) and pending:
                pending.pop(0)()
        osb, osbr, _ = OSB.next()
        S.add("dve", lambda e: e.tensor_copy(out=osb[0:65, :], in_=O[0:65, :]), reads=[Or], writes=[osbr])

        def fin():
            rr, rrr, _ = RR.next()
            S.add("dve", lambda e: e.reciprocal(out=rr[64:65, :], in_=osb[64:65, :]), reads=[osbr], writes=[rrr])
            bc, bcr, _ = PBC.next()
            S.add("pe", lambda e: e.matmul(bc[0:64, :], onesf[64:65, 0:64], rr[64:65, :], start=True, stop=True),
                  reads=[rrr, onesf_r], writes=[bcr])
            on, onr, _ = ON.next()
            S.add("dve", lambda e: e.tensor_tensor(out=on[:, :], in0=osb[0:64, :], in1=bc[0:64, :], op=ALU.mult),
                  reads=[osbr, bcr], writes=[onr])
            cont(on, onr)
        pending.append(fin)

    def store_y(u, qb, on, onr):
        S.add("sp", lambda e: e.dma_start(out=d["y"][u, :, qb * TS:(qb + 1) * TS], in_=on[:, :]),
              reads=[onr], writes=[y_r], dma_key="ystore")

    plan = []
    for h in range(4):
        plan.append(("A", h * 64, 64, h, h * 64, 64, [h]))
    for h in range(4):
        plan.append(("B", 256 + h * 96, 96, 10 + h, 256 + h * 96, 96, [4 + h]))
    for h in range(4):
        plan.append(("C", 640 + h * 64, 64, 4 + h, 640 + h * 64, 64, [8 + h]))
    for g in range(2):
        plan.append(("D", 896 + g * 64, 64, 8 + g, 896 + 2 * g * 64, 128, [12 + 2 * g, 13 + 2 * g]))
    import os
    units = [int(x) for x in os.environ.get("KB_UNITS", ",".join(str(i) for i in range(len(plan)))).split(",")]
    bt_plan = {}
    for ui in units:
        kind = plan[ui][0]
        if kind in ("A", "C"):
            h = ui % 4
            for qb in range(4):
                for m in range(2 if kind == "C" else 1):
                    for i, _e in enumerate(near_entries(kind, qb)):
                        bt_plan[(kind, h, qb, m, i)] = bts.plan(d["btl"][TILE_IDS[(kind, h, qb, i)]], 512)

    loaded = {}

    def load_unit(ui):
        kind, krow, kd, vu, qrow, qd, yus = plan[ui]
        kt, ktr, kkey = KT.next()
        vt, vtr, vkey = VT.next()
        qt, qtr, qkey = QT.next()
        for j in range(4):
            kr0 = kg_row(j, krow)
            S.add("sp", (lambda e, j=j, kr0=kr0: e.dma_start(out=kt[0:kd, j * T:(j + 1) * T], in_=d["kTg"][kr0:kr0 + kd, :])),
                  writes=[ktr], dma_key=kkey)
            if kind == "D":
                S.add("sp", (lambda e, j=j, kr0=kr0: e.dma_start(out=kt[64:128, j * T:(j + 1) * T], in_=d["kTg"][kr0:kr0 + kd, :])),
                      writes=[ktr], dma_key=kkey)
            vr0 = vg_row(j, vu)
            S.add("sp", (lambda e, j=j, vr0=vr0: e.dma_start(out=vt[:, 16 * j:16 * (j + 1), :],
                                                              in_=d["vg"][vr0:vr0 + 128, :].rearrange("p (c d) -> p c d", d=65))),
                  writes=[vtr], dma_key=vkey)
        if kind in ("A", "B"):
            pieces = [(0, qd, qrow)]
        elif kind == "C":
            pieces = [(0, 32, qrow), (32, 32, qrow + 32)]
        else:
            pieces = [(0, 64, qrow), (64, 64, qrow + 64)]
        qtiles = []
        for p0, pn, r0 in pieces:
            if qtiles or True:
                qt, qtr, qkey = (qt, qtr, qkey) if not qtiles else QT.next()
            S.add("pool", (lambda e, qt=qt: e.memset(qt[:, :], 0.0)), writes=[qtr])
            S.add("sp", (lambda e, qt=qt, p0=p0, pn=pn, r0=r0: e.dma_start(out=qt[p0:p0 + pn, :], in_=d["qT"][r0:r0 + pn, :])),
                  writes=[qtr], dma_key=qkey)
            qtiles.append((qt, qtr))
        own = None
        if kind in ("A", "C"):
            kto, ktor, kokey = KTO.next()
            vto, vtor, vokey = VTO.next()
            S.add("sp", lambda e: e.dma_start(out=kto[0:kd, :], in_=d["kT"][krow:krow + kd, :]), writes=[ktor], dma_key=kokey)
            S.add("sp", lambda e: e.dma_start(out=vto[:, :, :], in_=d["v"][vu]), writes=[vtor], dma_key=vokey)
            own = (kto, ktor, vto, vtor)
        loaded[ui] = (kt, ktr, vt, vtr, qtiles, own)

    load_unit(units[0])
    for n_, ui in enumerate(units):
        if n_ + 1 < len(units):
            load_unit(units[n_ + 1])
        kind, krow, kd, vu, qrow, qd, yus = plan[ui]
        kt, ktr, vt, vtr, qtiles, own = loaded.pop(ui)
        h = ui % 4
        dense = [(kt, ktr, vt, vtr, kc, None) for kc in range(64)]

        def near_list(kind, qb, m):
            out = []
            for i, (src, ch) in enumerate(near_entries(kind, qb)):
                mode = ("tile", bt_plan[(kind, h, qb, m, i)])
                if src == "own":
                    out.append((own[0], own[1], own[2], own[3], ch, mode))
                else:
                    out.append((kt, ktr, vt, vtr, ch, mode))
            return out

        for qb in range(4):
            if kind == "A":
                attend(qtiles[0][0], qtiles[0][1], 0, 128, qb, near_list("A", qb, 0),
                       (lambda on, onr, u=yus[0], qb=qb: store_y(u, qb, on, onr)))
            elif kind == "B":
                attend(qtiles[0][0], qtiles[0][1], 0, 128, qb, dense,
                       (lambda on, onr, u=yus[0], qb=qb: store_y(u, qb, on, onr)))
            elif kind == "D":
                for hh in range(2):
                    attend(qtiles[hh][0], qtiles[hh][1], 0, 128, qb, dense,
                           (lambda on, onr, u=yus[hh], qb=qb: store_y(u, qb, on, onr)))
            else:
                def make_vs(qb_, h=h, vt=vt, vtr=vtr):
                    sc, scr, _ = SC.next()
                    S.add("act", (lambda e, sc=sc: e.activation(out=sc[:, :], in_=t5c[:, (h * 4 + qb_) * 64:(h * 4 + qb_ + 1) * 64], func=AF.Exp)),
                          reads=[t5c_r], writes=[scr])
                    vs_, vsr_, _ = VS.next()
                    S.add("dve", (lambda e, sc=sc, vs_=vs_: e.tensor_tensor(
                        out=vs_[:, :, :], in0=vt[:, :, :], in1=sc[:, :].unsqueeze(2).broadcast_to([128, 64, 65]), op=ALU.mult)),
                        reads=[vtr, scr], writes=[vsr_])
                    return vs_, vsr_
                if qb == 0:
                    vs_next = make_vs(0)
                vs, vsr = vs_next
                if qb + 1 < 4:
                    vs_next = make_vs(qb + 1)
                state = {}

                def cont0(on, onr, state=state):
                    state["on0"] = (on, onr)

                def cont1(on1, on1r, state=state, u=yus[0], qb=qb):
                    on0, on0r = state["on0"]
                    yo, yor, _ = ON.next()
                    S.add("dve", (lambda e, yo=yo, on0=on0, on1=on1: e.scalar_tensor_tensor(
                        out=yo[:, :], in0=on1[:, :], scalar=nlam, in1=on0[:, :], op0=ALU.mult, op1=ALU.add)),
                        reads=[on0r, on1r, lt_r], writes=[yor])
                    store_y(u, qb, yo, yor)
                for m in range(2):
                    ch = [(kt, ktr, vs, vsr, kc, None) for kc in range(64)] + near_list("C", qb, m)
                    attend(qtiles[m][0], qtiles[m][1], 0, 128, qb, ch, cont0 if m == 0 else cont1)
    while pending:
        pending.pop(0)()


def build_b():
    nc = bass.Bass("TRN2", target_bir_lowering=False)
    dt = lambda name, shape, dtype, kind: nc.dram_tensor(name, shape, dtype, kind=kind).ap()
    d = {}
    gains = dt("gains", [128, NG], F32, "ExternalInput")
    lconst = dt("lconst", [128, 2], F32, "ExternalInput")
    cmat = dt("cmat", [128, 768], F32, "ExternalInput")
    d["qT"] = dt("qT", [QROWS, T], BF16, "ExternalInput")
    d["kT"] = dt("kT", [KROWS, T], BF16, "ExternalInput")
    d["v"] = dt("v", [NVU, 128, 16, 65], BF16, "ExternalInput")
    d["kTg"] = dt("kTg", [4 * KROWS, T], BF16, "ExternalInput")
    d["vg"] = dt("vg", [4 * NVU * 128, 16 * 65], BF16, "ExternalInput")
    d["btl"] = dt("btl", [NTILES, 128, 512], F32, "ExternalInput")
    d["t5c"] = dt("t5c", [128, 1024], F32, "ExternalInput")
    d["dlam"] = dt("dlam", [64, 128], F32, "ExternalInput")
    d["y"] = dt("y", [16, 64, T], F32, "ExternalOutput")
    d["y_r"] = Res("y")
    with contextlib.ExitStack() as st:
        S = Sched(nc)
        c = Ctx()
        common_setup(c, st, nc, S, gains, lconst, cmat)
        emit_phase_b(c, st, nc, S, d)
        S.add("sp", None, reads=[d["y_r"]])
        S.emit(st)
    return nc


def t5_bucket_np(rel):
    half = 16
    max_exact = 8
    n = np.abs(rel)
    large = max_exact + (np.log(np.maximum(n, 1).astype(np.float32) / max_exact)
                         / np.float32(math.log(128 / max_exact)) * (half - max_exact)).astype(np.int32)
    large = np.minimum(large, half - 1)
    return np.where(rel > 0, half, 0) + np.where(n < max_exact, n, large)


def bias_tiles_host(inp, l, core):
    r = core % 4
    rpb = np.asarray(inp["na_rpb"][l], np.float32)
    t5 = np.asarray(inp["t5_bias"], np.float32)
    btl = np.empty((NTILES, 128, 512), np.float32)
    kk = np.arange(128)[:, None]
    qq = np.arange(512)[None, :]
    NEG = np.float32(-1e30)
    for qb in range(4):
        qbg = 4 * r + qb
        qtok = qbg * 512 + qq
        qr, qc = qtok // 64, qtok % 64
        rs = np.clip(qr - 4, 0, 120)
        cs = np.clip(qc - 8, 0, 48)
        for i, (src, ch) in enumerate(near_entries("A", qb)):
            g = 16 * r + ch if src == "own" else ch
            ktok = g * 128 + kk
            kr, kcol = ktok // 64, ktok % 64
            valid = (kr >= rs) & (kr < rs + 8) & (kcol >= cs) & (kcol < cs + 16)
            dr = np.clip(kr - qr + 7, 0, 14)
            dc = np.clip(kcol - qc + 15, 0, 30)
            flat = np.where(valid, dr * 31 + dc, 15 * 31)
            for h in range(4):
                tab = np.concatenate([rpb[h].ravel(), np.array([NEG], np.float32)])
                btl[TILE_IDS[("A", h, qb, i)]] = tab[flat]
        for i, (src, ch) in enumerate(near_entries("C", qb)):
            g = 16 * r + ch if src == "own" else ch
            if 4 * qbg - 1 <= g <= 4 * qbg + 4:
                bk = t5_bucket_np((g * 128 + kk) - qtok)
                for h in range(4):
                    btl[TILE_IDS[("C", h, qb, i)]] = t5[:, h][bk]
            else:
                for h in range(4):
                    btl[TILE_IDS[("C", h, qb, i)]] = NEG
    t5c = np.zeros((128, 1024), np.float32)
    for qb in range(4):
        qbg = 4 * r + qb
        for kc in range(64):
            for h in range(4):
                if 4 * qbg - 1 <= kc <= 4 * qbg + 4:
                    val = NEG
                else:
                    val = t5[31, h] if kc > 4 * qbg + 3 else t5[15, h]
                t5c[:, (h * 4 + qb) * 64 + kc] = val
    return btl, t5c


_PROGS = {}


def _prog(name):
    if name not in _PROGS:
        _PROGS[name] = {"a": build_a, "b": build_b, "c": build_c, "f": build_fused}[name]()
    return _PROGS[name]


def to_featmajor(xc):
    return np.ascontiguousarray(xc.T.reshape(8, 128, T).transpose(1, 0, 2).reshape(128, 8 * T))


def from_featmajor(xT):
    return xT.reshape(128, 8, T).transpose(1, 0, 2).reshape(D, T).T


def kernel_unfused(**inp):
    x = np.asarray(inp["x"], np.float32).reshape(BATCH * SEQ, D)
    cm = const_mats()
    xTs = [to_featmajor(x[c * T:(c + 1) * T]) for c in range(NCORE)]
    tabs = [rope_tables(c) for c in range(NCORE)]
    cores = list(range(NCORE))
    for l in range(DEPTH):
        H = layer_host(inp, l)
        common = {"gains": H["gains"], "lconst": H["lconst"], "cmat": cm}
        maps = [dict(common, xT=xTs[c], wg=H["ffn1_wg"], wu=H["ffn1_wu"], wd=H["ffn1_wd"], winf=H["winf"],
                     winv=H["winv"], wsm=H["wsm"], tabs=tabs[c]) for c in cores]
        ra = run_bass_kernel_spmd(_prog("a"), maps, core_ids=cores).results
        maps = []
        dl = np.ascontiguousarray(np.broadcast_to(np.asarray(inp["diff_lambda"][l], np.float32).reshape(1, 128), (64, 128)))
        for c in cores:
            b = c // 4
            kTg = np.zeros((4 * KROWS, T), np.asarray(ra[0]["kT"]).dtype)
            vg = np.zeros((4 * NVU * 128, 1040), np.asarray(ra[0]["v"]).dtype)
            for j in range(4):
                kj = np.asarray(ra[4 * b + j]["kT"])
                vj = np.asarray(ra[4 * b + j]["v"]).reshape(NVU * 128, 1040)
                for a, bb in K_PARTS:
                    kTg[4 * a + j * (bb - a): 4 * a + (j + 1) * (bb - a)] = kj[a:bb]
                for a, bb in V_PARTS:
                    vg[(4 * a + j * (bb - a)) * 128: (4 * a + (j + 1) * (bb - a)) * 128] = vj[a * 128:bb * 128]
            btl, t5c = bias_tiles_host(inp, l, c)
            maps.append(dict(common, qT=np.asarray(ra[c]["qT"]), kT=np.asarray(ra[c]["kT"]), v=np.asarray(ra[c]["v"]),
                             kTg=kTg, vg=vg, btl=btl, t5c=t5c, dlam=dl))
        rb = run_bass_kernel_spmd(_prog("b"), maps, core_ids=cores).results
        wout = blk_cols(np.asarray(inp["w_out"][l], np.float32), 128)
        maps = [dict(common, xT=np.asarray(ra[c]["x1T"]), y=np.asarray(rb[c]["y"]), wg=H["ffn2_wg"], wu=H["ffn2_wu"],
                     wd=H["ffn2_wd"], wout=wout) for c in cores]
        rc = run_bass_kernel_spmd(_prog("c"), maps, core_ids=cores).results
        xTs = [np.asarray(rc[c]["xoT"]) for c in cores]
    out = np.concatenate([from_featmajor(xTs[c]) for c in cores], axis=0)
    return np.ascontiguousarray(out.reshape(BATCH, SEQ, D).astype(np.float32))


RG = [[0, 1, 2, 3], [4, 5, 6, 7]]
K_PARTS = [(0, 256), (256, 448), (448, 640), (640, 896), (896, 1024)]
V_PARTS = [(0, 3), (3, 6), (6, 9), (9, 12), (12, 14)]


def kg_row(j, row):
    for a, b in K_PARTS:
        if a <= row < b:
            return 4 * a + j * (b - a) + (row - a)
    raise AssertionError(row)


def vg_row(j, u):
    for a, b in V_PARTS:
        if a <= u < b:
            return (4 * a + j * (b - a) + (u - a)) * 128
    raise AssertionError(u)


def build_fused():
    nc = bass.Bass("TRN2", target_bir_lowering=False)
    dt = lambda name, shape, dtype, kind, **kw: nc.dram_tensor(name, shape, dtype, kind=kind, **kw).ap()
    xT_d = dt("xT", [128, 8 * T], F32, "ExternalInput")
    cmat_d = dt("cmat", [128, 768], F32, "ExternalInput")
    tabs_d = dt("tabs", [4, 128, T], F32, "ExternalInput")
    t5c_d = dt("t5c", [128, 1024], F32, "ExternalInput")
    xo_d = dt("xoT", [128, 8 * T], F32, "ExternalOutput")
    LD = []
    for l in range(DEPTH):
        sfx = "_%d" % l
        d = {}
        d["gains"] = dt("gains" + sfx, [128, NG], F32, "ExternalInput")
        d["lconst"] = dt("lconst" + sfx, [128, 2], F32, "ExternalInput")
        for nm in ("wg1", "wu1", "wg2", "wu2"):
            d[nm] = dt(nm + sfx, [11, 128, 2048], F32, "ExternalInput")
        for nm in ("wd1", "wd2"):
            d[nm] = dt(nm + sfx, [8, 128, 2816], F32, "ExternalInput")
        d["winf"] = dt("winf" + sfx, [7, 128, 2048], F32, "ExternalInput")
        d["winv"] = dt("winv" + sfx, [2, 128, 2560], F32, "ExternalInput")
        d["wsm"] = dt("wsm" + sfx, [128, WSM], F32, "ExternalInput")
        d["wout"] = dt("wout" + sfx, [8, 128, 1024], F32, "ExternalInput")
        d["btl"] = dt("btl" + sfx, [NTILES, 128, 512], F32, "ExternalInput")
        d["dlam"] = dt("dlam" + sfx, [64, 128], F32, "ExternalInput")
        d["qT"] = dt("s_qT" + sfx, [QROWS, T], BF16, "Internal")
        d["kT"] = dt("s_kT" + sfx, [KROWS, T], BF16, "Internal")
        d["v"] = dt("s_v" + sfx, [NVU, 128, 16, 65], BF16, "Internal")
        d["kTg"] = dt("s_kTg" + sfx, [4 * KROWS, T], BF16, "Internal", addr_space="Local")
        d["vg"] = dt("s_vg" + sfx, [4 * NVU * 128, 16 * 65], BF16, "Internal", addr_space="Local")
        d["y"] = dt("s_y" + sfx, [16, 64, T], F32, "Internal")
        LD.append(d)
    with contextlib.ExitStack() as top:
        ss = SemState(top)
        c = Ctx()
        c.xT = _sb(top, nc, "xT", [128, 8 * T], F32)

        def fresh_x():
            c.x_r = [Res("x%d" % t) for t in range(NT)]

        with contextlib.ExitStack() as st:
            S = Sched(nc, ss)
            fresh_x()
            for k in range(8):
                S.add("sp", (lambda e, k=k: e.dma_start(out=c.xT[:, k * T:(k + 1) * T], in_=xT_d[:, k * T:(k + 1) * T])),
                      writes=c.x_r, dma_key="xT")
            S.drain()
            S.emit(st)
        for l in range(DEPTH):
            d = LD[l]
            with contextlib.ExitStack() as st:
                S = Sched(nc, ss)
                fresh_x()
                common_setup(c, st, nc, S, d["gains"], d["lconst"], cmat_d)
                da = dict(wg=d["wg1"], wu=d["wu1"], wd=d["wd1"], winf=d["winf"], winv=d["winv"], wsm=d["wsm"], tabs=tabs_d,
                          qT=d["qT"], kT=d["kT"], v=d["v"], q_r=Res("q"), k_r=Res("k"), v_r=Res("v"))
                emit_phase_a(c, st, nc, S, da)
                S.drain()
                S.emit(st)
            with contextlib.ExitStack() as st:
                S = Sched(nc, ss)
                v2 = d["v"].rearrange("u p c d -> (u p) (c d)")
                for a, b in K_PARTS:
                    S.add("pool", (lambda e, d=d, a=a, b=b: e.collective_compute(
                        "AllGather", ALU.bypass, replica_groups=RG, ins=[d["kT"][a:b, :]], outs=[d["kTg"][4 * a:4 * b, :]])),
                        dma_key="cc", inc=1)
                for a, b in V_PARTS:
                    S.add("pool", (lambda e, d=d, a=a, b=b, v2=v2: e.collective_compute(
                        "AllGather", ALU.bypass, replica_groups=RG, ins=[v2[a * 128:b * 128, :]],
                        outs=[d["vg"][4 * a * 128:4 * b * 128, :]])), dma_key="cc", inc=1)
                S.drain()
                S.emit(st)
            with contextlib.ExitStack() as st:
                S = Sched(nc, ss)
                common_setup(c, st, nc, S, d["gains"], d["lconst"], cmat_d)
                db = dict(qT=d["qT"], kT=d["kT"], v=d["v"], kTg=d["kTg"], vg=d["vg"], btl=d["btl"], t5c=t5c_d,
                          dlam=d["dlam"], y=d["y"], y_r=Res("y"))
                emit_phase_b(c, st, nc, S, db)
                S.drain()
                S.emit(st)
            with contextlib.ExitStack() as st:
                S = Sched(nc, ss)
                fresh_x()
                common_setup(c, st, nc, S, d["gains"], d["lconst"], cmat_d)
                dc = dict(wg=d["wg2"], wu=d["wu2"], wd=d["wd2"], wout=d["wout"], y=d["y"], y_r=Res("y"))
                emit_phase_c(c, st, nc, S, dc)
                S.drain()
                S.emit(st)
        with contextlib.ExitStack() as st:
            S = Sched(nc, ss)
            fresh_x()
            xo_r = Res("xo")
            store_x(c, S, xo_d, xo_r)
            S.add("sp", None, reads=[xo_r])
            S.drain()
            S.emit(st)
    return nc


def kernel(**inp):
    x = np.asarray(inp["x"], np.float32).reshape(BATCH * SEQ, D)
    cm = const_mats()
    cores = list(range(NCORE))
    shared = {}
    for l in range(DEPTH):
        H = layer_host(inp, l)
        sfx = "_%d" % l
        shared.update({"gains" + sfx: H["gains"], "lconst" + sfx: H["lconst"], "wg1" + sfx: H["ffn1_wg"], "wu1" + sfx: H["ffn1_wu"],
                       "wd1" + sfx: H["ffn1_wd"], "wg2" + sfx: H["ffn2_wg"], "wu2" + sfx: H["ffn2_wu"], "wd2" + sfx: H["ffn2_wd"],
                       "winf" + sfx: H["winf"], "winv" + sfx: H["winv"], "wsm" + sfx: H["wsm"],
                       "wout" + sfx: blk_cols(np.asarray(inp["w_out"][l], np.float32), 128),
                       "dlam" + sfx: np.ascontiguousarray(np.broadcast_to(
                           np.asarray(inp["diff_lambda"][l], np.float32).reshape(1, 128), (64, 128)))})
    maps = []
    for c in cores:
        m = dict(shared, xT=to_featmajor(x[c * T:(c + 1) * T]), cmat=cm, tabs=rope_tables(c))
        for l in range(DEPTH):
            btl, t5c = bias_tiles_host(inp, l, c)
            m["btl_%d" % l] = btl
            m["t5c"] = t5c
        maps.append(m)
    res = run_bass_kernel_spmd(_prog("f"), maps, core_ids=cores).results
    out = np.concatenate([from_featmajor(np.asarray(res[c]["xoT"])) for c in cores], axis=0)
    return np.ascontiguousarray(out.reshape(BATCH, SEQ, D).astype(np.float32))
```

```python
import contextlib
import math
import numpy as np
import ml_dtypes
import concourse.bass as bass
import concourse.mybir as mybir
from concourse.bass_utils import run_bass_kernel_spmd

F32 = mybir.dt.float32
BF16 = mybir.dt.bfloat16
AF = mybir.ActivationFunctionType
ALU = mybir.AluOpType

_UID = [0]


def _uid():
    _UID[0] += 1
    return _UID[0]

ENGS = ("pe", "act", "dve", "pool", "sp")
EPOCH = 16000
SAME_ENGINE_SYNC = True


class Res:
    __slots__ = ("name", "writer", "readers")

    def __init__(self, name=""):
        self.name = name
        self.writer = None
        self.readers = {}


class Op:
    __slots__ = ("eng", "fn", "deps", "dma_key", "signaled", "count", "semid", "idx", "inc")


class SemState:
    def __init__(self, stack):
        self.stack = stack
        self.cnt = {e: 0 for e in ENGS}
        self.dcnt = {}
        self.sems = {}


class Sched:
    def __init__(self, nc, semstate=None):
        self.nc = nc
        self.ops = []
        self.by_eng = {e: [] for e in ENGS}
        self.semstate = semstate
        self.last_dma = {}

    def drain(self):
        op = self.add("sp", None)
        op.deps = list(self.last_dma.values())
        return op

    def add(self, eng, fn, reads=(), writes=(), dma_key=None, inc=16):
        op = Op()
        op.inc = inc
        op.eng = eng
        op.fn = fn
        op.dma_key = dma_key
        op.signaled = dma_key is not None
        op.count = 0
        op.semid = None
        op.idx = len(self.ops)
        deps = {}
        for r in reads:
            if r.writer is not None:
                deps[r.writer.idx] = r.writer
        for w in writes:
            if w.writer is not None:
                deps[w.writer.idx] = w.writer
            for rd in w.readers.values():
                deps[rd.idx] = rd
        dl = []
        for d in deps.values():
            if d.dma_key is not None or dma_key is not None:
                need = True
            elif d.eng == eng:
                need = (eng != "pe") and SAME_ENGINE_SYNC
            else:
                need = True
            if need:
                d.signaled = True
                dl.append(d)
        op.deps = dl
        for r in reads:
            key = eng if dma_key is None else ("dma", op.idx)
            r.readers[key] = op
        for w in writes:
            w.writer = op
            w.readers = {}
        self.ops.append(op)
        self.by_eng[eng].append(op)
        if dma_key is not None:
            self.last_dma[dma_key] = op
        return op

    def emit(self, stack):
        nc = self.nc
        ss = self.semstate if self.semstate is not None else SemState(stack)
        cnt = ss.cnt
        dcnt = ss.dcnt
        for op in self.ops:
            if op.dma_key is not None:
                dcnt[op.dma_key] = dcnt.get(op.dma_key, 0) + op.inc
                op.count = dcnt[op.dma_key]
                op.semid = ("d", op.dma_key)
            elif op.signaled:
                c = cnt[op.eng]
                cnt[op.eng] = c + 1
                op.semid = ("e", op.eng, c // EPOCH)
                op.count = (c % EPOCH) + 1
        self.check(ss)
        sems = ss.sems
        for op in self.ops:
            if op.semid is not None and op.semid not in sems:
                sems[op.semid] = ss.stack.enter_context(nc.semaphore("s%d" % len(sems)))
        self.nsems = len(sems)
        block = stack.enter_context(nc.Block())
        engobj = {"pe": block.tensor, "act": block.scalar, "dve": block.vector,
                  "pool": block.gpsimd, "sp": block.sync}

        def make(ename):
            ops = self.by_eng[ename]

            def body(e):
                waited = {}
                wepoch = {}
                for op in ops:
                    for d in op.deps:
                        sid = d.semid
                        c = d.count
                        if sid[0] == "e" and wepoch.get(sid[1], -1) > sid[2]:
                            continue
                        if waited.get(sid, 0) >= c:
                            continue
                        e.wait_ge(sems[sid], c)
                        waited[sid] = c
                        if sid[0] == "e":
                            wepoch[sid[1]] = max(wepoch.get(sid[1], -1), sid[2])
                    if op.fn is None:
                        continue
                    ins = op.fn(e)
                    if op.dma_key is not None and op.inc == 1:
                        ins.then_inc(sems[op.semid])
                    elif op.dma_key is not None:
                        ins.then_inc(sems[op.semid], 16)
                    elif op.signaled:
                        ins.then_inc(sems[op.semid], 1)
            return body

        for ename in ENGS:
            if self.by_eng[ename]:
                engobj[ename](make(ename))


def _sched_check(self, ss):
    val = dict(getattr(ss, "val", {}))
    pos = {e: 0 for e in ENGS}
    progress = True
    while progress:
        progress = False
        for e in ENGS:
            ops = self.by_eng[e]
            while pos[e] < len(ops):
                op = ops[pos[e]]
                if all(val.get(d.semid, 0) >= d.count and (d.semid[0] != "e" or True) for d in op.deps):
                    if op.semid is not None:
                        if op.semid[0] == "d":
                            val[op.semid] = val.get(op.semid, 0) + op.inc
                        else:
                            val[op.semid] = val.get(op.semid, 0) + 1
                        assert val[op.semid] == op.count or op.semid[0] == "d", (op.semid, val[op.semid], op.count)
                    pos[e] += 1
                    progress = True
                else:
                    break
    stuck = {e: pos[e] for e in ENGS if pos[e] < len(self.by_eng[e])}
    assert not stuck, "deadlock in schedule: %r" % stuck
    ss.val = val
    print("sched ok: ops=%d per-eng=%r" % (len(self.ops), {e: len(self.by_eng[e]) for e in ENGS}))


Sched.check = _sched_check


class Ring:
    def __init__(self, stack, nc, name, n, shape, dtype, psum=False):
        self.bufs = []
        for i in range(n):
            nm = "r_%s%d" % (name, i)
            tn = "%s_%d" % (nm, _uid())
            if psum:
                t = stack.enter_context(nc.psum_tensor(tn, shape, dtype))
            else:
                t = stack.enter_context(nc.sbuf_tensor(tn, shape, dtype))
            self.bufs.append((t, Res(nm), nm))
        self.i = 0

    def next(self):
        b = self.bufs[self.i % len(self.bufs)]
        self.i += 1
        return b


D = 1024
SEQ = 8192
BATCH = 2
DEPTH = 2
DFF = 2816
NCORE = 8
T = 2048
NT = 4
TS = 512
EPS = 1e-6
NG = 59
G_FFN1, G_MIX, G_NAQ, G_NAK, G_QLAT, G_KVLAT, G_MQ, G_MK, G_DQ, G_DK, G_GQ, G_GK = \
    0, 8, 16, 17, 18, 20, 21, 22, 23, 24, 25, 26
G_BETA, G_FFN2, G_FIN = 27, 43, 51
WS_UQ, WS_KN, WS_V, WS_KR = 0, 768, 1152, 1408
WSM = 2176
C_ONES, C_BD64, C_BD32, C_RB, C_RD, C_ID = 0, 1, 2, 3, 4, 5
QROWS = 1152
KROWS = 1024
NVU = 14


def lambda_init(l):
    return 0.8 - 0.6 * math.exp(-0.3 * l)


class Ctx:
    pass


def _sb(st, nc, name, shape, dt):
    return st.enter_context(nc.sbuf_tensor("sb_%s_%d" % (name, _uid()), shape, dt))


class WStream:
    def __init__(self, S, ring, pf):
        self.S = S
        self.ring = ring
        self.pf = pf
        self.items = []
        self.issued = 0
        self.handles = []

    def plan(self, src_ap, ncols, part=128):
        self.items.append((src_ap, ncols, part))
        return len(self.items) - 1

    def get(self, i):
        while self.issued < min(len(self.items), i + 1 + self.pf):
            src, ncols, part = self.items[self.issued]
            buf, res, nm = self.ring.next()
            self.S.add("pool", (lambda e, buf=buf, src=src, ncols=ncols, part=part:
                                e.dma_start(out=buf[0:part, 0:ncols], in_=src)),
                       writes=[res], dma_key=nm)
            self.handles.append((buf, res))
            self.issued += 1
        return self.handles[i]


def rms_feat(c, srcs, P, ones_idx, n, gain_cols, outs):
    S = c.S
    ones = c.cmat[0:P, ones_idx * 128: ones_idx * 128 + P]
    ps, psr, _ = c.PS.next()
    last = len(srcs) - 1
    sc = float(n) ** -0.5
    for i, (src, sres) in enumerate(srcs):
        sq, sqr, _ = c.SQ.next()
        S.add("act", (lambda e, sq=sq, src=src: e.activation(out=sq[0:P, :], in_=src, func=AF.Square, scale=sc)),
              reads=[sres], writes=[sqr])
        S.add("pe", (lambda e, sq=sq, i=i: e.matmul(ps[0:P, :], ones, sq[0:P, :], start=(i == 0), stop=(i == last))),
              reads=[sqr, c.cmat_r], writes=[psr])
    rs, rsr, _ = c.RS.next()
    S.add("act", lambda e: e.activation(out=rs[0:P, :], in_=ps[0:P, :], func=AF.Ln, bias=c.eps[0:P, 0:1]),
          reads=[psr, c.eps_r], writes=[rsr])
    S.add("act", lambda e: e.activation(out=rs[0:P, :], in_=rs[0:P, :], func=AF.Exp, scale=-0.5),
          reads=[rsr], writes=[rsr])
    for i, ((src, sres), (out, ores)) in enumerate(zip(srcs, outs)):
        g = c.gains[0:P, gain_cols[i]:gain_cols[i] + 1]
        S.add("dve", (lambda e, src=src, out=out, g=g: e.scalar_tensor_tensor(
            out=out, in0=src, scalar=g, in1=rs[0:P, :], op0=ALU.mult, op1=ALU.mult)),
            reads=[sres, rsr, c.gains_r], writes=[ores])


def rope_feat(c, qn, qnr, P, r_idx, cos_ap, sin_ap, tab_r, out, outr):
    S = c.S
    R = c.cmat[0:P, r_idx * 128: r_idx * 128 + P]
    ps, psr, _ = c.PS.next()
    S.add("pe", lambda e: e.matmul(ps[0:P, :], R, qn, start=True, stop=True), reads=[qnr, c.cmat_r], writes=[psr])
    t1, t1r, _ = c.TF.next()
    t2, t2r, _ = c.TF.next()
    S.add("pool", lambda e: e.tensor_tensor(out=t1[0:P, :], in0=qn, in1=cos_ap, op=ALU.mult), reads=[qnr, tab_r], writes=[t1r])
    S.add("dve", lambda e: e.tensor_tensor(out=t2[0:P, :], in0=ps[0:P, :], in1=sin_ap, op=ALU.mult), reads=[psr, tab_r], writes=[t2r])
    S.add("pool", lambda e: e.tensor_tensor(out=out, in0=t1[0:P, :], in1=t2[0:P, :], op=ALU.add), reads=[t1r, t2r], writes=[outr])


def norm_x_tile(c, t, gcol0):
    hT, hTr, _ = c.HT.next()
    srcs = [(c.xT[:, k * T + t * TS: k * T + (t + 1) * TS], c.x_r[t]) for k in range(8)]
    outs = [(hT[:, k * TS:(k + 1) * TS], hTr) for k in range(8)]
    rms_feat(c, srcs, 128, C_ONES, D, [gcol0 + k for k in range(8)], outs)
    return hT, hTr


def ffn_tile(c, t, hT, hTr, ws, base, mid=None):
    S = c.S
    for b in range(11):
        wg, wgr = ws.get(base + 2 * b)
        wu, wur = ws.get(base + 2 * b + 1)
        for jj in range(2):
            j = 2 * b + jj
            pg, pgr, _ = c.PS.next()
            pu, pur, _ = c.PS.next()
            for k in range(8):
                S.add("pe", (lambda e, k=k, pg=pg, wg=wg, jj=jj: e.matmul(
                    pg[:, :], wg[:, k * 256 + jj * 128: k * 256 + jj * 128 + 128], hT[:, k * TS:(k + 1) * TS],
                    start=(k == 0), stop=(k == 7))), reads=[wgr, hTr], writes=[pgr])
            for k in range(8):
                S.add("pe", (lambda e, k=k, pu=pu, wu=wu, jj=jj: e.matmul(
                    pu[:, :], wu[:, k * 256 + jj * 128: k * 256 + jj * 128 + 128], hT[:, k * TS:(k + 1) * TS],
                    start=(k == 0), stop=(k == 7))), reads=[wur, hTr], writes=[pur])
            sg, sgr, _ = c.TF.next()
            S.add("act", (lambda e, sg=sg, pg=pg: e.activation(out=sg[:, :], in_=pg[:, :], func=AF.Silu)),
                  reads=[pgr], writes=[sgr])
            S.add("dve", (lambda e, sg=sg, pu=pu, j=j: e.tensor_tensor(
                out=c.actT[:, j * TS:(j + 1) * TS], in0=sg[:, :], in1=pu[:, :], op=ALU.mult)),
                reads=[sgr, pur], writes=[c.act_r[j]])
    if mid is not None:
        mid()
    for m in range(8):
        wd, wdr = ws.get(base + 22 + m)
        pd, pdr, _ = c.PS.next()
        for j in range(22):
            S.add("pe", (lambda e, j=j, pd=pd, wd=wd: e.matmul(
                pd[:, :], wd[:, j * 128:(j + 1) * 128], c.actT[:, j * TS:(j + 1) * TS],
                start=(j == 0), stop=(j == 21))), reads=[wdr, c.act_r[j]], writes=[pdr])
        xs = c.xT[:, m * T + t * TS: m * T + (t + 1) * TS]
        S.add("dve", (lambda e, pd=pd, xs=xs: e.scalar_tensor_tensor(
            out=xs, in0=pd[:, :], scalar=0.5, in1=xs, op0=ALU.mult, op1=ALU.add)),
            reads=[pdr, c.x_r[t]], writes=[c.x_r[t]])


def plan_ffn(ws, wg, wu, wd):
    base = len(ws.items)
    for b in range(11):
        ws.plan(wg[b], 2048)
        ws.plan(wu[b], 2048)
    for m in range(8):
        ws.plan(wd[m], 2816)
    return base


def common_setup(c, st, nc, S, gains_d, lconst_d, cmat_d):
    c.S = S
    c.nc = nc
    c.gains = _sb(st, nc, "gains", [128, NG], F32)
    c.gains_r = Res("gains")
    c.lconst = _sb(st, nc, "lconst", [128, 2], F32)
    c.lconst_r = Res("lconst")
    c.cmat = _sb(st, nc, "cmat", [128, 6 * 128], BF16)
    c.cmat_r = Res("cmat")
    c.eps = _sb(st, nc, "eps", [128, 1], F32)
    c.eps_r = Res("eps")
    S.add("sp", lambda e: e.dma_start(out=c.gains[:, :], in_=gains_d), writes=[c.gains_r], dma_key="gains")
    S.add("sp", lambda e: e.dma_start(out=c.lconst[:, :], in_=lconst_d), writes=[c.lconst_r], dma_key="lconst")
    S.add("pool", lambda e: e.dma_start(out=c.cmat[:, :], in_=cmat_d), writes=[c.cmat_r], dma_key="cmat")
    S.add("dve", lambda e: e.memset(c.eps[:, :], EPS), writes=[c.eps_r])


def load_x(c, st, nc, S, x_d):
    c.xT = _sb(st, nc, "xT", [128, 8 * T], F32)
    c.x_r = [Res("x%d" % t) for t in range(NT)]
    for k in range(8):
        S.add("sp", (lambda e, k=k: e.dma_start(out=c.xT[:, k * T:(k + 1) * T], in_=x_d[:, k * T:(k + 1) * T])),
              writes=c.x_r, dma_key="xT")


def store_x(c, S, xo_d, xo_r):
    for k in range(8):
        S.add("sp", (lambda e, k=k: e.dma_start(out=xo_d[:, k * T:(k + 1) * T], in_=c.xT[:, k * T:(k + 1) * T])),
              reads=c.x_r, writes=[xo_r], dma_key="xT")


def emit_phase_a(c, st, nc, S, d):
    c.PS = Ring(st, nc, "ps", 8, [128, 512], F32, psum=True)
    c.SQ = Ring(st, nc, "sq", 3, [128, 512], BF16)
    c.RS = Ring(st, nc, "rs", 2, [128, 512], F32)
    c.TF = Ring(st, nc, "tf", 4, [128, 512], F32)
    c.HT = Ring(st, nc, "hT", 2, [128, 8 * TS], BF16)
    c.actT = _sb(st, nc, "actT", [128, 22 * TS], BF16)
    c.act_r = [Res("act%d" % j) for j in range(22)]
    wring = Ring(st, nc, "wb", 6, [128, 2816], BF16)
    ws = WStream(S, wring, 4)
    wsm = _sb(st, nc, "wsm", [128, WSM], BF16)
    wsm_r = Res("wsm")
    S.add("pool", lambda e: e.dma_start(out=wsm[:, :], in_=d["wsm"]), writes=[wsm_r], dma_key="wsm")
    tabs = _sb(st, nc, "tabs", [128, 4 * T], F32)
    tab_r = Res("tabs")
    for i in range(4):
        S.add("sp", (lambda e, i=i: e.dma_start(out=tabs[:, i * T:(i + 1) * T], in_=d["tabs"][i])),
              writes=[tab_r], dma_key="tabs")
    for col, sc in ((G_NAQ, 64 ** -0.5), (G_MQ, 96 ** -0.5), (G_DQ, 32 ** -0.5), (G_GQ, 64 ** -0.5)):
        S.add("dve", (lambda e, col=col, sc=sc: e.tensor_scalar(
            out=c.gains[:, col:col + 1], in0=c.gains[:, col:col + 1], scalar1=float(sc), scalar2=None, op0=ALU.mult)),
            reads=[c.gains_r], writes=[c.gains_r])

    bases_ffn = [plan_ffn(ws, d["wg"], d["wu"], d["wd"]) for t in range(NT)]
    bases_in = []
    for t in range(NT):
        bases_in.append(len(ws.items))
        for b in range(7):
            ws.plan(d["winf"][b], 2048)
        for vb in range(2):
            ws.plan(d["winv"][vb], 2560)

    import os
    n1 = int(os.environ.get("KA_N1", NT))
    n2 = int(os.environ.get("KA_N2", NT))
    if n1 > 0:
        hT, hTr = norm_x_tile(c, 0, G_FFN1)
    for t in range(n1):
        nxt = {}

        def mid(t=t, nxt=nxt):
            if t + 1 < n1:
                nxt["h"] = norm_x_tile(c, t + 1, G_FFN1)
        ffn_tile(c, t, hT, hTr, ws, bases_ffn[t], mid)
        if t + 1 < n1:
            hT, hTr = nxt["h"]

    STG = Ring(st, nc, "stg", 6, [128, 512], BF16)
    QN = Ring(st, nc, "qn", 3, [128, 512], BF16)
    cqn = _sb(st, nc, "cqn", [128, 2 * TS], BF16)
    cqn_r = Res("cqn")
    ckvn = _sb(st, nc, "ckvn", [128, TS], BF16)
    ckvn_r = Res("ckvn")
    vst = _sb(st, nc, "vst", [128, 4, NVU, 65], BF16)
    vst_r = Res("vst")
    S.add("pool", lambda e: e.memset(vst[:, :, :, :], 1.0), writes=[vst_r])
    q_r, k_r, v_r = d["q_r"], d["k_r"], d["v_r"]

    def store_rows(dst, dres, row0, P, t, src, sres, key):
        S.add("sp", lambda e: e.dma_start(out=dst[row0:row0 + P, t * TS:(t + 1) * TS], in_=src),
              reads=[sres], writes=[dres], dma_key=key)

    def stage2_tile(t, hT, hTr, mid=None):

        def proj_chunk(wb, wbr, jj):
            p, pr, _ = c.PS.next()
            for k in range(8):
                S.add("pe", (lambda e, k=k: e.matmul(
                    p[:, :], wb[:, k * 256 + jj * 128: k * 256 + jj * 128 + 128], hT[:, k * TS:(k + 1) * TS],
                    start=(k == 0), stop=(k == 7))), reads=[wbr, hTr], writes=[pr])
            return p, pr

        def simple_head_chunk(wb, wbr, jj, ones_idx, n, gcol, dst, dres, row0, rope):
            p, pr = proj_chunk(wb, wbr, jj)
            if rope:
                qn, qnr, _ = QN.next()
                rms_feat(c, [(p[:, :], pr)], 128, ones_idx, n, [gcol], [(qn[:, :], qnr)])
                sg, sgr, key = STG.next()
                rope_feat(c, qn[:, :], qnr, 128, C_RD, tabs[:, 2 * T + t * TS: 2 * T + (t + 1) * TS],
                          tabs[:, 3 * T + t * TS: 3 * T + (t + 1) * TS], tab_r, sg[:, :], sgr)
            else:
                sg, sgr, key = STG.next()
                rms_feat(c, [(p[:, :], pr)], 128, ones_idx, n, [gcol], [(sg[:, :], sgr)])
            store_rows(dst, dres, row0, 128, t, sg[:, :], sgr, key)

        b0 = bases_in[t]
        wb, wbr = ws.get(b0 + 0)
        for jj in range(2):
            simple_head_chunk(wb, wbr, jj, C_BD64, 64, G_NAQ, d["qT"], q_r, 0 + jj * 128, False)
        wb, wbr = ws.get(b0 + 1)
        for jj in range(2):
            simple_head_chunk(wb, wbr, jj, C_BD64, 64, G_NAK, d["kT"], k_r, 0 + jj * 128, False)
        wb, wbr = ws.get(b0 + 2)
        p0, p0r = proj_chunk(wb, wbr, 0)
        p1, p1r = proj_chunk(wb, wbr, 1)
        rms_feat(c, [(p0[:, :], p0r), (p1[:, :], p1r)], 128, C_ONES, 256, [G_QLAT, G_QLAT + 1],
                 [(cqn[:, 0:TS], cqn_r), (cqn[:, TS:2 * TS], cqn_r)])
        for h in range(4):
            p, pr, _ = c.PS.next()
            for k in range(2):
                S.add("pe", (lambda e, k=k, h=h, p=p: e.matmul(
                    p[0:96, :], wsm[:, WS_UQ + k * 384 + h * 96: WS_UQ + k * 384 + h * 96 + 96],
                    cqn[:, k * TS:(k + 1) * TS], start=(k == 0), stop=(k == 1))),
                    reads=[wsm_r, cqn_r], writes=[pr])
            qn, qnr, _ = QN.next()
            rms_feat(c, [(p[0:96, :], pr)], 96, C_ONES, 96, [G_MQ], [(qn[0:96, :], qnr)])
            sg, sgr, key = STG.next()
            rope_feat(c, qn[0:96, :], qnr, 96, C_RB, tabs[0:96, 0 * T + t * TS: 0 * T + (t + 1) * TS],
                      tabs[0:96, 1 * T + t * TS: 1 * T + (t + 1) * TS], tab_r, sg[0:96, :], sgr)
            store_rows(d["qT"], q_r, 256 + h * 96, 96, t, sg[0:96, :], sgr, key)
        wb, wbr = ws.get(b0 + 3)
        p, pr = proj_chunk(wb, wbr, 0)
        rms_feat(c, [(p[:, :], pr)], 128, C_ONES, 128, [G_KVLAT], [(ckvn[:, :], ckvn_r)])
        for h in range(4):
            p, pr, _ = c.PS.next()
            S.add("pe", (lambda e, h=h, p=p: e.matmul(
                p[0:96, :], wsm[:, WS_KN + h * 96: WS_KN + h * 96 + 96], ckvn[:, :], start=True, stop=False)),
                reads=[wsm_r, ckvn_r], writes=[pr])
            for k in range(8):
                S.add("pe", (lambda e, k=k, p=p: e.matmul(
                    p[0:96, :], wsm[:, WS_KR + k * 96: WS_KR + k * 96 + 96], hT[:, k * TS:(k + 1) * TS],
                    start=False, stop=(k == 7))), reads=[wsm_r, hTr], writes=[pr])
            qn, qnr, _ = QN.next()
            rms_feat(c, [(p[0:96, :], pr)], 96, C_ONES, 96, [G_MK], [(qn[0:96, :], qnr)])
            sg, sgr, key = STG.next()
            rope_feat(c, qn[0:96, :], qnr, 96, C_RB, tabs[0:96, 0 * T + t * TS: 0 * T + (t + 1) * TS],
                      tabs[0:96, 1 * T + t * TS: 1 * T + (t + 1) * TS], tab_r, sg[0:96, :], sgr)
            store_rows(d["kT"], k_r, 256 + h * 96, 96, t, sg[0:96, :], sgr, key)
        for s in range(4):
            p, pr, _ = c.PS.next()
            S.add("pe", (lambda e, s=s, p=p: e.matmul(
                p[:, 0:256], ckvn[:, s * 128:(s + 1) * 128], wsm[:, WS_V:WS_V + 256], start=True, stop=True)),
                reads=[wsm_r, ckvn_r], writes=[pr])
            S.add("act", (lambda e, s=s, p=p: e.activation(out=vst[:, s, 10:14, 0:64], in_=p[:, 0:256].rearrange("p (u d) -> p u d", d=64), func=AF.Copy)),
                  reads=[pr], writes=[vst_r])
        simple_head_chunk(wb, wbr, 1, C_BD64, 64, G_GK, d["kT"], k_r, 896, True)
        wb, wbr = ws.get(b0 + 4)
        for jj in range(2):
            simple_head_chunk(wb, wbr, jj, C_BD32, 32, G_DQ, d["qT"], q_r, 640 + jj * 128, False)
        wb, wbr = ws.get(b0 + 5)
        for jj in range(2):
            simple_head_chunk(wb, wbr, jj, C_BD32, 32, G_DK, d["kT"], k_r, 640 + jj * 128, False)
        wb, wbr = ws.get(b0 + 6)
        for jj in range(2):
            simple_head_chunk(wb, wbr, jj, C_BD64, 64, G_GQ, d["qT"], q_r, 896 + jj * 128, True)
        if mid is not None:
            mid()
        for vb in range(2):
            wv, wvr = ws.get(b0 + 7 + vb)
            for s in range(4):
                p, pr, _ = c.PS.next()
                for k in range(8):
                    S.add("pe", (lambda e, k=k, s=s, p=p, wv=wv: e.matmul(
                        p[:, 0:320], hT[:, k * TS + s * 128: k * TS + (s + 1) * 128], wv[:, k * 320:(k + 1) * 320],
                        start=(k == 0), stop=(k == 7))), reads=[wvr, hTr], writes=[pr])
                S.add("dve", (lambda e, s=s, p=p, vb=vb: e.tensor_copy(out=vst[:, s, vb * 5:(vb + 1) * 5, 0:64], in_=p[:, 0:320].rearrange("p (u d) -> p u d", d=64))),
                      reads=[pr], writes=[vst_r])
        for u in range(NVU):
            S.add("sp", (lambda e, u=u: e.dma_start(out=d["v"][u, :, t * 4:(t + 1) * 4, :], in_=vst[:, :, u, :])),
                  reads=[vst_r], writes=[v_r], dma_key="vst")

    if n2 > 0:
        hT, hTr = norm_x_tile(c, 0, G_MIX)
    for t in range(n2):
        nxt = {}

        def mid2(t=t, nxt=nxt):
            if t + 1 < n2:
                nxt["h"] = norm_x_tile(c, t + 1, G_MIX)
        stage2_tile(t, hT, hTr, mid2)
        if t + 1 < n2:
            hT, hTr = nxt["h"]


def build_a():
    nc = bass.Bass("TRN2", target_bir_lowering=False)
    dt = lambda name, shape, dtype, kind: nc.dram_tensor(name, shape, dtype, kind=kind).ap()
    d = {}
    xT = dt("xT", [128, 8 * T], F32, "ExternalInput")
    gains = dt("gains", [128, NG], F32, "ExternalInput")
    lconst = dt("lconst", [128, 2], F32, "ExternalInput")
    cmat = dt("cmat", [128, 768], F32, "ExternalInput")
    d["wg"] = dt("wg", [11, 128, 2048], F32, "ExternalInput")
    d["wu"] = dt("wu", [11, 128, 2048], F32, "ExternalInput")
    d["wd"] = dt("wd", [8, 128, 2816], F32, "ExternalInput")
    d["winf"] = dt("winf", [7, 128, 2048], F32, "ExternalInput")
    d["winv"] = dt("winv", [2, 128, 2560], F32, "ExternalInput")
    d["wsm"] = dt("wsm", [128, WSM], F32, "ExternalInput")
    d["tabs"] = dt("tabs", [4, 128, T], F32, "ExternalInput")
    xo = dt("x1T", [128, 8 * T], F32, "ExternalOutput")
    d["qT"] = dt("qT", [QROWS, T], BF16, "ExternalOutput")
    d["kT"] = dt("kT", [KROWS, T], BF16, "ExternalOutput")
    d["v"] = dt("v", [NVU, 128, 16, 65], BF16, "ExternalOutput")
    d["q_r"], d["k_r"], d["v_r"] = Res("qT"), Res("kT"), Res("v")
    xo_r = Res("xo")
    with contextlib.ExitStack() as st:
        S = Sched(nc)
        c = Ctx()
        common_setup(c, st, nc, S, gains, lconst, cmat)
        load_x(c, st, nc, S, xT)
        emit_phase_a(c, st, nc, S, d)
        store_x(c, S, xo, xo_r)
        S.add("sp", None, reads=[xo_r, d["q_r"], d["k_r"], d["v_r"]])
        S.emit(st)
    return nc


def blk_cols(W, cpb):
    K, N = W.shape
    kc = K // 128
    nb = N // cpb
    return np.ascontiguousarray(W.reshape(kc, 128, nb, cpb).transpose(2, 1, 0, 3).reshape(nb, 128, kc * cpb))


def const_mats():
    m = np.zeros((6, 128, 128), np.float32)
    m[C_ONES] = 1.0
    for b in range(2):
        m[C_BD64, b * 64:(b + 1) * 64, b * 64:(b + 1) * 64] = 1.0
    for b in range(4):
        m[C_BD32, b * 32:(b + 1) * 32, b * 32:(b + 1) * 32] = 1.0
    for i in range(16):
        m[C_RB, 80 + i, 64 + i] = -1.0
        m[C_RB, 64 + i, 80 + i] = 1.0
    for hb in (0, 64):
        for sb in (0, 32):
            for i in range(16):
                a = hb + sb + i
                b = a + 16
                m[C_RD, b, a] = -1.0
                m[C_RD, a, b] = 1.0
    m[C_ID] = np.eye(128, dtype=np.float32)
    return np.ascontiguousarray(m.transpose(1, 0, 2).reshape(128, 768))


def rope_tables(core):
    r = core % 4
    t = np.arange(r * T, (r + 1) * T)
    inv = np.exp(-math.log(10000.0) * np.arange(0, 32, 2, dtype=np.float32) / 32).astype(np.float32)
    a_seq = t.astype(np.float32)[:, None] * inv[None, :]
    a_row = (t // 64).astype(np.float32)[:, None] * inv[None, :]
    a_col = (t % 64).astype(np.float32)[:, None] * inv[None, :]
    tabs = np.zeros((4, 128, T), np.float32)
    tabs[0, 0:64] = 1.0
    for p in range(64, 96):
        tabs[0, p] = np.cos(a_seq[:, (p - 64) % 16])
        tabs[1, p] = np.sin(a_seq[:, (p - 64) % 16])
    for p in range(128):
        f = p % 64
        a = a_row if f < 32 else a_col
        tabs[2, p] = np.cos(a[:, f % 16])
        tabs[3, p] = np.sin(a[:, f % 16])
    return tabs


def tile64(v):
    return np.tile(v, 128 // v.shape[0])


def layer_host(inp, l):
    f = lambda k: np.asarray(inp[k][l], np.float32)
    h = {}
    for nm in ("ffn1", "ffn2"):
        h[nm + "_wg"] = blk_cols(f(nm + "_w_gate"), 256)
        h[nm + "_wu"] = blk_cols(f(nm + "_w_up"), 256)
        h[nm + "_wd"] = blk_cols(f(nm + "_w_down"), 128)
    win = f("w_in")
    cols_f = np.concatenate([np.arange(0, 256), np.arange(256, 512), np.arange(768, 1024), np.arange(1024, 1152),
                             np.arange(2208, 2336), np.arange(1184, 1440), np.arange(1440, 1696), np.arange(1952, 2208)])
    cols_v = np.concatenate([np.arange(512, 768), np.arange(1696, 1952), np.arange(2336, 2464)])
    h["winf"] = blk_cols(win[:, cols_f], 256)
    h["winv"] = blk_cols(win[:, cols_v], 320)
    wsm = np.zeros((128, WSM), np.float32)
    wuq = f("mla_w_uq")
    wsm[:, WS_UQ:WS_UQ + 768] = wuq.reshape(2, 128, 384).transpose(1, 0, 2).reshape(128, 768)
    wukv = f("mla_w_ukv")
    for hh in range(4):
        wsm[:, WS_KN + hh * 96: WS_KN + hh * 96 + 64] = wukv[:, hh * 128: hh * 128 + 64]
        wsm[:, WS_V + hh * 64: WS_V + (hh + 1) * 64] = wukv[:, hh * 128 + 64: hh * 128 + 128]
    kr = win[:, 1152:1184].reshape(8, 128, 32)
    for k in range(8):
        wsm[:, WS_KR + k * 96 + 64: WS_KR + k * 96 + 96] = kr[k]
    h["wsm"] = wsm
    g = np.zeros((128, NG), np.float32)
    g[:, G_FFN1:G_FFN1 + 8] = f("ffn1_norm").reshape(8, 128).T
    g[:, G_MIX:G_MIX + 8] = f("mix_norm").reshape(8, 128).T
    g[:, G_NAQ] = tile64(f("na_q_norm"))
    g[:, G_NAK] = tile64(f("na_k_norm"))
    g[:, G_QLAT:G_QLAT + 2] = f("mla_q_lat_norm").reshape(2, 128).T
    g[:, G_KVLAT] = f("mla_kv_lat_norm")
    g[0:96, G_MQ] = f("mla_q_norm")
    g[0:96, G_MK] = f("mla_k_norm")
    g[:, G_DQ] = tile64(f("diff_q_norm"))
    g[:, G_DK] = tile64(f("diff_k_norm"))
    g[:, G_GQ] = tile64(f("gqa_q_norm"))
    g[:, G_GK] = tile64(f("gqa_k_norm"))
    betas = np.concatenate([f("na_beta"), f("mla_beta"), np.tile(f("diff_subln"), 4), f("gqa_beta")])
    g[:, G_BETA:G_BETA + 8] = betas.reshape(8, 128).T
    g[:, G_FFN2:G_FFN2 + 8] = f("ffn2_norm").reshape(8, 128).T
    g[:, G_FIN:G_FIN + 8] = f("final_norm").reshape(8, 128).T
    h["gains"] = g
    lc = np.zeros((128, 2), np.float32)
    lc[:, 0] = lambda_init(l)
    lc[:, 1] = 1.0 - lambda_init(l)
    h["lconst"] = lc
    return h


def emit_phase_c(c, st, nc, S, d):
    c.PS = Ring(st, nc, "ps", 8, [128, 512], F32, psum=True)
    c.SQ = Ring(st, nc, "sq", 3, [128, 512], BF16)
    c.RS = Ring(st, nc, "rs", 2, [128, 512], F32)
    c.TF = Ring(st, nc, "tf", 4, [128, 512], F32)
    c.HT = Ring(st, nc, "hT", 2, [128, 8 * TS], BF16)
    c.actT = _sb(st, nc, "actT", [128, 22 * TS], BF16)
    c.act_r = [Res("cact%d" % j) for j in range(22)]
    wring = Ring(st, nc, "wb", 6, [128, 2816], BF16)
    ws = WStream(S, wring, 4)
    YT = Ring(st, nc, "yt", 1, [128, 8 * TS], F32)
    yn = _sb(st, nc, "yn", [128, 8 * TS], BF16)
    yn_r = Res("yn")
    S.add("dve", lambda e: e.tensor_scalar(out=c.gains[:, G_BETA + 4:G_BETA + 6], in0=c.gains[:, G_BETA + 4:G_BETA + 6],
                                            scalar1=c.lconst[:, 1:2], scalar2=None, op0=ALU.mult),
          reads=[c.gains_r, c.lconst_r], writes=[c.gains_r])
    bases = []
    for t in range(NT):
        b0 = len(ws.items)
        for m in range(8):
            ws.plan(d["wout"][m], 1024)
        plan_ffn(ws, d["wg"], d["wu"], d["wd"])
        bases.append(b0)

    def tile_c(t, yt, ytr, key):
        for u in range(16):
            cc, hf = u // 2, u % 2
            S.add("sp", (lambda e, u=u, cc=cc, hf=hf: e.dma_start(out=yt[hf * 64:(hf + 1) * 64, cc * TS:(cc + 1) * TS],
                                                                  in_=d["y"][u, :, t * TS:(t + 1) * TS])),
                  reads=[d["y_r"]], writes=[ytr], dma_key=key)
        for g in range(4):
            ccs = [2 * g, 2 * g + 1]
            if g == 2:
                for cc in ccs:
                    rms_feat(c, [(yt[:, cc * TS:(cc + 1) * TS], ytr)], 128, C_BD64, 64, [G_BETA + cc],
                             [(yn[:, cc * TS:(cc + 1) * TS], yn_r)])
            else:
                rms_feat(c, [(yt[:, cc * TS:(cc + 1) * TS], ytr) for cc in ccs], 128, C_ONES, 256, [G_BETA + cc for cc in ccs],
                         [(yn[:, cc * TS:(cc + 1) * TS], yn_r) for cc in ccs])
        for m in range(8):
            wb, wbr = ws.get(bases[t] + m)
            p, pr, _ = c.PS.next()
            for k in range(8):
                S.add("pe", (lambda e, k=k, p=p, wb=wb: e.matmul(
                    p[:, :], wb[:, k * 128:(k + 1) * 128], yn[:, k * TS:(k + 1) * TS],
                    start=(k == 0), stop=(k == 7))), reads=[wbr, yn_r], writes=[pr])
            xs = c.xT[:, m * T + t * TS: m * T + (t + 1) * TS]
            S.add("dve", (lambda e, p=p, xs=xs: e.tensor_tensor(out=xs, in0=p[:, :], in1=xs, op=ALU.add)),
                  reads=[pr, c.x_r[t]], writes=[c.x_r[t]])
        hT, hTr = norm_x_tile(c, t, G_FFN2)
        ffn_tile(c, t, hT, hTr, ws, bases[t] + 8)
        srcs = [(c.xT[:, k * T + t * TS: k * T + (t + 1) * TS], c.x_r[t]) for k in range(8)]
        rms_feat(c, srcs, 128, C_ONES, D, [G_FIN + k for k in range(8)], srcs)

    for t in range(NT):
        yt, ytr, key = YT.next()
        tile_c(t, yt, ytr, key)


def build_c():
    nc = bass.Bass("TRN2", target_bir_lowering=False)
    dt = lambda name, shape, dtype, kind: nc.dram_tensor(name, shape, dtype, kind=kind).ap()
    d = {}
    xT = dt("xT", [128, 8 * T], F32, "ExternalInput")
    gains = dt("gains", [128, NG], F32, "ExternalInput")
    lconst = dt("lconst", [128, 2], F32, "ExternalInput")
    cmat = dt("cmat", [128, 768], F32, "ExternalInput")
    d["wg"] = dt("wg", [11, 128, 2048], F32, "ExternalInput")
    d["wu"] = dt("wu", [11, 128, 2048], F32, "ExternalInput")
    d["wd"] = dt("wd", [8, 128, 2816], F32, "ExternalInput")
    d["wout"] = dt("wout", [8, 128, 1024], F32, "ExternalInput")
    d["y"] = dt("y", [16, 64, T], F32, "ExternalInput")
    d["y_r"] = Res("y")
    xo = dt("xoT", [128, 8 * T], F32, "ExternalOutput")
    xo_r = Res("xo")
    with contextlib.ExitStack() as st:
        S = Sched(nc)
        c = Ctx()
        common_setup(c, st, nc, S, gains, lconst, cmat)
        load_x(c, st, nc, S, xT)
        emit_phase_c(c, st, nc, S, d)
        store_x(c, S, xo, xo_r)
        S.add("sp", None, reads=[xo_r])
        S.emit(st)
    return nc


LA = 2


def near_entries(kind, qb):
    ent = []
    if kind == "A":
        lo, hi, halo = 4 * qb - 2, 4 * qb + 5, 2
    else:
        lo, hi, halo = 4 * qb - 1, 4 * qb + 4, 1
    for lc in range(max(lo, 0), min(hi, 15) + 1):
        ent.append(("own", lc))
    if qb == 0:
        for j in range(4):
            for x in range(16 - halo, 16):
                ent.append(("gath", 16 * j + x))
    if qb == 3:
        for j in range(4):
            for x in range(halo):
                ent.append(("gath", 16 * j + x))
    return ent


def tile_ids():
    ids = {}
    n = 0
    for kind in ("A", "C"):
        for h in range(4):
            for qb in range(4):
                for i, _e in enumerate(near_entries(kind, qb)):
                    ids[(kind, h, qb, i)] = n
                    n += 1
    return ids, n


TILE_IDS, NTILES = tile_ids()


def emit_phase_b(c, st, nc, S, d):
    PSS = Ring(st, nc, "pss", 3, [128, 1024], F32, psum=True)
    PO = Ring(st, nc, "po", 1, [128, 512], F32, psum=True)
    PBC = Ring(st, nc, "pbc", 1, [128, 512], F32, psum=True)
    PT = Ring(st, nc, "pt", 4, [128, 1024], BF16)
    KT = Ring(st, nc, "kt", 2, [128, SEQ], BF16)
    VT = Ring(st, nc, "vt", 2, [128, 64, 65], BF16)
    KTO = Ring(st, nc, "kto", 2, [128, T], BF16)
    VTO = Ring(st, nc, "vto", 2, [128, 16, 65], BF16)
    QT = Ring(st, nc, "qt", 4, [128, T], BF16)
    BT = Ring(st, nc, "bt", 12, [128, 512], BF16)
    VS = Ring(st, nc, "vs", 2, [128, 64, 65], BF16)
    SC = Ring(st, nc, "sc", 2, [128, 64], F32)
    OSB = Ring(st, nc, "osb", 2, [128, 512], F32)
    ON = Ring(st, nc, "on", 4, [64, 512], F32)
    RR = Ring(st, nc, "rr", 2, [128, 512], F32)
    for ring_ in (KT, KTO):
        for buf_, res_, _k in ring_.bufs:
            S.add("pool", (lambda e, buf_=buf_: e.memset(buf_[:, :], 0.0)), writes=[res_])
    bts = WStream(S, BT, 8)
    t5c = _sb(st, nc, "t5c", [128, 1024], F32)
    t5c_r = Res("t5c")
    S.add("sp", lambda e: e.dma_start(out=t5c[:, :], in_=d["t5c"]), writes=[t5c_r], dma_key="t5c")
    onesf = _sb(st, nc, "onesf", [128, 64], F32)
    onesf_r = Res("onesf")
    S.add("dve", lambda e: e.memset(onesf[:, :], 1.0), writes=[onesf_r])
    dl = _sb(st, nc, "dl", [64, 128], F32)
    dl_r = Res("dl")
    S.add("sp", lambda e: e.dma_start(out=dl[:, :], in_=d["dlam"]), writes=[dl_r], dma_key="dl")
    lt = _sb(st, nc, "lt", [64, 72], F32)
    lt_r = Res("lt")
    S.add("dve", lambda e: e.tensor_tensor(out=lt[:, 0:32], in0=dl[:, 0:32], in1=dl[:, 32:64], op=ALU.mult), reads=[dl_r], writes=[lt_r])
    S.add("dve", lambda e: e.tensor_tensor(out=lt[:, 32:64], in0=dl[:, 64:96], in1=dl[:, 96:128], op=ALU.mult), reads=[dl_r, lt_r], writes=[lt_r])
    S.add("dve", lambda e: e.reduce_sum(lt[:, 64:65], lt[:, 0:32], axis=mybir.AxisListType.X), reads=[lt_r], writes=[lt_r])
    S.add("dve", lambda e: e.reduce_sum(lt[:, 65:66], lt[:, 32:64], axis=mybir.AxisListType.X), reads=[lt_r], writes=[lt_r])
    S.add("act", lambda e: e.activation(out=lt[:, 66:68], in_=lt[:, 64:66], func=AF.Exp), reads=[lt_r], writes=[lt_r])
    S.add("dve", lambda e: e.tensor_tensor(out=lt[:, 68:69], in0=lt[:, 67:68], in1=lt[:, 66:67], op=ALU.subtract), reads=[lt_r], writes=[lt_r])
    S.add("dve", lambda e: e.tensor_scalar(out=lt[:, 69:70], in0=lt[:, 68:69], scalar1=c.lconst[0:64, 0:1], scalar2=None, op0=ALU.subtract),
          reads=[lt_r, c.lconst_r], writes=[lt_r])
    nlam = lt[:, 69:70]
    ident = c.cmat[:, C_ID * 128:(C_ID + 1) * 128]
    y_r = d["y_r"]

    pending = []

    def attend(qt, qtr, qp0, dq, qb, chunks, cont):
        O, Or, _ = PO.next()
        steps = []
        i = 0
        isc = lambda ch: ch[5] is not None and ch[5][0] == "const"
        while i < len(chunks):
            if i + 1 < len(chunks) and not isc(chunks[i]) and not isc(chunks[i + 1]):
                steps.append([chunks[i], chunks[i + 1]])
                i += 2
            else:
                steps.append([chunks[i]])
                i += 1
        n = len(steps)
        total = len(chunks)
        pend = []
        pvi = [0]
        for i in range(n + LA):
            if i < n:
                st_ = steps[i]
                s_, sr, _ = PSS.next()
                w = 512 * len(st_)
                for hf, (kt, ktr, vt, vtr, kc, mode) in enumerate(st_):
                    sl = s_[:, hf * 512:(hf + 1) * 512]
                    if mode is not None and mode[0] == "tile":
                        bt, btr = bts.get(mode[1])
                        S.add("pe", (lambda e, sl=sl, bt=bt: e.matmul(sl, ident, bt[:, :], start=True, stop=False)),
                              reads=[btr, c.cmat_r], writes=[sr])
                        first = False
                    else:
                        first = True
                    S.add("pe", (lambda e, sl=sl, kc=kc, first=first, kt=kt: e.matmul(
                        sl, kt[qp0:qp0 + dq, kc * 128:(kc + 1) * 128], qt[qp0:qp0 + dq, qb * TS:(qb + 1) * TS],
                        start=first, stop=True)), reads=[ktr, qtr], writes=[sr])
                p_, pr, _ = PT.next()
                mode = st_[0][5]
                if len(st_) == 1 and mode is not None and mode[0] == "const":
                    bcol = mode[1]
                    S.add("act", (lambda e, s_=s_, p_=p_, bcol=bcol: e.activation(out=p_[:, 0:512], in_=s_[:, 0:512], func=AF.Exp, bias=bcol)),
                          reads=[sr, t5c_r], writes=[pr])
                else:
                    S.add("act", (lambda e, s_=s_, p_=p_, w=w: e.activation(out=p_[:, 0:w], in_=s_[:, 0:w], func=AF.Exp)),
                          reads=[sr], writes=[pr])
                pend.append((st_, p_, pr))
            j = i - LA
            if j >= 0:
                st_, p_, pr = pend[j]
                for hf, (kt, ktr, vt, vtr, kc, mode) in enumerate(st_):
                    k_ = pvi[0]
                    pvi[0] += 1
                    S.add("pe", (lambda e, kc=kc, p_=p_, k_=k_, vt=vt, hf=hf: e.matmul(
                        O[0:65, :], vt[:, kc, :], p_[:, hf * 512:(hf + 1) * 512], start=(k_ == 0), stop=(k_ == total - 1))),
                        reads=[vtr, pr], writes=[Or])
            if i == min(3, n + LA - 1) and pending:
                pending.pop(0)()
        osb, osbr, _ = OSB.next()
        S.add("dve", lambda e: e.tensor_copy(out=osb[0:65, :], in_=O[0:65, :]), reads=[Or], writes=[osbr])

        def fin():
            rr, rrr, _ = RR.next()
            S.add("dve", lambda e: e.reciprocal(out=rr[64:65, :], in_=osb[64:65, :]), reads=[osbr], writes=[rrr])
            bc, bcr, _ = PBC.next()
            S.add("pe", lambda e: e.matmul(bc[0:64, :], onesf[64:65, 0:64], rr[64:65, :], start=True, stop=True),
                  reads=[rrr, onesf_r], writes=[bcr])
            on, onr, _ = ON.next()
            S.add("dve", lambda e: e.tensor_tensor(out=on[:, :], in0=osb[0:64, :], in1=bc[0:64, :], op=ALU.mult),
                  reads=[osbr, bcr], writes=[onr])
            cont(on, onr)
        pending.append(fin)

    def store_y(u, qb, on, onr):
        S.add("sp", lambda e: e.dma_start(out=d["y"][u, :, qb * TS:(qb + 1) * TS], in_=on[:, :]),
              reads=[onr], writes=[y_r], dma_key="ystore")

    plan = []
    for h in range(4):
        plan.append(("A", h * 64, 64, h, h * 64, 64, [h]))
    for h in range(4):
        plan.append(("B", 256 + h * 96, 96, 10 + h, 256 + h * 96, 96, [4 + h]))
    for h in range(4):
        plan.append(("C", 640 + h * 64, 64, 4 + h, 640 + h * 64, 64, [8 + h]))
    for g in range(2):
        plan.append(("D", 896 + g * 64, 64, 8 + g, 896 + 2 * g * 64, 128, [12 + 2 * g, 13 + 2 * g]))
    import os
    units = [int(x) for x in os.environ.get("KB_UNITS", ",".join(str(i) for i in range(len(plan)))).split(",")]
    bt_plan = {}
    for ui in units:
        kind = plan[ui][0]
        if kind in ("A", "C"):
            h = ui % 4
            for qb in range(4):
                for m in range(2 if kind == "C" else 1):
                    for i, _e in enumerate(near_entries(kind, qb)):
                        bt_plan[(kind, h, qb, m, i)] = bts.plan(d["btl"][TILE_IDS[(kind, h, qb, i)]], 512)

    loaded = {}

    def load_unit(ui):
        kind, krow, kd, vu, qrow, qd, yus = plan[ui]
        kt, ktr, kkey = KT.next()
        vt, vtr, vkey = VT.next()
        qt, qtr, qkey = QT.next()
        for j in range(4):
            kr0 = kg_row(j, krow)
            S.add("sp", (lambda e, j=j, kr0=kr0: e.dma_start(out=kt[0:kd, j * T:(j + 1) * T], in_=d["kTg"][kr0:kr0 + kd, :])),
                  writes=[ktr], dma_key=kkey)
            if kind == "D":
                S.add("sp", (lambda e, j=j, kr0=kr0: e.dma_start(out=kt[64:128, j * T:(j + 1) * T], in_=d["kTg"][kr0:kr0 + kd, :])),
                      writes=[ktr], dma_key=kkey)
            vr0 = vg_row(j, vu)
            S.add("sp", (lambda e, j=j, vr0=vr0: e.dma_start(out=vt[:, 16 * j:16 * (j + 1), :],
                                                              in_=d["vg"][vr0:vr0 + 128, :].rearrange("p (c d) -> p c d", d=65))),
                  writes=[vtr], dma_key=vkey)
        if kind in ("A", "B"):
            pieces = [(0, qd, qrow)]
        elif kind == "C":
            pieces = [(0, 32, qrow), (32, 32, qrow + 32)]
        else:
            pieces = [(0, 64, qrow), (64, 64, qrow + 64)]
        qtiles = []
        for p0, pn, r0 in pieces:
            if qtiles or True:
                qt, qtr, qkey = (qt, qtr, qkey) if not qtiles else QT.next()
            S.add("pool", (lambda e, qt=qt: e.memset(qt[:, :], 0.0)), writes=[qtr])
            S.add("sp", (lambda e, qt=qt, p0=p0, pn=pn, r0=r0: e.dma_start(out=qt[p0:p0 + pn, :], in_=d["qT"][r0:r0 + pn, :])),
                  writes=[qtr], dma_key=qkey)
            qtiles.append((qt, qtr))
        own = None
        if kind in ("A", "C"):
            kto, ktor, kokey = KTO.next()
            vto, vtor, vokey = VTO.next()
            S.add("sp", lambda e: e.dma_start(out=kto[0:kd, :], in_=d["kT"][krow:krow + kd, :]), writes=[ktor], dma_key=kokey)
            S.add("sp", lambda e: e.dma_start(out=vto[:, :, :], in_=d["v"][vu]), writes=[vtor], dma_key=vokey)
            own = (kto, ktor, vto, vtor)
        loaded[ui] = (kt, ktr, vt, vtr, qtiles, own)

    load_unit(units[0])
    for n_, ui in enumerate(units):
        if n_ + 1 < len(units):
            load_unit(units[n_ + 1])
        kind, krow, kd, vu, qrow, qd, yus = plan[ui]
        kt, ktr, vt, vtr, qtiles, own = loaded.pop(ui)
        h = ui % 4
        dense = [(kt, ktr, vt, vtr, kc, None) for kc in range(64)]

        def near_list(kind, qb, m):
            out = []
            for i, (src, ch) in enumerate(near_entries(kind, qb)):
                mode = ("tile", bt_plan[(kind, h, qb, m, i)])
                if src == "own":
                    out.append((own[0], own[1], own[2], own[3], ch, mode))
                else:
                    out.append((kt, ktr, vt, vtr, ch, mode))
            return out

        for qb in range(4):
            if kind == "A":
                attend(qtiles[0][0], qtiles[0][1], 0, 128, qb, near_list("A", qb, 0),
                       (lambda on, onr, u=yus[0], qb=qb: store_y(u, qb, on, onr)))
            elif kind == "B":
                attend(qtiles[0][0], qtiles[0][1], 0, 128, qb, dense,
                       (lambda on, onr, u=yus[0], qb=qb: store_y(u, qb, on, onr)))
            elif kind == "D":
                for hh in range(2):
                    attend(qtiles[hh][0], qtiles[hh][1], 0, 128, qb, dense,
                           (lambda on, onr, u=yus[hh], qb=qb: store_y(u, qb, on, onr)))
            else:
                sc, scr, _ = SC.next()
                S.add("act", (lambda e, sc=sc, qb=qb, h=h: e.activation(out=sc[:, :], in_=t5c[:, (h * 4 + qb) * 64:(h * 4 + qb + 1) * 64], func=AF.Exp)),
                      reads=[t5c_r], writes=[scr])
                vs, vsr, _ = VS.next()
                S.add("dve", (lambda e, sc=sc, vs=vs, vt=vt: e.tensor_tensor(
                    out=vs[:, :, :], in0=vt[:, :, :], in1=sc[:, :].unsqueeze(2).broadcast_to([128, 64, 65]), op=ALU.mult)),
                    reads=[vtr, scr], writes=[vsr])
                state = {}

                def cont0(on, onr, state=state):
                    state["on0"] = (on, onr)

                def cont1(on1, on1r, state=state, u=yus[0], qb=qb):
                    on0, on0r = state["on0"]
                    yo, yor, _ = ON.next()
                    S.add("dve", (lambda e, yo=yo, on0=on0, on1=on1: e.scalar_tensor_tensor(
                        out=yo[:, :], in0=on1[:, :], scalar=nlam, in1=on0[:, :], op0=ALU.mult, op1=ALU.add)),
                        reads=[on0r, on1r, lt_r], writes=[yor])
                    store_y(u, qb, yo, yor)
                for m in range(2):
                    ch = [(kt, ktr, vs, vsr, kc, None) for kc in range(64)] + near_list("C", qb, m)
                    attend(qtiles[m][0], qtiles[m][1], 0, 128, qb, ch, cont0 if m == 0 else cont1)
    while pending:
        pending.pop(0)()


def build_b():
    nc = bass.Bass("TRN2", target_bir_lowering=False)
    dt = lambda name, shape, dtype, kind: nc.dram_tensor(name, shape, dtype, kind=kind).ap()
    d = {}
    gains = dt("gains", [128, NG], F32, "ExternalInput")
    lconst = dt("lconst", [128, 2], F32, "ExternalInput")
    cmat = dt("cmat", [128, 768], F32, "ExternalInput")
    d["qT"] = dt("qT", [QROWS, T], BF16, "ExternalInput")
    d["kT"] = dt("kT", [KROWS, T], BF16, "ExternalInput")
    d["v"] = dt("v", [NVU, 128, 16, 65], BF16, "ExternalInput")
    d["kTg"] = dt("kTg", [4 * KROWS, T], BF16, "ExternalInput")
    d["vg"] = dt("vg", [4 * NVU * 128, 16 * 65], BF16, "ExternalInput")
    d["btl"] = dt("btl", [NTILES, 128, 512], F32, "ExternalInput")
    d["t5c"] = dt("t5c", [128, 1024], F32, "ExternalInput")
    d["dlam"] = dt("dlam", [64, 128], F32, "ExternalInput")
    d["y"] = dt("y", [16, 64, T], F32, "ExternalOutput")
    d["y_r"] = Res("y")
    with contextlib.ExitStack() as st:
        S = Sched(nc)
        c = Ctx()
        common_setup(c, st, nc, S, gains, lconst, cmat)
        emit_phase_b(c, st, nc, S, d)
        S.add("sp", None, reads=[d["y_r"]])
        S.emit(st)
    return nc


def t5_bucket_np(rel):
    half = 16
    max_exact = 8
    n = np.abs(rel)
    large = max_exact + (np.log(np.maximum(n, 1).astype(np.float32) / max_exact)
                         / np.float32(math.log(128 / max_exact)) * (half - max_exact)).astype(np.int32)
    large = np.minimum(large, half - 1)
    return np.where(rel > 0, half, 0) + np.where(n < max_exact, n, large)


def bias_tiles_host(inp, l, core):
    r = core % 4
    rpb = np.asarray(inp["na_rpb"][l], np.float32)
    t5 = np.asarray(inp["t5_bias"], np.float32)
    btl = np.empty((NTILES, 128, 512), np.float32)
    kk = np.arange(128)[:, None]
    qq = np.arange(512)[None, :]
    NEG = np.float32(-1e30)
    for qb in range(4):
        qbg = 4 * r + qb
        qtok = qbg * 512 + qq
        qr, qc = qtok // 64, qtok % 64
        rs = np.clip(qr - 4, 0, 120)
        cs = np.clip(qc - 8, 0, 48)
        for i, (src, ch) in enumerate(near_entries("A", qb)):
            g = 16 * r + ch if src == "own" else ch
            ktok = g * 128 + kk
            kr, kcol = ktok // 64, ktok % 64
            valid = (kr >= rs) & (kr < rs + 8) & (kcol >= cs) & (kcol < cs + 16)
            dr = np.clip(kr - qr + 7, 0, 14)
            dc = np.clip(kcol - qc + 15, 0, 30)
            flat = np.where(valid, dr * 31 + dc, 15 * 31)
            for h in range(4):
                tab = np.concatenate([rpb[h].ravel(), np.array([NEG], np.float32)])
                btl[TILE_IDS[("A", h, qb, i)]] = tab[flat]
        for i, (src, ch) in enumerate(near_entries("C", qb)):
            g = 16 * r + ch if src == "own" else ch
            if 4 * qbg - 1 <= g <= 4 * qbg + 4:
                bk = t5_bucket_np((g * 128 + kk) - qtok)
                for h in range(4):
                    btl[TILE_IDS[("C", h, qb, i)]] = t5[:, h][bk]
            else:
                for h in range(4):
                    btl[TILE_IDS[("C", h, qb, i)]] = NEG
    t5c = np.zeros((128, 1024), np.float32)
    for qb in range(4):
        qbg = 4 * r + qb
        for kc in range(64):
            for h in range(4):
                if 4 * qbg - 1 <= kc <= 4 * qbg + 4:
                    val = NEG
                else:
                    val = t5[31, h] if kc > 4 * qbg + 3 else t5[15, h]
                t5c[:, (h * 4 + qb) * 64 + kc] = val
    return btl, t5c


_PROGS = {}


def _prog(name):
    if name not in _PROGS:
        _PROGS[name] = {"a": build_a, "b": build_b, "c": build_c, "f": build_fused}[name]()
    return _PROGS[name]


def to_featmajor(xc):
    return np.ascontiguousarray(xc.T.reshape(8, 128, T).transpose(1, 0, 2).reshape(128, 8 * T))


def from_featmajor(xT):
    return xT.reshape(128, 8, T).transpose(1, 0, 2).reshape(D, T).T


def kernel_unfused(**inp):
    x = np.asarray(inp["x"], np.float32).reshape(BATCH * SEQ, D)
    cm = const_mats()
    xTs = [to_featmajor(x[c * T:(c + 1) * T]) for c in range(NCORE)]
    tabs = [rope_tables(c) for c in range(NCORE)]
    cores = list(range(NCORE))
    for l in range(DEPTH):
        H = layer_host(inp, l)
        common = {"gains": H["gains"], "lconst": H["lconst"], "cmat": cm}
        maps = [dict(common, xT=xTs[c], wg=H["ffn1_wg"], wu=H["ffn1_wu"], wd=H["ffn1_wd"], winf=H["winf"],
                     winv=H["winv"], wsm=H["wsm"], tabs=tabs[c]) for c in cores]
        ra = run_bass_kernel_spmd(_prog("a"), maps, core_ids=cores).results
        maps = []
        dl = np.ascontiguousarray(np.broadcast_to(np.asarray(inp["diff_lambda"][l], np.float32).reshape(1, 128), (64, 128)))
        for c in cores:
            b = c // 4
            kTg = np.zeros((4 * KROWS, T), np.asarray(ra[0]["kT"]).dtype)
            vg = np.zeros((4 * NVU * 128, 1040), np.asarray(ra[0]["v"]).dtype)
            for j in range(4):
                kj = np.asarray(ra[4 * b + j]["kT"])
                vj = np.asarray(ra[4 * b + j]["v"]).reshape(NVU * 128, 1040)
                for a, bb in K_PARTS:
                    kTg[4 * a + j * (bb - a): 4 * a + (j + 1) * (bb - a)] = kj[a:bb]
                for a, bb in V_PARTS:
                    vg[(4 * a + j * (bb - a)) * 128: (4 * a + (j + 1) * (bb - a)) * 128] = vj[a * 128:bb * 128]
            btl, t5c = bias_tiles_host(inp, l, c)
            maps.append(dict(common, qT=np.asarray(ra[c]["qT"]), kT=np.asarray(ra[c]["kT"]), v=np.asarray(ra[c]["v"]),
                             kTg=kTg, vg=vg, btl=btl, t5c=t5c, dlam=dl))
        rb = run_bass_kernel_spmd(_prog("b"), maps, core_ids=cores).results
        wout = blk_cols(np.asarray(inp["w_out"][l], np.float32), 128)
        maps = [dict(common, xT=np.asarray(ra[c]["x1T"]), y=np.asarray(rb[c]["y"]), wg=H["ffn2_wg"], wu=H["ffn2_wu"],
                     wd=H["ffn2_wd"], wout=wout) for c in cores]
        rc = run_bass_kernel_spmd(_prog("c"), maps, core_ids=cores).results
        xTs = [np.asarray(rc[c]["xoT"]) for c in cores]
    out = np.concatenate([from_featmajor(xTs[c]) for c in cores], axis=0)
    return np.ascontiguousarray(out.reshape(BATCH, SEQ, D).astype(np.float32))


RG = [[0, 1, 2, 3], [4, 5, 6, 7]]
K_PARTS = [(0, 256), (256, 448), (448, 640), (640, 896), (896, 1024)]
V_PARTS = [(0, 3), (3, 6), (6, 9), (9, 12), (12, 14)]


def kg_row(j, row):
    for a, b in K_PARTS:
        if a <= row < b:
            return 4 * a + j * (b - a) + (row - a)
    raise AssertionError(row)


def vg_row(j, u):
    for a, b in V_PARTS:
        if a <= u < b:
            return (4 * a + j * (b - a) + (u - a)) * 128
    raise AssertionError(u)


def build_fused():
    nc = bass.Bass("TRN2", target_bir_lowering=False)
    dt = lambda name, shape, dtype, kind, **kw: nc.dram_tensor(name, shape, dtype, kind=kind, **kw).ap()
    xT_d = dt("xT", [128, 8 * T], F32, "ExternalInput")
    cmat_d = dt("cmat", [128, 768], F32, "ExternalInput")
    tabs_d = dt("tabs", [4, 128, T], F32, "ExternalInput")
    t5c_d = dt("t5c", [128, 1024], F32, "ExternalInput")
    xo_d = dt("xoT", [128, 8 * T], F32, "ExternalOutput")
    LD = []
    for l in range(DEPTH):
        sfx = "_%d" % l
        d = {}
        d["gains"] = dt("gains" + sfx, [128, NG], F32, "ExternalInput")
        d["lconst"] = dt("lconst" + sfx, [128, 2], F32, "ExternalInput")
        for nm in ("wg1", "wu1", "wg2", "wu2"):
            d[nm] = dt(nm + sfx, [11, 128, 2048], F32, "ExternalInput")
        for nm in ("wd1", "wd2"):
            d[nm] = dt(nm + sfx, [8, 128, 2816], F32, "ExternalInput")
        d["winf"] = dt("winf" + sfx, [7, 128, 2048], F32, "ExternalInput")
        d["winv"] = dt("winv" + sfx, [2, 128, 2560], F32, "ExternalInput")
        d["wsm"] = dt("wsm" + sfx, [128, WSM], F32, "ExternalInput")
        d["wout"] = dt("wout" + sfx, [8, 128, 1024], F32, "ExternalInput")
        d["btl"] = dt("btl" + sfx, [NTILES, 128, 512], F32, "ExternalInput")
        d["dlam"] = dt("dlam" + sfx, [64, 128], F32, "ExternalInput")
        d["qT"] = dt("s_qT" + sfx, [QROWS, T], BF16, "Internal")
        d["kT"] = dt("s_kT" + sfx, [KROWS, T], BF16, "Internal")
        d["v"] = dt("s_v" + sfx, [NVU, 128, 16, 65], BF16, "Internal")
        d["kTg"] = dt("s_kTg" + sfx, [4 * KROWS, T], BF16, "Internal", addr_space="Local")
        d["vg"] = dt("s_vg" + sfx, [4 * NVU * 128, 16 * 65], BF16, "Internal", addr_space="Local")
        d["y"] = dt("s_y" + sfx, [16, 64, T], F32, "Internal")
        LD.append(d)
    with contextlib.ExitStack() as top:
        ss = SemState(top)
        c = Ctx()
        c.xT = _sb(top, nc, "xT", [128, 8 * T], F32)

        def fresh_x():
            c.x_r = [Res("x%d" % t) for t in range(NT)]

        with contextlib.ExitStack() as st:
            S = Sched(nc, ss)
            fresh_x()
            for k in range(8):
                S.add("sp", (lambda e, k=k: e.dma_start(out=c.xT[:, k * T:(k + 1) * T], in_=xT_d[:, k * T:(k + 1) * T])),
                      writes=c.x_r, dma_key="xT")
            S.drain()
            S.emit(st)
        for l in range(DEPTH):
            d = LD[l]
            with contextlib.ExitStack() as st:
                S = Sched(nc, ss)
                fresh_x()
                common_setup(c, st, nc, S, d["gains"], d["lconst"], cmat_d)
                da = dict(wg=d["wg1"], wu=d["wu1"], wd=d["wd1"], winf=d["winf"], winv=d["winv"], wsm=d["wsm"], tabs=tabs_d,
                          qT=d["qT"], kT=d["kT"], v=d["v"], q_r=Res("q"), k_r=Res("k"), v_r=Res("v"))
                emit_phase_a(c, st, nc, S, da)
                S.drain()
                S.emit(st)
            with contextlib.ExitStack() as st:
                S = Sched(nc, ss)
                v2 = d["v"].rearrange("u p c d -> (u p) (c d)")
                for a, b in K_PARTS:
                    S.add("pool", (lambda e, d=d, a=a, b=b: e.collective_compute(
                        "AllGather", ALU.bypass, replica_groups=RG, ins=[d["kT"][a:b, :]], outs=[d["kTg"][4 * a:4 * b, :]])),
                        dma_key="cc", inc=1)
                for a, b in V_PARTS:
                    S.add("pool", (lambda e, d=d, a=a, b=b, v2=v2: e.collective_compute(
                        "AllGather", ALU.bypass, replica_groups=RG, ins=[v2[a * 128:b * 128, :]],
                        outs=[d["vg"][4 * a * 128:4 * b * 128, :]])), dma_key="cc", inc=1)
                S.drain()
                S.emit(st)
            with contextlib.ExitStack() as st:
                S = Sched(nc, ss)
                common_setup(c, st, nc, S, d["gains"], d["lconst"], cmat_d)
                db = dict(qT=d["qT"], kT=d["kT"], v=d["v"], kTg=d["kTg"], vg=d["vg"], btl=d["btl"], t5c=t5c_d,
                          dlam=d["dlam"], y=d["y"], y_r=Res("y"))
                emit_phase_b(c, st, nc, S, db)
                S.drain()
                S.emit(st)
            with contextlib.ExitStack() as st:
                S = Sched(nc, ss)
                fresh_x()
                common_setup(c, st, nc, S, d["gains"], d["lconst"], cmat_d)
                dc = dict(wg=d["wg2"], wu=d["wu2"], wd=d["wd2"], wout=d["wout"], y=d["y"], y_r=Res("y"))
                emit_phase_c(c, st, nc, S, dc)
                S.drain()
                S.emit(st)
        with contextlib.ExitStack() as st:
            S = Sched(nc, ss)
            fresh_x()
            xo_r = Res("xo")
            store_x(c, S, xo_d, xo_r)
            S.add("sp", None, reads=[xo_r])
            S.drain()
            S.emit(st)
    return nc


def kernel(**inp):
    x = np.asarray(inp["x"], np.float32).reshape(BATCH * SEQ, D)
    cm = const_mats()
    cores = list(range(NCORE))
    shared = {}
    for l in range(DEPTH):
        H = layer_host(inp, l)
        sfx = "_%d" % l
        shared.update({"gains" + sfx: H["gains"], "lconst" + sfx: H["lconst"], "wg1" + sfx: H["ffn1_wg"], "wu1" + sfx: H["ffn1_wu"],
                       "wd1" + sfx: H["ffn1_wd"], "wg2" + sfx: H["ffn2_wg"], "wu2" + sfx: H["ffn2_wu"], "wd2" + sfx: H["ffn2_wd"],
                       "winf" + sfx: H["winf"], "winv" + sfx: H["winv"], "wsm" + sfx: H["wsm"],
                       "wout" + sfx: blk_cols(np.asarray(inp["w_out"][l], np.float32), 128),
                       "dlam" + sfx: np.ascontiguousarray(np.broadcast_to(
                           np.asarray(inp["diff_lambda"][l], np.float32).reshape(1, 128), (64, 128)))})
    maps = []
    for c in cores:
        m = dict(shared, xT=to_featmajor(x[c * T:(c + 1) * T]), cmat=cm, tabs=rope_tables(c))
        for l in range(DEPTH):
            btl, t5c = bias_tiles_host(inp, l, c)
            m["btl_%d" % l] = btl
            m["t5c"] = t5c
        maps.append(m)
    res = run_bass_kernel_spmd(_prog("f"), maps, core_ids=cores).results
    out = np.concatenate([from_featmajor(np.asarray(res[c]["xoT"])) for c in cores], axis=0)
    return np.ascontiguousarray(out.reshape(BATCH, SEQ, D).astype(np.float32))
```

```python
import contextlib
import math
import numpy as np
import ml_dtypes
import concourse.bass as bass
import concourse.mybir as mybir
from concourse.bass_utils import run_bass_kernel_spmd

F32 = mybir.dt.float32
BF16 = mybir.dt.bfloat16
AF = mybir.ActivationFunctionType
ALU = mybir.AluOpType

_UID = [0]


def _uid():
    _UID[0] += 1
    return _UID[0]

ENGS = ("pe", "act", "dve", "pool", "sp")
EPOCH = 16000
SAME_ENGINE_SYNC = True


class Res:
    __slots__ = ("name", "writer", "readers")

    def __init__(self, name=""):
        self.name = name
        self.writer = None
        self.readers = {}


class Op:
    __slots__ = ("eng", "fn", "deps", "dma_key", "signaled", "count", "semid", "idx", "inc")


class SemState:
    def __init__(self, stack):
        self.stack = stack
        self.cnt = {e: 0 for e in ENGS}
        self.dcnt = {}
        self.sems = {}


class Sched:
    def __init__(self, nc, semstate=None):
        self.nc = nc
        self.ops = []
        self.by_eng = {e: [] for e in ENGS}
        self.semstate = semstate
        self.last_dma = {}

    def drain(self):
        op = self.add("sp", None)
        op.deps = list(self.last_dma.values())
        return op

    def add(self, eng, fn, reads=(), writes=(), dma_key=None, inc=16):
        op = Op()
        op.inc = inc
        op.eng = eng
        op.fn = fn
        op.dma_key = dma_key
        op.signaled = dma_key is not None
        op.count = 0
        op.semid = None
        op.idx = len(self.ops)
        deps = {}
        for r in reads:
            if r.writer is not None:
                deps[r.writer.idx] = r.writer
        for w in writes:
            if w.writer is not None:
                deps[w.writer.idx] = w.writer
            for rd in w.readers.values():
                deps[rd.idx] = rd
        dl = []
        for d in deps.values():
            if d.dma_key is not None or dma_key is not None:
                need = True
            elif d.eng == eng:
                need = (eng != "pe") and SAME_ENGINE_SYNC
            else:
                need = True
            if need:
                d.signaled = True
                dl.append(d)
        op.deps = dl
        for r in reads:
            key = eng if dma_key is None else ("dma", op.idx)
            r.readers[key] = op
        for w in writes:
            w.writer = op
            w.readers = {}
        self.ops.append(op)
        self.by_eng[eng].append(op)
        if dma_key is not None:
            self.last_dma[dma_key] = op
        return op

    def emit(self, stack):
        nc = self.nc
        ss = self.semstate if self.semstate is not None else SemState(stack)
        cnt = ss.cnt
        dcnt = ss.dcnt
        for op in self.ops:
            if op.dma_key is not None:
                dcnt[op.dma_key] = dcnt.get(op.dma_key, 0) + op.inc
                op.count = dcnt[op.dma_key]
                op.semid = ("d", op.dma_key)
            elif op.signaled:
                c = cnt[op.eng]
                cnt[op.eng] = c + 1
                op.semid = ("e", op.eng, c // EPOCH)
                op.count = (c % EPOCH) + 1
        self.check(ss)
        sems = ss.sems
        for op in self.ops:
            if op.semid is not None and op.semid not in sems:
                sems[op.semid] = ss.stack.enter_context(nc.semaphore("s%d" % len(sems)))
        self.nsems = len(sems)
        block = stack.enter_context(nc.Block())
        engobj = {"pe": block.tensor, "act": block.scalar, "dve": block.vector,
                  "pool": block.gpsimd, "sp": block.sync}

        def make(ename):
            ops = self.by_eng[ename]

            def body(e):
                waited = {}
                wepoch = {}
                for op in ops:
                    for d in op.deps:
                        sid = d.semid
                        c = d.count
                        if sid[0] == "e" and wepoch.get(sid[1], -1) > sid[2]:
                            continue
                        if waited.get(sid, 0) >= c:
                            continue
                        e.wait_ge(sems[sid], c)
                        waited[sid] = c
                        if sid[0] == "e":
                            wepoch[sid[1]] = max(wepoch.get(sid[1], -1), sid[2])
                    if op.fn is None:
                        continue
                    ins = op.fn(e)
                    if op.dma_key is not None and op.inc == 1:
                        ins.then_inc(sems[op.semid])
                    elif op.dma_key is not None:
                        ins.then_inc(sems[op.semid], 16)
                    elif op.signaled:
                        ins.then_inc(sems[op.semid], 1)
            return body

        for ename in ENGS:
            if self.by_eng[ename]:
                engobj[ename](make(ename))


def _sched_check(self, ss):
    val = dict(getattr(ss, "val", {}))
    pos = {e: 0 for e in ENGS}
    progress = True
    while progress:
        progress = False
        for e in ENGS:
            ops = self.by_eng[e]
            while pos[e] < len(ops):
                op = ops[pos[e]]
                if all(val.get(d.semid, 0) >= d.count and (d.semid[0] != "e" or True) for d in op.deps):
                    if op.semid is not None:
                        if op.semid[0] == "d":
                            val[op.semid] = val.get(op.semid, 0) + op.inc
                        else:
                            val[op.semid] = val.get(op.semid, 0) + 1
                        assert val[op.semid] == op.count or op.semid[0] == "d", (op.semid, val[op.semid], op.count)
                    pos[e] += 1
                    progress = True
                else:
                    break
    stuck = {e: pos[e] for e in ENGS if pos[e] < len(self.by_eng[e])}
    assert not stuck, "deadlock in schedule: %r" % stuck
    ss.val = val
    print("sched ok: ops=%d per-eng=%r" % (len(self.ops), {e: len(self.by_eng[e]) for e in ENGS}))


Sched.check = _sched_check


class Ring:
    def __init__(self, stack, nc, name, n, shape, dtype, psum=False):
        self.bufs = []
        for i in range(n):
            nm = "r_%s%d" % (name, i)
            tn = "%s_%d" % (nm, _uid())
            if psum:
                t = stack.enter_context(nc.psum_tensor(tn, shape, dtype))
            else:
                t = stack.enter_context(nc.sbuf_tensor(tn, shape, dtype))
            self.bufs.append((t, Res(nm), nm))
        self.i = 0

    def next(self):
        b = self.bufs[self.i % len(self.bufs)]
        self.i += 1
        return b


D = 1024
SEQ = 8192
BATCH = 2
DEPTH = 2
DFF = 2816
NCORE = 8
T = 2048
NT = 4
TS = 512
EPS = 1e-6
NG = 59
G_FFN1, G_MIX, G_NAQ, G_NAK, G_QLAT, G_KVLAT, G_MQ, G_MK, G_DQ, G_DK, G_GQ, G_GK = \
    0, 8, 16, 17, 18, 20, 21, 22, 23, 24, 25, 26
G_BETA, G_FFN2, G_FIN = 27, 43, 51
WS_UQ, WS_KN, WS_V, WS_KR = 0, 768, 1152, 1408
WSM = 2176
C_ONES, C_BD64, C_BD32, C_RB, C_RD, C_ID = 0, 1, 2, 3, 4, 5
QROWS = 1152
KROWS = 1024
NVU = 14


def lambda_init(l):
    return 0.8 - 0.6 * math.exp(-0.3 * l)


class Ctx:
    pass


def _sb(st, nc, name, shape, dt):
    return st.enter_context(nc.sbuf_tensor("sb_%s_%d" % (name, _uid()), shape, dt))


class WStream:
    def __init__(self, S, ring, pf):
        self.S = S
        self.ring = ring
        self.pf = pf
        self.items = []
        self.issued = 0
        self.handles = []

    def plan(self, src_ap, ncols, part=128):
        self.items.append((src_ap, ncols, part))
        return len(self.items) - 1

    def get(self, i):
        while self.issued < min(len(self.items), i + 1 + self.pf):
            src, ncols, part = self.items[self.issued]
            buf, res, nm = self.ring.next()
            self.S.add("pool", (lambda e, buf=buf, src=src, ncols=ncols, part=part:
                                e.dma_start(out=buf[0:part, 0:ncols], in_=src)),
                       writes=[res], dma_key=nm)
            self.handles.append((buf, res))
            self.issued += 1
        return self.handles[i]


def rms_feat(c, srcs, P, ones_idx, n, gain_cols, outs):
    S = c.S
    ones = c.cmat[0:P, ones_idx * 128: ones_idx * 128 + P]
    ps, psr, _ = c.PS.next()
    last = len(srcs) - 1
    sc = float(n) ** -0.5
    for i, (src, sres) in enumerate(srcs):
        sq, sqr, _ = c.SQ.next()
        S.add("act", (lambda e, sq=sq, src=src: e.activation(out=sq[0:P, :], in_=src, func=AF.Square, scale=sc)),
              reads=[sres], writes=[sqr])
        S.add("pe", (lambda e, sq=sq, i=i: e.matmul(ps[0:P, :], ones, sq[0:P, :], start=(i == 0), stop=(i == last))),
              reads=[sqr, c.cmat_r], writes=[psr])
    rs, rsr, _ = c.RS.next()
    S.add("act", lambda e: e.activation(out=rs[0:P, :], in_=ps[0:P, :], func=AF.Ln, bias=c.eps[0:P, 0:1]),
          reads=[psr, c.eps_r], writes=[rsr])
    S.add("act", lambda e: e.activation(out=rs[0:P, :], in_=rs[0:P, :], func=AF.Exp, scale=-0.5),
          reads=[rsr], writes=[rsr])
    for i, ((src, sres), (out, ores)) in enumerate(zip(srcs, outs)):
        g = c.gains[0:P, gain_cols[i]:gain_cols[i] + 1]
        S.add("dve", (lambda e, src=src, out=out, g=g: e.scalar_tensor_tensor(
            out=out, in0=src, scalar=g, in1=rs[0:P, :], op0=ALU.mult, op1=ALU.mult)),
            reads=[sres, rsr, c.gains_r], writes=[ores])


def rope_feat(c, qn, qnr, P, r_idx, cos_ap, sin_ap, tab_r, out, outr):
    S = c.S
    R = c.cmat[0:P, r_idx * 128: r_idx * 128 + P]
    ps, psr, _ = c.PS.next()
    S.add("pe", lambda e: e.matmul(ps[0:P, :], R, qn, start=True, stop=True), reads=[qnr, c.cmat_r], writes=[psr])
    t1, t1r, _ = c.TF.next()
    t2, t2r, _ = c.TF.next()
    S.add("pool", lambda e: e.tensor_tensor(out=t1[0:P, :], in0=qn, in1=cos_ap, op=ALU.mult), reads=[qnr, tab_r], writes=[t1r])
    S.add("dve", lambda e: e.tensor_tensor(out=t2[0:P, :], in0=ps[0:P, :], in1=sin_ap, op=ALU.mult), reads=[psr, tab_r], writes=[t2r])
    S.add("pool", lambda e: e.tensor_tensor(out=out, in0=t1[0:P, :], in1=t2[0:P, :], op=ALU.add), reads=[t1r, t2r], writes=[outr])


def norm_x_tile(c, t, gcol0):
    hT, hTr, _ = c.HT.next()
    srcs = [(c.xT[:, k * T + t * TS: k * T + (t + 1) * TS], c.x_r[t]) for k in range(8)]
    outs = [(hT[:, k * TS:(k + 1) * TS], hTr) for k in range(8)]
    rms_feat(c, srcs, 128, C_ONES, D, [gcol0 + k for k in range(8)], outs)
    return hT, hTr


def ffn_tile(c, t, hT, hTr, ws, base, mid=None):
    S = c.S
    for b in range(11):
        wg, wgr = ws.get(base + 2 * b)
        wu, wur = ws.get(base + 2 * b + 1)
        for jj in range(2):
            j = 2 * b + jj
            pg, pgr, _ = c.PS.next()
            pu, pur, _ = c.PS.next()
            for k in range(8):
                S.add("pe", (lambda e, k=k, pg=pg, wg=wg, jj=jj: e.matmul(
                    pg[:, :], wg[:, k * 256 + jj * 128: k * 256 + jj * 128 + 128], hT[:, k * TS:(k + 1) * TS],
                    start=(k == 0), stop=(k == 7))), reads=[wgr, hTr], writes=[pgr])
            for k in range(8):
                S.add("pe", (lambda e, k=k, pu=pu, wu=wu, jj=jj: e.matmul(
                    pu[:, :], wu[:, k * 256 + jj * 128: k * 256 + jj * 128 + 128], hT[:, k * TS:(k + 1) * TS],
                    start=(k == 0), stop=(k == 7))), reads=[wur, hTr], writes=[pur])
            sg, sgr, _ = c.TF.next()
            S.add("act", (lambda e, sg=sg, pg=pg: e.activation(out=sg[:, :], in_=pg[:, :], func=AF.Silu)),
                  reads=[pgr], writes=[sgr])
            S.add("dve", (lambda e, sg=sg, pu=pu, j=j: e.tensor_tensor(
                out=c.actT[:, j * TS:(j + 1) * TS], in0=sg[:, :], in1=pu[:, :], op=ALU.mult)),
                reads=[sgr, pur], writes=[c.act_r[j]])
    if mid is not None:
        mid()
    for m in range(8):
        wd, wdr = ws.get(base + 22 + m)
        pd, pdr, _ = c.PS.next()
        for j in range(22):
            S.add("pe", (lambda e, j=j, pd=pd, wd=wd: e.matmul(
                pd[:, :], wd[:, j * 128:(j + 1) * 128], c.actT[:, j * TS:(j + 1) * TS],
                start=(j == 0), stop=(j == 21))), reads=[wdr, c.act_r[j]], writes=[pdr])
        xs = c.xT[:, m * T + t * TS: m * T + (t + 1) * TS]
        S.add("dve", (lambda e, pd=pd, xs=xs: e.scalar_tensor_tensor(
            out=xs, in0=pd[:, :], scalar=0.5, in1=xs, op0=ALU.mult, op1=ALU.add)),
            reads=[pdr, c.x_r[t]], writes=[c.x_r[t]])


def plan_ffn(ws, wg, wu, wd):
    base = len(ws.items)
    for b in range(11):
        ws.plan(wg[b], 2048)
        ws.plan(wu[b], 2048)
    for m in range(8):
        ws.plan(wd[m], 2816)
    return base


def common_setup(c, st, nc, S, gains_d, lconst_d, cmat_d):
    c.S = S
    c.nc = nc
    c.gains = _sb(st, nc, "gains", [128, NG], F32)
    c.gains_r = Res("gains")
    c.lconst = _sb(st, nc, "lconst", [128, 2], F32)
    c.lconst_r = Res("lconst")
    c.cmat = _sb(st, nc, "cmat", [128, 6 * 128], BF16)
    c.cmat_r = Res("cmat")
    c.eps = _sb(st, nc, "eps", [128, 1], F32)
    c.eps_r = Res("eps")
    S.add("sp", lambda e: e.dma_start(out=c.gains[:, :], in_=gains_d), writes=[c.gains_r], dma_key="gains")
    S.add("sp", lambda e: e.dma_start(out=c.lconst[:, :], in_=lconst_d), writes=[c.lconst_r], dma_key="lconst")
    S.add("pool", lambda e: e.dma_start(out=c.cmat[:, :], in_=cmat_d), writes=[c.cmat_r], dma_key="cmat")
    S.add("dve", lambda e: e.memset(c.eps[:, :], EPS), writes=[c.eps_r])


def load_x(c, st, nc, S, x_d):
    c.xT = _sb(st, nc, "xT", [128, 8 * T], F32)
    c.x_r = [Res("x%d" % t) for t in range(NT)]
    for k in range(8):
        S.add("sp", (lambda e, k=k: e.dma_start(out=c.xT[:, k * T:(k + 1) * T], in_=x_d[:, k * T:(k + 1) * T])),
              writes=c.x_r, dma_key="xT")


def store_x(c, S, xo_d, xo_r):
    for k in range(8):
        S.add("sp", (lambda e, k=k: e.dma_start(out=xo_d[:, k * T:(k + 1) * T], in_=c.xT[:, k * T:(k + 1) * T])),
              reads=c.x_r, writes=[xo_r], dma_key="xT")


def emit_phase_a(c, st, nc, S, d):
    c.PS = Ring(st, nc, "ps", 8, [128, 512], F32, psum=True)
    c.SQ = Ring(st, nc, "sq", 3, [128, 512], BF16)
    c.RS = Ring(st, nc, "rs", 2, [128, 512], F32)
    c.TF = Ring(st, nc, "tf", 4, [128, 512], F32)
    c.HT = Ring(st, nc, "hT", 2, [128, 8 * TS], BF16)
    c.actT = _sb(st, nc, "actT", [128, 22 * TS], BF16)
    c.act_r = [Res("act%d" % j) for j in range(22)]
    wring = Ring(st, nc, "wb", 6, [128, 2816], BF16)
    ws = WStream(S, wring, 4)
    wsm = _sb(st, nc, "wsm", [128, WSM], BF16)
    wsm_r = Res("wsm")
    S.add("pool", lambda e: e.dma_start(out=wsm[:, :], in_=d["wsm"]), writes=[wsm_r], dma_key="wsm")
    tabs = _sb(st, nc, "tabs", [128, 4 * T], F32)
    tab_r = Res("tabs")
    for i in range(4):
        S.add("sp", (lambda e, i=i: e.dma_start(out=tabs[:, i * T:(i + 1) * T], in_=d["tabs"][i])),
              writes=[tab_r], dma_key="tabs")
    for col, sc in ((G_NAQ, 64 ** -0.5), (G_MQ, 96 ** -0.5), (G_DQ, 32 ** -0.5), (G_GQ, 64 ** -0.5)):
        S.add("dve", (lambda e, col=col, sc=sc: e.tensor_scalar(
            out=c.gains[:, col:col + 1], in0=c.gains[:, col:col + 1], scalar1=float(sc), scalar2=None, op0=ALU.mult)),
            reads=[c.gains_r], writes=[c.gains_r])

    bases_ffn = [plan_ffn(ws, d["wg"], d["wu"], d["wd"]) for t in range(NT)]
    bases_in = []
    for t in range(NT):
        bases_in.append(len(ws.items))
        for b in range(7):
            ws.plan(d["winf"][b], 2048)
        for vb in range(2):
            ws.plan(d["winv"][vb], 2560)

    import os
    n1 = int(os.environ.get("KA_N1", NT))
    n2 = int(os.environ.get("KA_N2", NT))
    if n1 > 0:
        hT, hTr = norm_x_tile(c, 0, G_FFN1)
    for t in range(n1):
        nxt = {}

        def mid(t=t, nxt=nxt):
            if t + 1 < n1:
                nxt["h"] = norm_x_tile(c, t + 1, G_FFN1)
        ffn_tile(c, t, hT, hTr, ws, bases_ffn[t], mid)
        if t + 1 < n1:
            hT, hTr = nxt["h"]

    STG = Ring(st, nc, "stg", 6, [128, 512], BF16)
    QN = Ring(st, nc, "qn", 3, [128, 512], BF16)
    cqn = _sb(st, nc, "cqn", [128, 2 * TS], BF16)
    cqn_r = Res("cqn")
    ckvn = _sb(st, nc, "ckvn", [128, TS], BF16)
    ckvn_r = Res("ckvn")
    vst = _sb(st, nc, "vst", [128, 4, NVU, 65], BF16)
    vst_r = Res("vst")
    S.add("pool", lambda e: e.memset(vst[:, :, :, :], 1.0), writes=[vst_r])
    q_r, k_r, v_r = d["q_r"], d["k_r"], d["v_r"]

    def store_rows(dst, dres, row0, P, t, src, sres, key):
        S.add("sp", lambda e: e.dma_start(out=dst[row0:row0 + P, t * TS:(t + 1) * TS], in_=src),
              reads=[sres], writes=[dres], dma_key=key)

    def stage2_tile(t, hT, hTr, mid=None):

        def proj_chunk(wb, wbr, jj):
            p, pr, _ = c.PS.next()
            for k in range(8):
                S.add("pe", (lambda e, k=k: e.matmul(
                    p[:, :], wb[:, k * 256 + jj * 128: k * 256 + jj * 128 + 128], hT[:, k * TS:(k + 1) * TS],
                    start=(k == 0), stop=(k == 7))), reads=[wbr, hTr], writes=[pr])
            return p, pr

        def simple_head_chunk(wb, wbr, jj, ones_idx, n, gcol, dst, dres, row0, rope):
            p, pr = proj_chunk(wb, wbr, jj)
            if rope:
                qn, qnr, _ = QN.next()
                rms_feat(c, [(p[:, :], pr)], 128, ones_idx, n, [gcol], [(qn[:, :], qnr)])
                sg, sgr, key = STG.next()
                rope_feat(c, qn[:, :], qnr, 128, C_RD, tabs[:, 2 * T + t * TS: 2 * T + (t + 1) * TS],
                          tabs[:, 3 * T + t * TS: 3 * T + (t + 1) * TS], tab_r, sg[:, :], sgr)
            else:
                sg, sgr, key = STG.next()
                rms_feat(c, [(p[:, :], pr)], 128, ones_idx, n, [gcol], [(sg[:, :], sgr)])
            store_rows(dst, dres, row0, 128, t, sg[:, :], sgr, key)

        b0 = bases_in[t]
        wb, wbr = ws.get(b0 + 0)
        for jj in range(2):
            simple_head_chunk(wb, wbr, jj, C_BD64, 64, G_NAQ, d["qT"], q_r, 0 + jj * 128, False)
        wb, wbr = ws.get(b0 + 1)
        for jj in range(2):
            simple_head_chunk(wb, wbr, jj, C_BD64, 64, G_NAK, d["kT"], k_r, 0 + jj * 128, False)
        wb, wbr = ws.get(b0 + 2)
        p0, p0r = proj_chunk(wb, wbr, 0)
        p1, p1r = proj_chunk(wb, wbr, 1)
        rms_feat(c, [(p0[:, :], p0r), (p1[:, :], p1r)], 128, C_ONES, 256, [G_QLAT, G_QLAT + 1],
                 [(cqn[:, 0:TS], cqn_r), (cqn[:, TS:2 * TS], cqn_r)])
        for h in range(4):
            p, pr, _ = c.PS.next()
            for k in range(2):
                S.add("pe", (lambda e, k=k, h=h, p=p: e.matmul(
                    p[0:96, :], wsm[:, WS_UQ + k * 384 + h * 96: WS_UQ + k * 384 + h * 96 + 96],
                    cqn[:, k * TS:(k + 1) * TS], start=(k == 0), stop=(k == 1))),
                    reads=[wsm_r, cqn_r], writes=[pr])
            qn, qnr, _ = QN.next()
            rms_feat(c, [(p[0:96, :], pr)], 96, C_ONES, 96, [G_MQ], [(qn[0:96, :], qnr)])
            sg, sgr, key = STG.next()
            rope_feat(c, qn[0:96, :], qnr, 96, C_RB, tabs[0:96, 0 * T + t * TS: 0 * T + (t + 1) * TS],
                      tabs[0:96, 1 * T + t * TS: 1 * T + (t + 1) * TS], tab_r, sg[0:96, :], sgr)
            store_rows(d["qT"], q_r, 256 + h * 96, 96, t, sg[0:96, :], sgr, key)
        wb, wbr = ws.get(b0 + 3)
        p, pr = proj_chunk(wb, wbr, 0)
        rms_feat(c, [(p[:, :], pr)], 128, C_ONES, 128, [G_KVLAT], [(ckvn[:, :], ckvn_r)])
        for h in range(4):
            p, pr, _ = c.PS.next()
            S.add("pe", (lambda e, h=h, p=p: e.matmul(
                p[0:96, :], wsm[:, WS_KN + h * 96: WS_KN + h * 96 + 96], ckvn[:, :], start=True, stop=False)),
                reads=[wsm_r, ckvn_r], writes=[pr])
            for k in range(8):
                S.add("pe", (lambda e, k=k, p=p: e.matmul(
                    p[0:96, :], wsm[:, WS_KR + k * 96: WS_KR + k * 96 + 96], hT[:, k * TS:(k + 1) * TS],
                    start=False, stop=(k == 7))), reads=[wsm_r, hTr], writes=[pr])
            qn, qnr, _ = QN.next()
            rms_feat(c, [(p[0:96, :], pr)], 96, C_ONES, 96, [G_MK], [(qn[0:96, :], qnr)])
            sg, sgr, key = STG.next()
            rope_feat(c, qn[0:96, :], qnr, 96, C_RB, tabs[0:96, 0 * T + t * TS: 0 * T + (t + 1) * TS],
                      tabs[0:96, 1 * T + t * TS: 1 * T + (t + 1) * TS], tab_r, sg[0:96, :], sgr)
            store_rows(d["kT"], k_r, 256 + h * 96, 96, t, sg[0:96, :], sgr, key)
        for s in range(4):
            p, pr, _ = c.PS.next()
            S.add("pe", (lambda e, s=s, p=p: e.matmul(
                p[:, 0:256], ckvn[:, s * 128:(s + 1) * 128], wsm[:, WS_V:WS_V + 256], start=True, stop=True)),
                reads=[wsm_r, ckvn_r], writes=[pr])
            S.add("act", (lambda e, s=s, p=p: e.activation(out=vst[:, s, 10:14, 0:64], in_=p[:, 0:256].rearrange("p (u d) -> p u d", d=64), func=AF.Copy)),
                  reads=[pr], writes=[vst_r])
        simple_head_chunk(wb, wbr, 1, C_BD64, 64, G_GK, d["kT"], k_r, 896, True)
        wb, wbr = ws.get(b0 + 4)
        for jj in range(2):
            simple_head_chunk(wb, wbr, jj, C_BD32, 32, G_DQ, d["qT"], q_r, 640 + jj * 128, False)
        wb, wbr = ws.get(b0 + 5)
        for jj in range(2):
            simple_head_chunk(wb, wbr, jj, C_BD32, 32, G_DK, d["kT"], k_r, 640 + jj * 128, False)
        wb, wbr = ws.get(b0 + 6)
        for jj in range(2):
            simple_head_chunk(wb, wbr, jj, C_BD64, 64, G_GQ, d["qT"], q_r, 896 + jj * 128, True)
        if mid is not None:
            mid()
        for vb in range(2):
            wv, wvr = ws.get(b0 + 7 + vb)
            for s in range(4):
                p, pr, _ = c.PS.next()
                for k in range(8):
                    S.add("pe", (lambda e, k=k, s=s, p=p, wv=wv: e.matmul(
                        p[:, 0:320], hT[:, k * TS + s * 128: k * TS + (s + 1) * 128], wv[:, k * 320:(k + 1) * 320],
                        start=(k == 0), stop=(k == 7))), reads=[wvr, hTr], writes=[pr])
                S.add("dve", (lambda e, s=s, p=p, vb=vb: e.tensor_copy(out=vst[:, s, vb * 5:(vb + 1) * 5, 0:64], in_=p[:, 0:320].rearrange("p (u d) -> p u d", d=64))),
                      reads=[pr], writes=[vst_r])
        for u in range(NVU):
            S.add("sp", (lambda e, u=u: e.dma_start(out=d["v"][u, :, t * 4:(t + 1) * 4, :], in_=vst[:, :, u, :])),
                  reads=[vst_r], writes=[v_r], dma_key="vst")

    if n2 > 0:
        hT, hTr = norm_x_tile(c, 0, G_MIX)
    for t in range(n2):
        nxt = {}

        def mid2(t=t, nxt=nxt):
            if t + 1 < n2:
                nxt["h"] = norm_x_tile(c, t + 1, G_MIX)
        stage2_tile(t, hT, hTr, mid2)
        if t + 1 < n2:
            hT, hTr = nxt["h"]


def build_a():
    nc = bass.Bass("TRN2", target_bir_lowering=False)
    dt = lambda name, shape, dtype, kind: nc.dram_tensor(name, shape, dtype, kind=kind).ap()
    d = {}
    xT = dt("xT", [128, 8 * T], F32, "ExternalInput")
    gains = dt("gains", [128, NG], F32, "ExternalInput")
    lconst = dt("lconst", [128, 2], F32, "ExternalInput")
    cmat = dt("cmat", [128, 768], F32, "ExternalInput")
    d["wg"] = dt("wg", [11, 128, 2048], F32, "ExternalInput")
    d["wu"] = dt("wu", [11, 128, 2048], F32, "ExternalInput")
    d["wd"] = dt("wd", [8, 128, 2816], F32, "ExternalInput")
    d["winf"] = dt("winf", [7, 128, 2048], F32, "ExternalInput")
    d["winv"] = dt("winv", [2, 128, 2560], F32, "ExternalInput")
    d["wsm"] = dt("wsm", [128, WSM], F32, "ExternalInput")
    d["tabs"] = dt("tabs", [4, 128, T], F32, "ExternalInput")
    xo = dt("x1T", [128, 8 * T], F32, "ExternalOutput")
    d["qT"] = dt("qT", [QROWS, T], BF16, "ExternalOutput")
    d["kT"] = dt("kT", [KROWS, T], BF16, "ExternalOutput")
    d["v"] = dt("v", [NVU, 128, 16, 65], BF16, "ExternalOutput")
    d["q_r"], d["k_r"], d["v_r"] = Res("qT"), Res("kT"), Res("v")
    xo_r = Res("xo")
    with contextlib.ExitStack() as st:
        S = Sched(nc)
        c = Ctx()
        common_setup(c, st, nc, S, gains, lconst, cmat)
        load_x(c, st, nc, S, xT)
        emit_phase_a(c, st, nc, S, d)
        store_x(c, S, xo, xo_r)
        S.add("sp", None, reads=[xo_r, d["q_r"], d["k_r"], d["v_r"]])
        S.emit(st)
    return nc


def blk_cols(W, cpb):
    K, N = W.shape
    kc = K // 128
    nb = N // cpb
    return np.ascontiguousarray(W.reshape(kc, 128, nb, cpb).transpose(2, 1, 0, 3).reshape(nb, 128, kc * cpb))


def const_mats():
    m = np.zeros((6, 128, 128), np.float32)
    m[C_ONES] = 1.0
    for b in range(2):
        m[C_BD64, b * 64:(b + 1) * 64, b * 64:(b + 1) * 64] = 1.0
    for b in range(4):
        m[C_BD32, b * 32:(b + 1) * 32, b * 32:(b + 1) * 32] = 1.0
    for i in range(16):
        m[C_RB, 80 + i, 64 + i] = -1.0
        m[C_RB, 64 + i, 80 + i] = 1.0
    for hb in (0, 64):
        for sb in (0, 32):
            for i in range(16):
                a = hb + sb + i
                b = a + 16
                m[C_RD, b, a] = -1.0
                m[C_RD, a, b] = 1.0
    m[C_ID] = np.eye(128, dtype=np.float32)
    return np.ascontiguousarray(m.transpose(1, 0, 2).reshape(128, 768))


def rope_tables(core):
    r = core % 4
    t = np.arange(r * T, (r + 1) * T)
    inv = np.exp(-math.log(10000.0) * np.arange(0, 32, 2, dtype=np.float32) / 32).astype(np.float32)
    a_seq = t.astype(np.float32)[:, None] * inv[None, :]
    a_row = (t // 64).astype(np.float32)[:, None] * inv[None, :]
    a_col = (t % 64).astype(np.float32)[:, None] * inv[None, :]
    tabs = np.zeros((4, 128, T), np.float32)
    tabs[0, 0:64] = 1.0
    for p in range(64, 96):
        tabs[0, p] = np.cos(a_seq[:, (p - 64) % 16])
        tabs[1, p] = np.sin(a_seq[:, (p - 64) % 16])
    for p in range(128):
        f = p % 64
        a = a_row if f < 32 else a_col
        tabs[2, p] = np.cos(a[:, f % 16])
        tabs[3, p] = np.sin(a[:, f % 16])
    return tabs


def tile64(v):
    return np.tile(v, 128 // v.shape[0])


def layer_host(inp, l):
    f = lambda k: np.asarray(inp[k][l], np.float32)
    h = {}
    for nm in ("ffn1", "ffn2"):
        h[nm + "_wg"] = blk_cols(f(nm + "_w_gate"), 256)
        h[nm + "_wu"] = blk_cols(f(nm + "_w_up"), 256)
        h[nm + "_wd"] = blk_cols(f(nm + "_w_down"), 128)
    win = f("w_in")
    cols_f = np.concatenate([np.arange(0, 256), np.arange(256, 512), np.arange(768, 1024), np.arange(1024, 1152),
                             np.arange(2208, 2336), np.arange(1184, 1440), np.arange(1440, 1696), np.arange(1952, 2208)])
    cols_v = np.concatenate([np.arange(512, 768), np.arange(1696, 1952), np.arange(2336, 2464)])
    h["winf"] = blk_cols(win[:, cols_f], 256)
    h["winv"] = blk_cols(win[:, cols_v], 320)
    wsm = np.zeros((128, WSM), np.float32)
    wuq = f("mla_w_uq")
    wsm[:, WS_UQ:WS_UQ + 768] = wuq.reshape(2, 128, 384).transpose(1, 0, 2).reshape(128, 768)
    wukv = f("mla_w_ukv")
    for hh in range(4):
        wsm[:, WS_KN + hh * 96: WS_KN + hh * 96 + 64] = wukv[:, hh * 128: hh * 128 + 64]
        wsm[:, WS_V + hh * 64: WS_V + (hh + 1) * 64] = wukv[:, hh * 128 + 64: hh * 128 + 128]
    kr = win[:, 1152:1184].reshape(8, 128, 32)
    for k in range(8):
        wsm[:, WS_KR + k * 96 + 64: WS_KR + k * 96 + 96] = kr[k]
    h["wsm"] = wsm
    g = np.zeros((128, NG), np.float32)
    g[:, G_FFN1:G_FFN1 + 8] = f("ffn1_norm").reshape(8, 128).T
    g[:, G_MIX:G_MIX + 8] = f("mix_norm").reshape(8, 128).T
    g[:, G_NAQ] = tile64(f("na_q_norm"))
    g[:, G_NAK] = tile64(f("na_k_norm"))
    g[:, G_QLAT:G_QLAT + 2] = f("mla_q_lat_norm").reshape(2, 128).T
    g[:, G_KVLAT] = f("mla_kv_lat_norm")
    g[0:96, G_MQ] = f("mla_q_norm")
    g[0:96, G_MK] = f("mla_k_norm")
    g[:, G_DQ] = tile64(f("diff_q_norm"))
    g[:, G_DK] = tile64(f("diff_k_norm"))
    g[:, G_GQ] = tile64(f("gqa_q_norm"))
    g[:, G_GK] = tile64(f("gqa_k_norm"))
    betas = np.concatenate([f("na_beta"), f("mla_beta"), np.tile(f("diff_subln"), 4), f("gqa_beta")])
    g[:, G_BETA:G_BETA + 8] = betas.reshape(8, 128).T
    g[:, G_FFN2:G_FFN2 + 8] = f("ffn2_norm").reshape(8, 128).T
    g[:, G_FIN:G_FIN + 8] = f("final_norm").reshape(8, 128).T
    h["gains"] = g
    lc = np.zeros((128, 2), np.float32)
    lc[:, 0] = lambda_init(l)
    lc[:, 1] = 1.0 - lambda_init(l)
    h["lconst"] = lc
    return h


def emit_phase_c(c, st, nc, S, d):
    c.PS = Ring(st, nc, "ps", 8, [128, 512], F32, psum=True)
    c.SQ = Ring(st, nc, "sq", 3, [128, 512], BF16)
    c.RS = Ring(st, nc, "rs", 2, [128, 512], F32)
    c.TF = Ring(st, nc, "tf", 4, [128, 512], F32)
    c.HT = Ring(st, nc, "hT", 2, [128, 8 * TS], BF16)
    c.actT = _sb(st, nc, "actT", [128, 22 * TS], BF16)
    c.act_r = [Res("cact%d" % j) for j in range(22)]
    wring = Ring(st, nc, "wb", 6, [128, 2816], BF16)
    ws = WStream(S, wring, 4)
    YT = Ring(st, nc, "yt", 1, [128, 8 * TS], F32)
    yn = _sb(st, nc, "yn", [128, 8 * TS], BF16)
    yn_r = Res("yn")
    S.add("dve", lambda e: e.tensor_scalar(out=c.gains[:, G_BETA + 4:G_BETA + 6], in0=c.gains[:, G_BETA + 4:G_BETA + 6],
                                            scalar1=c.lconst[:, 1:2], scalar2=None, op0=ALU.mult),
          reads=[c.gains_r, c.lconst_r], writes=[c.gains_r])
    bases = []
    for t in range(NT):
        b0 = len(ws.items)
        for m in range(8):
            ws.plan(d["wout"][m], 1024)
        plan_ffn(ws, d["wg"], d["wu"], d["wd"])
        bases.append(b0)

    def tile_c(t, yt, ytr, key):
        for u in range(16):
            cc, hf = u // 2, u % 2
            S.add("sp", (lambda e, u=u, cc=cc, hf=hf: e.dma_start(out=yt[hf * 64:(hf + 1) * 64, cc * TS:(cc + 1) * TS],
                                                                  in_=d["y"][u, :, t * TS:(t + 1) * TS])),
                  reads=[d["y_r"]], writes=[ytr], dma_key=key)
        for g in range(4):
            ccs = [2 * g, 2 * g + 1]
            if g == 2:
                for cc in ccs:
                    rms_feat(c, [(yt[:, cc * TS:(cc + 1) * TS], ytr)], 128, C_BD64, 64, [G_BETA + cc],
                             [(yn[:, cc * TS:(cc + 1) * TS], yn_r)])
            else:
                rms_feat(c, [(yt[:, cc * TS:(cc + 1) * TS], ytr) for cc in ccs], 128, C_ONES, 256, [G_BETA + cc for cc in ccs],
                         [(yn[:, cc * TS:(cc + 1) * TS], yn_r) for cc in ccs])
        for m in range(8):
            wb, wbr = ws.get(bases[t] + m)
            p, pr, _ = c.PS.next()
            for k in range(8):
                S.add("pe", (lambda e, k=k, p=p, wb=wb: e.matmul(
                    p[:, :], wb[:, k * 128:(k + 1) * 128], yn[:, k * TS:(k + 1) * TS],
                    start=(k == 0), stop=(k == 7))), reads=[wbr, yn_r], writes=[pr])
            xs = c.xT[:, m * T + t * TS: m * T + (t + 1) * TS]
            S.add("dve", (lambda e, p=p, xs=xs: e.tensor_tensor(out=xs, in0=p[:, :], in1=xs, op=ALU.add)),
                  reads=[pr, c.x_r[t]], writes=[c.x_r[t]])
        hT, hTr = norm_x_tile(c, t, G_FFN2)

        def mid(t=t):
            if t > 0:
                final_norm(t - 1)
        ffn_tile(c, t, hT, hTr, ws, bases[t] + 8, mid)
        if t == NT - 1:
            final_norm(t)

    def final_norm(t):
        srcs = [(c.xT[:, k * T + t * TS: k * T + (t + 1) * TS], c.x_r[t]) for k in range(8)]
        rms_feat(c, srcs, 128, C_ONES, D, [G_FIN + k for k in range(8)], srcs)

    for t in range(NT):
        yt, ytr, key = YT.next()
        tile_c(t, yt, ytr, key)


def build_c():
    nc = bass.Bass("TRN2", target_bir_lowering=False)
    dt = lambda name, shape, dtype, kind: nc.dram_tensor(name, shape, dtype, kind=kind).ap()
    d = {}
    xT = dt("xT", [128, 8 * T], F32, "ExternalInput")
    gains = dt("gains", [128, NG], F32, "ExternalInput")
    lconst = dt("lconst", [128, 2], F32, "ExternalInput")
    cmat = dt("cmat", [128, 768], F32, "ExternalInput")
    d["wg"] = dt("wg", [11, 128, 2048], F32, "ExternalInput")
    d["wu"] = dt("wu", [11, 128, 2048], F32, "ExternalInput")
    d["wd"] = dt("wd", [8, 128, 2816], F32, "ExternalInput")
    d["wout"] = dt("wout", [8, 128, 1024], F32, "ExternalInput")
    d["y"] = dt("y", [16, 64, T], F32, "ExternalInput")
    d["y_r"] = Res("y")
    xo = dt("xoT", [128, 8 * T], F32, "ExternalOutput")
    xo_r = Res("xo")
    with contextlib.ExitStack() as st:
        S = Sched(nc)
        c = Ctx()
        common_setup(c, st, nc, S, gains, lconst, cmat)
        load_x(c, st, nc, S, xT)
        emit_phase_c(c, st, nc, S, d)
        store_x(c, S, xo, xo_r)
        S.add("sp", None, reads=[xo_r])
        S.emit(st)
    return nc


LA = 2


def near_entries(kind, qb):
    ent = []
    if kind == "A":
        lo, hi, halo = 4 * qb - 2, 4 * qb + 5, 2
    else:
        lo, hi, halo = 4 * qb - 1, 4 * qb + 4, 1
    for lc in range(max(lo, 0), min(hi, 15) + 1):
        ent.append(("own", lc))
    if qb == 0:
        for j in range(4):
            for x in range(16 - halo, 16):
                ent.append(("gath", 16 * j + x))
    if qb == 3:
        for j in range(4):
            for x in range(halo):
                ent.append(("gath", 16 * j + x))
    return ent


def tile_ids():
    ids = {}
    n = 0
    for kind in ("A", "C"):
        for h in range(4):
            for qb in range(4):
                for i, _e in enumerate(near_entries(kind, qb)):
                    ids[(kind, h, qb, i)] = n
                    n += 1
    return ids, n


TILE_IDS, NTILES = tile_ids()


def emit_phase_b(c, st, nc, S, d):
    PSS = Ring(st, nc, "pss", 3, [128, 1024], F32, psum=True)
    PO = Ring(st, nc, "po", 1, [128, 512], F32, psum=True)
    PBC = Ring(st, nc, "pbc", 1, [128, 512], F32, psum=True)
    PT = Ring(st, nc, "pt", 4, [128, 1024], BF16)
    KT = Ring(st, nc, "kt", 2, [128, SEQ], BF16)
    VT = Ring(st, nc, "vt", 2, [128, 64, 65], BF16)
    KTO = Ring(st, nc, "kto", 2, [128, T], BF16)
    VTO = Ring(st, nc, "vto", 2, [128, 16, 65], BF16)
    QT = Ring(st, nc, "qt", 4, [128, T], BF16)
    BT = Ring(st, nc, "bt", 12, [128, 512], BF16)
    VS = Ring(st, nc, "vs", 2, [128, 64, 65], BF16)
    SC = Ring(st, nc, "sc", 2, [128, 64], F32)
    OSB = Ring(st, nc, "osb", 2, [128, 512], F32)
    ON = Ring(st, nc, "on", 4, [64, 512], F32)
    RR = Ring(st, nc, "rr", 2, [128, 512], F32)
    for ring_ in (KT, KTO):
        for buf_, res_, _k in ring_.bufs:
            S.add("pool", (lambda e, buf_=buf_: e.memset(buf_[:, :], 0.0)), writes=[res_])
    bts = WStream(S, BT, 8)
    t5c = _sb(st, nc, "t5c", [128, 1024], F32)
    t5c_r = Res("t5c")
    S.add("sp", lambda e: e.dma_start(out=t5c[:, :], in_=d["t5c"]), writes=[t5c_r], dma_key="t5c")
    onesf = _sb(st, nc, "onesf", [128, 64], F32)
    onesf_r = Res("onesf")
    S.add("dve", lambda e: e.memset(onesf[:, :], 1.0), writes=[onesf_r])
    dl = _sb(st, nc, "dl", [64, 128], F32)
    dl_r = Res("dl")
    S.add("sp", lambda e: e.dma_start(out=dl[:, :], in_=d["dlam"]), writes=[dl_r], dma_key="dl")
    lt = _sb(st, nc, "lt", [64, 72], F32)
    lt_r = Res("lt")
    S.add("dve", lambda e: e.tensor_tensor(out=lt[:, 0:32], in0=dl[:, 0:32], in1=dl[:, 32:64], op=ALU.mult), reads=[dl_r], writes=[lt_r])
    S.add("dve", lambda e: e.tensor_tensor(out=lt[:, 32:64], in0=dl[:, 64:96], in1=dl[:, 96:128], op=ALU.mult), reads=[dl_r, lt_r], writes=[lt_r])
    S.add("dve", lambda e: e.reduce_sum(lt[:, 64:65], lt[:, 0:32], axis=mybir.AxisListType.X), reads=[lt_r], writes=[lt_r])
    S.add("dve", lambda e: e.reduce_sum(lt[:, 65:66], lt[:, 32:64], axis=mybir.AxisListType.X), reads=[lt_r], writes=[lt_r])
    S.add("act", lambda e: e.activation(out=lt[:, 66:68], in_=lt[:, 64:66], func=AF.Exp), reads=[lt_r], writes=[lt_r])
    S.add("dve", lambda e: e.tensor_tensor(out=lt[:, 68:69], in0=lt[:, 67:68], in1=lt[:, 66:67], op=ALU.subtract), reads=[lt_r], writes=[lt_r])
    S.add("dve", lambda e: e.tensor_scalar(out=lt[:, 69:70], in0=lt[:, 68:69], scalar1=c.lconst[0:64, 0:1], scalar2=None, op0=ALU.subtract),
          reads=[lt_r, c.lconst_r], writes=[lt_r])
    nlam = lt[:, 69:70]
    ident = c.cmat[:, C_ID * 128:(C_ID + 1) * 128]
    y_r = d["y_r"]

    pending = []

    def attend(qt, qtr, qp0, dq, qb, chunks, cont):
        O, Or, _ = PO.next()
        steps = []
        i = 0
        isc = lambda ch: ch[5] is not None and ch[5][0] == "const"
        while i < len(chunks):
            if i + 1 < len(chunks) and not isc(chunks[i]) and not isc(chunks[i + 1]):
                steps.append([chunks[i], chunks[i + 1]])
                i += 2
            else:
                steps.append([chunks[i]])
                i += 1
        n = len(steps)
        total = len(chunks)
        pend = []
        pvi = [0]
        for i in range(n + LA):
            if i < n:
                st_ = steps[i]
                s_, sr, _ = PSS.next()
                w = 512 * len(st_)
                for hf, (kt, ktr, vt, vtr, kc, mode) in enumerate(st_):
                    sl = s_[:, hf * 512:(hf + 1) * 512]
                    if mode is not None and mode[0] == "tile":
                        bt, btr = bts.get(mode[1])
                        S.add("pe", (lambda e, sl=sl, bt=bt: e.matmul(sl, ident, bt[:, :], start=True, stop=False)),
                              reads=[btr, c.cmat_r], writes=[sr])
                        first = False
                    else:
                        first = True
                    S.add("pe", (lambda e, sl=sl, kc=kc, first=first, kt=kt: e.matmul(
                        sl, kt[qp0:qp0 + dq, kc * 128:(kc + 1) * 128], qt[qp0:qp0 + dq, qb * TS:(qb + 1) * TS],
                        start=first, stop=True)), reads=[ktr, qtr], writes=[sr])
                p_, pr, _ = PT.next()
                mode = st_[0][5]
                if len(st_) == 1 and mode is not None and mode[0] == "const":
                    bcol = mode[1]
                    S.add("act", (lambda e, s_=s_, p_=p_, bcol=bcol: e.activation(out=p_[:, 0:512], in_=s_[:, 0:512], func=AF.Exp, bias=bcol)),
                          reads=[sr, t5c_r], writes=[pr])
                else:
                    S.add("act", (lambda e, s_=s_, p_=p_, w=w: e.activation(out=p_[:, 0:w], in_=s_[:, 0:w], func=AF.Exp)),
                          reads=[sr], writes=[pr])
                pend.append((st_, p_, pr))
            j = i - LA
            if j >= 0:
                st_, p_, pr = pend[j]
                for hf, (kt, ktr, vt, vtr, kc, mode) in enumerate(st_):
                    k_ = pvi[0]
                    pvi[0] += 1
                    S.add("pe", (lambda e, kc=kc, p_=p_, k_=k_, vt=vt, hf=hf: e.matmul(
                        O[0:65, :], vt[:, kc, :], p_[:, hf * 512:(hf + 1) * 512], start=(k_ == 0), stop=(k_ == total - 1))),
                        reads=[vtr, pr], writes=[Or])
            if i == min(3, n + LA - 1) and pending:
                pending.pop(0)()
        osb, osbr, _ = OSB.next()
        S.add("dve", lambda e: e.tensor_copy(out=osb[0:65, :], in_=O[0:65, :]), reads=[Or], writes=[osbr])

        def fin():
            rr, rrr, _ = RR.next()
            S.add("dve", lambda e: e.reciprocal(out=rr[64:65, :], in_=osb[64:65, :]), reads=[osbr], writes=[rrr])
            bc, bcr, _ = PBC.next()
            S.add("pe", lambda e: e.matmul(bc[0:64, :], onesf[64:65, 0:64], rr[64:65, :], start=True, stop=True),
                  reads=[rrr, onesf_r], writes=[bcr])
            on, onr, _ = ON.next()
            S.add("dve", lambda e: e.tensor_tensor(out=on[:, :], in0=osb[0:64, :], in1=bc[0:64, :], op=ALU.mult),
                  reads=[osbr, bcr], writes=[onr])
            cont(on, onr)
        pending.append(fin)

    def store_y(u, qb, on, onr):
        S.add("sp", lambda e: e.dma_start(out=d["y"][u, :, qb * TS:(qb + 1) * TS], in_=on[:, :]),
              reads=[onr], writes=[y_r], dma_key="ystore")

    plan = []
    for h in range(4):
        plan.append(("A", h * 64, 64, h, h * 64, 64, [h]))
    for h in range(4):
        plan.append(("B", 256 + h * 96, 96, 10 + h, 256 + h * 96, 96, [4 + h]))
    for h in range(4):
        plan.append(("C", 640 + h * 64, 64, 4 + h, 640 + h * 64, 64, [8 + h]))
    for g in range(2):
        plan.append(("D", 896 + g * 64, 64, 8 + g, 896 + 2 * g * 64, 128, [12 + 2 * g, 13 + 2 * g]))
    import os
    units = [int(x) for x in os.environ.get("KB_UNITS", ",".join(str(i) for i in range(len(plan)))).split(",")]
    bt_plan = {}
    for ui in units:
        kind = plan[ui][0]
        if kind in ("A", "C"):
            h = ui % 4
            for qb in range(4):
                for m in range(2 if kind == "C" else 1):
                    for i, _e in enumerate(near_entries(kind, qb)):
                        bt_plan[(kind, h, qb, m, i)] = bts.plan(d["btl"][TILE_IDS[(kind, h, qb, i)]], 512)

    loaded = {}

    def load_unit(ui):
        kind, krow, kd, vu, qrow, qd, yus = plan[ui]
        kt, ktr, kkey = KT.next()
        vt, vtr, vkey = VT.next()
        qt, qtr, qkey = QT.next()
        for j in range(4):
            kr0 = kg_row(j, krow)
            S.add("sp", (lambda e, j=j, kr0=kr0: e.dma_start(out=kt[0:kd, j * T:(j + 1) * T], in_=d["kTg"][kr0:kr0 + kd, :])),
                  writes=[ktr], dma_key=kkey)
            if kind == "D":
                S.add("sp", (lambda e, j=j, kr0=kr0: e.dma_start(out=kt[64:128, j * T:(j + 1) * T], in_=d["kTg"][kr0:kr0 + kd, :])),
                      writes=[ktr], dma_key=kkey)
            vr0 = vg_row(j, vu)
            S.add("sp", (lambda e, j=j, vr0=vr0: e.dma_start(out=vt[:, 16 * j:16 * (j + 1), :],
                                                              in_=d["vg"][vr0:vr0 + 128, :].rearrange("p (c d) -> p c d", d=65))),
                  writes=[vtr], dma_key=vkey)
        if kind in ("A", "B"):
            pieces = [(0, qd, qrow)]
        elif kind == "C":
            pieces = [(0, 32, qrow), (32, 32, qrow + 32)]
        else:
            pieces = [(0, 64, qrow), (64, 64, qrow + 64)]
        qtiles = []
        for p0, pn, r0 in pieces:
            if qtiles or True:
                qt, qtr, qkey = (qt, qtr, qkey) if not qtiles else QT.next()
            S.add("pool", (lambda e, qt=qt: e.memset(qt[:, :], 0.0)), writes=[qtr])
            S.add("sp", (lambda e, qt=qt, p0=p0, pn=pn, r0=r0: e.dma_start(out=qt[p0:p0 + pn, :], in_=d["qT"][r0:r0 + pn, :])),
                  writes=[qtr], dma_key=qkey)
            qtiles.append((qt, qtr))
        own = None
        if kind in ("A", "C"):
            kto, ktor, kokey = KTO.next()
            vto, vtor, vokey = VTO.next()
            S.add("sp", lambda e: e.dma_start(out=kto[0:kd, :], in_=d["kT"][krow:krow + kd, :]), writes=[ktor], dma_key=kokey)
            S.add("sp", lambda e: e.dma_start(out=vto[:, :, :], in_=d["v"][vu]), writes=[vtor], dma_key=vokey)
            own = (kto, ktor, vto, vtor)
        loaded[ui] = (kt, ktr, vt, vtr, qtiles, own)

    load_unit(units[0])
    for n_, ui in enumerate(units):
        if n_ + 1 < len(units):
            load_unit(units[n_ + 1])
        kind, krow, kd, vu, qrow, qd, yus = plan[ui]
        kt, ktr, vt, vtr, qtiles, own = loaded.pop(ui)
        h = ui % 4
        dense = [(kt, ktr, vt, vtr, kc, None) for kc in range(64)]

        def near_list(kind, qb, m):
            out = []
            for i, (src, ch) in enumerate(near_entries(kind, qb)):
                mode = ("tile", bt_plan[(kind, h, qb, m, i)])
                if src == "own":
                    out.append((own[0], own[1], own[2], own[3], ch, mode))
                else:
                    out.append((kt, ktr, vt, vtr, ch, mode))
            return out

        for qb in range(4):
            if kind == "A":
                attend(qtiles[0][0], qtiles[0][1], 0, 128, qb, near_list("A", qb, 0),
                       (lambda on, onr, u=yus[0], qb=qb: store_y(u, qb, on, onr)))
            elif kind == "B":
                attend(qtiles[0][0], qtiles[0][1], 0, 128, qb, dense,
                       (lambda on, onr, u=yus[0], qb=qb: store_y(u, qb, on, onr)))
            elif kind == "D":
                for hh in range(2):
                    attend(qtiles[hh][0], qtiles[hh][1], 0, 128, qb, dense,
                           (lambda on, onr, u=yus[hh], qb=qb: store_y(u, qb, on, onr)))
            else:
                sc, scr, _ = SC.next()
                S.add("act", (lambda e, sc=sc, qb=qb, h=h: e.activation(out=sc[:, :], in_=t5c[:, (h * 4 + qb) * 64:(h * 4 + qb + 1) * 64], func=AF.Exp)),
                      reads=[t5c_r], writes=[scr])
                vs, vsr, _ = VS.next()
                S.add("dve", (lambda e, sc=sc, vs=vs, vt=vt: e.tensor_tensor(
                    out=vs[:, :, :], in0=vt[:, :, :], in1=sc[:, :].unsqueeze(2).broadcast_to([128, 64, 65]), op=ALU.mult)),
                    reads=[vtr, scr], writes=[vsr])
                state = {}

                def cont0(on, onr, state=state):
                    state["on0"] = (on, onr)

                def cont1(on1, on1r, state=state, u=yus[0], qb=qb):
                    on0, on0r = state["on0"]
                    yo, yor, _ = ON.next()
                    S.add("dve", (lambda e, yo=yo, on0=on0, on1=on1: e.scalar_tensor_tensor(
                        out=yo[:, :], in0=on1[:, :], scalar=nlam, in1=on0[:, :], op0=ALU.mult, op1=ALU.add)),
                        reads=[on0r, on1r, lt_r], writes=[yor])
                    store_y(u, qb, yo, yor)
                for m in range(2):
                    ch = [(kt, ktr, vs, vsr, kc, None) for kc in range(64)] + near_list("C", qb, m)
                    attend(qtiles[m][0], qtiles[m][1], 0, 128, qb, ch, cont0 if m == 0 else cont1)
    while pending:
        pending.pop(0)()


def build_b():
    nc = bass.Bass("TRN2", target_bir_lowering=False)
    dt = lambda name, shape, dtype, kind: nc.dram_tensor(name, shape, dtype, kind=kind).ap()
    d = {}
    gains = dt("gains", [128, NG], F32, "ExternalInput")
    lconst = dt("lconst", [128, 2], F32, "ExternalInput")
    cmat = dt("cmat", [128, 768], F32, "ExternalInput")
    d["qT"] = dt("qT", [QROWS, T], BF16, "ExternalInput")
    d["kT"] = dt("kT", [KROWS, T], BF16, "ExternalInput")
    d["v"] = dt("v", [NVU, 128, 16, 65], BF16, "ExternalInput")
    d["kTg"] = dt("kTg", [4 * KROWS, T], BF16, "ExternalInput")
    d["vg"] = dt("vg", [4 * NVU * 128, 16 * 65], BF16, "ExternalInput")
    d["btl"] = dt("btl", [NTILES, 128, 512], F32, "ExternalInput")
    d["t5c"] = dt("t5c", [128, 1024], F32, "ExternalInput")
    d["dlam"] = dt("dlam", [64, 128], F32, "ExternalInput")
    d["y"] = dt("y", [16, 64, T], F32, "ExternalOutput")
    d["y_r"] = Res("y")
    with contextlib.ExitStack() as st:
        S = Sched(nc)
        c = Ctx()
        common_setup(c, st, nc, S, gains, lconst, cmat)
        emit_phase_b(c, st, nc, S, d)
        S.add("sp", None, reads=[d["y_r"]])
        S.emit(st)
    return nc


def t5_bucket_np(rel):
    half = 16
    max_exact = 8
    n = np.abs(rel)
    large = max_exact + (np.log(np.maximum(n, 1).astype(np.float32) / max_exact)
                         / np.float32(math.log(128 / max_exact)) * (half - max_exact)).astype(np.int32)
    large = np.minimum(large, half - 1)
    return np.where(rel > 0, half, 0) + np.where(n < max_exact, n, large)


def bias_tiles_host(inp, l, core):
    r = core % 4
    rpb = np.asarray(inp["na_rpb"][l], np.float32)
    t5 = np.asarray(inp["t5_bias"], np.float32)
    btl = np.empty((NTILES, 128, 512), np.float32)
    kk = np.arange(128)[:, None]
    qq = np.arange(512)[None, :]
    NEG = np.float32(-1e30)
    for qb in range(4):
        qbg = 4 * r + qb
        qtok = qbg * 512 + qq
        qr, qc = qtok // 64, qtok % 64
        rs = np.clip(qr - 4, 0, 120)
        cs = np.clip(qc - 8, 0, 48)
        for i, (src, ch) in enumerate(near_entries("A", qb)):
            g = 16 * r + ch if src == "own" else ch
            ktok = g * 128 + kk
            kr, kcol = ktok // 64, ktok % 64
            valid = (kr >= rs) & (kr < rs + 8) & (kcol >= cs) & (kcol < cs + 16)
            dr = np.clip(kr - qr + 7, 0, 14)
            dc = np.clip(kcol - qc + 15, 0, 30)
            flat = np.where(valid, dr * 31 + dc, 15 * 31)
            for h in range(4):
                tab = np.concatenate([rpb[h].ravel(), np.array([NEG], np.float32)])
                btl[TILE_IDS[("A", h, qb, i)]] = tab[flat]
        for i, (src, ch) in enumerate(near_entries("C", qb)):
            g = 16 * r + ch if src == "own" else ch
            if 4 * qbg - 1 <= g <= 4 * qbg + 4:
                bk = t5_bucket_np((g * 128 + kk) - qtok)
                for h in range(4):
                    btl[TILE_IDS[("C", h, qb, i)]] = t5[:, h][bk]
            else:
                for h in range(4):
                    btl[TILE_IDS[("C", h, qb, i)]] = NEG
    t5c = np.zeros((128, 1024), np.float32)
    for qb in range(4):
        qbg = 4 * r + qb
        for kc in range(64):
            for h in range(4):
                if 4 * qbg - 1 <= kc <= 4 * qbg + 4:
                    val = NEG
                else:
                    val = t5[31, h] if kc > 4 * qbg + 3 else t5[15, h]
                t5c[:, (h * 4 + qb) * 64 + kc] = val
    return btl, t5c


_PROGS = {}


def _prog(name):
    if name not in _PROGS:
        _PROGS[name] = {"a": build_a, "b": build_b, "c": build_c, "f": build_fused}[name]()
    return _PROGS[name]


def to_featmajor(xc):
    return np.ascontiguousarray(xc.T.reshape(8, 128, T).transpose(1, 0, 2).reshape(128, 8 * T))


def from_featmajor(xT):
    return xT.reshape(128, 8, T).transpose(1, 0, 2).reshape(D, T).T


def kernel_unfused(**inp):
    x = np.asarray(inp["x"], np.float32).reshape(BATCH * SEQ, D)
    cm = const_mats()
    xTs = [to_featmajor(x[c * T:(c + 1) * T]) for c in range(NCORE)]
    tabs = [rope_tables(c) for c in range(NCORE)]
    cores = list(range(NCORE))
    for l in range(DEPTH):
        H = layer_host(inp, l)
        common = {"gains": H["gains"], "lconst": H["lconst"], "cmat": cm}
        maps = [dict(common, xT=xTs[c], wg=H["ffn1_wg"], wu=H["ffn1_wu"], wd=H["ffn1_wd"], winf=H["winf"],
                     winv=H["winv"], wsm=H["wsm"], tabs=tabs[c]) for c in cores]
        ra = run_bass_kernel_spmd(_prog("a"), maps, core_ids=cores).results
        maps = []
        dl = np.ascontiguousarray(np.broadcast_to(np.asarray(inp["diff_lambda"][l], np.float32).reshape(1, 128), (64, 128)))
        for c in cores:
            b = c // 4
            kTg = np.zeros((4 * KROWS, T), np.asarray(ra[0]["kT"]).dtype)
            vg = np.zeros((4 * NVU * 128, 1040), np.asarray(ra[0]["v"]).dtype)
            for j in range(4):
                kj = np.asarray(ra[4 * b + j]["kT"])
                vj = np.asarray(ra[4 * b + j]["v"]).reshape(NVU * 128, 1040)
                for a, bb in K_PARTS:
                    kTg[4 * a + j * (bb - a): 4 * a + (j + 1) * (bb - a)] = kj[a:bb]
                for a, bb in V_PARTS:
                    vg[(4 * a + j * (bb - a)) * 128: (4 * a + (j + 1) * (bb - a)) * 128] = vj[a * 128:bb * 128]
            btl, t5c = bias_tiles_host(inp, l, c)
            maps.append(dict(common, qT=np.asarray(ra[c]["qT"]), kT=np.asarray(ra[c]["kT"]), v=np.asarray(ra[c]["v"]),
                             kTg=kTg, vg=vg, btl=btl, t5c=t5c, dlam=dl))
        rb = run_bass_kernel_spmd(_prog("b"), maps, core_ids=cores).results
        wout = blk_cols(np.asarray(inp["w_out"][l], np.float32), 128)
        maps = [dict(common, xT=np.asarray(ra[c]["x1T"]), y=np.asarray(rb[c]["y"]), wg=H["ffn2_wg"], wu=H["ffn2_wu"],
                     wd=H["ffn2_wd"], wout=wout) for c in cores]
        rc = run_bass_kernel_spmd(_prog("c"), maps, core_ids=cores).results
        xTs = [np.asarray(rc[c]["xoT"]) for c in cores]
    out = np.concatenate([from_featmajor(xTs[c]) for c in cores], axis=0)
    return np.ascontiguousarray(out.reshape(BATCH, SEQ, D).astype(np.float32))


RG = [[0, 1, 2, 3], [4, 5, 6, 7]]
K_PARTS = [(0, 256), (256, 448), (448, 640), (640, 896), (896, 1024)]
V_PARTS = [(0, 3), (3, 6), (6, 9), (9, 12), (12, 14)]


def kg_row(j, row):
    for a, b in K_PARTS:
        if a <= row < b:
            return 4 * a + j * (b - a) + (row - a)
    raise AssertionError(row)


def vg_row(j, u):
    for a, b in V_PARTS:
        if a <= u < b:
            return (4 * a + j * (b - a) + (u - a)) * 128
    raise AssertionError(u)


def build_fused():
    nc = bass.Bass("TRN2", target_bir_lowering=False)
    dt = lambda name, shape, dtype, kind, **kw: nc.dram_tensor(name, shape, dtype, kind=kind, **kw).ap()
    xT_d = dt("xT", [128, 8 * T], F32, "ExternalInput")
    cmat_d = dt("cmat", [128, 768], F32, "ExternalInput")
    tabs_d = dt("tabs", [4, 128, T], F32, "ExternalInput")
    t5c_d = dt("t5c", [128, 1024], F32, "ExternalInput")
    xo_d = dt("xoT", [128, 8 * T], F32, "ExternalOutput")
    LD = []
    for l in range(DEPTH):
        sfx = "_%d" % l
        d = {}
        d["gains"] = dt("gains" + sfx, [128, NG], F32, "ExternalInput")
        d["lconst"] = dt("lconst" + sfx, [128, 2], F32, "ExternalInput")
        for nm in ("wg1", "wu1", "wg2", "wu2"):
            d[nm] = dt(nm + sfx, [11, 128, 2048], F32, "ExternalInput")
        for nm in ("wd1", "wd2"):
            d[nm] = dt(nm + sfx, [8, 128, 2816], F32, "ExternalInput")
        d["winf"] = dt("winf" + sfx, [7, 128, 2048], F32, "ExternalInput")
        d["winv"] = dt("winv" + sfx, [2, 128, 2560], F32, "ExternalInput")
        d["wsm"] = dt("wsm" + sfx, [128, WSM], F32, "ExternalInput")
        d["wout"] = dt("wout" + sfx, [8, 128, 1024], F32, "ExternalInput")
        d["btl"] = dt("btl" + sfx, [NTILES, 128, 512], F32, "ExternalInput")
        d["dlam"] = dt("dlam" + sfx, [64, 128], F32, "ExternalInput")
        d["qT"] = dt("s_qT" + sfx, [QROWS, T], BF16, "Internal")
        d["kT"] = dt("s_kT" + sfx, [KROWS, T], BF16, "Internal")
        d["v"] = dt("s_v" + sfx, [NVU, 128, 16, 65], BF16, "Internal")
        d["kTg"] = dt("s_kTg" + sfx, [4 * KROWS, T], BF16, "Internal", addr_space="Local")
        d["vg"] = dt("s_vg" + sfx, [4 * NVU * 128, 16 * 65], BF16, "Internal", addr_space="Local")
        d["y"] = dt("s_y" + sfx, [16, 64, T], F32, "Internal")
        LD.append(d)
    with contextlib.ExitStack() as top:
        ss = SemState(top)
        c = Ctx()
        c.xT = _sb(top, nc, "xT", [128, 8 * T], F32)

        def fresh_x():
            c.x_r = [Res("x%d" % t) for t in range(NT)]

        with contextlib.ExitStack() as st:
            S = Sched(nc, ss)
            fresh_x()
            for k in range(8):
                S.add("sp", (lambda e, k=k: e.dma_start(out=c.xT[:, k * T:(k + 1) * T], in_=xT_d[:, k * T:(k + 1) * T])),
                      writes=c.x_r, dma_key="xT")
            S.drain()
            S.emit(st)
        for l in range(DEPTH):
            d = LD[l]
            with contextlib.ExitStack() as st:
                S = Sched(nc, ss)
                fresh_x()
                common_setup(c, st, nc, S, d["gains"], d["lconst"], cmat_d)
                da = dict(wg=d["wg1"], wu=d["wu1"], wd=d["wd1"], winf=d["winf"], winv=d["winv"], wsm=d["wsm"], tabs=tabs_d,
                          qT=d["qT"], kT=d["kT"], v=d["v"], q_r=Res("q"), k_r=Res("k"), v_r=Res("v"))
                emit_phase_a(c, st, nc, S, da)
                S.drain()
                S.emit(st)
            with contextlib.ExitStack() as st:
                S = Sched(nc, ss)
                v2 = d["v"].rearrange("u p c d -> (u p) (c d)")
                for a, b in K_PARTS:
                    S.add("pool", (lambda e, d=d, a=a, b=b: e.collective_compute(
                        "AllGather", ALU.bypass, replica_groups=RG, ins=[d["kT"][a:b, :]], outs=[d["kTg"][4 * a:4 * b, :]])),
                        dma_key="cc", inc=1)
                for a, b in V_PARTS:
                    S.add("pool", (lambda e, d=d, a=a, b=b, v2=v2: e.collective_compute(
                        "AllGather", ALU.bypass, replica_groups=RG, ins=[v2[a * 128:b * 128, :]],
                        outs=[d["vg"][4 * a * 128:4 * b * 128, :]])), dma_key="cc", inc=1)
                S.drain()
                S.emit(st)
            with contextlib.ExitStack() as st:
                S = Sched(nc, ss)
                common_setup(c, st, nc, S, d["gains"], d["lconst"], cmat_d)
                db = dict(qT=d["qT"], kT=d["kT"], v=d["v"], kTg=d["kTg"], vg=d["vg"], btl=d["btl"], t5c=t5c_d,
                          dlam=d["dlam"], y=d["y"], y_r=Res("y"))
                emit_phase_b(c, st, nc, S, db)
                S.drain()
                S.emit(st)
            with contextlib.ExitStack() as st:
                S = Sched(nc, ss)
                fresh_x()
                common_setup(c, st, nc, S, d["gains"], d["lconst"], cmat_d)
                dc = dict(wg=d["wg2"], wu=d["wu2"], wd=d["wd2"], wout=d["wout"], y=d["y"], y_r=Res("y"))
                emit_phase_c(c, st, nc, S, dc)
                S.drain()
                S.emit(st)
        with contextlib.ExitStack() as st:
            S = Sched(nc, ss)
            fresh_x()
            xo_r = Res("xo")
            store_x(c, S, xo_d, xo_r)
            S.add("sp", None, reads=[xo_r])
            S.drain()
            S.emit(st)
    return nc


def kernel(**inp):
    x = np.asarray(inp["x"], np.float32).reshape(BATCH * SEQ, D)
    cm = const_mats()
    cores = list(range(NCORE))
    shared = {}
    for l in range(DEPTH):
        H = layer_host(inp, l)
        sfx = "_%d" % l
        shared.update({"gains" + sfx: H["gains"], "lconst" + sfx: H["lconst"], "wg1" + sfx: H["ffn1_wg"], "wu1" + sfx: H["ffn1_wu"],
                       "wd1" + sfx: H["ffn1_wd"], "wg2" + sfx: H["ffn2_wg"], "wu2" + sfx: H["ffn2_wu"], "wd2" + sfx: H["ffn2_wd"],
                       "winf" + sfx: H["winf"], "winv" + sfx: H["winv"], "wsm" + sfx: H["wsm"],
                       "wout" + sfx: blk_cols(np.asarray(inp["w_out"][l], np.float32), 128),
                       "dlam" + sfx: np.ascontiguousarray(np.broadcast_to(
                           np.asarray(inp["diff_lambda"][l], np.float32).reshape(1, 128), (64, 128)))})
    maps = []
    for c in cores:
        m = dict(shared, xT=to_featmajor(x[c * T:(c + 1) * T]), cmat=cm, tabs=rope_tables(c))
        for l in range(DEPTH):
            btl, t5c = bias_tiles_host(inp, l, c)
            m["btl_%d" % l] = btl
            m["t5c"] = t5c
        maps.append(m)
    res = run_bass_kernel_spmd(_prog("f"), maps, core_ids=cores).results
    out = np.concatenate([from_featmajor(np.asarray(res[c]["xoT"])) for c in cores], axis=0)
    return np.ascontiguousarray(out.reshape(BATCH, SEQ, D).astype(np.float32))
```
